# Optimizing a Trainium2 kernel written in Bass

```python
import jax, jax.numpy as jnp
from jax import lax
import numpy as np

D_MODEL = 1024
BATCH = 16
SEQ = 2048
DEPTH = 2
DEC_BATCH = 4
DEC_SEQ = 8192
PAST_LEN = 128

FFN_DIM = 2816
SGU_DIM = 3 * D_MODEL
SGU_GROUPS = 8
SGU_CHUNK = 128
HGRN_DK = 128
HGRN_HEADS = D_MODEL // HGRN_DK
HGRN_DV = D_MODEL // HGRN_HEADS
HGRN_HK = HGRN_HEADS * HGRN_DK
HGRN_HV = HGRN_HEADS * HGRN_DV
HGRN_CHUNK = 64
N_SGU_LAYERS = (DEPTH + 1) // 2
N_HGRN_LAYERS = DEPTH // 2
EPS = 1e-6

kernel_name = 'hybrid_sgu_hgrn2_macaron_encoder'


def _rmsnorm(x, w):
    xf = x.astype(jnp.float32)
    y = xf * lax.rsqrt(jnp.mean(xf * xf, axis=-1, keepdims=True) + EPS)
    return (y * w.astype(jnp.float32)).astype(x.dtype)


def _layernorm(x, g, b):
    xf = x.astype(jnp.float32)
    mu = jnp.mean(xf, axis=-1, keepdims=True)
    xc = xf - mu
    y = xc * lax.rsqrt(jnp.mean(xc * xc, axis=-1, keepdims=True) + EPS)
    return (y * g.astype(jnp.float32) + b.astype(jnp.float32)).astype(x.dtype)


def _swiglu(h, wg, wu, wd):
    return (jax.nn.silu(h @ wg) * (h @ wu)) @ wd


def _spatial_gating(h, w_in, ln_g, ln_b, w_s, b_s, w_out):
    B, L, _ = h.shape
    z = jax.nn.gelu(h @ w_in, approximate=False)
    u, v = jnp.split(z, 2, axis=-1)
    v = _layernorm(v, ln_g, ln_b)
    v = v.reshape(B, L // SGU_CHUNK, SGU_CHUNK, SGU_GROUPS, SGU_DIM // SGU_GROUPS)
    s = jnp.einsum('gts,bnsge->bntge', w_s, v) + b_s.T[None, None, :, :, None]
    return (u * s.reshape(B, L, SGU_DIM)) @ w_out


def _gla_scan(q, k, v, g):
    B, L, H, DK = q.shape
    DV = v.shape[-1]
    n = L // HGRN_CHUNK

    def to_chunks(t):
        return t.reshape(B, n, HGRN_CHUNK, H, t.shape[-1]).transpose(1, 0, 3, 2, 4)

    mask = jnp.tril(jnp.ones((HGRN_CHUNK, HGRN_CHUNK), dtype=bool))[:, :, None]

    def step(S, inp):
        qc, kc, vc, gc = inp
        b = jnp.cumsum(gc, axis=2)
        diff = b[:, :, :, None, :] - b[:, :, None, :, :]
        decay = jnp.exp(jnp.where(mask, diff, -jnp.inf))
        a = jnp.einsum('bhtsk,bhsk->bhts', qc[:, :, :, None, :] * decay, kc)
        o = jnp.einsum('bhts,bhsv->bhtv', a, vc) + jnp.einsum('bhtk,bhkv->bhtv', qc * jnp.exp(b), S)
        b_last = b[:, :, -1:, :]
        S = jnp.exp(b_last)[:, :, 0, :, None] * S + jnp.einsum('bhsk,bhsv->bhkv', kc * jnp.exp(b_last - b), vc)
        return S, o

    S0 = jnp.zeros((B, H, DK, DV), jnp.float32)
    _, o = lax.scan(step, S0, (to_chunks(q), to_chunks(k), to_chunks(v), to_chunks(g)))
    return o.transpose(1, 0, 3, 2, 4).reshape(B, L, H, DV)


def _hgrn2(h, w_in, lb_f, lb_b, norm_w, w_out):
    B, L, _ = h.shape
    proj = (h @ w_in).astype(jnp.float32)
    q, ff, fb, i, g = jnp.split(proj, [HGRN_HK, 2 * HGRN_HK, 3 * HGRN_HK, 3 * HGRN_HK + HGRN_HV], axis=-1)

    def heads(t):
        return t.reshape(B, L, HGRN_HEADS, -1)

    def gates(logit, lb):
        f = lb + (1.0 - lb) * jax.nn.sigmoid(logit)
        return heads(1.0 - f), heads(jnp.log(f))

    q = heads(jax.nn.silu(q))
    i = heads(i)
    kf, gf = gates(ff, lb_f)
    kb, gb = gates(fb, lb_b)
    o_f = _gla_scan(q, kf, i, gf)
    o_b = jnp.flip(_gla_scan(jnp.flip(q, 1), jnp.flip(kb, 1), jnp.flip(i, 1), jnp.flip(gb, 1)), 1)
    o = o_f + o_b
    o = o * lax.rsqrt(jnp.mean(o * o, axis=-1, keepdims=True) + EPS)
    o = o.reshape(B, L, HGRN_HV) * norm_w.astype(jnp.float32) * jax.nn.silu(g)
    return o.astype(h.dtype) @ w_out


def _trunk(x, norm_w, ffn_gate, ffn_up, ffn_down, sgu_w_in, sgu_ln_g, sgu_ln_b, sgu_w_s, sgu_b_s,
           sgu_w_out, hgrn_w_in, lb, hgrn_norm_w, hgrn_w_out, final_norm):
    for layer in range(DEPTH):
        x = x + 0.5 * _swiglu(_rmsnorm(x, norm_w[layer, 0]), ffn_gate[layer, 0], ffn_up[layer, 0], ffn_down[layer, 0])
        h = _rmsnorm(x, norm_w[layer, 1])
        j = layer // 2
        if layer % 2 == 0:
            x = x + _spatial_gating(h, sgu_w_in[j], sgu_ln_g[j], sgu_ln_b[j], sgu_w_s[j], sgu_b_s[j], sgu_w_out[j])
        else:
            x = x + _hgrn2(h, hgrn_w_in[j], lb[0, layer], lb[1, layer], hgrn_norm_w[j], hgrn_w_out[j])
        x = x + 0.5 * _swiglu(_rmsnorm(x, norm_w[layer, 2]), ffn_gate[layer, 1], ffn_up[layer, 1], ffn_down[layer, 1])
    return _rmsnorm(x, final_norm)


def setup_inputs(seed: int = 0) -> dict:
    key = jax.random.key(seed)
    ks = jax.random.split(key, 20)
    nrm = jax.random.normal
    D, F = D_MODEL, FFN_DIM
    return {
        'x_prompt': nrm(ks[0], (BATCH, SEQ, D), jnp.float32),
        'x_sample': nrm(ks[1], (DEC_BATCH, DEC_SEQ, D), jnp.float32),
        'norm_w': 1.0 + 0.02 * nrm(ks[2], (DEPTH, 3, D), jnp.float32),
        'ffn_gate': nrm(ks[3], (DEPTH, 2, D, F), jnp.float32) * D ** -0.5,
        'ffn_up': nrm(ks[4], (DEPTH, 2, D, F), jnp.float32) * D ** -0.5,
        'ffn_down': nrm(ks[5], (DEPTH, 2, F, D), jnp.float32) * F ** -0.5,
        'sgu_w_in': nrm(ks[6], (N_SGU_LAYERS, D, 2 * SGU_DIM), jnp.float32) * D ** -0.5,
        'sgu_ln_g': 1.0 + 0.02 * nrm(ks[7], (N_SGU_LAYERS, SGU_DIM), jnp.float32),
        'sgu_ln_b': 0.02 * nrm(ks[8], (N_SGU_LAYERS, SGU_DIM), jnp.float32),
        'sgu_w_s': nrm(ks[9], (N_SGU_LAYERS, SGU_GROUPS, SGU_CHUNK, SGU_CHUNK), jnp.float32) * SGU_CHUNK ** -0.5,
        'sgu_b_s': 1.0 + 0.02 * nrm(ks[10], (N_SGU_LAYERS, SGU_GROUPS, SGU_CHUNK), jnp.float32),
        'sgu_w_out': nrm(ks[11], (N_SGU_LAYERS, SGU_DIM, D), jnp.float32) * SGU_DIM ** -0.5,
        'hgrn_w_in': nrm(ks[12], (N_HGRN_LAYERS, D, 3 * HGRN_HK + 2 * HGRN_HV), jnp.float32) * D ** -0.5,
        'hgrn_lb_raw': 0.1 * nrm(ks[13], (2, DEPTH, HGRN_HK), jnp.float32),
        'hgrn_norm_w': 1.0 + 0.02 * nrm(ks[14], (N_HGRN_LAYERS, HGRN_HV), jnp.float32),
        'hgrn_w_out': nrm(ks[15], (N_HGRN_LAYERS, HGRN_HV, D), jnp.float32) * HGRN_HV ** -0.5,
        'final_norm': 1.0 + 0.02 * nrm(ks[16], (D,), jnp.float32),
    }


def reference(x_prompt, x_sample, norm_w, ffn_gate, ffn_up, ffn_down, sgu_w_in, sgu_ln_g, sgu_ln_b,
              sgu_w_s, sgu_b_s, sgu_w_out, hgrn_w_in, hgrn_lb_raw, hgrn_norm_w, hgrn_w_out, final_norm):
    p = jax.nn.softmax(hgrn_lb_raw.astype(jnp.float32), axis=1)
    lb = jnp.cumsum(p, axis=1) - p[:, :1]
    y_prompt = _trunk(x_prompt, norm_w, ffn_gate, ffn_up, ffn_down, sgu_w_in, sgu_ln_g, sgu_ln_b, sgu_w_s,
                      sgu_b_s, sgu_w_out, hgrn_w_in, lb, hgrn_norm_w, hgrn_w_out, final_norm)
    y_sample = _trunk(x_sample, norm_w, ffn_gate, ffn_up, ffn_down, sgu_w_in, sgu_ln_g, sgu_ln_b, sgu_w_s,
                      sgu_b_s, sgu_w_out, hgrn_w_in, lb, hgrn_norm_w, hgrn_w_out, final_norm)
    return (y_prompt, y_sample)
```

```python
import numpy as np
from contextlib import ExitStack
import concourse.bass as bass
import concourse.mybir as mybir
from concourse.bass_utils import run_bass_kernel_spmd

F32, BF16 = mybir.dt.float32, mybir.dt.bfloat16
AF = mybir.ActivationFunctionType
ALU = mybir.AluOpType

D = 1024
FF = 2816
NF = FF // 128
SG = 3072
NE = SG // 128
NH = 8
EPS = 1e-6
NCORES = 8
NT_FULL = 8192
SEQ_PROMPT = 2048


class Tok:
    __slots__ = ("sem", "val")

    def __init__(self, sem, val):
        self.sem = sem
        self.val = val


class Buf:
    __slots__ = ("w", "r", "name")

    def __init__(self, name=""):
        self.w = None
        self.r = []
        self.name = name


class DSem:
    def __init__(self, h):
        self.h = h
        self.cnt = 0


class Eng:
    def __init__(self, name):
        self.name = name
        self.ops = []
        self.sem = None
        self.cnt = 0
        self.known = {}


class Sched:
    def __init__(self, nc, es):
        self.nc = nc
        self.es = es
        self.eng = {n: Eng(n) for n in ("pe", "act", "dve", "pool", "sp")}
        self.nsem = 0
        self.dsems = []
        self.free_dsems = []
        self.phase_dsems = []
        self.bg_dsems = []
        self._new_engine_sems()

    def _newsem(self, name):
        self.nsem += 1
        return self.es.enter_context(self.nc.semaphore(f"{name}{self.nsem}"))

    def _new_engine_sems(self):
        for n in ("pe", "act", "dve", "pool"):
            E = self.eng[n]
            E.sem = self._newsem("e" + n)
            E.cnt = 0

    def dsem(self, name="d", bg=False):
        if bg:
            d = DSem(self._newsem(name))
            self.bg_dsems.append(d)
            return d
        if self.free_dsems:
            d = self.free_dsems.pop()
        else:
            d = DSem(self._newsem(name))
            self.dsems.append(d)
        self.phase_dsems.append(d)
        return d

    def _deps(self, E, reads, writes):
        need = {}

        def add(t):
            if t is None:
                return
            if E.name == "pe" and t.sem is E.sem:
                return
            k = id(t.sem)
            if E.known.get(k, 0) >= t.val:
                return
            if k not in need or need[k].val < t.val:
                need[k] = t

        for b in reads:
            add(b.w)
        for b in writes:
            add(b.w)
            for t in b.r:
                add(t)
        for k, t in need.items():
            E.known[k] = t.val
        return list(need.values())

    def _reg(self, tok, reads, writes):
        for b in reads:
            b.r.append(tok)
        for b in writes:
            b.w = tok
            b.r = []

    def op(self, en, fn, reads=(), writes=()):
        return self.group(en, [fn], reads, writes)

    def group(self, en, fns, reads=(), writes=()):
        E = self.eng[en]
        waits = self._deps(E, reads, writes)
        E.cnt += 1
        sem = E.sem
        tok = Tok(sem, E.cnt)

        def run(e):
            for t in waits:
                e.wait_ge(t.sem, t.val)
            ins = None
            for fn in fns:
                ins = getattr(e, fn[0])(*fn[1], **fn[2])
            ins.then_inc(sem, 1)

        E.ops.append(run)
        self._reg(tok, reads, writes)
        return tok

    def dma(self, en, out, in_, ds, reads=(), writes=(), **kw):
        E = self.eng[en]
        waits = self._deps(E, reads, writes)
        ds.cnt += 16
        tok = Tok(ds.h, ds.cnt)
        h = ds.h

        def run(e):
            for t in waits:
                e.wait_ge(t.sem, t.val)
            e.dma_start(out=out, in_=in_, **kw).then_inc(h, 16)

        E.ops.append(run)
        self._reg(tok, reads, writes)
        return tok

    def barrier(self, final=False):
        toks = [Tok(self.eng[n].sem, self.eng[n].cnt) for n in ("pe", "act", "dve", "pool") if self.eng[n].cnt > 0]
        toks += [Tok(d.h, d.cnt) for d in self.dsems if d.cnt > 0]
        for n, E in self.eng.items():
            ws = []
            for t in toks:
                if E.name == "pe" and t.sem is E.sem:
                    continue
                k = id(t.sem)
                if E.known.get(k, 0) >= t.val:
                    continue
                E.known[k] = t.val
                ws.append(t)

            def run(e, ws=ws):
                for t in ws:
                    e.wait_ge(t.sem, t.val)

            E.ops.append(run)

    def flush(self):
        self.barrier()
        lists = {n: E.ops for n, E in self.eng.items()}
        for E in self.eng.values():
            E.ops = []
        with self.nc.Block() as block:
            @block.tensor
            def _(e):
                for f in lists["pe"]:
                    f(e)

            @block.scalar
            def _(e):
                for f in lists["act"]:
                    f(e)

            @block.vector
            def _(e):
                for f in lists["dve"]:
                    f(e)

            @block.gpsimd
            def _(e):
                for f in lists["pool"]:
                    f(e)

            @block.sync
            def _(e):
                for f in lists["sp"]:
                    f(e)
        self.free_dsems.extend(self.phase_dsems)
        self.phase_dsems = []


def I(name, *a, **kw):
    return (name, a, kw)


class DramAct:
    def __init__(self, ap, nblk, name, sub=None):
        self.ap = ap
        if sub is None:
            self.bufs = [Buf(f"{name}{i}") for i in range(nblk)]
        else:
            self.bufs = [[Buf(f"{name}{i}_{j}") for j in range(sub)] for i in range(nblk)]


class Prog:
    def __init__(self, NT, stop_after=None, debug=False):
        assert NT % 512 == 0
        self.debug = debug
        self.NT = NT
        self.NB = NT // 128
        self.NTL = NT // 512
        self.stop_after = stop_after
        self.nc = bass.Bass("TRN2", target_bir_lowering=False)
        self.es = ExitStack()
        self.S = Sched(self.nc, self.es)
        self.build()
        self.es.close()

    def dram(self, name, shape, dt, kind="Internal"):
        return self.nc.dram_tensor(name, list(shape), dt, kind=kind).ap()

    def sb(self, stack, name, shape, dt):
        self._uid = getattr(self, "_uid", 0) + 1
        return stack.enter_context(self.nc.sbuf_tensor(f"{name}_{self._uid}", list(shape), dt))

    def dbg(self, name, ap, bufs, dt, eng="sp"):
        if not getattr(self, "debug", False):
            return
        out = self.dram("dbg_" + name, list(ap.shape), dt, "ExternalOutput")
        self.S.dma(eng, out, ap, self.S.dsem("dbg"), reads=bufs)

    def build(self):
        nc, S, NT, NB = self.nc, self.S, self.NT, self.NB
        es = self.es
        self.x_in = DramAct(self.dram("x", [NT, D], F32, "ExternalInput"), NB, "xin")
        self.y_out = DramAct(self.dram("y", [NT, D], F32, "ExternalOutput"), NB, "yout")
        self.norm_w = self.dram("norm_w", [2, 3, D], F32, "ExternalInput")
        self.ffn_gate = self.dram("ffn_gate", [2, 2, D, FF], F32, "ExternalInput")
        self.ffn_up = self.dram("ffn_up", [2, 2, D, FF], F32, "ExternalInput")
        self.ffn_down = self.dram("ffn_down", [2, 2, FF, D], F32, "ExternalInput")
        self.sgu_w_in = self.dram("sgu_w_in", [D, 2 * SG], F32, "ExternalInput")
        self.sgu_ln_g = self.dram("sgu_ln_g", [SG], F32, "ExternalInput")
        self.sgu_ln_b = self.dram("sgu_ln_b", [SG], F32, "ExternalInput")
        self.sgu_w_s = self.dram("sgu_w_s", [8, 128, 128], F32, "ExternalInput")
        self.sgu_b_s = self.dram("sgu_b_s", [8, 128], F32, "ExternalInput")
        self.sgu_w_out = self.dram("sgu_w_out", [SG, D], F32, "ExternalInput")
        self.hgrn_w_in = self.dram("hgrn_w_in", [D, 5 * D], F32, "ExternalInput")
        self.hgrn_lb_raw = self.dram("hgrn_lb_raw", [2, 2, D], F32, "ExternalInput")
        self.hgrn_norm_w = self.dram("hgrn_norm_w", [D], F32, "ExternalInput")
        self.hgrn_w_out = self.dram("hgrn_w_out", [D, D], F32, "ExternalInput")
        self.final_norm = self.dram("final_norm", [D], F32, "ExternalInput")
        self.carry = self.dram("carry", [128, 2 * NB], F32, "ExternalInput")
        self.xa = DramAct(self.dram("xa", [NT, D], F32), NB, "xa")
        self.xb = DramAct(self.dram("xb", [NT, D], F32), NB, "xb")
        self.sT = DramAct(self.dram("sT", [SG, NT], BF16), self.NTL, "sT")
        NTL = self.NTL
        self.H = dict(
            qe=[DramAct(self.dram(f"qe{d}", [D, NT], BF16), NTL, f"qe{d}", 8) for d in range(2)],
            kd=[DramAct(self.dram(f"kd{d}", [D, NT], BF16), NTL, f"kd{d}", 8) for d in range(2)],
            ks=[DramAct(self.dram(f"ks{d}", [NT, D], BF16), NTL, f"ks{d}", 1) for d in range(2)],
            v=DramAct(self.dram("vtok", [NT, D], BF16), NTL, "vtok", 1),
            gs=DramAct(self.dram("gsT", [D, NT], BF16), NTL, "gsT", 8),
            of=DramAct(self.dram("ofT", [D, NT], F32), NTL, "ofT", 1),
        )

        self.ident = self.sb(es, "ident", [128, 128], BF16)
        self.identf = self.sb(es, "identf", [128, 128], F32)
        self.ssA = self.sb(es, "ssA", [128, NB], F32)
        self.ssB = self.sb(es, "ssB", [128, NB], F32)
        self.rstd = self.sb(es, "rstd", [128, NB], F32)
        self.B_ident = Buf("ident")
        self.B_ssA = Buf("ssA")
        self.B_ssB = Buf("ssB")
        self.B_rstd = Buf("rstd")
        self.bank = [es.enter_context(nc.psum_tensor(f"bank{i}", [128, 512], F32)) for i in range(8)]
        self.B_bank = [Buf(f"bank{i}") for i in range(8)]

        S.op("pool", I("memset", self.identf[:], 0.0), writes=[self.B_ident])
        S.op("pool", I("affine_select", out=self.identf[:], in_=self.identf[:], pattern=[[-1, 128]],
                                               compare_op=ALU.not_equal, fill=1.0, base=0, channel_multiplier=1),
             writes=[self.B_ident])
        S.op("dve", I("tensor_copy", out=self.ident[:], in_=self.identf[:]), writes=[self.B_ident])
        S.op("pool", I("memset", self.ssA[:], 0.0), writes=[self.B_ssA])
        S.op("pool", I("memset", self.ssB[:], 0.0), writes=[self.B_ssB])

        w1s = ExitStack()
        W1 = self.ffn_weights(w1s, 0, 0, direct=True)
        self.phase_stats(self.x_in, self.ssA, self.B_ssA)
        self.convert_all()
        self.phase_ffn(self.x_in, self.xa, 0, 0, 0, self.ssA, self.B_ssA, self.ssB, self.B_ssB, W=W1)
        w1s.close()
        if self.stop_after == "ffn1":
            self.phase_final(self.xa, self.ssB, self.B_ssB, copy_only=True)
            return
        self.phase_sgu_a(self.xa, self.ssB, self.B_ssB, self.sT)
        self.phase_sgu_b(self.xa, self.xb, self.ssB, self.B_ssB, self.ssA, self.B_ssA, self.sT)
        if self.stop_after == "sgu":
            self.phase_final(self.xb, self.ssA, self.B_ssA, copy_only=True)
            return
        self.phase_ffn(self.xb, self.xa, 0, 2, 1, self.ssA, self.B_ssA, self.ssB, self.B_ssB)
        self.phase_ffn(self.xa, self.xb, 1, 0, 0, self.ssB, self.B_ssB, self.ssA, self.B_ssA)
        if self.stop_after == "ffn3":
            self.phase_final(self.xb, self.ssA, self.B_ssA, copy_only=True)
            return
        with ExitStack() as hes:
            self.hgrn_consts(hes)
            self.phase_hgrn_a(self.xb, self.ssA, self.B_ssA, self.H)
            self.phase_hgrn_scan(0, self.H)
            self.phase_hgrn_scan(1, self.H, self.xb, self.xa, self.ssB, self.B_ssB)
        if self.stop_after == "hgrn":
            self.phase_final(self.xa, self.ssB, self.B_ssB, copy_only=True)
            return
        self.phase_ffn(self.xa, self.y_out, 1, 2, 1, self.ssB, self.B_ssB, self.ssA, self.B_ssA, final=True)

    def phase_stats(self, src, ss, B_ss):
        nc, S = self.nc, self.S
        with ExitStack() as ps:
            xl = [self.sb(ps, f"st_x{i}", [128, D], F32) for i in range(3)]
            junk = self.sb(ps, "st_junk", [128, D], BF16)
            Bx = [Buf() for _ in range(3)]
            Bj = Buf()
            ds = [S.dsem("stx") for _ in range(3)]
            for i in range(self.NB):
                k = i % 3
                S.dma("sp", xl[k][:], src.ap[i * 128:(i + 1) * 128, :], ds[k], reads=[src.bufs[i]], writes=[Bx[k]])
                S.op("act", I("activation", out=junk[:], in_=xl[k][:], func=AF.Square,
                                                            accum_out=ss[:, i:i + 1]),
                     reads=[Bx[k], B_ss], writes=[Bj])
            B_ss.w = Tok(S.eng["act"].sem, S.eng["act"].cnt)
            S.flush()

    def compute_rstd(self, ss, B_ss, ps):
        S = self.S
        tmp = self.sb(ps, "rs_tmp", [128, self.NB], F32)
        Bt = Buf()
        S.op("act", I("activation", out=tmp[:], in_=ss[:], func=AF.Sqrt, bias=EPS, scale=1.0 / D),
             reads=[B_ss], writes=[Bt])
        S.op("dve", I("reciprocal", out=self.rstd[:], in_=tmp[:]), reads=[Bt], writes=[self.B_rstd])

    def convert_all(self):
        S = self.S
        self.wbf, self.Bconv = {}, {}
        jobs = [("sgu_in", self.sgu_w_in, [D, 2 * SG]), ("sgu_out", self.sgu_w_out, [SG, D])]
        for (l, j) in ((0, 1), (1, 0)):
            jobs += [(f"gate{l}{j}", self.ffn_gate[l, j], [D, FF]), (f"up{l}{j}", self.ffn_up[l, j], [D, FF]),
                     (f"down{l}{j}", self.ffn_down[l, j], [FF, D])]
        jobs += [("hg_in", self.hgrn_w_in, [D, 5 * D]), ("hg_out", self.hgrn_w_out, [D, D])]
        jobs += [("gate11", self.ffn_gate[1, 1], [D, FF]), ("up11", self.ffn_up[1, 1], [D, FF]),
                 ("down11", self.ffn_down[1, 1], [FF, D])]
        prev = []
        cvs = [S.dsem("cv", bg=True) for _ in range(3)]
        RC = 256
        n = 0
        for key, src, shp in jobs:
            dst = self.dram("wbf_" + key, shp, BF16)
            self.wbf[key] = dst
            self.Bconv[key] = []
            for r0 in range(0, shp[0], RC):
                B = Buf(f"conv_{key}_{r0}")
                S.dma("pool", dst[r0:r0 + RC, :], src[r0:r0 + RC, :], cvs[n % 3], reads=prev[-3:-1], writes=[B],
                      max_dma_last_dim=4096)
                n += 1
                prev.append(B)
                self.Bconv[key].append(B)

    class WT:
        def __init__(self, tile, bounds):
            self.tile = tile
            self.bounds = bounds
            self.B = [Buf() for _ in bounds]

        def rb(self, lo, hi):
            return [b for (a, c), b in zip(self.bounds, self.B) if a < hi and c > lo]

    def load_w(self, ps, name, nk, ncols, axis, bounds, key=None, col0=0, direct=None):
        S = self.S
        t = self.sb(ps, name, [128, nk, ncols], BF16)
        W = Prog.WT(t, bounds)
        if direct is not None:
            v = direct.rearrange("(k p) f -> p k f", p=128)
        else:
            v = self.wbf[key].rearrange("(k p) f -> p k f", p=128)
        for (lo, hi), B in zip(bounds, W.B):
            if axis == "col":
                o, i = t[:, :, lo:hi], v[:, :, col0 + lo:col0 + hi]
            else:
                o, i = t[:, lo:hi, :], v[:, lo:hi, col0:col0 + ncols]
            if direct is not None:
                self._w1prev = getattr(self, "_w1prev", [])
                S.dma("pool", o, i, S.dsem("w1", bg=True), reads=self._w1prev[-3:-2], writes=[B],
                      max_dma_last_dim=4096)
                self._w1prev.append(B)
            else:
                S.dma("sp", o, i, S.dsem("wl"), reads=self.Bconv[key], writes=[B])
        return W

    def ffn_weights(self, ps, l, j, direct=False):
        cb = [(0, 768), (768, 1536), (1536, 2176), (2176, 2816)]
        rb = [(0, 6), (6, 12), (12, 17), (17, 22)]
        if direct:
            Wg = self.load_w(ps, "Wg", 8, FF, "col", cb, direct=self.ffn_gate[l, j])
            Wu = self.load_w(ps, "Wu", 8, FF, "col", cb, direct=self.ffn_up[l, j])
            Wd = self.load_w(ps, "Wd", NF, D, "row", rb, direct=self.ffn_down[l, j])
        else:
            Wg = self.load_w(ps, "Wg", 8, FF, "col", cb, key=f"gate{l}{j}")
            Wu = self.load_w(ps, "Wu", 8, FF, "col", cb, key=f"up{l}{j}")
            Wd = self.load_w(ps, "Wd", NF, D, "row", rb, key=f"down{l}{j}")
        return Wg, Wu, Wd

    def load_weight_bf16(self, dst_tile, src_ap, nk, ncols, ds, B):
        S = self.S
        for k in range(nk):
            S.dma("pool", dst_tile[:, k, :], src_ap[k * 128:(k + 1) * 128, :], ds, max_dma_last_dim=4096)

    class LNT:
        pass

    def lnt_alloc(self, ps, src, normw_row, evac_eng="act"):
        S = self.S
        L = Prog.LNT()
        L.src = src
        L.xld = [self.sb(ps, f"xld{i}", [128, D], F32) for i in range(2)]
        L.Bxld = [Buf() for _ in range(2)]
        L.dxld = [S.dsem("xld") for _ in range(2)]
        L.xn = [self.sb(ps, f"xn{i}", [128, D], BF16) for i in range(2)]
        L.Bxn = [Buf() for _ in range(2)]
        L.xnT = [self.sb(ps, f"xnT{i}", [128, 8, 512], BF16) for i in range(2)]
        L.BxnT = [[Buf() for _ in range(4)] for _ in range(2)]
        L.wb = self.sb(ps, "wb", [128, D], F32)
        L.Bwb = Buf()
        L.dwb = S.dsem("wb")
        S.dma("sp", L.wb[:], normw_row.partition_broadcast(128), L.dwb, writes=[L.Bwb])
        L.cnt = 0
        L.slot = {}
        L.evac_eng = evac_eng
        return L

    def lnt_dma(self, L, t, s):
        S = self.S
        i = t * 4 + s
        k = L.cnt % 2
        L.cnt += 1
        L.slot[(t, s)] = k
        S.dma("sp", L.xld[k][:], L.src.ap[i * 128:(i + 1) * 128, :], L.dxld[k],
              reads=[L.src.bufs[i]], writes=[L.Bxld[k]])

    def lnt_sub(self, L, t, s):
        S = self.S
        i = t * 4 + s
        k = L.slot.pop((t, s))
        S.op("dve", I("scalar_tensor_tensor", out=L.xn[k][:], in0=L.xld[k][:], scalar=self.rstd[:, i:i + 1],
                      in1=L.wb[:], op0=ALU.mult, op1=ALU.mult),
             reads=[L.Bxld[k], self.B_rstd, L.Bwb], writes=[L.Bxn[k]])
        self.lnt_transpose_one(L, t, s, k)

    def lnt_steps(self, L, t):
        return [lambda: (self.lnt_dma(L, t, 0), self.lnt_dma(L, t, 1)),
                lambda: (self.lnt_sub(L, t, 0), self.lnt_dma(L, t, 2)),
                lambda: (self.lnt_sub(L, t, 1), self.lnt_dma(L, t, 3)),
                lambda: self.lnt_sub(L, t, 2),
                lambda: self.lnt_sub(L, t, 3)]

    def lnt_load(self, L, t):
        for f in self.lnt_steps(L, t):
            f()

    def lnt_transpose_one(self, L, t, s, k):
        S = self.S
        tb = t % 2
        bk = 7
        pt = self.bank[bk][:].bitcast(BF16)
        fns = [(I("transpose", out=pt[:, c * 128:(c + 1) * 128], in_=L.xn[k][:, c * 128:(c + 1) * 128],
                                           identity=self.ident[:])) for c in range(8)]
        S.group("pe", fns, reads=[L.Bxn[k], self.B_ident], writes=[self.B_bank[bk]])
        if L.evac_eng == "act":
            S.op("act", I("activation", out=L.xnT[tb][:, :, s * 128:(s + 1) * 128],
                          in_=pt.rearrange("p (c t) -> p c t", c=8), func=AF.Copy),
                 reads=[self.B_bank[bk]], writes=[L.BxnT[tb][s]])
        else:
            S.op("dve", I("tensor_copy", out=L.xnT[tb][:, :, s * 128:(s + 1) * 128],
                          in_=pt.rearrange("p (c t) -> p c t", c=8)),
                 reads=[self.B_bank[bk]], writes=[L.BxnT[tb][s]])

    def phase_ffn(self, src, dst, layer, normi, ffni, ss_in, B_ss_in, ss_out, B_ss_out, W=None, final=False):
        nc, S = self.nc, self.S
        with ExitStack() as ps:
            WG, WU, WD = W if W is not None else self.ffn_weights(ps, layer, ffni)
            Wg, Wu, Wd = WG.tile, WU.tile, WD.tile
            self.compute_rstd(ss_in, B_ss_in, ps)
            L = self.lnt_alloc(ps, src, self.norm_w[layer, normi])
            hT = self.sb(ps, "hT", [128, NF, 512], BF16)
            BhT = [Buf() for _ in range(NF)]
            sg = [self.sb(ps, f"sg{i}", [128, 512], F32) for i in range(2)]
            Bsg = [Buf() for _ in range(2)]
            xr = [self.sb(ps, f"xr{i}", [128, D], F32) for i in range(2)]
            Bxr = [[Buf(), Buf()] for _ in range(2)]
            dxr = [S.dsem("xr") for _ in range(2)]
            junk = self.sb(ps, "fjunk", [128, D], BF16)
            Bj = Buf()
            S.op("dve", I("memset", ss_out[:], 0.0), writes=[B_ss_out])
            Bfs, Bwfb = Buf(), Buf()
            if final:
                fs = self.sb(ps, "ffs", [128, 4], F32)
                wfb = self.sb(ps, "wfb", [128, D], F32)
                S.dma("sp", wfb[:], self.final_norm.partition_broadcast(128), S.dsem("wfb"), writes=[Bwfb])

            self.lnt_load(L, 0)
            self.dbg("xnT0", L.xnT[0][:], L.BxnT[0], BF16)
            self.dbg("rstd", self.rstd[:], [self.B_rstd], F32)
            gu = 0
            dn = 0
            xrc = 0
            for t in range(self.NTL):
                tb = t % 2
                for f in range(NF):
                    bg, bu = (gu % 2) * 2, (gu % 2) * 2 + 1
                    gu += 1
                    fg = [(I("matmul", self.bank[bg][:], lhsT=Wg[:, k, f * 128:(f + 1) * 128],
                                                              rhs=L.xnT[tb][:, k, :], start=(k == 0), stop=(k == 7)))
                          for k in range(8)]
                    S.group("pe", fg, reads=WG.rb(f * 128, f * 128 + 128) + L.BxnT[tb], writes=[self.B_bank[bg]])
                    fu = [(I("matmul", self.bank[bu][:], lhsT=Wu[:, k, f * 128:(f + 1) * 128],
                                                              rhs=L.xnT[tb][:, k, :], start=(k == 0), stop=(k == 7)))
                          for k in range(8)]
                    S.group("pe", fu, reads=WU.rb(f * 128, f * 128 + 128) + L.BxnT[tb], writes=[self.B_bank[bu]])
                    sk = f % 2
                    if self.debug and t == 0 and f == 0:
                        self.dbgt = self.sb(ps, "dbgt", [128, 512], F32)
                        Bd = Buf()
                        S.op("dve", I("tensor_copy", out=self.dbgt[:], in_=self.bank[bg][:]),
                             reads=[self.B_bank[bg]], writes=[Bd])
                        self.dbg("g0", self.dbgt[:], [Bd], F32)
                    S.op("act", I("activation", out=sg[sk][:], in_=self.bank[bg][:], func=AF.Silu),
                         reads=[self.B_bank[bg]], writes=[Bsg[sk]])
                    S.op("dve", I("tensor_tensor", out=hT[:, f, :], in0=sg[sk][:],
                                                                            in1=self.bank[bu][:], op=ALU.mult),
                         reads=[Bsg[sk], self.B_bank[bu]], writes=[BhT[f]])
                    if t == 0 and f in (0, 21):
                        self.dbg(f"sg{f}", sg[sk][:], [Bsg[sk]], F32)
                    if f == 10 and t + 1 < self.NTL:
                        self.lnt_load(L, t + 1)
                if t == 0:
                    self.dbg("hT0", hT[:], BhT, BF16)
                    self.dbg("xnT0b", L.xnT[0][:], L.BxnT[0], BF16)
                for s in range(4):
                    i = t * 4 + s
                    xk = xrc % 2
                    xrc += 1
                    S.dma("sp", xr[xk][:], src.ap[i * 128:(i + 1) * 128, :], dxr[xk], reads=[src.bufs[i]],
                          writes=Bxr[xk])
                    for nh in range(2):
                        bo = 4 + dn % 3
                        dn += 1
                        fd = [(I("matmul",
                            self.bank[bo][:], lhsT=hT[:, f, s * 128:(s + 1) * 128],
                            rhs=Wd[:, f, nh * 512:(nh + 1) * 512], start=(f == 0), stop=(f == NF - 1)))
                            for f in range(NF)]
                        S.group("pe", fd, reads=WD.B + BhT, writes=[self.B_bank[bo]])
                        S.op("dve", I("scalar_tensor_tensor",
                            out=xr[xk][:, nh * 512:(nh + 1) * 512], in0=self.bank[bo][:], scalar=0.5,
                            in1=xr[xk][:, nh * 512:(nh + 1) * 512], op0=ALU.mult, op1=ALU.add),
                            reads=[self.B_bank[bo]], writes=[Bxr[xk][nh]])
                    S.op("act", I("activation", out=junk[:], in_=xr[xk][:], func=AF.Square,
                                                                  accum_out=ss_out[:, i:i + 1]),
                         reads=Bxr[xk] + [B_ss_out], writes=[Bj, Bfs])
                    if final:
                        S.op("act", I("activation", out=fs[:, xk:xk + 1], in_=ss_out[:, i:i + 1], func=AF.Sqrt,
                                      bias=EPS, scale=1.0 / D), reads=[Bfs], writes=[Bfs])
                        S.op("dve", I("reciprocal", out=fs[:, 2 + xk:3 + xk], in_=fs[:, xk:xk + 1]),
                             reads=[Bfs], writes=[Bfs])
                        S.op("dve", I("scalar_tensor_tensor", out=xr[xk][:], in0=xr[xk][:],
                                      scalar=fs[:, 2 + xk:3 + xk], in1=wfb[:], op0=ALU.mult, op1=ALU.mult),
                             reads=[Bfs, Bwfb], writes=Bxr[xk])
                    S.dma("sp", dst.ap[i * 128:(i + 1) * 128, :], xr[xk][:], dxr[xk], reads=Bxr[xk],
                          writes=[dst.bufs[i]])
            B_ss_out.w = Tok(S.eng["act"].sem, S.eng["act"].cnt)
            S.flush()


    def resid_alloc(self, ps, src, dst, ss_out, B_ss_out):
        S = self.S
        R = Prog.LNT()
        R.src, R.dst, R.ss_out, R.B_ss_out = src, dst, ss_out, B_ss_out
        R.xr = [self.sb(ps, f"xr{i}", [128, D], F32) for i in range(2)]
        R.Bxr = [[Buf(), Buf()] for _ in range(2)]
        R.dxr = [S.dsem("xr") for _ in range(2)]
        R.junk = self.sb(ps, "rjunk", [128, D], BF16)
        R.Bjunk = Buf()
        R.cnt = 0
        S.op("dve", I("memset", ss_out[:], 0.0), writes=[B_ss_out])
        return R

    def resid_begin(self, R, i):
        S = self.S
        xk = R.cnt % 2
        R.cnt += 1
        S.dma("sp", R.xr[xk][:], R.src.ap[i * 128:(i + 1) * 128, :], R.dxr[xk], reads=[R.src.bufs[i]],
              writes=R.Bxr[xk])
        return xk

    def resid_add(self, R, xk, nh, bo, scale):
        S = self.S
        S.op("dve", I("scalar_tensor_tensor", out=R.xr[xk][:, nh * 512:(nh + 1) * 512], in0=self.bank[bo][:],
                      scalar=scale, in1=R.xr[xk][:, nh * 512:(nh + 1) * 512], op0=ALU.mult, op1=ALU.add),
             reads=[self.B_bank[bo]], writes=[R.Bxr[xk][nh]])

    def resid_end(self, R, xk, i):
        S = self.S
        S.op("act", I("activation", out=R.junk[:], in_=R.xr[xk][:], func=AF.Square,
                      accum_out=R.ss_out[:, i:i + 1]), reads=R.Bxr[xk] + [R.B_ss_out], writes=[R.Bjunk])
        S.dma("sp", R.dst.ap[i * 128:(i + 1) * 128, :], R.xr[xk][:], R.dxr[xk], reads=R.Bxr[xk],
              writes=[R.dst.bufs[i]])

    def resid_finish(self, R):
        R.B_ss_out.w = Tok(self.S.eng["act"].sem, self.S.eng["act"].cnt)

    def load_cols(self, ps, name, vec_ap, ncol):
        S = self.S
        t = self.sb(ps, name, [128, ncol], F32)
        B = Buf()
        S.dma("sp", t[:], vec_ap.rearrange("(c p) -> p c", p=128), S.dsem(name), writes=[B],
              allow_slow_non_contiguous=True)
        return t, B

    def phase_sgu_a(self, src, ss_in, B_ss_in, sT):
        S = self.S
        with ExitStack() as ps:
            WV = self.load_w(ps, "Wv", 8, SG, "col", [(n * 512, (n + 1) * 512) for n in range(6)], key="sgu_in",
                             col0=SG)
            Wv = WV.tile
            self.compute_rstd(ss_in, B_ss_in, ps)
            L = self.lnt_alloc(ps, src, self.norm_w[0, 1])
            lng, Blng = self.load_cols(ps, "lng", self.sgu_ln_g, NE)
            lnb, Blnb = self.load_cols(ps, "lnb", self.sgu_ln_b, NE)
            wsf = self.sb(ps, "wsf", [128, 8, 128], F32)
            wsb = self.sb(ps, "wsb", [128, 8, 128], BF16)
            wsT = self.sb(ps, "wsT", [128, 8, 128], BF16)
            ones = self.sb(ps, "ones", [128, 128], BF16)
            bsb = self.sb(ps, "bsb", [128, 8, 128], F32)
            C = self.sb(ps, "Cmat", [128, NE, 128], F32)
            Bwsf, Bwsb, BwsT, Bones, Bbsb, BC = Buf(), Buf(), Buf(), Buf(), Buf(), Buf()
            S.dma("sp", wsf[:], self.sgu_w_s.rearrange("g t s -> t g s"), S.dsem("wsf"), writes=[Bwsf])
            S.dma("sp", bsb[:].rearrange("p g t -> p (g t)"),
                  self.sgu_b_s.rearrange("g t -> (g t)").partition_broadcast(128), S.dsem("bsb"), writes=[Bbsb])
            S.op("dve", I("tensor_copy", out=wsb[:], in_=wsf[:]), reads=[Bwsf], writes=[Bwsb])
            S.op("dve", I("memset", ones[:], 1.0), writes=[Bones])
            pt = self.bank[7][:].bitcast(BF16)
            S.group("pe", [I("transpose", out=pt[:, g * 128:(g + 1) * 128], in_=wsb[:, g, :], identity=self.ident[:])
                           for g in range(8)], reads=[Bwsb, self.B_ident], writes=[self.B_bank[7]])
            S.op("act", I("activation", out=wsT[:].rearrange("p g t -> p (g t)"), in_=pt, func=AF.Copy),
                 reads=[self.B_bank[7]], writes=[BwsT])
            for half in range(2):
                S.group("pe", [I("matmul", self.bank[half][:, j * 128:(j + 1) * 128], lhsT=ones[:],
                                 rhs=wsT[:, half * 4 + j, :], start=True, stop=True) for j in range(4)],
                        reads=[Bones, BwsT], writes=[self.B_bank[half]])
            for ec in range(NE):
                g = ec // 3
                S.op("dve", I("scalar_tensor_tensor", out=C[:, ec, :],
                              in0=self.bank[g // 4][:, (g % 4) * 128:(g % 4 + 1) * 128], scalar=lnb[:, ec:ec + 1],
                              in1=bsb[:, g, :], op0=ALU.mult, op1=ALU.add),
                     reads=[self.B_bank[g // 4], Blnb, Bbsb], writes=[BC])
            vf = [self.sb(ps, f"vf{i}", [128, SG], F32) for i in range(4)]
            Bvf = [Buf() for _ in range(4)]
            vh = [self.sb(ps, f"vh{i}", [128, SG], BF16) for i in range(4)]
            Bvh = [Buf() for _ in range(4)]
            sTt = self.sb(ps, "sTt", [128, NE, 512], BF16)
            BsTt = [Buf() for _ in range(NE)]
            dsT = S.dsem("sTst")
            sjunk = self.sb(ps, "sjunk", [128, SG // 2], BF16)
            Bsjunk = Buf()
            s1 = self.sb(ps, "s1", [128, 24], F32)
            s2 = self.sb(ps, "s2", [128, 8], F32)
            st = [self.sb(ps, f"stt{i}", [128, 4], F32) for i in range(7)]
            Bs1, Bs2, Bst = Buf(), Buf(), Buf()
            sT_v = sT.ap.rearrange("(c p) t -> p c t", p=128)
            def spatial(ec):
                g = ec // 3
                b = ec % 2
                S.group("pe", [I("matmul", self.bank[b][:, c * 128:(c + 1) * 128],
                                 lhsT=vh[c][:, ec * 128:(ec + 1) * 128], rhs=wsT[:, g, :], start=True, stop=True)
                               for c in range(4)], reads=Bvh + [BwsT], writes=[self.B_bank[b]])
                S.op("dve", I("scalar_tensor_tensor", out=sTt[:, ec, :].rearrange("p (c t) -> p c t", c=4),
                              in0=self.bank[b][:].rearrange("p (c t) -> p c t", c=4), scalar=lng[:, ec:ec + 1],
                              in1=C[:, ec, :].unsqueeze(1).broadcast_to([128, 4, 128]), op0=ALU.mult, op1=ALU.add),
                     reads=[self.B_bank[b], Blng, BC], writes=[BsTt[ec]])

            self.lnt_load(L, 0)
            bk = 0
            prev_t = None
            for t in range(self.NTL):
                tb = t % 2
                S.op("dve", I("memset", s1[:], 0.0), writes=[Bs1])
                S.op("dve", I("memset", s2[:], 0.0), writes=[Bs2])
                for c in range(4):
                    for n in range(6):
                        if prev_t is not None:
                            spatial(c * 6 + n)
                            if c * 6 + n == NE - 1:
                                S.dma("sp", sT_v[:, :, prev_t * 512:(prev_t + 1) * 512], sTt[:], dsT, reads=BsTt,
                                      writes=[sT.bufs[prev_t]])
                        b = 2 + bk % 4
                        bk += 1
                        S.group("pe", [I("matmul", self.bank[b][:], lhsT=L.xnT[tb][:, k, c * 128:(c + 1) * 128],
                                         rhs=Wv[:, k, n * 512:(n + 1) * 512], start=(k == 0), stop=(k == 7))
                                       for k in range(8)], reads=WV.rb(n * 512, (n + 1) * 512) + L.BxnT[tb],
                                writes=[self.B_bank[b]])
                        S.op("act", I("activation", out=vf[c][:, n * 512:(n + 1) * 512], in_=self.bank[b][:],
                                      func=AF.Gelu, accum_out=s1[:, c * 6 + n:c * 6 + n + 1]),
                             reads=[self.B_bank[b], Bs1], writes=[Bvf[c]])
                    for hh in range(2):
                        S.op("act", I("activation", out=sjunk[:], in_=vf[c][:, hh * (SG // 2):(hh + 1) * (SG // 2)],
                                      func=AF.Square, accum_out=s2[:, c * 2 + hh:c * 2 + hh + 1]),
                             reads=[Bvf[c], Bs2], writes=[Bsjunk])
                    if c == 1 and t + 1 < self.NTL:
                        self.lnt_load(L, t + 1)
                Bs1.w = Bs2.w = Tok(S.eng["act"].sem, S.eng["act"].cnt)
                msum, mean, msq, var, rs, nmr, s2s = st
                S.op("dve", I("tensor_reduce", out=s2s[:], in_=s2[:].rearrange("p (c n) -> p c n", n=2),
                              axis=mybir.AxisListType.X, op=ALU.add), reads=[Bs2], writes=[Bst])
                S.op("dve", I("tensor_reduce", out=msum[:], in_=s1[:].rearrange("p (c n) -> p c n", n=6),
                              axis=mybir.AxisListType.X, op=ALU.add), reads=[Bs1], writes=[Bst])
                S.op("dve", I("tensor_scalar", out=mean[:], in0=msum[:], scalar1=1.0 / SG, scalar2=None, op0=ALU.mult),
                     reads=[Bst], writes=[Bst])
                S.op("dve", I("tensor_tensor", out=msq[:], in0=mean[:], in1=mean[:], op=ALU.mult),
                     reads=[Bst], writes=[Bst])
                S.op("dve", I("scalar_tensor_tensor", out=var[:], in0=s2s[:], scalar=1.0 / SG, in1=msq[:],
                              op0=ALU.mult, op1=ALU.subtract), reads=[Bst, Bs2], writes=[Bst])
                S.op("act", I("activation", out=var[:], in_=var[:], func=AF.Sqrt, bias=EPS, scale=1.0),
                     reads=[Bst], writes=[Bst])
                S.op("dve", I("reciprocal", out=rs[:], in_=var[:]), reads=[Bst], writes=[Bst])
                S.op("dve", I("scalar_tensor_tensor", out=nmr[:], in0=mean[:], scalar=-1.0, in1=rs[:],
                              op0=ALU.mult, op1=ALU.mult), reads=[Bst], writes=[Bst])
                for c in range(4):
                    S.op("dve", I("tensor_scalar", out=vh[c][:], in0=vf[c][:], scalar1=rs[:, c:c + 1],
                                  scalar2=nmr[:, c:c + 1], op0=ALU.mult, op1=ALU.add),
                         reads=[Bvf[c], Bst], writes=[Bvh[c]])
                prev_t = t
            for ec in range(NE):
                spatial(ec)
            S.dma("sp", sT_v[:, :, prev_t * 512:(prev_t + 1) * 512], sTt[:], dsT, reads=BsTt,
                  writes=[sT.bufs[prev_t]])
            S.flush()

    def phase_sgu_b(self, src, dst, ss_in, B_ss_in, ss_out, B_ss_out, sT):
        S = self.S
        with ExitStack() as ps:
            WU = self.load_w(ps, "Wuin", 8, SG, "col", [(n * 768, (n + 1) * 768) for n in range(4)], key="sgu_in")
            WO = self.load_w(ps, "Wo", NE, D, "row", [(n * 6, (n + 1) * 6) for n in range(4)], key="sgu_out")
            Wu, Wo = WU.tile, WO.tile
            self.compute_rstd(ss_in, B_ss_in, ps)
            L = self.lnt_alloc(ps, src, self.norm_w[0, 1])
            R = self.resid_alloc(ps, src, dst, ss_out, B_ss_out)
            sTt = [self.sb(ps, f"sTb{i}", [128, NE, 512], BF16) for i in range(2)]
            BsTt = [[Buf() for _ in range(NE)] for _ in range(2)]
            dsT = [S.dsem("sTld") for _ in range(2)]
            ug = [self.sb(ps, f"ug{i}", [128, 512], BF16) for i in range(2)]
            Bug = [Buf(), Buf()]
            sT_v = sT.ap.rearrange("(c p) t -> p c t", p=128)
            self.lnt_load(L, 0)
            S.dma("sp", sTt[0][:], sT_v[:, :, 0:512], dsT[0], reads=[sT.bufs[0]], writes=BsTt[0])
            gu = 0
            dn = 0
            for t in range(self.NTL):
                tb = t % 2
                for ec in range(NE):
                    b = gu % 4
                    gu += 1
                    S.group("pe", [I("matmul", self.bank[b][:], lhsT=Wu[:, k, ec * 128:(ec + 1) * 128],
                                     rhs=L.xnT[tb][:, k, :], start=(k == 0), stop=(k == 7)) for k in range(8)],
                            reads=WU.rb(ec * 128, ec * 128 + 128) + L.BxnT[tb], writes=[self.B_bank[b]])
                    uk = ec % 2
                    S.op("act", I("activation", out=ug[uk][:], in_=self.bank[b][:], func=AF.Gelu),
                         reads=[self.B_bank[b]], writes=[Bug[uk]])
                    S.op("pool", I("tensor_tensor", out=sTt[tb][:, ec, :], in0=ug[uk][:], in1=sTt[tb][:, ec, :],
                                   op=ALU.mult), reads=[Bug[uk]], writes=[BsTt[tb][ec]])
                    if ec == 10 and t + 1 < self.NTL:
                        self.lnt_load(L, t + 1)
                        S.dma("sp", sTt[1 - tb][:], sT_v[:, :, (t + 1) * 512:(t + 2) * 512], dsT[1 - tb],
                              reads=[sT.bufs[t + 1]], writes=BsTt[1 - tb])
                for s in range(4):
                    i = t * 4 + s
                    xk = self.resid_begin(R, i)
                    for nh in range(2):
                        bo = 4 + dn % 3
                        dn += 1
                        S.group("pe", [I("matmul", self.bank[bo][:], lhsT=sTt[tb][:, ec, s * 128:(s + 1) * 128],
                                         rhs=Wo[:, ec, nh * 512:(nh + 1) * 512], start=(ec == 0), stop=(ec == NE - 1))
                                       for ec in range(NE)], reads=WO.B + BsTt[tb], writes=[self.B_bank[bo]])
                        self.resid_add(R, xk, nh, bo, 1.0)
                    self.resid_end(R, xk, i)
            self.resid_finish(R)
            S.flush()


    def hgrn_consts(self, es):
        S, NB = self.S, self.NB
        self.Dd = [self.sb(es, f"Dd{d}", [128, NH, NB], F32) for d in range(2)]
        self.B_Dd = [Buf(), Buf()]
        self.carry_t = self.sb(es, "carry_t", [128, 2 * NB], F32)
        self.B_carry = Buf()
        S.dma("sp", self.carry_t[:], self.carry, S.dsem("carry"), writes=[self.B_carry])
        self.mask = [self.sb(es, f"mask{d}", [128, 128], F32) for d in range(2)]
        self.B_mask = Buf()
        for d in range(2):
            S.op("pool", I("memset", self.mask[d][:], 1.0), writes=[self.B_mask])
        S.op("pool", I("affine_select", out=self.mask[0][:], in_=self.mask[0][:], pattern=[[1, 128]],
                       compare_op=ALU.is_ge, fill=0.0, base=0, channel_multiplier=-1), writes=[self.B_mask])
        S.op("pool", I("affine_select", out=self.mask[1][:], in_=self.mask[1][:], pattern=[[-1, 128]],
                       compare_op=ALU.is_ge, fill=0.0, base=0, channel_multiplier=1), writes=[self.B_mask])
        self.mreset = self.sb(es, "mreset", [128, 512], F32)
        self.B_mreset = Buf()
        S.op("pool", I("memset", self.mreset[:], 1.0), writes=[self.B_mreset])
        for c in range(4):
            S.op("pool", I("memset", self.mreset[:, c * 128:c * 128 + 1], 0.0), writes=[self.B_mreset])
        self.onesb = self.sb(es, "onesb", [128, 128], BF16)
        self.B_onesb = Buf()
        S.op("pool", I("memset", self.onesb[:], 1.0), writes=[self.B_onesb])

    def phase_hgrn_a(self, src, ss_in, B_ss_in, H):
        S = self.S
        with ExitStack() as ps:
            WIN = self.load_w(ps, "Whin", 8, 5 * D, "col", [(c * D, (c + 1) * D) for c in (0, 4, 3, 1, 2)],
                              key="hg_in")
            Win = WIN.tile
            self.compute_rstd(ss_in, B_ss_in, ps)
            L = self.lnt_alloc(ps, src, self.norm_w[1, 1], evac_eng="dve")
            noml, lnoml, lbc, Blb = [], [], [], Buf()
            for d in range(2):
                r0, B0 = self.load_cols(ps, f"r0{d}", self.hgrn_lb_raw[d, 0], 8)
                r1, B1 = self.load_cols(ps, f"r1{d}", self.hgrn_lb_raw[d, 1], 8)
                tmp = self.sb(ps, f"lbt{d}", [128, 8], F32)
                nm = self.sb(ps, f"noml{d}", [128, 8], F32)
                lo = self.sb(ps, f"lnoml{d}", [128, 8], F32)
                S.op("dve", I("tensor_tensor", out=tmp[:], in0=r0[:], in1=r1[:], op=ALU.subtract),
                     reads=[B0, B1], writes=[Blb])
                S.op("act", I("activation", out=tmp[:], in_=tmp[:], func=AF.Exp), reads=[Blb], writes=[Blb])
                S.op("dve", I("tensor_scalar", out=tmp[:], in0=tmp[:], scalar1=1.0, scalar2=None, op0=ALU.add),
                     reads=[Blb], writes=[Blb])
                S.op("dve", I("reciprocal", out=tmp[:], in_=tmp[:]), reads=[Blb], writes=[Blb])
                lbc_ = self.sb(ps, f"lbc{d}", [128, 8], F32)
                S.op("dve", I("tensor_copy", out=lbc_[:], in_=tmp[:]), reads=[Blb], writes=[Blb])
                lbc.append(lbc_)
                S.op("dve", I("tensor_scalar", out=nm[:], in0=tmp[:], scalar1=1.0, scalar2=None, op0=ALU.subtract),
                     reads=[Blb], writes=[Blb])
                S.op("dve", I("tensor_scalar", out=tmp[:], in0=nm[:], scalar1=-1.0, scalar2=None, op0=ALU.mult),
                     reads=[Blb], writes=[Blb])
                S.op("act", I("activation", out=lo[:], in_=tmp[:], func=AF.Ln), reads=[Blb], writes=[Blb])
                noml.append(nm)
                lnoml.append(lo)
            qs = self.sb(ps, "qs", [128, NH, 512], F32)
            Bqs = [Buf() for _ in range(NH)]
            gso = [self.sb(ps, f"gso{i}", [128, 512], BF16) for i in range(2)]
            Bgso = [Buf(), Buf()]
            dgso = [S.dsem("gso") for _ in range(2)]
            vtok = self.sb(ps, "vtok", [128, 4, D], BF16)
            Bvtok = [Buf() for _ in range(4)]
            dvtok = S.dsem("vtok")
            kstok = [self.sb(ps, f"kstok{d}", [128, 4, D], BF16) for d in range(2)]
            Bkstok = [[Buf() for _ in range(NH)] for _ in range(2)]
            dkstok = [S.dsem("kstok") for _ in range(2)]
            NR = 3
            qeo = [self.sb(ps, f"qeo{i}", [128, 512], BF16) for i in range(NR)]
            kdo = [self.sb(ps, f"kdo{i}", [128, 512], BF16) for i in range(NR)]
            Bqeo = [Buf() for _ in range(NR)]
            Bkdo = [Buf() for _ in range(NR)]
            dqeo = [S.dsem("qeo") for _ in range(NR)]
            dkdo = [S.dsem("kdo") for _ in range(NR)]
            NTMP = 3
            TN = ("te", "tA", "tB", "tP", "tX")
            T_ = [{n: self.sb(ps, f"{n}{i}", [128, 512], F32) for n in TN} for i in range(NTMP)]
            BT = [{n: Buf() for n in TN + ("tks",)} for i in range(NTMP)]
            tks = [self.sb(ps, f"tks{i}", [128, 512], BF16) for i in range(NTMP)]
            v_v = H["v"].ap.rearrange("(n p) f -> p n f", p=128)
            ks_v = [H["ks"][d].ap.rearrange("(n p) f -> p n f", p=128) for d in range(2)]
            self.lnt_load(L, 0)
            bk = 0
            ro = 0
            hd = 0
            for t in range(self.NTL):
                tb = t % 2
                tok = slice(t * 512, (t + 1) * 512)
                for h in range(NH):
                    for which in range(2):
                        col = (0 if which == 0 else 4 * D) + h * 128
                        b = bk % 6
                        bk += 1
                        S.group("pe", [I("matmul", self.bank[b][:], lhsT=Win[:, k, col:col + 128],
                                         rhs=L.xnT[tb][:, k, :], start=(k == 0), stop=(k == 7)) for k in range(8)],
                                reads=WIN.rb(col, col + 128) + L.BxnT[tb], writes=[self.B_bank[b]])
                        if which == 0:
                            S.op("act", I("activation", out=qs[:, h, :], in_=self.bank[b][:], func=AF.Silu),
                                 reads=[self.B_bank[b]], writes=[Bqs[h]])
                        else:
                            gk = h % 2
                            S.op("act", I("activation", out=gso[gk][:], in_=self.bank[b][:], func=AF.Silu),
                                 reads=[self.B_bank[b]], writes=[Bgso[gk]])
                            S.dma("sp", H["gs"].ap[h * 128:(h + 1) * 128, tok], gso[gk][:], dgso[gk],
                                  reads=[Bgso[gk]], writes=[H["gs"].bufs[t][h]])
                def vgroup(c, n):
                    nonlocal bk
                    b = bk % 6
                    bk += 1
                    S.group("pe", [I("matmul", self.bank[b][:], lhsT=L.xnT[tb][:, k, c * 128:(c + 1) * 128],
                                     rhs=Win[:, k, 3 * D + n * 512:3 * D + (n + 1) * 512], start=(k == 0),
                                     stop=(k == 7)) for k in range(8)],
                            reads=WIN.rb(3 * D, 4 * D) + L.BxnT[tb], writes=[self.B_bank[b]])
                    S.op("dve", I("tensor_copy", out=vtok[:, c, n * 512:(n + 1) * 512], in_=self.bank[b][:]),
                         reads=[self.B_bank[b]], writes=[Bvtok[c]])
                    if c == 3 and n == 1:
                        S.dma("sp", v_v[:, t * 4:(t + 1) * 4, :], vtok[:], dvtok, reads=Bvtok, writes=H["v"].bufs[t])

                vlist = [(c, n) for c in range(4) for n in range(2)]
                items = [(d, h) for d in range(2) for h in range(NH)]

                def stage0(d, h):
                    nonlocal bk
                    col = (1 + d) * D + h * 128
                    b = bk % 6
                    bk += 1
                    S.group("pe", [I("matmul", self.bank[b][:], lhsT=Win[:, k, col:col + 128],
                                     rhs=L.xnT[tb][:, k, :], start=(k == 0), stop=(k == 7)) for k in range(8)],
                            reads=WIN.rb(col, col + 128) + L.BxnT[tb], writes=[self.B_bank[b]])
                    return b

                def stage1(d, h, i_, b):
                    T, B_ = T_[i_], BT[i_]
                    S.op("act", I("activation", out=T["te"][:], in_=self.bank[b][:], func=AF.Exp),
                         reads=[self.B_bank[b]], writes=[B_["te"]])
                    S.op("act", I("activation", out=T["tA"][:], in_=T["te"][:], func=AF.Ln, bias=lbc[d][:, h:h + 1],
                                  scale=1.0), reads=[B_["te"], Blb], writes=[B_["tA"]])
                    S.op("act", I("activation", out=T["tB"][:], in_=T["te"][:], func=AF.Ln, bias=1.0, scale=1.0),
                         reads=[B_["te"]], writes=[B_["tB"]])
                    S.op("dve", I("tensor_tensor", out=T["tA"][:], in0=T["tA"][:], in1=T["tB"][:], op=ALU.subtract),
                         reads=[B_["tB"]], writes=[B_["tA"]])
                    S.op("dve", I("tensor_tensor_scan", out=T["tP"][:], data0=self.mreset[:], data1=T["tA"][:],
                                  initial=0.0, op0=ALU.mult, op1=ALU.add),
                         reads=[B_["tA"], self.B_mreset], writes=[B_["tP"]])
                    v4 = lambda ap: ap.rearrange("p (c t) -> p c t", c=4)
                    Tb = v4(T["tP"][:])[:, :, 127:128].broadcast_to([128, 4, 128])
                    if d == 0:
                        S.op("pool", I("tensor_tensor", out=T["te"][:], in0=T["tB"][:], in1=T["tP"][:], op=ALU.add),
                             reads=[B_["tB"], B_["tP"]], writes=[B_["te"]])
                        S.op("pool", I("tensor_tensor", out=v4(T["tX"][:]), in0=Tb, in1=v4(T["te"][:]),
                                       op=ALU.subtract), reads=[B_["tP"], B_["te"]], writes=[B_["tX"]])
                    else:
                        S.op("dve", I("tensor_tensor", out=T["tA"][:], in0=T["tP"][:], in1=T["tA"][:],
                                      op=ALU.subtract), reads=[B_["tP"]], writes=[B_["tA"]])
                        S.op("pool", I("tensor_tensor", out=T["tB"][:], in0=T["tA"][:], in1=T["tB"][:],
                                       op=ALU.subtract), reads=[B_["tA"]], writes=[B_["tB"]])
                        S.op("pool", I("tensor_tensor", out=v4(T["tA"][:]), in0=Tb, in1=v4(T["tA"][:]),
                                       op=ALU.subtract), reads=[B_["tP"]], writes=[B_["tA"]])
                        S.op("pool", I("tensor_tensor", out=v4(T["te"][:]), in0=Tb, in1=v4(T["tB"][:]),
                                       op=ALU.subtract), reads=[B_["tP"], B_["tB"]], writes=[B_["te"]])

                def stage2(d, h, i_):
                    nonlocal ro
                    T, B_ = T_[i_], BT[i_]
                    oml_b = lnoml[d][:, h:h + 1]
                    r_ = ro % NR
                    ro += 1
                    S.op("act", I("activation", out=self.Dd[d][:, h, t * 4:(t + 1) * 4],
                                  in_=T["tP"][:, 127:512:128], func=AF.Exp), reads=[B_["tP"]])
                    qarg, qB = (T["tP"], B_["tP"]) if d == 0 else (T["tA"], B_["tA"])
                    ksarg, ksB = (T["tX"], B_["tX"]) if d == 0 else (T["tB"], B_["tB"])
                    S.op("act", I("activation", out=qarg[:], in_=qarg[:], func=AF.Exp), reads=[qB], writes=[qB])
                    S.op("act", I("activation", out=kdo[r_][:], in_=T["te"][:], func=AF.Exp, scale=-1.0, bias=oml_b),
                         reads=[B_["te"], Blb], writes=[Bkdo[r_]])
                    S.op("act", I("activation", out=tks[i_][:], in_=ksarg[:], func=AF.Exp, bias=oml_b),
                         reads=[ksB, Blb], writes=[B_["tks"]])
                    S.op("dve", I("tensor_tensor", out=qeo[r_][:], in0=qs[:, h, :], in1=qarg[:], op=ALU.mult),
                         reads=[Bqs[h], qB], writes=[Bqeo[r_]])
                    S.dma("sp", H["qe"][d].ap[h * 128:(h + 1) * 128, tok], qeo[r_][:], dqeo[r_],
                          reads=[Bqeo[r_]], writes=[H["qe"][d].bufs[t][h]])
                    S.dma("sp", H["kd"][d].ap[h * 128:(h + 1) * 128, tok], kdo[r_][:], dkdo[r_],
                          reads=[Bkdo[r_]], writes=[H["kd"][d].bufs[t][h]])
                    pt = self.bank[6 + (h % 2)][:].bitcast(BF16)
                    S.group("pe", [I("transpose", out=pt[:, c * 128:(c + 1) * 128],
                                     in_=tks[i_][:, c * 128:(c + 1) * 128], identity=self.ident[:])
                                   for c in range(4)], reads=[B_["tks"], self.B_ident],
                            writes=[self.B_bank[6 + (h % 2)]])
                    S.op("dve", I("tensor_copy", out=kstok[d][:, :, h * 128:(h + 1) * 128],
                                  in_=pt[:, 0:512].rearrange("p (c k) -> p c k", c=4)),
                         reads=[self.B_bank[6 + (h % 2)]], writes=[Bkstok[d][h]])
                    if h == NH - 1:
                        S.dma("sp", ks_v[d][:, t * 4:(t + 1) * 4, :], kstok[d][:], dkstok[d], reads=Bkstok[d],
                              writes=H["ks"][d].bufs[t])

                AH = 4
                banks_ = [stage0(*items[q]) for q in range(AH)]
                stage1(*items[0], hd % NTMP, banks_[0])
                stage1(*items[1], (hd + 1) % NTMP, banks_[1])
                for j, (d, h) in enumerate(items):
                    if j + AH < len(items):
                        banks_.append(stage0(*items[j + AH]))
                    if j + 2 < len(items):
                        stage1(*items[j + 2], (hd + 2) % NTMP, banks_[j + 2])
                    stage2(d, h, hd % NTMP)
                    hd += 1
                    if j % 2 == 1:
                        vgroup(*vlist[j // 2])
                    if 1 <= j <= 5 and t + 1 < self.NTL:
                        if j == 1:
                            nsteps = self.lnt_steps(L, t + 1)
                        nsteps[j - 1]()
            self.B_Dd[0].w = self.B_Dd[1].w = Tok(S.eng["act"].sem, S.eng["act"].cnt)
            S.flush()

    def phase_hgrn_scan(self, d, H, src=None, dst=None, ss_out=None, B_ss_out=None):
        S, NB, NTL = self.S, self.NB, self.NTL
        with ExitStack() as ps:
            NO = 3 if d == 1 else 2
            qeT = [self.sb(ps, f"qeT{i}", [128, NH, 512], BF16) for i in range(2)]
            kdT = [self.sb(ps, f"kdT{i}", [128, NH, 512], BF16) for i in range(2)]
            kst = [self.sb(ps, f"kst{i}", [128, 4, D], BF16) for i in range(2)]
            vt = [self.sb(ps, f"vt{i}", [128, 4, D], BF16) for i in range(2)]
            oT = [self.sb(ps, f"oT{i}", [128, NH, 512], F32) for i in range(NO)]
            Bld = [[Buf() for _ in range(4)] for _ in range(2)]
            BoT = [[Buf() for _ in range(NH)] for _ in range(NO)]
            dld = [[S.dsem("scld") for _ in range(4)] for _ in range(2)]
            doT = [S.dsem("oT") for _ in range(NO)]
            Am = [self.sb(ps, f"Am{i}", [128, 4, 128], BF16) for i in range(2)]
            BAm = [Buf(), Buf()]
            St = self.sb(ps, "St", [128, NH, 128], F32)
            BSt = Buf()
            Sb = [self.sb(ps, f"Sb{i}", [128, NH, 128], BF16) for i in range(2)]
            BSb = [Buf(), Buf()]
            S.op("pool", I("memset", St[:], 0.0), writes=[BSt])
            S.op("pool", I("memset", Sb[0][:], 0.0), writes=[BSb[0]])
            S.op("pool", I("memset", Sb[1][:], 0.0), writes=[BSb[1]])
            qe_v = H["qe"][d].ap.rearrange("(h p) t -> p h t", p=128)
            kd_v = H["kd"][d].ap.rearrange("(h p) t -> p h t", p=128)
            ks_v = H["ks"][d].ap.rearrange("(n p) f -> p n f", p=128)
            v_v = H["v"].ap.rearrange("(n p) f -> p n f", p=128)
            of_v = H["of"].ap.rearrange("(h p) t -> p h t", p=128)
            if d == 1:
                WO = self.load_w(ps, "Who", NH, D, "row", [(0, NH)], key="hg_out")
                Wo = WO.tile
                nw, Bnw = self.load_cols(ps, "hnw", self.hgrn_norm_w, NH)
                for h in range(NH):
                    S.op("dve", I("tensor_scalar", out=Wo[:, h, :], in0=Wo[:, h, :], scalar1=nw[:, h:h + 1],
                                  scalar2=None, op0=ALU.mult), reads=WO.B + [Bnw], writes=WO.B)
                gsT = [self.sb(ps, f"gsT{i}", [128, NH, 512], BF16) for i in range(NO)]
                Bgs = [Buf() for _ in range(NO)]
                dgs = [S.dsem("gsld") for _ in range(NO)]
                gs_v = H["gs"].ap.rearrange("(h p) t -> p h t", p=128)
                sq = [self.sb(ps, f"sq{i}", [128, 512], BF16) for i in range(2)]
                Bsq = [Buf(), Buf()]
                rt = [self.sb(ps, f"rt{i}", [128, 512], F32) for i in range(2)]
                Brt = [Buf(), Buf()]
                onT = [self.sb(ps, f"onT{i}", [128, NH, 512], BF16) for i in range(2)]
                BonT = [[Buf() for _ in range(NH)] for _ in range(2)]
                R = self.resid_alloc(ps, src, dst, ss_out, B_ss_out)

            order = list(range(NTL)) if d == 0 else list(range(NTL - 1, -1, -1))

            def issue_loads(j, parts=(0, 1, 2, 3, 4, 5)):
                t = order[j]
                p = j % 2
                po = j % NO
                tok = slice(t * 512, (t + 1) * 512)
                if 0 in parts:
                    S.dma("sp", qeT[p][:], qe_v[:, :, tok], dld[p][0], reads=H["qe"][d].bufs[t], writes=[Bld[p][0]])
                if 1 in parts:
                    S.dma("sp", kdT[p][:], kd_v[:, :, tok], dld[p][1], reads=H["kd"][d].bufs[t], writes=[Bld[p][1]])
                if 2 in parts:
                    S.dma("sp", kst[p][:], ks_v[:, t * 4:(t + 1) * 4, :], dld[p][2], reads=H["ks"][d].bufs[t],
                          writes=[Bld[p][2]])
                if 3 in parts:
                    S.dma("sp", vt[p][:], v_v[:, t * 4:(t + 1) * 4, :], dld[p][3], reads=H["v"].bufs[t],
                          writes=[Bld[p][3]])
                if d == 1 and 4 in parts:
                    S.dma("sp", oT[po][:], of_v[:, :, tok], doT[po], reads=H["of"].bufs[t], writes=BoT[po])
                if d == 1 and 5 in parts:
                    S.dma("sp", gsT[po][:], gs_v[:, :, tok], dgs[po], reads=H["gs"].bufs[t], writes=[Bgs[po]])

            class TW:
                pass

            def tile_work(j):
                w = TW()
                w.t = order[j]
                w.po = j % NO
                w.pn = j % 2
                w.xk = {}
                return w

            def sqr(w, h):
                k2 = h % 2
                S.op("act", I("activation", out=sq[k2][:], in_=oT[w.po][:, h, :], func=AF.Square),
                     reads=[BoT[w.po][h]], writes=[Bsq[k2]])

            def mmn(w, h):
                k2 = h % 2
                S.group("pe", [I("matmul", self.bank[6][:], lhsT=self.onesb[:], rhs=sq[k2][:], start=True, stop=True)],
                        reads=[Bsq[k2], self.B_onesb], writes=[self.B_bank[6]])
                S.op("act", I("activation", out=rt[k2][:], in_=self.bank[6][:], func=AF.Ln, bias=EPS,
                              scale=1.0 / 128), reads=[self.B_bank[6]], writes=[Brt[k2]])
                S.op("act", I("activation", out=rt[k2][:], in_=rt[k2][:], func=AF.Exp, scale=-0.5),
                     reads=[Brt[k2]], writes=[Brt[k2]])

            def fin(w, h):
                k2 = h % 2
                S.op("pool", I("tensor_tensor", out=rt[k2][:], in0=oT[w.po][:, h, :], in1=rt[k2][:], op=ALU.mult),
                     reads=[BoT[w.po][h], Brt[k2]], writes=[Brt[k2]])
                S.op("pool", I("tensor_tensor", out=onT[w.pn][:, h, :], in0=rt[k2][:], in1=gsT[w.po][:, h, :],
                               op=ALU.mult), reads=[Brt[k2], Bgs[w.po]], writes=[BonT[w.pn][h]])

            def outp(w, s_, nh, pe_only=False, dve_only=False, load_only=False):
                i = w.t * 4 + s_
                if load_only:
                    w.xk[s_] = self.resid_begin(R, i)
                    return
                if not dve_only:
                    if nh == 0 and s_ not in w.xk:
                        w.xk[s_] = self.resid_begin(R, i)
                    S.group("pe", [I("matmul", self.bank[7][:], lhsT=onT[w.pn][:, h, s_ * 128:(s_ + 1) * 128],
                                     rhs=Wo[:, h, nh * 512:(nh + 1) * 512], start=(h == 0), stop=(h == NH - 1))
                                   for h in range(NH)], reads=WO.B + BonT[w.pn], writes=[self.B_bank[7]])
                if pe_only:
                    return
                xk = w.xk[s_]
                self.resid_add(R, xk, nh, 7, 1.0)
                if nh == 1:
                    self.resid_end(R, xk, i)

            issue_loads(0)
            sbi = 0
            normW = None
            outW = None
            for j in range(NTL):
                t = order[j]
                p = j % 2
                po = j % NO
                if d == 0 and j + 1 < NTL:
                    issue_loads(j + 1)
                corder = range(4) if d == 0 else range(3, -1, -1)
                for ci, c in enumerate(corder):
                    n = t * 4 + c
                    cs = slice(c * 128, (c + 1) * 128)
                    if normW is not None:
                        sqr(normW, 2 * ci)
                        sqr(normW, 2 * ci + 1)
                    if d == 1 and outW is not None:
                        outp(outW, ci, 0, load_only=True)
                    if d == 1 and j + 1 < NTL and ci < 3:
                        issue_loads(j + 1, parts=((0, 1), (2, 3), (4, 5))[ci])
                    for hb in range(2):
                        S.group("pe", [I("matmul", self.bank[hb][:, q * 128:(q + 1) * 128],
                                         lhsT=kdT[p][:, hb * 4 + q, cs], rhs=qeT[p][:, hb * 4 + q, cs],
                                         start=True, stop=True) for q in range(4)],
                                reads=[Bld[p][0], Bld[p][1]], writes=[self.B_bank[hb]])
                    for hb in range(2):
                        S.group("pe", [I("matmul", self.bank[2 + hb][:, q * 128:(q + 1) * 128],
                                         lhsT=kst[p][:, c, (hb * 4 + q) * 128:(hb * 4 + q + 1) * 128],
                                         rhs=vt[p][:, c, (hb * 4 + q) * 128:(hb * 4 + q + 1) * 128],
                                         start=True, stop=True) for q in range(4)],
                                reads=[Bld[p][2], Bld[p][3]], writes=[self.B_bank[2 + hb]])
                    if normW is not None:
                        mmn(normW, 2 * ci)
                    if outW is not None:
                        outp(outW, ci, 0, pe_only=True)
                    for hb in range(2):
                        S.op("dve", I("tensor_tensor", out=Am[hb][:],
                                      in0=self.bank[hb][:].rearrange("p (q t) -> p q t", q=4),
                                      in1=self.mask[d][:].unsqueeze(1).broadcast_to([128, 4, 128]), op=ALU.mult),
                             reads=[self.B_bank[hb], self.B_mask], writes=[BAm[hb]])
                    sp_ = sbi % 2
                    for hb in range(2):
                        fns = []
                        for q in range(4):
                            h = hb * 4 + q
                            fns.append(I("matmul", self.bank[4 + hb][:, q * 128:(q + 1) * 128],
                                         lhsT=vt[p][:, c, h * 128:(h + 1) * 128], rhs=Am[hb][:, q, :],
                                         start=True, stop=False))
                            fns.append(I("matmul", self.bank[4 + hb][:, q * 128:(q + 1) * 128],
                                         lhsT=Sb[sp_][:, h, :], rhs=qeT[p][:, h, cs], start=False, stop=True))
                        S.group("pe", fns, reads=[Bld[p][3], Bld[p][0], BAm[hb], BSb[sp_]],
                                writes=[self.B_bank[4 + hb]])
                    S.op("dve", I("tensor_tensor", out=St[:], in0=St[:],
                                  in1=self.Dd[d][:, :, n:n + 1].broadcast_to([128, NH, 128]), op=ALU.mult),
                         reads=[self.B_Dd[d]], writes=[BSt])
                    for hb in range(2):
                        S.op("dve", I("tensor_tensor", out=St[:, hb * 4:(hb + 1) * 4, :],
                                      in0=St[:, hb * 4:(hb + 1) * 4, :],
                                      in1=self.bank[2 + hb][:].rearrange("p (q t) -> p q t", q=4), op=ALU.add),
                             reads=[self.B_bank[2 + hb]], writes=[BSt])
                    nxt = n + 1 if d == 0 else n - 1
                    if 0 <= nxt < NB:
                        bnd = (nxt % 16 == 0) if d == 0 else (nxt % 16 == 15)
                        if bnd:
                            ccol = nxt if d == 0 else NB + nxt
                            S.op("dve", I("tensor_scalar", out=St[:], in0=St[:],
                                          scalar1=self.carry_t[:, ccol:ccol + 1], scalar2=None, op0=ALU.mult),
                                 reads=[self.B_carry], writes=[BSt])
                    sbi += 1
                    S.op("act", I("activation", out=Sb[sbi % 2][:], in_=St[:], func=AF.Copy),
                         reads=[BSt], writes=[BSb[sbi % 2]])
                    if outW is not None:
                        outp(outW, ci, 0, dve_only=True)
                    if normW is not None:
                        mmn(normW, 2 * ci + 1)
                    if outW is not None:
                        outp(outW, ci, 1)
                    for hb in range(2):
                        ov = oT[po][:, hb * 4:(hb + 1) * 4, cs]
                        bv = self.bank[4 + hb][:].rearrange("p (q t) -> p q t", q=4)
                        if d == 0:
                            S.op("act", I("activation", out=ov, in_=bv, func=AF.Copy),
                                 reads=[self.B_bank[4 + hb]], writes=BoT[po][hb * 4:(hb + 1) * 4])
                        else:
                            S.op("dve", I("tensor_tensor", out=ov, in0=bv, in1=ov, op=ALU.add),
                                 reads=[self.B_bank[4 + hb]], writes=BoT[po][hb * 4:(hb + 1) * 4])
                    if normW is not None:
                        fin(normW, 2 * ci)
                        fin(normW, 2 * ci + 1)
                if d == 0:
                    S.dma("sp", of_v[:, :, t * 512:(t + 1) * 512], oT[po][:], doT[po], reads=BoT[po],
                          writes=H["of"].bufs[t])
                else:
                    outW = normW
                    normW = tile_work(j)
            if d == 1:
                if outW is not None:
                    for s_ in range(4):
                        outp(outW, s_, 0)
                        outp(outW, s_, 1)
                for h in range(NH):
                    sqr(normW, h)
                    mmn(normW, h)
                    fin(normW, h)
                for s_ in range(4):
                    outp(normW, s_, 0)
                    outp(normW, s_, 1)
                self.resid_finish(R)
            S.flush()

    def phase_final(self, src, ss, B_ss, copy_only=False):
        S = self.S
        with ExitStack() as ps:
            self.compute_rstd(ss, B_ss, ps)
            wb = self.sb(ps, "fwb", [128, D], F32)
            Bwb = Buf()
            dwb = S.dsem("fwb")
            S.dma("sp", wb[:], self.final_norm.partition_broadcast(128), dwb, writes=[Bwb])
            xl = [self.sb(ps, f"fx{i}", [128, D], F32) for i in range(3)]
            Bx = [Buf() for _ in range(3)]
            dx = [S.dsem("fx") for _ in range(3)]
            for i in range(self.NB):
                k = i % 3
                S.dma("sp", xl[k][:], src.ap[i * 128:(i + 1) * 128, :], dx[k], reads=[src.bufs[i]], writes=[Bx[k]])
                if not copy_only:
                    S.op("dve", I("scalar_tensor_tensor", out=xl[k][:], in0=xl[k][:],
                                                                          scalar=self.rstd[:, i:i + 1], in1=wb[:],
                                                                          op0=ALU.mult, op1=ALU.mult),
                         reads=[self.B_rstd, Bwb], writes=[Bx[k]])
                S.dma("sp", self.y_out.ap[i * 128:(i + 1) * 128, :], xl[k][:], dx[k], reads=[Bx[k]],
                      writes=[self.y_out.bufs[i]])
            S.flush()


W_NAMES = ["norm_w", "ffn_gate", "ffn_up", "ffn_down", "sgu_w_in", "sgu_ln_g", "sgu_ln_b", "sgu_w_s", "sgu_b_s",
           "sgu_w_out", "hgrn_w_in", "hgrn_lb_raw", "hgrn_norm_w", "hgrn_w_out", "final_norm"]
W_SQUEEZE = {"sgu_w_in", "sgu_ln_g", "sgu_ln_b", "sgu_w_s", "sgu_b_s", "sgu_w_out", "hgrn_w_in", "hgrn_norm_w",
             "hgrn_w_out"}


def make_carry(NB, seq_blocks):
    c = np.ones((128, 2 * NB), np.float32)
    for n in range(NB):
        if n % seq_blocks == 0:
            c[:, n] = 0.0
        if n % seq_blocks == seq_blocks - 1:
            c[:, NB + n] = 0.0
    return c


def kernel(**inputs):
    xp = np.ascontiguousarray(inputs["x_prompt"], dtype=np.float32)
    xs = np.ascontiguousarray(inputs["x_sample"], dtype=np.float32)
    NT = NT_FULL
    prog = Prog(NT)
    wmap = {}
    for n in W_NAMES:
        a = np.ascontiguousarray(inputs[n], dtype=np.float32)
        if n in W_SQUEEZE:
            a = a[0]
        wmap[n] = a
    in_maps = []
    for c in range(NCORES):
        m = dict(wmap)
        if c < 4:
            m["x"] = xp[4 * c:4 * c + 4].reshape(NT, D)
            m["carry"] = make_carry(NT // 128, SEQ_PROMPT // 128)
        else:
            m["x"] = xs[c - 4].reshape(NT, D)
            m["carry"] = make_carry(NT // 128, NT // 128)
        in_maps.append(m)
    res = run_bass_kernel_spmd(prog.nc, in_maps, core_ids=list(range(NCORES)))
    ys = [np.asarray(r["y"], dtype=np.float32) for r in res.results]
    y_prompt = np.stack(ys[:4]).reshape(16, 2048, D)
    y_sample = np.stack(ys[4:]).reshape(4, 8192, D)
    return (y_prompt, y_sample)
```

```python
import numpy as np
from contextlib import ExitStack
import concourse.bass as bass
import concourse.mybir as mybir
from concourse.bass_utils import run_bass_kernel_spmd

F32, BF16 = mybir.dt.float32, mybir.dt.bfloat16
AF = mybir.ActivationFunctionType
ALU = mybir.AluOpType

D = 1024
FF = 2816
NF = FF // 128
SG = 3072
NE = SG // 128
NH = 8
EPS = 1e-6
NCORES = 8
NT_FULL = 8192
SEQ_PROMPT = 2048


class Tok:
    __slots__ = ("sem", "val")

    def __init__(self, sem, val):
        self.sem = sem
        self.val = val


class Buf:
    __slots__ = ("w", "r", "name")

    def __init__(self, name=""):
        self.w = None
        self.r = []
        self.name = name


class DSem:
    def __init__(self, h):
        self.h = h
        self.cnt = 0


class Eng:
    def __init__(self, name):
        self.name = name
        self.ops = []
        self.sem = None
        self.cnt = 0
        self.known = {}


class Sched:
    def __init__(self, nc, es):
        self.nc = nc
        self.es = es
        self.eng = {n: Eng(n) for n in ("pe", "act", "dve", "pool", "sp")}
        self.nsem = 0
        self.dsems = []
        self.free_dsems = []
        self.phase_dsems = []
        self.bg_dsems = []
        self._new_engine_sems()

    def _newsem(self, name):
        self.nsem += 1
        return self.es.enter_context(self.nc.semaphore(f"{name}{self.nsem}"))

    def _new_engine_sems(self):
        for n in ("pe", "act", "dve", "pool"):
            E = self.eng[n]
            E.sem = self._newsem("e" + n)
            E.cnt = 0

    def dsem(self, name="d", bg=False):
        if bg:
            d = DSem(self._newsem(name))
            self.bg_dsems.append(d)
            return d
        if self.free_dsems:
            d = self.free_dsems.pop()
        else:
            d = DSem(self._newsem(name))
            self.dsems.append(d)
        self.phase_dsems.append(d)
        return d

    def _deps(self, E, reads, writes):
        need = {}

        def add(t):
            if t is None:
                return
            if E.name == "pe" and t.sem is E.sem:
                return
            k = id(t.sem)
            if E.known.get(k, 0) >= t.val:
                return
            if k not in need or need[k].val < t.val:
                need[k] = t

        for b in reads:
            add(b.w)
        for b in writes:
            add(b.w)
            for t in b.r:
                add(t)
        for k, t in need.items():
            E.known[k] = t.val
        return list(need.values())

    def _reg(self, tok, reads, writes):
        for b in reads:
            b.r.append(tok)
        for b in writes:
            b.w = tok
            b.r = []

    def op(self, en, fn, reads=(), writes=()):
        return self.group(en, [fn], reads, writes)

    def group(self, en, fns, reads=(), writes=()):
        E = self.eng[en]
        waits = self._deps(E, reads, writes)
        E.cnt += 1
        sem = E.sem
        tok = Tok(sem, E.cnt)

        def run(e):
            for t in waits:
                e.wait_ge(t.sem, t.val)
            ins = None
            for fn in fns:
                ins = getattr(e, fn[0])(*fn[1], **fn[2])
            ins.then_inc(sem, 1)

        E.ops.append(run)
        self._reg(tok, reads, writes)
        return tok

    def dma(self, en, out, in_, ds, reads=(), writes=(), **kw):
        E = self.eng[en]
        waits = self._deps(E, reads, writes)
        ds.cnt += 16
        tok = Tok(ds.h, ds.cnt)
        h = ds.h

        def run(e):
            for t in waits:
                e.wait_ge(t.sem, t.val)
            e.dma_start(out=out, in_=in_, **kw).then_inc(h, 16)

        E.ops.append(run)
        self._reg(tok, reads, writes)
        return tok

    def barrier(self, final=False):
        toks = [Tok(self.eng[n].sem, self.eng[n].cnt) for n in ("pe", "act", "dve", "pool") if self.eng[n].cnt > 0]
        toks += [Tok(d.h, d.cnt) for d in self.dsems if d.cnt > 0]
        for n, E in self.eng.items():
            ws = []
            for t in toks:
                if E.name == "pe" and t.sem is E.sem:
                    continue
                k = id(t.sem)
                if E.known.get(k, 0) >= t.val:
                    continue
                E.known[k] = t.val
                ws.append(t)

            def run(e, ws=ws):
                for t in ws:
                    e.wait_ge(t.sem, t.val)

            E.ops.append(run)

    def flush(self):
        self.barrier()
        lists = {n: E.ops for n, E in self.eng.items()}
        for E in self.eng.values():
            E.ops = []
        with self.nc.Block() as block:
            @block.tensor
            def _(e):
                for f in lists["pe"]:
                    f(e)

            @block.scalar
            def _(e):
                for f in lists["act"]:
                    f(e)

            @block.vector
            def _(e):
                for f in lists["dve"]:
                    f(e)

            @block.gpsimd
            def _(e):
                for f in lists["pool"]:
                    f(e)

            @block.sync
            def _(e):
                for f in lists["sp"]:
                    f(e)
        self.free_dsems.extend(self.phase_dsems)
        self.phase_dsems = []


def I(name, *a, **kw):
    return (name, a, kw)


class DramAct:
    def __init__(self, ap, nblk, name, sub=None):
        self.ap = ap
        if sub is None:
            self.bufs = [Buf(f"{name}{i}") for i in range(nblk)]
        else:
            self.bufs = [[Buf(f"{name}{i}_{j}") for j in range(sub)] for i in range(nblk)]


class Prog:
    def __init__(self, NT, stop_after=None, debug=False):
        assert NT % 512 == 0
        self.debug = debug
        self.NT = NT
        self.NB = NT // 128
        self.NTL = NT // 512
        self.stop_after = stop_after
        self.nc = bass.Bass("TRN2", target_bir_lowering=False)
        self.es = ExitStack()
        self.S = Sched(self.nc, self.es)
        self.build()
        self.es.close()

    def dram(self, name, shape, dt, kind="Internal"):
        return self.nc.dram_tensor(name, list(shape), dt, kind=kind).ap()

    def sb(self, stack, name, shape, dt):
        self._uid = getattr(self, "_uid", 0) + 1
        return stack.enter_context(self.nc.sbuf_tensor(f"{name}_{self._uid}", list(shape), dt))

    def dbg(self, name, ap, bufs, dt, eng="sp"):
        if not getattr(self, "debug", False):
            return
        out = self.dram("dbg_" + name, list(ap.shape), dt, "ExternalOutput")
        self.S.dma(eng, out, ap, self.S.dsem("dbg"), reads=bufs)

    def build(self):
        nc, S, NT, NB = self.nc, self.S, self.NT, self.NB
        es = self.es
        self.x_in = DramAct(self.dram("x", [NT, D], F32, "ExternalInput"), NB, "xin")
        self.y_out = DramAct(self.dram("y", [NT, D], F32, "ExternalOutput"), NB, "yout")
        self.norm_w = self.dram("norm_w", [2, 3, D], F32, "ExternalInput")
        self.ffn_gate = self.dram("ffn_gate", [2, 2, D, FF], F32, "ExternalInput")
        self.ffn_up = self.dram("ffn_up", [2, 2, D, FF], F32, "ExternalInput")
        self.ffn_down = self.dram("ffn_down", [2, 2, FF, D], F32, "ExternalInput")
        self.sgu_w_in = self.dram("sgu_w_in", [D, 2 * SG], F32, "ExternalInput")
        self.sgu_ln_g = self.dram("sgu_ln_g", [SG], F32, "ExternalInput")
        self.sgu_ln_b = self.dram("sgu_ln_b", [SG], F32, "ExternalInput")
        self.sgu_w_s = self.dram("sgu_w_s", [8, 128, 128], F32, "ExternalInput")
        self.sgu_b_s = self.dram("sgu_b_s", [8, 128], F32, "ExternalInput")
        self.sgu_w_out = self.dram("sgu_w_out", [SG, D], F32, "ExternalInput")
        self.hgrn_w_in = self.dram("hgrn_w_in", [D, 5 * D], F32, "ExternalInput")
        self.hgrn_lb_raw = self.dram("hgrn_lb_raw", [2, 2, D], F32, "ExternalInput")
        self.hgrn_norm_w = self.dram("hgrn_norm_w", [D], F32, "ExternalInput")
        self.hgrn_w_out = self.dram("hgrn_w_out", [D, D], F32, "ExternalInput")
        self.final_norm = self.dram("final_norm", [D], F32, "ExternalInput")
        self.carry = self.dram("carry", [128, 2 * NB], F32, "ExternalInput")
        self.xa = DramAct(self.dram("xa", [NT, D], F32), NB, "xa")
        self.xb = DramAct(self.dram("xb", [NT, D], F32), NB, "xb")
        self.sT = DramAct(self.dram("sT", [SG, NT], BF16), self.NTL, "sT")
        NTL = self.NTL
        self.H = dict(
            qe=[DramAct(self.dram(f"qe{d}", [D, NT], BF16), NTL, f"qe{d}", 8) for d in range(2)],
            kd=[DramAct(self.dram(f"kd{d}", [D, NT], BF16), NTL, f"kd{d}", 8) for d in range(2)],
            ks=[DramAct(self.dram(f"ks{d}", [NT, D], BF16), NTL, f"ks{d}", 1) for d in range(2)],
            v=DramAct(self.dram("vtok", [NT, D], BF16), NTL, "vtok", 1),
            gs=DramAct(self.dram("gsT", [D, NT], BF16), NTL, "gsT", 8),
            of=DramAct(self.dram("ofT", [D, NT], F32), NTL, "ofT", 1),
        )

        self.ident = self.sb(es, "ident", [128, 128], BF16)
        self.identf = self.sb(es, "identf", [128, 128], F32)
        self.ssA = self.sb(es, "ssA", [128, NB], F32)
        self.ssB = self.sb(es, "ssB", [128, NB], F32)
        self.rstd = self.sb(es, "rstd", [128, NB], F32)
        self.B_ident = Buf("ident")
        self.B_ssA = Buf("ssA")
        self.B_ssB = Buf("ssB")
        self.B_rstd = Buf("rstd")
        self.bank = [es.enter_context(nc.psum_tensor(f"bank{i}", [128, 512], F32)) for i in range(8)]
        self.B_bank = [Buf(f"bank{i}") for i in range(8)]

        S.op("pool", I("memset", self.identf[:], 0.0), writes=[self.B_ident])
        S.op("pool", I("affine_select", out=self.identf[:], in_=self.identf[:], pattern=[[-1, 128]],
                                               compare_op=ALU.not_equal, fill=1.0, base=0, channel_multiplier=1),
             writes=[self.B_ident])
        S.op("dve", I("tensor_copy", out=self.ident[:], in_=self.identf[:]), writes=[self.B_ident])
        S.op("pool", I("memset", self.ssA[:], 0.0), writes=[self.B_ssA])
        S.op("pool", I("memset", self.ssB[:], 0.0), writes=[self.B_ssB])

        w1s = ExitStack()
        W1 = self.ffn_weights(w1s, 0, 0, direct=True)
        self.phase_stats(self.x_in, self.ssA, self.B_ssA)
        self.convert_all()
        self.phase_ffn(self.x_in, self.xa, 0, 0, 0, self.ssA, self.B_ssA, self.ssB, self.B_ssB, W=W1)
        w1s.close()
        if self.stop_after == "ffn1":
            self.phase_final(self.xa, self.ssB, self.B_ssB, copy_only=True)
            return
        self.phase_sgu_a(self.xa, self.ssB, self.B_ssB, self.sT)
        self.phase_sgu_b(self.xa, self.xb, self.ssB, self.B_ssB, self.ssA, self.B_ssA, self.sT)
        if self.stop_after == "sgu":
            self.phase_final(self.xb, self.ssA, self.B_ssA, copy_only=True)
            return
        self.phase_ffn(self.xb, self.xa, 0, 2, 1, self.ssA, self.B_ssA, self.ssB, self.B_ssB)
        self.phase_ffn(self.xa, self.xb, 1, 0, 0, self.ssB, self.B_ssB, self.ssA, self.B_ssA)
        if self.stop_after == "ffn3":
            self.phase_final(self.xb, self.ssA, self.B_ssA, copy_only=True)
            return
        with ExitStack() as hes:
            self.hgrn_consts(hes)
            self.phase_hgrn_a(self.xb, self.ssA, self.B_ssA, self.H)
            self.phase_hgrn_scan(0, self.H)
            self.phase_hgrn_scan(1, self.H, self.xb, self.xa, self.ssB, self.B_ssB)
        if self.stop_after == "hgrn":
            self.phase_final(self.xa, self.ssB, self.B_ssB, copy_only=True)
            return
        self.phase_ffn(self.xa, self.y_out, 1, 2, 1, self.ssB, self.B_ssB, self.ssA, self.B_ssA, final=True)

    def phase_stats(self, src, ss, B_ss):
        nc, S = self.nc, self.S
        with ExitStack() as ps:
            xl = [self.sb(ps, f"st_x{i}", [128, D], F32) for i in range(3)]
            junk = self.sb(ps, "st_junk", [128, D], BF16)
            Bx = [Buf() for _ in range(3)]
            Bj = Buf()
            ds = [S.dsem("stx") for _ in range(3)]
            for i in range(self.NB):
                k = i % 3
                S.dma("sp", xl[k][:], src.ap[i * 128:(i + 1) * 128, :], ds[k], reads=[src.bufs[i]], writes=[Bx[k]])
                S.op("act", I("activation", out=junk[:], in_=xl[k][:], func=AF.Square,
                                                            accum_out=ss[:, i:i + 1]),
                     reads=[Bx[k], B_ss], writes=[Bj])
            B_ss.w = Tok(S.eng["act"].sem, S.eng["act"].cnt)
            S.flush()

    def compute_rstd(self, ss, B_ss, ps):
        S = self.S
        tmp = self.sb(ps, "rs_tmp", [128, self.NB], F32)
        Bt = Buf()
        S.op("act", I("activation", out=tmp[:], in_=ss[:], func=AF.Sqrt, bias=EPS, scale=1.0 / D),
             reads=[B_ss], writes=[Bt])
        S.op("dve", I("reciprocal", out=self.rstd[:], in_=tmp[:]), reads=[Bt], writes=[self.B_rstd])

    def convert_all(self):
        S = self.S
        self.wbf, self.Bconv = {}, {}
        jobs = [("sgu_in", self.sgu_w_in, [D, 2 * SG]), ("sgu_out", self.sgu_w_out, [SG, D])]
        for (l, j) in ((0, 1), (1, 0)):
            jobs += [(f"gate{l}{j}", self.ffn_gate[l, j], [D, FF]), (f"up{l}{j}", self.ffn_up[l, j], [D, FF]),
                     (f"down{l}{j}", self.ffn_down[l, j], [FF, D])]
        jobs += [("hg_in", self.hgrn_w_in, [D, 5 * D]), ("hg_out", self.hgrn_w_out, [D, D])]
        jobs += [("gate11", self.ffn_gate[1, 1], [D, FF]), ("up11", self.ffn_up[1, 1], [D, FF]),
                 ("down11", self.ffn_down[1, 1], [FF, D])]
        prev = []
        cvs = [S.dsem("cv", bg=True) for _ in range(3)]
        RC = 256
        n = 0
        for key, src, shp in jobs:
            dst = self.dram("wbf_" + key, shp, BF16)
            self.wbf[key] = dst
            self.Bconv[key] = []
            for r0 in range(0, shp[0], RC):
                B = Buf(f"conv_{key}_{r0}")
                S.dma("pool", dst[r0:r0 + RC, :], src[r0:r0 + RC, :], cvs[n % 3], reads=prev[-3:-1], writes=[B],
                      max_dma_last_dim=4096)
                n += 1
                prev.append(B)
                self.Bconv[key].append(B)

    class WT:
        def __init__(self, tile, bounds):
            self.tile = tile
            self.bounds = bounds
            self.B = [Buf() for _ in bounds]

        def rb(self, lo, hi):
            return [b for (a, c), b in zip(self.bounds, self.B) if a < hi and c > lo]

    def load_w(self, ps, name, nk, ncols, axis, bounds, key=None, col0=0, direct=None):
        S = self.S
        t = self.sb(ps, name, [128, nk, ncols], BF16)
        W = Prog.WT(t, bounds)
        if direct is not None:
            v = direct.rearrange("(k p) f -> p k f", p=128)
        else:
            v = self.wbf[key].rearrange("(k p) f -> p k f", p=128)
        for (lo, hi), B in zip(bounds, W.B):
            if axis == "col":
                o, i = t[:, :, lo:hi], v[:, :, col0 + lo:col0 + hi]
            else:
                o, i = t[:, lo:hi, :], v[:, lo:hi, col0:col0 + ncols]
            if direct is not None:
                self._w1prev = getattr(self, "_w1prev", [])
                S.dma("pool", o, i, S.dsem("w1", bg=True), reads=self._w1prev[-3:-2], writes=[B],
                      max_dma_last_dim=4096)
                self._w1prev.append(B)
            else:
                S.dma("sp", o, i, S.dsem("wl"), reads=self.Bconv[key], writes=[B])
        return W

    def ffn_weights(self, ps, l, j, direct=False):
        cb = [(0, 768), (768, 1536), (1536, 2176), (2176, 2816)]
        rb = [(0, 6), (6, 12), (12, 17), (17, 22)]
        if direct:
            Wg = self.load_w(ps, "Wg", 8, FF, "col", cb, direct=self.ffn_gate[l, j])
            Wu = self.load_w(ps, "Wu", 8, FF, "col", cb, direct=self.ffn_up[l, j])
            Wd = self.load_w(ps, "Wd", NF, D, "row", rb, direct=self.ffn_down[l, j])
        else:
            Wg = self.load_w(ps, "Wg", 8, FF, "col", cb, key=f"gate{l}{j}")
            Wu = self.load_w(ps, "Wu", 8, FF, "col", cb, key=f"up{l}{j}")
            Wd = self.load_w(ps, "Wd", NF, D, "row", rb, key=f"down{l}{j}")
        return Wg, Wu, Wd

    def load_weight_bf16(self, dst_tile, src_ap, nk, ncols, ds, B):
        S = self.S
        for k in range(nk):
            S.dma("pool", dst_tile[:, k, :], src_ap[k * 128:(k + 1) * 128, :], ds, max_dma_last_dim=4096)

    class LNT:
        pass

    def lnt_alloc(self, ps, src, normw_row, evac_eng="act"):
        S = self.S
        L = Prog.LNT()
        L.src = src
        L.xld = [self.sb(ps, f"xld{i}", [128, D], F32) for i in range(2)]
        L.Bxld = [Buf() for _ in range(2)]
        L.dxld = [S.dsem("xld") for _ in range(2)]
        L.xn = [self.sb(ps, f"xn{i}", [128, D], BF16) for i in range(2)]
        L.Bxn = [Buf() for _ in range(2)]
        L.xnT = [self.sb(ps, f"xnT{i}", [128, 8, 512], BF16) for i in range(2)]
        L.BxnT = [[Buf() for _ in range(4)] for _ in range(2)]
        L.wb = self.sb(ps, "wb", [128, D], F32)
        L.Bwb = Buf()
        L.dwb = S.dsem("wb")
        S.dma("sp", L.wb[:], normw_row.partition_broadcast(128), L.dwb, writes=[L.Bwb])
        L.cnt = 0
        L.slot = {}
        L.evac_eng = evac_eng
        return L

    def lnt_dma(self, L, t, s):
        S = self.S
        i = t * 4 + s
        k = L.cnt % 2
        L.cnt += 1
        L.slot[(t, s)] = k
        S.dma("sp", L.xld[k][:], L.src.ap[i * 128:(i + 1) * 128, :], L.dxld[k],
              reads=[L.src.bufs[i]], writes=[L.Bxld[k]])

    def lnt_sub(self, L, t, s):
        S = self.S
        i = t * 4 + s
        k = L.slot.pop((t, s))
        S.op("dve", I("scalar_tensor_tensor", out=L.xn[k][:], in0=L.xld[k][:], scalar=self.rstd[:, i:i + 1],
                      in1=L.wb[:], op0=ALU.mult, op1=ALU.mult),
             reads=[L.Bxld[k], self.B_rstd, L.Bwb], writes=[L.Bxn[k]])
        self.lnt_transpose_one(L, t, s, k)

    def lnt_steps(self, L, t):
        return [lambda: (self.lnt_dma(L, t, 0), self.lnt_dma(L, t, 1)),
                lambda: (self.lnt_sub(L, t, 0), self.lnt_dma(L, t, 2)),
                lambda: (self.lnt_sub(L, t, 1), self.lnt_dma(L, t, 3)),
                lambda: self.lnt_sub(L, t, 2),
                lambda: self.lnt_sub(L, t, 3)]

    def lnt_load(self, L, t):
        for f in self.lnt_steps(L, t):
            f()

    def lnt_transpose_one(self, L, t, s, k):
        S = self.S
        tb = t % 2
        bk = 7
        pt = self.bank[bk][:].bitcast(BF16)
        fns = [(I("transpose", out=pt[:, c * 128:(c + 1) * 128], in_=L.xn[k][:, c * 128:(c + 1) * 128],
                                           identity=self.ident[:])) for c in range(8)]
        S.group("pe", fns, reads=[L.Bxn[k], self.B_ident], writes=[self.B_bank[bk]])
        if L.evac_eng == "act":
            S.op("act", I("activation", out=L.xnT[tb][:, :, s * 128:(s + 1) * 128],
                          in_=pt.rearrange("p (c t) -> p c t", c=8), func=AF.Copy),
                 reads=[self.B_bank[bk]], writes=[L.BxnT[tb][s]])
        else:
            S.op("dve", I("tensor_copy", out=L.xnT[tb][:, :, s * 128:(s + 1) * 128],
                          in_=pt.rearrange("p (c t) -> p c t", c=8)),
                 reads=[self.B_bank[bk]], writes=[L.BxnT[tb][s]])

    def phase_ffn(self, src, dst, layer, normi, ffni, ss_in, B_ss_in, ss_out, B_ss_out, W=None, final=False):
        nc, S = self.nc, self.S
        with ExitStack() as ps:
            WG, WU, WD = W if W is not None else self.ffn_weights(ps, layer, ffni)
            Wg, Wu, Wd = WG.tile, WU.tile, WD.tile
            self.compute_rstd(ss_in, B_ss_in, ps)
            L = self.lnt_alloc(ps, src, self.norm_w[layer, normi])
            hT = self.sb(ps, "hT", [128, NF, 512], BF16)
            BhT = [Buf() for _ in range(NF)]
            sg = [self.sb(ps, f"sg{i}", [128, 512], F32) for i in range(2)]
            Bsg = [Buf() for _ in range(2)]
            xr = [self.sb(ps, f"xr{i}", [128, D], F32) for i in range(2)]
            Bxr = [[Buf(), Buf()] for _ in range(2)]
            dxr = [S.dsem("xr") for _ in range(2)]
            junk = self.sb(ps, "fjunk", [128, D], BF16)
            Bj = Buf()
            S.op("dve", I("memset", ss_out[:], 0.0), writes=[B_ss_out])
            Bfs, Bwfb = Buf(), Buf()
            if final:
                fs = self.sb(ps, "ffs", [128, 4], F32)
                wfb = self.sb(ps, "wfb", [128, D], F32)
                S.dma("sp", wfb[:], self.final_norm.partition_broadcast(128), S.dsem("wfb"), writes=[Bwfb])

            self.lnt_load(L, 0)
            self.dbg("xnT0", L.xnT[0][:], L.BxnT[0], BF16)
            self.dbg("rstd", self.rstd[:], [self.B_rstd], F32)
            gu = 0
            dn = 0
            xrc = 0
            for t in range(self.NTL):
                tb = t % 2
                for f in range(NF):
                    bg, bu = (gu % 2) * 2, (gu % 2) * 2 + 1
                    gu += 1
                    fg = [(I("matmul", self.bank[bg][:], lhsT=Wg[:, k, f * 128:(f + 1) * 128],
                                                              rhs=L.xnT[tb][:, k, :], start=(k == 0), stop=(k == 7)))
                          for k in range(8)]
                    S.group("pe", fg, reads=WG.rb(f * 128, f * 128 + 128) + L.BxnT[tb], writes=[self.B_bank[bg]])
                    fu = [(I("matmul", self.bank[bu][:], lhsT=Wu[:, k, f * 128:(f + 1) * 128],
                                                              rhs=L.xnT[tb][:, k, :], start=(k == 0), stop=(k == 7)))
                          for k in range(8)]
                    S.group("pe", fu, reads=WU.rb(f * 128, f * 128 + 128) + L.BxnT[tb], writes=[self.B_bank[bu]])
                    sk = f % 2
                    if self.debug and t == 0 and f == 0:
                        self.dbgt = self.sb(ps, "dbgt", [128, 512], F32)
                        Bd = Buf()
                        S.op("dve", I("tensor_copy", out=self.dbgt[:], in_=self.bank[bg][:]),
                             reads=[self.B_bank[bg]], writes=[Bd])
                        self.dbg("g0", self.dbgt[:], [Bd], F32)
                    S.op("act", I("activation", out=sg[sk][:], in_=self.bank[bg][:], func=AF.Silu),
                         reads=[self.B_bank[bg]], writes=[Bsg[sk]])
                    S.op("dve", I("tensor_tensor", out=hT[:, f, :], in0=sg[sk][:],
                                                                            in1=self.bank[bu][:], op=ALU.mult),
                         reads=[Bsg[sk], self.B_bank[bu]], writes=[BhT[f]])
                    if t == 0 and f in (0, 21):
                        self.dbg(f"sg{f}", sg[sk][:], [Bsg[sk]], F32)
                    if f == 10 and t + 1 < self.NTL:
                        self.lnt_load(L, t + 1)
                if t == 0:
                    self.dbg("hT0", hT[:], BhT, BF16)
                    self.dbg("xnT0b", L.xnT[0][:], L.BxnT[0], BF16)
                for s in range(4):
                    i = t * 4 + s
                    xk = xrc % 2
                    xrc += 1
                    S.dma("sp", xr[xk][:], src.ap[i * 128:(i + 1) * 128, :], dxr[xk], reads=[src.bufs[i]],
                          writes=Bxr[xk])
                    for nh in range(2):
                        bo = 4 + dn % 3
                        dn += 1
                        fd = [(I("matmul",
                            self.bank[bo][:], lhsT=hT[:, f, s * 128:(s + 1) * 128],
                            rhs=Wd[:, f, nh * 512:(nh + 1) * 512], start=(f == 0), stop=(f == NF - 1)))
                            for f in range(NF)]
                        S.group("pe", fd, reads=WD.B + BhT, writes=[self.B_bank[bo]])
                        S.op("dve", I("scalar_tensor_tensor",
                            out=xr[xk][:, nh * 512:(nh + 1) * 512], in0=self.bank[bo][:], scalar=0.5,
                            in1=xr[xk][:, nh * 512:(nh + 1) * 512], op0=ALU.mult, op1=ALU.add),
                            reads=[self.B_bank[bo]], writes=[Bxr[xk][nh]])
                    S.op("act", I("activation", out=junk[:], in_=xr[xk][:], func=AF.Square,
                                                                  accum_out=ss_out[:, i:i + 1]),
                         reads=Bxr[xk] + [B_ss_out], writes=[Bj, Bfs])
                    if final:
                        S.op("act", I("activation", out=fs[:, xk:xk + 1], in_=ss_out[:, i:i + 1], func=AF.Sqrt,
                                      bias=EPS, scale=1.0 / D), reads=[Bfs], writes=[Bfs])
                        S.op("dve", I("reciprocal", out=fs[:, 2 + xk:3 + xk], in_=fs[:, xk:xk + 1]),
                             reads=[Bfs], writes=[Bfs])
                        S.op("dve", I("scalar_tensor_tensor", out=xr[xk][:], in0=xr[xk][:],
                                      scalar=fs[:, 2 + xk:3 + xk], in1=wfb[:], op0=ALU.mult, op1=ALU.mult),
                             reads=[Bfs, Bwfb], writes=Bxr[xk])
                    S.dma("sp", dst.ap[i * 128:(i + 1) * 128, :], xr[xk][:], dxr[xk], reads=Bxr[xk],
                          writes=[dst.bufs[i]])
            B_ss_out.w = Tok(S.eng["act"].sem, S.eng["act"].cnt)
            S.flush()


    def resid_alloc(self, ps, src, dst, ss_out, B_ss_out):
        S = self.S
        R = Prog.LNT()
        R.src, R.dst, R.ss_out, R.B_ss_out = src, dst, ss_out, B_ss_out
        R.xr = [self.sb(ps, f"xr{i}", [128, D], F32) for i in range(2)]
        R.Bxr = [[Buf(), Buf()] for _ in range(2)]
        R.dxr = [S.dsem("xr") for _ in range(2)]
        R.junk = self.sb(ps, "rjunk", [128, D], BF16)
        R.Bjunk = Buf()
        R.cnt = 0
        S.op("dve", I("memset", ss_out[:], 0.0), writes=[B_ss_out])
        return R

    def resid_begin(self, R, i):
        S = self.S
        xk = R.cnt % 2
        R.cnt += 1
        S.dma("sp", R.xr[xk][:], R.src.ap[i * 128:(i + 1) * 128, :], R.dxr[xk], reads=[R.src.bufs[i]],
              writes=R.Bxr[xk])
        return xk

    def resid_add(self, R, xk, nh, bo, scale):
        S = self.S
        S.op("dve", I("scalar_tensor_tensor", out=R.xr[xk][:, nh * 512:(nh + 1) * 512], in0=self.bank[bo][:],
                      scalar=scale, in1=R.xr[xk][:, nh * 512:(nh + 1) * 512], op0=ALU.mult, op1=ALU.add),
             reads=[self.B_bank[bo]], writes=[R.Bxr[xk][nh]])

    def resid_end(self, R, xk, i):
        S = self.S
        S.op("act", I("activation", out=R.junk[:], in_=R.xr[xk][:], func=AF.Square,
                      accum_out=R.ss_out[:, i:i + 1]), reads=R.Bxr[xk] + [R.B_ss_out], writes=[R.Bjunk])
        S.dma("sp", R.dst.ap[i * 128:(i + 1) * 128, :], R.xr[xk][:], R.dxr[xk], reads=R.Bxr[xk],
              writes=[R.dst.bufs[i]])

    def resid_finish(self, R):
        R.B_ss_out.w = Tok(self.S.eng["act"].sem, self.S.eng["act"].cnt)

    def load_cols(self, ps, name, vec_ap, ncol):
        S = self.S
        t = self.sb(ps, name, [128, ncol], F32)
        B = Buf()
        S.dma("sp", t[:], vec_ap.rearrange("(c p) -> p c", p=128), S.dsem(name), writes=[B],
              allow_slow_non_contiguous=True)
        return t, B

    def phase_sgu_a(self, src, ss_in, B_ss_in, sT):
        S = self.S
        with ExitStack() as ps:
            WV = self.load_w(ps, "Wv", 8, SG, "col", [(n * 512, (n + 1) * 512) for n in range(6)], key="sgu_in",
                             col0=SG)
            Wv = WV.tile
            self.compute_rstd(ss_in, B_ss_in, ps)
            L = self.lnt_alloc(ps, src, self.norm_w[0, 1])
            lng, Blng = self.load_cols(ps, "lng", self.sgu_ln_g, NE)
            lnb, Blnb = self.load_cols(ps, "lnb", self.sgu_ln_b, NE)
            wsf = self.sb(ps, "wsf", [128, 8, 128], F32)
            wsb = self.sb(ps, "wsb", [128, 8, 128], BF16)
            wsT = self.sb(ps, "wsT", [128, 8, 128], BF16)
            ones = self.sb(ps, "ones", [128, 128], BF16)
            bsb = self.sb(ps, "bsb", [128, 8, 128], F32)
            C = self.sb(ps, "Cmat", [128, NE, 128], F32)
            Bwsf, Bwsb, BwsT, Bones, Bbsb, BC = Buf(), Buf(), Buf(), Buf(), Buf(), Buf()
            S.dma("sp", wsf[:], self.sgu_w_s.rearrange("g t s -> t g s"), S.dsem("wsf"), writes=[Bwsf])
            S.dma("sp", bsb[:].rearrange("p g t -> p (g t)"),
                  self.sgu_b_s.rearrange("g t -> (g t)").partition_broadcast(128), S.dsem("bsb"), writes=[Bbsb])
            S.op("dve", I("tensor_copy", out=wsb[:], in_=wsf[:]), reads=[Bwsf], writes=[Bwsb])
            S.op("dve", I("memset", ones[:], 1.0), writes=[Bones])
            pt = self.bank[7][:].bitcast(BF16)
            S.group("pe", [I("transpose", out=pt[:, g * 128:(g + 1) * 128], in_=wsb[:, g, :], identity=self.ident[:])
                           for g in range(8)], reads=[Bwsb, self.B_ident], writes=[self.B_bank[7]])
            S.op("act", I("activation", out=wsT[:].rearrange("p g t -> p (g t)"), in_=pt, func=AF.Copy),
                 reads=[self.B_bank[7]], writes=[BwsT])
            for half in range(2):
                S.group("pe", [I("matmul", self.bank[half][:, j * 128:(j + 1) * 128], lhsT=ones[:],
                                 rhs=wsT[:, half * 4 + j, :], start=True, stop=True) for j in range(4)],
                        reads=[Bones, BwsT], writes=[self.B_bank[half]])
            for ec in range(NE):
                g = ec // 3
                S.op("dve", I("scalar_tensor_tensor", out=C[:, ec, :],
                              in0=self.bank[g // 4][:, (g % 4) * 128:(g % 4 + 1) * 128], scalar=lnb[:, ec:ec + 1],
                              in1=bsb[:, g, :], op0=ALU.mult, op1=ALU.add),
                     reads=[self.B_bank[g // 4], Blnb, Bbsb], writes=[BC])
            vf = [[self.sb(ps, f"vf{q}_{i}", [128, SG], BF16) for i in range(4)] for q in range(2)]
            Bvf = [[Buf() for _ in range(4)] for _ in range(2)]
            vh = [self.sb(ps, f"vh{i}", [128, SG], BF16) for i in range(4)]
            Bvh = [Buf() for _ in range(4)]
            sTt = self.sb(ps, "sTt", [128, NE, 512], BF16)
            BsTt = [Buf() for _ in range(NE)]
            dsT = S.dsem("sTst")
            sjunk = self.sb(ps, "sjunk", [128, 512], BF16)
            Bsjunk = Buf()
            s1 = self.sb(ps, "s1", [128, 24], F32)
            s2 = self.sb(ps, "s2", [128, 24], F32)
            st = [self.sb(ps, f"stt{i}", [128, 4], F32) for i in range(7)]
            Bs1, Bs2, Bst = Buf(), Buf(), Buf()
            sT_v = sT.ap.rearrange("(c p) t -> p c t", p=128)
            def spatial(ec):
                g = ec // 3
                b = ec % 2
                S.group("pe", [I("matmul", self.bank[b][:, c * 128:(c + 1) * 128],
                                 lhsT=vh[c][:, ec * 128:(ec + 1) * 128], rhs=wsT[:, g, :], start=True, stop=True)
                               for c in range(4)], reads=Bvh + [BwsT], writes=[self.B_bank[b]])
                S.op("dve", I("scalar_tensor_tensor", out=sTt[:, ec, :].rearrange("p (c t) -> p c t", c=4),
                              in0=self.bank[b][:].rearrange("p (c t) -> p c t", c=4), scalar=lng[:, ec:ec + 1],
                              in1=C[:, ec, :].unsqueeze(1).broadcast_to([128, 4, 128]), op0=ALU.mult, op1=ALU.add),
                     reads=[self.B_bank[b], Blng, BC], writes=[BsTt[ec]])

            self.lnt_load(L, 0)
            bk = 0
            prev_t = None
            for t in range(self.NTL):
                tb = t % 2
                S.op("dve", I("memset", s1[:], 0.0), writes=[Bs1])
                S.op("dve", I("memset", s2[:], 0.0), writes=[Bs2])
                for c in range(4):
                    for n in range(6):
                        gi = c * 6 + n
                        if prev_t is not None and 6 <= gi < 18:
                            for ec in (2 * (gi - 6), 2 * (gi - 6) + 1):
                                spatial(ec)
                            if gi == 17:
                                S.dma("sp", sT_v[:, :, prev_t * 512:(prev_t + 1) * 512], sTt[:], dsT, reads=BsTt,
                                      writes=[sT.bufs[prev_t]])
                        b = 2 + bk % 4
                        bk += 1
                        S.group("pe", [I("matmul", self.bank[b][:], lhsT=L.xnT[tb][:, k, c * 128:(c + 1) * 128],
                                         rhs=Wv[:, k, n * 512:(n + 1) * 512], start=(k == 0), stop=(k == 7))
                                       for k in range(8)], reads=WV.rb(n * 512, (n + 1) * 512) + L.BxnT[tb],
                                writes=[self.B_bank[b]])
                        S.op("act", I("activation", out=vf[tb][c][:, n * 512:(n + 1) * 512], in_=self.bank[b][:],
                                      func=AF.Gelu, accum_out=s1[:, gi:gi + 1]),
                             reads=[self.B_bank[b], Bs1], writes=[Bvf[tb][c]])
                        S.op("act", I("activation", out=sjunk[:], in_=vf[tb][c][:, n * 512:(n + 1) * 512],
                                      func=AF.Square, accum_out=s2[:, gi:gi + 1]),
                             reads=[Bvf[tb][c], Bs2], writes=[Bsjunk])
                    if c == 1 and t + 1 < self.NTL:
                        self.lnt_load(L, t + 1)
                Bs1.w = Bs2.w = Tok(S.eng["act"].sem, S.eng["act"].cnt)
                msum, mean, msq, var, rs, nmr, s2s = st
                S.op("dve", I("tensor_reduce", out=s2s[:], in_=s2[:].rearrange("p (c n) -> p c n", n=6),
                              axis=mybir.AxisListType.X, op=ALU.add), reads=[Bs2], writes=[Bst])
                S.op("dve", I("tensor_reduce", out=msum[:], in_=s1[:].rearrange("p (c n) -> p c n", n=6),
                              axis=mybir.AxisListType.X, op=ALU.add), reads=[Bs1], writes=[Bst])
                S.op("dve", I("tensor_scalar", out=mean[:], in0=msum[:], scalar1=1.0 / SG, scalar2=None, op0=ALU.mult),
                     reads=[Bst], writes=[Bst])
                S.op("dve", I("tensor_tensor", out=msq[:], in0=mean[:], in1=mean[:], op=ALU.mult),
                     reads=[Bst], writes=[Bst])
                S.op("dve", I("scalar_tensor_tensor", out=var[:], in0=s2s[:], scalar=1.0 / SG, in1=msq[:],
                              op0=ALU.mult, op1=ALU.subtract), reads=[Bst, Bs2], writes=[Bst])
                S.op("act", I("activation", out=var[:], in_=var[:], func=AF.Sqrt, bias=EPS, scale=1.0),
                     reads=[Bst], writes=[Bst])
                S.op("dve", I("reciprocal", out=rs[:], in_=var[:]), reads=[Bst], writes=[Bst])
                S.op("dve", I("scalar_tensor_tensor", out=nmr[:], in0=mean[:], scalar=-1.0, in1=rs[:],
                              op0=ALU.mult, op1=ALU.mult), reads=[Bst], writes=[Bst])
                for c in range(4):
                    S.op("dve", I("tensor_scalar", out=vh[c][:], in0=vf[tb][c][:], scalar1=rs[:, c:c + 1],
                                  scalar2=nmr[:, c:c + 1], op0=ALU.mult, op1=ALU.add),
                         reads=[Bvf[tb][c], Bst], writes=[Bvh[c]])
                prev_t = t
            for ec in range(NE):
                spatial(ec)
            S.dma("sp", sT_v[:, :, prev_t * 512:(prev_t + 1) * 512], sTt[:], dsT, reads=BsTt,
                  writes=[sT.bufs[prev_t]])
            S.flush()

    def phase_sgu_b(self, src, dst, ss_in, B_ss_in, ss_out, B_ss_out, sT):
        S = self.S
        with ExitStack() as ps:
            WU = self.load_w(ps, "Wuin", 8, SG, "col", [(n * 768, (n + 1) * 768) for n in range(4)], key="sgu_in")
            WO = self.load_w(ps, "Wo", NE, D, "row", [(n * 6, (n + 1) * 6) for n in range(4)], key="sgu_out")
            Wu, Wo = WU.tile, WO.tile
            self.compute_rstd(ss_in, B_ss_in, ps)
            L = self.lnt_alloc(ps, src, self.norm_w[0, 1])
            R = self.resid_alloc(ps, src, dst, ss_out, B_ss_out)
            sTt = [self.sb(ps, f"sTb{i}", [128, NE, 512], BF16) for i in range(2)]
            BsTt = [[Buf() for _ in range(NE)] for _ in range(2)]
            dsT = [S.dsem("sTld") for _ in range(2)]
            ug = [self.sb(ps, f"ug{i}", [128, 512], BF16) for i in range(2)]
            Bug = [Buf(), Buf()]
            sT_v = sT.ap.rearrange("(c p) t -> p c t", p=128)
            self.lnt_load(L, 0)
            S.dma("sp", sTt[0][:], sT_v[:, :, 0:512], dsT[0], reads=[sT.bufs[0]], writes=BsTt[0])
            gu = 0
            dn = 0
            for t in range(self.NTL):
                tb = t % 2
                for ec in range(NE):
                    b = gu % 4
                    gu += 1
                    S.group("pe", [I("matmul", self.bank[b][:], lhsT=Wu[:, k, ec * 128:(ec + 1) * 128],
                                     rhs=L.xnT[tb][:, k, :], start=(k == 0), stop=(k == 7)) for k in range(8)],
                            reads=WU.rb(ec * 128, ec * 128 + 128) + L.BxnT[tb], writes=[self.B_bank[b]])
                    uk = ec % 2
                    S.op("act", I("activation", out=ug[uk][:], in_=self.bank[b][:], func=AF.Gelu),
                         reads=[self.B_bank[b]], writes=[Bug[uk]])
                    S.op("pool", I("tensor_tensor", out=sTt[tb][:, ec, :], in0=ug[uk][:], in1=sTt[tb][:, ec, :],
                                   op=ALU.mult), reads=[Bug[uk]], writes=[BsTt[tb][ec]])
                    if ec == 10 and t + 1 < self.NTL:
                        self.lnt_load(L, t + 1)
                        S.dma("sp", sTt[1 - tb][:], sT_v[:, :, (t + 1) * 512:(t + 2) * 512], dsT[1 - tb],
                              reads=[sT.bufs[t + 1]], writes=BsTt[1 - tb])
                for s in range(4):
                    i = t * 4 + s
                    xk = self.resid_begin(R, i)
                    for nh in range(2):
                        bo = 4 + dn % 3
                        dn += 1
                        S.group("pe", [I("matmul", self.bank[bo][:], lhsT=sTt[tb][:, ec, s * 128:(s + 1) * 128],
                                         rhs=Wo[:, ec, nh * 512:(nh + 1) * 512], start=(ec == 0), stop=(ec == NE - 1))
                                       for ec in range(NE)], reads=WO.B + BsTt[tb], writes=[self.B_bank[bo]])
                        self.resid_add(R, xk, nh, bo, 1.0)
                    self.resid_end(R, xk, i)
            self.resid_finish(R)
            S.flush()


    def hgrn_consts(self, es):
        S, NB = self.S, self.NB
        self.Dd = [self.sb(es, f"Dd{d}", [128, NH, NB], F32) for d in range(2)]
        self.B_Dd = [Buf(), Buf()]
        self.carry_t = self.sb(es, "carry_t", [128, 2 * NB], F32)
        self.B_carry = Buf()
        S.dma("sp", self.carry_t[:], self.carry, S.dsem("carry"), writes=[self.B_carry])
        self.mask = [self.sb(es, f"mask{d}", [128, 128], F32) for d in range(2)]
        self.B_mask = Buf()
        for d in range(2):
            S.op("pool", I("memset", self.mask[d][:], 1.0), writes=[self.B_mask])
        S.op("pool", I("affine_select", out=self.mask[0][:], in_=self.mask[0][:], pattern=[[1, 128]],
                       compare_op=ALU.is_ge, fill=0.0, base=0, channel_multiplier=-1), writes=[self.B_mask])
        S.op("pool", I("affine_select", out=self.mask[1][:], in_=self.mask[1][:], pattern=[[-1, 128]],
                       compare_op=ALU.is_ge, fill=0.0, base=0, channel_multiplier=1), writes=[self.B_mask])
        self.mreset = self.sb(es, "mreset", [128, 512], F32)
        self.B_mreset = Buf()
        S.op("pool", I("memset", self.mreset[:], 1.0), writes=[self.B_mreset])
        for c in range(4):
            S.op("pool", I("memset", self.mreset[:, c * 128:c * 128 + 1], 0.0), writes=[self.B_mreset])
        self.onesb = self.sb(es, "onesb", [128, 128], BF16)
        self.B_onesb = Buf()
        S.op("pool", I("memset", self.onesb[:], 1.0), writes=[self.B_onesb])

    def phase_hgrn_a(self, src, ss_in, B_ss_in, H):
        S = self.S
        with ExitStack() as ps:
            WIN = self.load_w(ps, "Whin", 8, 5 * D, "col", [(c * D, (c + 1) * D) for c in (0, 4, 3, 1, 2)],
                              key="hg_in")
            Win = WIN.tile
            self.compute_rstd(ss_in, B_ss_in, ps)
            L = self.lnt_alloc(ps, src, self.norm_w[1, 1], evac_eng="dve")
            noml, lnoml, lbc, Blb = [], [], [], Buf()
            for d in range(2):
                r0, B0 = self.load_cols(ps, f"r0{d}", self.hgrn_lb_raw[d, 0], 8)
                r1, B1 = self.load_cols(ps, f"r1{d}", self.hgrn_lb_raw[d, 1], 8)
                tmp = self.sb(ps, f"lbt{d}", [128, 8], F32)
                nm = self.sb(ps, f"noml{d}", [128, 8], F32)
                lo = self.sb(ps, f"lnoml{d}", [128, 8], F32)
                S.op("dve", I("tensor_tensor", out=tmp[:], in0=r0[:], in1=r1[:], op=ALU.subtract),
                     reads=[B0, B1], writes=[Blb])
                S.op("act", I("activation", out=tmp[:], in_=tmp[:], func=AF.Exp), reads=[Blb], writes=[Blb])
                S.op("dve", I("tensor_scalar", out=tmp[:], in0=tmp[:], scalar1=1.0, scalar2=None, op0=ALU.add),
                     reads=[Blb], writes=[Blb])
                S.op("dve", I("reciprocal", out=tmp[:], in_=tmp[:]), reads=[Blb], writes=[Blb])
                lbc_ = self.sb(ps, f"lbc{d}", [128, 8], F32)
                S.op("dve", I("tensor_copy", out=lbc_[:], in_=tmp[:]), reads=[Blb], writes=[Blb])
                lbc.append(lbc_)
                S.op("dve", I("tensor_scalar", out=nm[:], in0=tmp[:], scalar1=1.0, scalar2=None, op0=ALU.subtract),
                     reads=[Blb], writes=[Blb])
                S.op("dve", I("tensor_scalar", out=tmp[:], in0=nm[:], scalar1=-1.0, scalar2=None, op0=ALU.mult),
                     reads=[Blb], writes=[Blb])
                S.op("act", I("activation", out=lo[:], in_=tmp[:], func=AF.Ln), reads=[Blb], writes=[Blb])
                noml.append(nm)
                lnoml.append(lo)
            qs = self.sb(ps, "qs", [128, NH, 512], F32)
            Bqs = [Buf() for _ in range(NH)]
            gso = [self.sb(ps, f"gso{i}", [128, 512], BF16) for i in range(2)]
            Bgso = [Buf(), Buf()]
            dgso = [S.dsem("gso") for _ in range(2)]
            vtok = self.sb(ps, "vtok", [128, 4, D], BF16)
            Bvtok = [Buf() for _ in range(4)]
            dvtok = S.dsem("vtok")
            kstok = [self.sb(ps, f"kstok{d}", [128, 4, D], BF16) for d in range(2)]
            Bkstok = [[Buf() for _ in range(NH)] for _ in range(2)]
            dkstok = [S.dsem("kstok") for _ in range(2)]
            NR = 3
            qeo = [self.sb(ps, f"qeo{i}", [128, 512], BF16) for i in range(NR)]
            kdo = [self.sb(ps, f"kdo{i}", [128, 512], BF16) for i in range(NR)]
            Bqeo = [Buf() for _ in range(NR)]
            Bkdo = [Buf() for _ in range(NR)]
            dqeo = [S.dsem("qeo") for _ in range(NR)]
            dkdo = [S.dsem("kdo") for _ in range(NR)]
            NTMP = 3
            TN = ("te", "tA", "tB", "tP", "tX")
            T_ = [{n: self.sb(ps, f"{n}{i}", [128, 512], F32) for n in TN} for i in range(NTMP)]
            BT = [{n: Buf() for n in TN + ("tks",)} for i in range(NTMP)]
            tks = [self.sb(ps, f"tks{i}", [128, 512], BF16) for i in range(NTMP)]
            v_v = H["v"].ap.rearrange("(n p) f -> p n f", p=128)
            ks_v = [H["ks"][d].ap.rearrange("(n p) f -> p n f", p=128) for d in range(2)]
            self.lnt_load(L, 0)
            bk = 0
            ro = 0
            hd = 0
            for t in range(self.NTL):
                tb = t % 2
                tok = slice(t * 512, (t + 1) * 512)
                for h in range(NH):
                    for which in range(2):
                        col = (0 if which == 0 else 4 * D) + h * 128
                        b = bk % 6
                        bk += 1
                        S.group("pe", [I("matmul", self.bank[b][:], lhsT=Win[:, k, col:col + 128],
                                         rhs=L.xnT[tb][:, k, :], start=(k == 0), stop=(k == 7)) for k in range(8)],
                                reads=WIN.rb(col, col + 128) + L.BxnT[tb], writes=[self.B_bank[b]])
                        if which == 0:
                            S.op("act", I("activation", out=qs[:, h, :], in_=self.bank[b][:], func=AF.Silu),
                                 reads=[self.B_bank[b]], writes=[Bqs[h]])
                        else:
                            gk = h % 2
                            S.op("act", I("activation", out=gso[gk][:], in_=self.bank[b][:], func=AF.Silu),
                                 reads=[self.B_bank[b]], writes=[Bgso[gk]])
                            S.dma("sp", H["gs"].ap[h * 128:(h + 1) * 128, tok], gso[gk][:], dgso[gk],
                                  reads=[Bgso[gk]], writes=[H["gs"].bufs[t][h]])
                def vgroup(c, n):
                    nonlocal bk
                    b = bk % 6
                    bk += 1
                    S.group("pe", [I("matmul", self.bank[b][:], lhsT=L.xnT[tb][:, k, c * 128:(c + 1) * 128],
                                     rhs=Win[:, k, 3 * D + n * 512:3 * D + (n + 1) * 512], start=(k == 0),
                                     stop=(k == 7)) for k in range(8)],
                            reads=WIN.rb(3 * D, 4 * D) + L.BxnT[tb], writes=[self.B_bank[b]])
                    S.op("dve", I("tensor_copy", out=vtok[:, c, n * 512:(n + 1) * 512], in_=self.bank[b][:]),
                         reads=[self.B_bank[b]], writes=[Bvtok[c]])
                    if c == 3 and n == 1:
                        S.dma("sp", v_v[:, t * 4:(t + 1) * 4, :], vtok[:], dvtok, reads=Bvtok, writes=H["v"].bufs[t])

                vlist = [(c, n) for c in range(4) for n in range(2)]
                items = [(d, h) for d in range(2) for h in range(NH)]

                def stage0(d, h):
                    nonlocal bk
                    col = (1 + d) * D + h * 128
                    b = bk % 6
                    bk += 1
                    S.group("pe", [I("matmul", self.bank[b][:], lhsT=Win[:, k, col:col + 128],
                                     rhs=L.xnT[tb][:, k, :], start=(k == 0), stop=(k == 7)) for k in range(8)],
                            reads=WIN.rb(col, col + 128) + L.BxnT[tb], writes=[self.B_bank[b]])
                    return b

                def stage1(d, h, i_, b):
                    T, B_ = T_[i_], BT[i_]
                    S.op("act", I("activation", out=T["te"][:], in_=self.bank[b][:], func=AF.Exp),
                         reads=[self.B_bank[b]], writes=[B_["te"]])
                    S.op("act", I("activation", out=T["tA"][:], in_=T["te"][:], func=AF.Ln, bias=lbc[d][:, h:h + 1],
                                  scale=1.0), reads=[B_["te"], Blb], writes=[B_["tA"]])
                    S.op("act", I("activation", out=T["tB"][:], in_=T["te"][:], func=AF.Ln, bias=1.0, scale=1.0),
                         reads=[B_["te"]], writes=[B_["tB"]])
                    S.op("dve", I("tensor_tensor", out=T["tA"][:], in0=T["tA"][:], in1=T["tB"][:], op=ALU.subtract),
                         reads=[B_["tB"]], writes=[B_["tA"]])
                    S.op("dve", I("tensor_tensor_scan", out=T["tP"][:], data0=self.mreset[:], data1=T["tA"][:],
                                  initial=0.0, op0=ALU.mult, op1=ALU.add),
                         reads=[B_["tA"], self.B_mreset], writes=[B_["tP"]])
                    v4 = lambda ap: ap.rearrange("p (c t) -> p c t", c=4)
                    Tb = v4(T["tP"][:])[:, :, 127:128].broadcast_to([128, 4, 128])
                    if d == 0:
                        S.op("pool", I("tensor_tensor", out=T["te"][:], in0=T["tB"][:], in1=T["tP"][:], op=ALU.add),
                             reads=[B_["tB"], B_["tP"]], writes=[B_["te"]])
                        S.op("pool", I("tensor_tensor", out=v4(T["tX"][:]), in0=Tb, in1=v4(T["te"][:]),
                                       op=ALU.subtract), reads=[B_["tP"], B_["te"]], writes=[B_["tX"]])
                    else:
                        S.op("dve", I("tensor_tensor", out=T["tA"][:], in0=T["tP"][:], in1=T["tA"][:],
                                      op=ALU.subtract), reads=[B_["tP"]], writes=[B_["tA"]])
                        S.op("pool", I("tensor_tensor", out=T["tB"][:], in0=T["tA"][:], in1=T["tB"][:],
                                       op=ALU.subtract), reads=[B_["tA"]], writes=[B_["tB"]])
                        S.op("pool", I("tensor_tensor", out=v4(T["tA"][:]), in0=Tb, in1=v4(T["tA"][:]),
                                       op=ALU.subtract), reads=[B_["tP"]], writes=[B_["tA"]])
                        S.op("pool", I("tensor_tensor", out=v4(T["te"][:]), in0=Tb, in1=v4(T["tB"][:]),
                                       op=ALU.subtract), reads=[B_["tP"], B_["tB"]], writes=[B_["te"]])

                def stage2(d, h, i_):
                    nonlocal ro
                    T, B_ = T_[i_], BT[i_]
                    oml_b = lnoml[d][:, h:h + 1]
                    r_ = ro % NR
                    ro += 1
                    S.op("act", I("activation", out=self.Dd[d][:, h, t * 4:(t + 1) * 4],
                                  in_=T["tP"][:, 127:512:128], func=AF.Exp), reads=[B_["tP"]])
                    qarg, qB = (T["tP"], B_["tP"]) if d == 0 else (T["tA"], B_["tA"])
                    ksarg, ksB = (T["tX"], B_["tX"]) if d == 0 else (T["tB"], B_["tB"])
                    S.op("act", I("activation", out=qarg[:], in_=qarg[:], func=AF.Exp), reads=[qB], writes=[qB])
                    S.op("act", I("activation", out=kdo[r_][:], in_=T["te"][:], func=AF.Exp, scale=-1.0, bias=oml_b),
                         reads=[B_["te"], Blb], writes=[Bkdo[r_]])
                    S.op("act", I("activation", out=tks[i_][:], in_=ksarg[:], func=AF.Exp, bias=oml_b),
                         reads=[ksB, Blb], writes=[B_["tks"]])
                    S.op("dve", I("tensor_tensor", out=qeo[r_][:], in0=qs[:, h, :], in1=qarg[:], op=ALU.mult),
                         reads=[Bqs[h], qB], writes=[Bqeo[r_]])
                    S.dma("sp", H["qe"][d].ap[h * 128:(h + 1) * 128, tok], qeo[r_][:], dqeo[r_],
                          reads=[Bqeo[r_]], writes=[H["qe"][d].bufs[t][h]])
                    S.dma("sp", H["kd"][d].ap[h * 128:(h + 1) * 128, tok], kdo[r_][:], dkdo[r_],
                          reads=[Bkdo[r_]], writes=[H["kd"][d].bufs[t][h]])
                    pt = self.bank[6 + (h % 2)][:].bitcast(BF16)
                    S.group("pe", [I("transpose", out=pt[:, c * 128:(c + 1) * 128],
                                     in_=tks[i_][:, c * 128:(c + 1) * 128], identity=self.ident[:])
                                   for c in range(4)], reads=[B_["tks"], self.B_ident],
                            writes=[self.B_bank[6 + (h % 2)]])
                    S.op("dve", I("tensor_copy", out=kstok[d][:, :, h * 128:(h + 1) * 128],
                                  in_=pt[:, 0:512].rearrange("p (c k) -> p c k", c=4)),
                         reads=[self.B_bank[6 + (h % 2)]], writes=[Bkstok[d][h]])
                    if h == NH - 1:
                        S.dma("sp", ks_v[d][:, t * 4:(t + 1) * 4, :], kstok[d][:], dkstok[d], reads=Bkstok[d],
                              writes=H["ks"][d].bufs[t])

                AH = 4
                banks_ = [stage0(*items[q]) for q in range(AH)]
                stage1(*items[0], hd % NTMP, banks_[0])
                stage1(*items[1], (hd + 1) % NTMP, banks_[1])
                for j, (d, h) in enumerate(items):
                    if j + AH < len(items):
                        banks_.append(stage0(*items[j + AH]))
                    if j + 2 < len(items):
                        stage1(*items[j + 2], (hd + 2) % NTMP, banks_[j + 2])
                    stage2(d, h, hd % NTMP)
                    hd += 1
                    if j % 2 == 1:
                        vgroup(*vlist[j // 2])
                    if 1 <= j <= 5 and t + 1 < self.NTL:
                        if j == 1:
                            nsteps = self.lnt_steps(L, t + 1)
                        nsteps[j - 1]()
            self.B_Dd[0].w = self.B_Dd[1].w = Tok(S.eng["act"].sem, S.eng["act"].cnt)
            S.flush()

    def phase_hgrn_scan(self, d, H, src=None, dst=None, ss_out=None, B_ss_out=None):
        S, NB, NTL = self.S, self.NB, self.NTL
        with ExitStack() as ps:
            NO = 3 if d == 1 else 2
            qeT = [self.sb(ps, f"qeT{i}", [128, NH, 512], BF16) for i in range(2)]
            kdT = [self.sb(ps, f"kdT{i}", [128, NH, 512], BF16) for i in range(2)]
            kst = [self.sb(ps, f"kst{i}", [128, 4, D], BF16) for i in range(2)]
            vt = [self.sb(ps, f"vt{i}", [128, 4, D], BF16) for i in range(2)]
            oT = [self.sb(ps, f"oT{i}", [128, NH, 512], F32) for i in range(NO)]
            Bld = [[Buf() for _ in range(4)] for _ in range(2)]
            BoT = [[Buf() for _ in range(NH)] for _ in range(NO)]
            dld = [[S.dsem("scld") for _ in range(4)] for _ in range(2)]
            doT = [S.dsem("oT") for _ in range(NO)]
            Am = [self.sb(ps, f"Am{i}", [128, 4, 128], BF16) for i in range(2)]
            BAm = [Buf(), Buf()]
            St = self.sb(ps, "St", [128, NH, 128], F32)
            BSt = Buf()
            Sb = [self.sb(ps, f"Sb{i}", [128, NH, 128], BF16) for i in range(2)]
            BSb = [Buf(), Buf()]
            S.op("pool", I("memset", St[:], 0.0), writes=[BSt])
            S.op("pool", I("memset", Sb[0][:], 0.0), writes=[BSb[0]])
            S.op("pool", I("memset", Sb[1][:], 0.0), writes=[BSb[1]])
            qe_v = H["qe"][d].ap.rearrange("(h p) t -> p h t", p=128)
            kd_v = H["kd"][d].ap.rearrange("(h p) t -> p h t", p=128)
            ks_v = H["ks"][d].ap.rearrange("(n p) f -> p n f", p=128)
            v_v = H["v"].ap.rearrange("(n p) f -> p n f", p=128)
            of_v = H["of"].ap.rearrange("(h p) t -> p h t", p=128)
            if d == 1:
                WO = self.load_w(ps, "Who", NH, D, "row", [(0, NH)], key="hg_out")
                Wo = WO.tile
                nw, Bnw = self.load_cols(ps, "hnw", self.hgrn_norm_w, NH)
                for h in range(NH):
                    S.op("dve", I("tensor_scalar", out=Wo[:, h, :], in0=Wo[:, h, :], scalar1=nw[:, h:h + 1],
                                  scalar2=None, op0=ALU.mult), reads=WO.B + [Bnw], writes=WO.B)
                gsT = [self.sb(ps, f"gsT{i}", [128, NH, 512], BF16) for i in range(NO)]
                Bgs = [Buf() for _ in range(NO)]
                dgs = [S.dsem("gsld") for _ in range(NO)]
                gs_v = H["gs"].ap.rearrange("(h p) t -> p h t", p=128)
                sq = [self.sb(ps, f"sq{i}", [128, 512], BF16) for i in range(2)]
                Bsq = [Buf(), Buf()]
                rt = [self.sb(ps, f"rt{i}", [128, 512], F32) for i in range(2)]
                Brt = [Buf(), Buf()]
                onT = [self.sb(ps, f"onT{i}", [128, NH, 512], BF16) for i in range(2)]
                BonT = [[Buf() for _ in range(NH)] for _ in range(2)]
                R = self.resid_alloc(ps, src, dst, ss_out, B_ss_out)

            order = list(range(NTL)) if d == 0 else list(range(NTL - 1, -1, -1))

            def issue_loads(j, parts=(0, 1, 2, 3, 4, 5)):
                t = order[j]
                p = j % 2
                po = j % NO
                tok = slice(t * 512, (t + 1) * 512)
                if 0 in parts:
                    S.dma("sp", qeT[p][:], qe_v[:, :, tok], dld[p][0], reads=H["qe"][d].bufs[t], writes=[Bld[p][0]])
                if 1 in parts:
                    S.dma("sp", kdT[p][:], kd_v[:, :, tok], dld[p][1], reads=H["kd"][d].bufs[t], writes=[Bld[p][1]])
                if 2 in parts:
                    S.dma("sp", kst[p][:], ks_v[:, t * 4:(t + 1) * 4, :], dld[p][2], reads=H["ks"][d].bufs[t],
                          writes=[Bld[p][2]])
                if 3 in parts:
                    S.dma("sp", vt[p][:], v_v[:, t * 4:(t + 1) * 4, :], dld[p][3], reads=H["v"].bufs[t],
                          writes=[Bld[p][3]])
                if d == 1 and 4 in parts:
                    S.dma("sp", oT[po][:], of_v[:, :, tok], doT[po], reads=H["of"].bufs[t], writes=BoT[po])
                if d == 1 and 5 in parts:
                    S.dma("sp", gsT[po][:], gs_v[:, :, tok], dgs[po], reads=H["gs"].bufs[t], writes=[Bgs[po]])

            class TW:
                pass

            def tile_work(j):
                w = TW()
                w.t = order[j]
                w.po = j % NO
                w.pn = j % 2
                w.xk = {}
                return w

            def sqr(w, h):
                k2 = h % 2
                S.op("act", I("activation", out=sq[k2][:], in_=oT[w.po][:, h, :], func=AF.Square),
                     reads=[BoT[w.po][h]], writes=[Bsq[k2]])

            def mmn(w, h):
                k2 = h % 2
                S.group("pe", [I("matmul", self.bank[6][:], lhsT=self.onesb[:], rhs=sq[k2][:], start=True, stop=True)],
                        reads=[Bsq[k2], self.B_onesb], writes=[self.B_bank[6]])
                S.op("act", I("activation", out=rt[k2][:], in_=self.bank[6][:], func=AF.Ln, bias=EPS,
                              scale=1.0 / 128), reads=[self.B_bank[6]], writes=[Brt[k2]])
                S.op("act", I("activation", out=rt[k2][:], in_=rt[k2][:], func=AF.Exp, scale=-0.5),
                     reads=[Brt[k2]], writes=[Brt[k2]])

            def fin(w, h):
                k2 = h % 2
                S.op("pool", I("tensor_tensor", out=rt[k2][:], in0=oT[w.po][:, h, :], in1=rt[k2][:], op=ALU.mult),
                     reads=[BoT[w.po][h], Brt[k2]], writes=[Brt[k2]])
                S.op("pool", I("tensor_tensor", out=onT[w.pn][:, h, :], in0=rt[k2][:], in1=gsT[w.po][:, h, :],
                               op=ALU.mult), reads=[Brt[k2], Bgs[w.po]], writes=[BonT[w.pn][h]])

            def outp(w, s_, nh, pe_only=False, dve_only=False, load_only=False):
                i = w.t * 4 + s_
                if load_only:
                    w.xk[s_] = self.resid_begin(R, i)
                    return
                if not dve_only:
                    if nh == 0 and s_ not in w.xk:
                        w.xk[s_] = self.resid_begin(R, i)
                    S.group("pe", [I("matmul", self.bank[7][:], lhsT=onT[w.pn][:, h, s_ * 128:(s_ + 1) * 128],
                                     rhs=Wo[:, h, nh * 512:(nh + 1) * 512], start=(h == 0), stop=(h == NH - 1))
                                   for h in range(NH)], reads=WO.B + BonT[w.pn], writes=[self.B_bank[7]])
                if pe_only:
                    return
                xk = w.xk[s_]
                self.resid_add(R, xk, nh, 7, 1.0)
                if nh == 1:
                    self.resid_end(R, xk, i)

            issue_loads(0)
            sbi = 0
            normW = None
            outW = None
            for j in range(NTL):
                t = order[j]
                p = j % 2
                po = j % NO
                if d == 0 and j + 1 < NTL:
                    issue_loads(j + 1)
                corder = range(4) if d == 0 else range(3, -1, -1)
                for ci, c in enumerate(corder):
                    n = t * 4 + c
                    cs = slice(c * 128, (c + 1) * 128)
                    if normW is not None:
                        sqr(normW, 2 * ci)
                        sqr(normW, 2 * ci + 1)
                    if d == 1 and outW is not None:
                        outp(outW, ci, 0, load_only=True)
                    if d == 1 and j + 1 < NTL and ci < 3:
                        issue_loads(j + 1, parts=((0, 1), (2, 3), (4, 5))[ci])
                    for hb in range(2):
                        S.group("pe", [I("matmul", self.bank[hb][:, q * 128:(q + 1) * 128],
                                         lhsT=kdT[p][:, hb * 4 + q, cs], rhs=qeT[p][:, hb * 4 + q, cs],
                                         start=True, stop=True) for q in range(4)],
                                reads=[Bld[p][0], Bld[p][1]], writes=[self.B_bank[hb]])
                    for hb in range(2):
                        S.group("pe", [I("matmul", self.bank[2 + hb][:, q * 128:(q + 1) * 128],
                                         lhsT=kst[p][:, c, (hb * 4 + q) * 128:(hb * 4 + q + 1) * 128],
                                         rhs=vt[p][:, c, (hb * 4 + q) * 128:(hb * 4 + q + 1) * 128],
                                         start=True, stop=True) for q in range(4)],
                                reads=[Bld[p][2], Bld[p][3]], writes=[self.B_bank[2 + hb]])
                    if normW is not None:
                        mmn(normW, 2 * ci)
                    if outW is not None:
                        outp(outW, ci, 0, pe_only=True)
                    for hb in range(2):
                        S.op("dve", I("tensor_tensor", out=Am[hb][:],
                                      in0=self.bank[hb][:].rearrange("p (q t) -> p q t", q=4),
                                      in1=self.mask[d][:].unsqueeze(1).broadcast_to([128, 4, 128]), op=ALU.mult),
                             reads=[self.B_bank[hb], self.B_mask], writes=[BAm[hb]])
                    sp_ = sbi % 2
                    for hb in range(2):
                        fns = []
                        for q in range(4):
                            h = hb * 4 + q
                            fns.append(I("matmul", self.bank[4 + hb][:, q * 128:(q + 1) * 128],
                                         lhsT=vt[p][:, c, h * 128:(h + 1) * 128], rhs=Am[hb][:, q, :],
                                         start=True, stop=False))
                            fns.append(I("matmul", self.bank[4 + hb][:, q * 128:(q + 1) * 128],
                                         lhsT=Sb[sp_][:, h, :], rhs=qeT[p][:, h, cs], start=False, stop=True))
                        S.group("pe", fns, reads=[Bld[p][3], Bld[p][0], BAm[hb], BSb[sp_]],
                                writes=[self.B_bank[4 + hb]])
                    S.op("dve", I("tensor_tensor", out=St[:], in0=St[:],
                                  in1=self.Dd[d][:, :, n:n + 1].broadcast_to([128, NH, 128]), op=ALU.mult),
                         reads=[self.B_Dd[d]], writes=[BSt])
                    for hb in range(2):
                        S.op("dve", I("tensor_tensor", out=St[:, hb * 4:(hb + 1) * 4, :],
                                      in0=St[:, hb * 4:(hb + 1) * 4, :],
                                      in1=self.bank[2 + hb][:].rearrange("p (q t) -> p q t", q=4), op=ALU.add),
                             reads=[self.B_bank[2 + hb]], writes=[BSt])
                    nxt = n + 1 if d == 0 else n - 1
                    if 0 <= nxt < NB:
                        bnd = (nxt % 16 == 0) if d == 0 else (nxt % 16 == 15)
                        if bnd:
                            ccol = nxt if d == 0 else NB + nxt
                            S.op("dve", I("tensor_scalar", out=St[:], in0=St[:],
                                          scalar1=self.carry_t[:, ccol:ccol + 1], scalar2=None, op0=ALU.mult),
                                 reads=[self.B_carry], writes=[BSt])
                    sbi += 1
                    S.op("act", I("activation", out=Sb[sbi % 2][:], in_=St[:], func=AF.Copy),
                         reads=[BSt], writes=[BSb[sbi % 2]])
                    if outW is not None:
                        outp(outW, ci, 0, dve_only=True)
                    if normW is not None:
                        mmn(normW, 2 * ci + 1)
                    if outW is not None:
                        outp(outW, ci, 1)
                    for hb in range(2):
                        ov = oT[po][:, hb * 4:(hb + 1) * 4, cs]
                        bv = self.bank[4 + hb][:].rearrange("p (q t) -> p q t", q=4)
                        if d == 0:
                            S.op("act", I("activation", out=ov, in_=bv, func=AF.Copy),
                                 reads=[self.B_bank[4 + hb]], writes=BoT[po][hb * 4:(hb + 1) * 4])
                        else:
                            S.op("dve", I("tensor_tensor", out=ov, in0=bv, in1=ov, op=ALU.add),
                                 reads=[self.B_bank[4 + hb]], writes=BoT[po][hb * 4:(hb + 1) * 4])
                    if normW is not None:
                        fin(normW, 2 * ci)
                        fin(normW, 2 * ci + 1)
                if d == 0:
                    S.dma("sp", of_v[:, :, t * 512:(t + 1) * 512], oT[po][:], doT[po], reads=BoT[po],
                          writes=H["of"].bufs[t])
                else:
                    outW = normW
                    normW = tile_work(j)
            if d == 1:
                if outW is not None:
                    for s_ in range(4):
                        outp(outW, s_, 0)
                        outp(outW, s_, 1)
                for h in range(NH):
                    sqr(normW, h)
                    mmn(normW, h)
                    fin(normW, h)
                for s_ in range(4):
                    outp(normW, s_, 0)
                    outp(normW, s_, 1)
                self.resid_finish(R)
            S.flush()

    def phase_final(self, src, ss, B_ss, copy_only=False):
        S = self.S
        with ExitStack() as ps:
            self.compute_rstd(ss, B_ss, ps)
            wb = self.sb(ps, "fwb", [128, D], F32)
            Bwb = Buf()
            dwb = S.dsem("fwb")
            S.dma("sp", wb[:], self.final_norm.partition_broadcast(128), dwb, writes=[Bwb])
            xl = [self.sb(ps, f"fx{i}", [128, D], F32) for i in range(3)]
            Bx = [Buf() for _ in range(3)]
            dx = [S.dsem("fx") for _ in range(3)]
            for i in range(self.NB):
                k = i % 3
                S.dma("sp", xl[k][:], src.ap[i * 128:(i + 1) * 128, :], dx[k], reads=[src.bufs[i]], writes=[Bx[k]])
                if not copy_only:
                    S.op("dve", I("scalar_tensor_tensor", out=xl[k][:], in0=xl[k][:],
                                                                          scalar=self.rstd[:, i:i + 1], in1=wb[:],
                                                                          op0=ALU.mult, op1=ALU.mult),
                         reads=[self.B_rstd, Bwb], writes=[Bx[k]])
                S.dma("sp", self.y_out.ap[i * 128:(i + 1) * 128, :], xl[k][:], dx[k], reads=[Bx[k]],
                      writes=[self.y_out.bufs[i]])
            S.flush()


W_NAMES = ["norm_w", "ffn_gate", "ffn_up", "ffn_down", "sgu_w_in", "sgu_ln_g", "sgu_ln_b", "sgu_w_s", "sgu_b_s",
           "sgu_w_out", "hgrn_w_in", "hgrn_lb_raw", "hgrn_norm_w", "hgrn_w_out", "final_norm"]
W_SQUEEZE = {"sgu_w_in", "sgu_ln_g", "sgu_ln_b", "sgu_w_s", "sgu_b_s", "sgu_w_out", "hgrn_w_in", "hgrn_norm_w",
             "hgrn_w_out"}


def make_carry(NB, seq_blocks):
    c = np.ones((128, 2 * NB), np.float32)
    for n in range(NB):
        if n % seq_blocks == 0:
            c[:, n] = 0.0
        if n % seq_blocks == seq_blocks - 1:
            c[:, NB + n] = 0.0
    return c


def kernel(**inputs):
    xp = np.ascontiguousarray(inputs["x_prompt"], dtype=np.float32)
    xs = np.ascontiguousarray(inputs["x_sample"], dtype=np.float32)
    NT = NT_FULL
    prog = Prog(NT)
    wmap = {}
    for n in W_NAMES:
        a = np.ascontiguousarray(inputs[n], dtype=np.float32)
        if n in W_SQUEEZE:
            a = a[0]
        wmap[n] = a
    in_maps = []
    for c in range(NCORES):
        m = dict(wmap)
        if c < 4:
            m["x"] = xp[4 * c:4 * c + 4].reshape(NT, D)
            m["carry"] = make_carry(NT // 128, SEQ_PROMPT // 128)
        else:
            m["x"] = xs[c - 4].reshape(NT, D)
            m["carry"] = make_carry(NT // 128, NT // 128)
        in_maps.append(m)
    res = run_bass_kernel_spmd(prog.nc, in_maps, core_ids=list(range(NCORES)))
    ys = [np.asarray(r["y"], dtype=np.float32) for r in res.results]
    y_prompt = np.stack(ys[:4]).reshape(16, 2048, D)
    y_sample = np.stack(ys[4:]).reshape(4, 8192, D)
    return (y_prompt, y_sample)
```

```python
import numpy as np
from contextlib import ExitStack
import concourse.bass as bass
import concourse.mybir as mybir
from concourse.bass_utils import run_bass_kernel_spmd

F32, BF16 = mybir.dt.float32, mybir.dt.bfloat16
AF = mybir.ActivationFunctionType
ALU = mybir.AluOpType

D = 1024
FF = 2816
NF = FF // 128
SG = 3072
NE = SG // 128
NH = 8
EPS = 1e-6
NCORES = 8
NT_FULL = 8192
SEQ_PROMPT = 2048


class Tok:
    __slots__ = ("sem", "val")

    def __init__(self, sem, val):
        self.sem = sem
        self.val = val


class Buf:
    __slots__ = ("w", "r", "name")

    def __init__(self, name=""):
        self.w = None
        self.r = []
        self.name = name


class DSem:
    def __init__(self, h):
        self.h = h
        self.cnt = 0


class Eng:
    def __init__(self, name):
        self.name = name
        self.ops = []
        self.sem = None
        self.cnt = 0
        self.known = {}
        self.rec = []


class Sched:
    def __init__(self, nc, es):
        self.nc = nc
        self.es = es
        self.eng = {n: Eng(n) for n in ("pe", "act", "dve", "pool", "sp")}
        self.nsem = 0
        self.dsems = []
        self.free_dsems = []
        self.phase_dsems = []
        self.bg_dsems = []
        self._new_engine_sems()

    def _newsem(self, name):
        self.nsem += 1
        return self.es.enter_context(self.nc.semaphore(f"{name}{self.nsem}"))

    def _new_engine_sems(self):
        for n in ("pe", "act", "dve", "pool"):
            E = self.eng[n]
            E.sem = self._newsem("e" + n)
            E.cnt = 0

    def dsem(self, name="d", bg=False):
        if bg:
            d = DSem(self._newsem(name))
            self.bg_dsems.append(d)
            return d
        if self.free_dsems:
            d = self.free_dsems.pop()
        else:
            d = DSem(self._newsem(name))
            self.dsems.append(d)
        self.phase_dsems.append(d)
        return d

    def _deps(self, E, reads, writes):
        need = {}

        def add(t):
            if t is None:
                return
            if E.name == "pe" and t.sem is E.sem:
                return
            k = id(t.sem)
            if E.known.get(k, 0) >= t.val:
                return
            if k not in need or need[k].val < t.val:
                need[k] = t

        for b in reads:
            add(b.w)
        for b in writes:
            add(b.w)
            for t in b.r:
                add(t)
        for k, t in need.items():
            E.known[k] = t.val
        return list(need.values())

    def _reg(self, tok, reads, writes):
        for b in reads:
            b.r.append(tok)
        for b in writes:
            b.w = tok
            b.r = []

    def op(self, en, fn, reads=(), writes=()):
        return self.group(en, [fn], reads, writes)

    def group(self, en, fns, reads=(), writes=()):
        E = self.eng[en]
        waits = self._deps(E, reads, writes)
        E.cnt += 1
        sem = E.sem
        tok = Tok(sem, E.cnt)

        def run(e):
            for t in waits:
                e.wait_ge(t.sem, t.val)
            ins = None
            for fn in fns:
                ins = getattr(e, fn[0])(*fn[1], **fn[2])
            ins.then_inc(sem, 1)

        E.ops.append(run)
        E.rec.append(([(id(t.sem), t.val) for t in waits], (id(sem), 1)))
        self._reg(tok, reads, writes)
        return tok

    def dma(self, en, out, in_, ds, reads=(), writes=(), **kw):
        E = self.eng[en]
        waits = self._deps(E, reads, writes)
        ds.cnt += 16
        tok = Tok(ds.h, ds.cnt)
        h = ds.h

        def run(e):
            for t in waits:
                e.wait_ge(t.sem, t.val)
            e.dma_start(out=out, in_=in_, **kw).then_inc(h, 16)

        E.ops.append(run)
        E.rec.append(([(id(t.sem), t.val) for t in waits], (id(h), 16)))
        self._reg(tok, reads, writes)
        return tok

    def check_deadlock(self):
        vals = getattr(self, "_simvals", {})
        ptr = {n: 0 for n in self.eng}
        prog = True
        while prog:
            prog = False
            for n, E in self.eng.items():
                while ptr[n] < len(E.rec):
                    waits, inc = E.rec[ptr[n]]
                    if all(vals.get(k, 0) >= v for k, v in waits):
                        if inc is not None:
                            vals[inc[0]] = vals.get(inc[0], 0) + inc[1]
                        ptr[n] += 1
                        prog = True
                    else:
                        break
        stuck = {n: (ptr[n], len(E.rec)) for n, E in self.eng.items() if ptr[n] < len(E.rec)}
        self._simvals = vals
        if stuck:
            msg = []
            for n, (p, tot) in stuck.items():
                waits, inc = self.eng[n].rec[p]
                msg.append(f"{n}: op {p}/{tot} waits " + str([(k % 10000, v, vals.get(k, 0)) for k, v in waits if vals.get(k, 0) < v]))
            raise RuntimeError("schedule deadlock: " + "; ".join(msg))
        for E in self.eng.values():
            E.rec = []

    def barrier(self, final=False):
        toks = [Tok(self.eng[n].sem, self.eng[n].cnt) for n in ("pe", "act", "dve", "pool") if self.eng[n].cnt > 0]
        toks += [Tok(d.h, d.cnt) for d in self.dsems if d.cnt > 0]
        for n, E in self.eng.items():
            ws = []
            for t in toks:
                if E.name == "pe" and t.sem is E.sem:
                    continue
                k = id(t.sem)
                if E.known.get(k, 0) >= t.val:
                    continue
                E.known[k] = t.val
                ws.append(t)

            def run(e, ws=ws):
                for t in ws:
                    e.wait_ge(t.sem, t.val)

            E.ops.append(run)
            E.rec.append(([(id(t.sem), t.val) for t in ws], None))

    def flush(self):
        self.barrier()
        self.check_deadlock()
        lists = {n: E.ops for n, E in self.eng.items()}
        for E in self.eng.values():
            E.ops = []
        with self.nc.Block() as block:
            @block.tensor
            def _(e):
                for f in lists["pe"]:
                    f(e)

            @block.scalar
            def _(e):
                for f in lists["act"]:
                    f(e)

            @block.vector
            def _(e):
                for f in lists["dve"]:
                    f(e)

            @block.gpsimd
            def _(e):
                for f in lists["pool"]:
                    f(e)

            @block.sync
            def _(e):
                for f in lists["sp"]:
                    f(e)
        self.free_dsems.extend(self.phase_dsems)
        self.phase_dsems = []


def I(name, *a, **kw):
    return (name, a, kw)


class DramAct:
    def __init__(self, ap, nblk, name, sub=None):
        self.ap = ap
        if sub is None:
            self.bufs = [Buf(f"{name}{i}") for i in range(nblk)]
        else:
            self.bufs = [[Buf(f"{name}{i}_{j}") for j in range(sub)] for i in range(nblk)]


class Prog:
    def __init__(self, NT, stop_after=None, debug=False):
        assert NT % 512 == 0
        self.debug = debug
        self.NT = NT
        self.NB = NT // 128
        self.NTL = NT // 512
        self.stop_after = stop_after
        self.nc = bass.Bass("TRN2", target_bir_lowering=False)
        self.es = ExitStack()
        self.S = Sched(self.nc, self.es)
        self.build()
        self.es.close()

    def dram(self, name, shape, dt, kind="Internal"):
        return self.nc.dram_tensor(name, list(shape), dt, kind=kind).ap()

    def sb(self, stack, name, shape, dt):
        self._uid = getattr(self, "_uid", 0) + 1
        return stack.enter_context(self.nc.sbuf_tensor(f"{name}_{self._uid}", list(shape), dt))

    def dbg(self, name, ap, bufs, dt, eng="sp"):
        if not getattr(self, "debug", False):
            return
        out = self.dram("dbg_" + name, list(ap.shape), dt, "ExternalOutput")
        self.S.dma(eng, out, ap, self.S.dsem("dbg"), reads=bufs)

    def build(self):
        nc, S, NT, NB = self.nc, self.S, self.NT, self.NB
        es = self.es
        self.x_in = DramAct(self.dram("x", [NT, D], F32, "ExternalInput"), NB, "xin")
        self.y_out = DramAct(self.dram("y", [NT, D], F32, "ExternalOutput"), NB, "yout")
        self.norm_w = self.dram("norm_w", [2, 3, D], F32, "ExternalInput")
        self.ffn_gate = self.dram("ffn_gate", [2, 2, D, FF], F32, "ExternalInput")
        self.ffn_up = self.dram("ffn_up", [2, 2, D, FF], F32, "ExternalInput")
        self.ffn_down = self.dram("ffn_down", [2, 2, FF, D], F32, "ExternalInput")
        self.sgu_w_in = self.dram("sgu_w_in", [D, 2 * SG], F32, "ExternalInput")
        self.sgu_ln_g = self.dram("sgu_ln_g", [SG], F32, "ExternalInput")
        self.sgu_ln_b = self.dram("sgu_ln_b", [SG], F32, "ExternalInput")
        self.sgu_w_s = self.dram("sgu_w_s", [8, 128, 128], F32, "ExternalInput")
        self.sgu_b_s = self.dram("sgu_b_s", [8, 128], F32, "ExternalInput")
        self.sgu_w_out = self.dram("sgu_w_out", [SG, D], F32, "ExternalInput")
        self.hgrn_w_in = self.dram("hgrn_w_in", [D, 5 * D], F32, "ExternalInput")
        self.hgrn_lb_raw = self.dram("hgrn_lb_raw", [2, 2, D], F32, "ExternalInput")
        self.hgrn_norm_w = self.dram("hgrn_norm_w", [D], F32, "ExternalInput")
        self.hgrn_w_out = self.dram("hgrn_w_out", [D, D], F32, "ExternalInput")
        self.final_norm = self.dram("final_norm", [D], F32, "ExternalInput")
        self.carry = self.dram("carry", [128, 2 * NB], F32, "ExternalInput")
        self.xa = DramAct(self.dram("xa", [NT, D], F32), NB, "xa")
        self.xb = DramAct(self.dram("xb", [NT, D], F32), NB, "xb")
        self.sT = DramAct(self.dram("sT", [SG, NT], BF16), self.NTL, "sT")
        NTL = self.NTL
        self.H = dict(
            qe=[DramAct(self.dram(f"qe{d}", [D, NT], BF16), NTL, f"qe{d}", 8) for d in range(2)],
            kd=[DramAct(self.dram(f"kd{d}", [D, NT], BF16), NTL, f"kd{d}", 8) for d in range(2)],
            ks=[DramAct(self.dram(f"ks{d}", [NT, D], BF16), NTL, f"ks{d}", 1) for d in range(2)],
            v=DramAct(self.dram("vtok", [NT, D], BF16), NTL, "vtok", 1),
            gs=DramAct(self.dram("gsT", [D, NT], BF16), NTL, "gsT", 8),
            of=DramAct(self.dram("ofT", [D, NT], F32), NTL, "ofT", 1),
        )

        self.ident = self.sb(es, "ident", [128, 128], BF16)
        self.identf = self.sb(es, "identf", [128, 128], F32)
        self.ssA = self.sb(es, "ssA", [128, NB], F32)
        self.ssB = self.sb(es, "ssB", [128, NB], F32)
        self.rstd = self.sb(es, "rstd", [128, NB], F32)
        self.B_ident = Buf("ident")
        self.B_ssA = Buf("ssA")
        self.B_ssB = Buf("ssB")
        self.B_rstd = Buf("rstd")
        self.bank = [es.enter_context(nc.psum_tensor(f"bank{i}", [128, 512], F32)) for i in range(8)]
        self.B_bank = [Buf(f"bank{i}") for i in range(8)]

        S.op("pool", I("memset", self.identf[:], 0.0), writes=[self.B_ident])
        S.op("pool", I("affine_select", out=self.identf[:], in_=self.identf[:], pattern=[[-1, 128]],
                                               compare_op=ALU.not_equal, fill=1.0, base=0, channel_multiplier=1),
             writes=[self.B_ident])
        S.op("dve", I("tensor_copy", out=self.ident[:], in_=self.identf[:]), writes=[self.B_ident])
        S.op("pool", I("memset", self.ssA[:], 0.0), writes=[self.B_ssA])
        S.op("pool", I("memset", self.ssB[:], 0.0), writes=[self.B_ssB])

        w1s = ExitStack()
        W1 = self.ffn_weights(w1s, 0, 0, direct=True)
        self.phase_stats(self.x_in, self.ssA, self.B_ssA)
        self.convert_all()
        self.phase_ffn(self.x_in, self.xa, 0, 0, 0, self.ssA, self.B_ssA, self.ssB, self.B_ssB, W=W1)
        w1s.close()
        if self.stop_after == "ffn1":
            self.phase_final(self.xa, self.ssB, self.B_ssB, copy_only=True)
            return
        self.phase_sgu_a(self.xa, self.ssB, self.B_ssB, self.sT)
        self.phase_sgu_b(self.xa, self.xb, self.ssB, self.B_ssB, self.ssA, self.B_ssA, self.sT)
        if self.stop_after == "sgu":
            self.phase_final(self.xb, self.ssA, self.B_ssA, copy_only=True)
            return
        self.phase_ffn(self.xb, self.xa, 0, 2, 1, self.ssA, self.B_ssA, self.ssB, self.B_ssB)
        self.phase_ffn(self.xa, self.xb, 1, 0, 0, self.ssB, self.B_ssB, self.ssA, self.B_ssA)
        if self.stop_after == "ffn3":
            self.phase_final(self.xb, self.ssA, self.B_ssA, copy_only=True)
            return
        with ExitStack() as hes:
            self.hgrn_consts(hes)
            self.phase_hgrn_a(self.xb, self.ssA, self.B_ssA, self.H)
            self.phase_hgrn_scan(0, self.H)
            self.phase_hgrn_scan(1, self.H, self.xb, self.xa, self.ssB, self.B_ssB)
        if self.stop_after == "hgrn":
            self.phase_final(self.xa, self.ssB, self.B_ssB, copy_only=True)
            return
        self.phase_ffn(self.xa, self.y_out, 1, 2, 1, self.ssB, self.B_ssB, self.ssA, self.B_ssA, final=True)

    def phase_stats(self, src, ss, B_ss):
        nc, S = self.nc, self.S
        with ExitStack() as ps:
            xl = [self.sb(ps, f"st_x{i}", [128, D], F32) for i in range(3)]
            junk = self.sb(ps, "st_junk", [128, D], BF16)
            Bx = [Buf() for _ in range(3)]
            Bj = Buf()
            ds = [S.dsem("stx") for _ in range(3)]
            for i in range(self.NB):
                k = i % 3
                S.dma("sp", xl[k][:], src.ap[i * 128:(i + 1) * 128, :], ds[k], reads=[src.bufs[i]], writes=[Bx[k]])
                S.op("act", I("activation", out=junk[:], in_=xl[k][:], func=AF.Square,
                                                            accum_out=ss[:, i:i + 1]),
                     reads=[Bx[k], B_ss], writes=[Bj])
            B_ss.w = Tok(S.eng["act"].sem, S.eng["act"].cnt)
            S.flush()

    def compute_rstd(self, ss, B_ss, ps):
        S = self.S
        tmp = self.sb(ps, "rs_tmp", [128, self.NB], F32)
        Bt = Buf()
        S.op("act", I("activation", out=tmp[:], in_=ss[:], func=AF.Sqrt, bias=EPS, scale=1.0 / D),
             reads=[B_ss], writes=[Bt])
        S.op("dve", I("reciprocal", out=self.rstd[:], in_=tmp[:]), reads=[Bt], writes=[self.B_rstd])

    def convert_all(self):
        S = self.S
        self.wbf, self.Bconv = {}, {}
        jobs = [("sgu_in", self.sgu_w_in, [D, 2 * SG]), ("sgu_out", self.sgu_w_out, [SG, D])]
        for (l, j) in ((0, 1), (1, 0)):
            jobs += [(f"gate{l}{j}", self.ffn_gate[l, j], [D, FF]), (f"up{l}{j}", self.ffn_up[l, j], [D, FF]),
                     (f"down{l}{j}", self.ffn_down[l, j], [FF, D])]
        jobs += [("hg_in", self.hgrn_w_in, [D, 5 * D]), ("hg_out", self.hgrn_w_out, [D, D])]
        jobs += [("gate11", self.ffn_gate[1, 1], [D, FF]), ("up11", self.ffn_up[1, 1], [D, FF]),
                 ("down11", self.ffn_down[1, 1], [FF, D])]
        prev = []
        cvs = [S.dsem("cv", bg=True) for _ in range(3)]
        RC = 256
        n = 0
        for key, src, shp in jobs:
            dst = self.dram("wbf_" + key, shp, BF16)
            self.wbf[key] = dst
            self.Bconv[key] = []
            for r0 in range(0, shp[0], RC):
                B = Buf(f"conv_{key}_{r0}")
                S.dma("pool", dst[r0:r0 + RC, :], src[r0:r0 + RC, :], cvs[n % 3], reads=prev[-3:-1], writes=[B],
                      max_dma_last_dim=4096)
                n += 1
                prev.append(B)
                self.Bconv[key].append(B)

    class WT:
        def __init__(self, tile, bounds):
            self.tile = tile
            self.bounds = bounds
            self.B = [Buf() for _ in bounds]

        def rb(self, lo, hi):
            return [b for (a, c), b in zip(self.bounds, self.B) if a < hi and c > lo]

    def load_w(self, ps, name, nk, ncols, axis, bounds, key=None, col0=0, direct=None):
        S = self.S
        t = self.sb(ps, name, [128, nk, ncols], BF16)
        W = Prog.WT(t, bounds)
        if direct is not None:
            v = direct.rearrange("(k p) f -> p k f", p=128)
        else:
            v = self.wbf[key].rearrange("(k p) f -> p k f", p=128)
        for (lo, hi), B in zip(bounds, W.B):
            if axis == "col":
                o, i = t[:, :, lo:hi], v[:, :, col0 + lo:col0 + hi]
            else:
                o, i = t[:, lo:hi, :], v[:, lo:hi, col0:col0 + ncols]
            if direct is not None:
                self._w1prev = getattr(self, "_w1prev", [])
                S.dma("pool", o, i, S.dsem("w1", bg=True), reads=self._w1prev[-3:-2], writes=[B],
                      max_dma_last_dim=4096)
                self._w1prev.append(B)
            else:
                S.dma("sp", o, i, S.dsem("wl"), reads=self.Bconv[key], writes=[B])
        return W

    def ffn_weights(self, ps, l, j, direct=False):
        cb = [(0, 768), (768, 1536), (1536, 2176), (2176, 2816)]
        rb = [(0, 6), (6, 12), (12, 17), (17, 22)]
        if direct:
            Wg = self.load_w(ps, "Wg", 8, FF, "col", cb, direct=self.ffn_gate[l, j])
            Wu = self.load_w(ps, "Wu", 8, FF, "col", cb, direct=self.ffn_up[l, j])
            Wd = self.load_w(ps, "Wd", NF, D, "row", rb, direct=self.ffn_down[l, j])
        else:
            Wg = self.load_w(ps, "Wg", 8, FF, "col", cb, key=f"gate{l}{j}")
            Wu = self.load_w(ps, "Wu", 8, FF, "col", cb, key=f"up{l}{j}")
            Wd = self.load_w(ps, "Wd", NF, D, "row", rb, key=f"down{l}{j}")
        return Wg, Wu, Wd

    def load_weight_bf16(self, dst_tile, src_ap, nk, ncols, ds, B):
        S = self.S
        for k in range(nk):
            S.dma("pool", dst_tile[:, k, :], src_ap[k * 128:(k + 1) * 128, :], ds, max_dma_last_dim=4096)

    class LNT:
        pass

    def lnt_alloc(self, ps, src, normw_row, evac_eng="act"):
        S = self.S
        L = Prog.LNT()
        L.src = src
        L.xld = [self.sb(ps, f"xld{i}", [128, D], F32) for i in range(2)]
        L.Bxld = [Buf() for _ in range(2)]
        L.dxld = [S.dsem("xld") for _ in range(2)]
        L.xn = [self.sb(ps, f"xn{i}", [128, D], BF16) for i in range(2)]
        L.Bxn = [Buf() for _ in range(2)]
        L.xnT = [self.sb(ps, f"xnT{i}", [128, 8, 512], BF16) for i in range(2)]
        L.BxnT = [[Buf() for _ in range(4)] for _ in range(2)]
        L.wb = self.sb(ps, "wb", [128, D], F32)
        L.Bwb = Buf()
        L.dwb = S.dsem("wb")
        S.dma("sp", L.wb[:], normw_row.partition_broadcast(128), L.dwb, writes=[L.Bwb])
        L.cnt = 0
        L.slot = {}
        L.evac_eng = evac_eng
        return L

    def lnt_dma(self, L, t, s):
        S = self.S
        i = t * 4 + s
        k = L.cnt % 2
        L.cnt += 1
        L.slot[(t, s)] = k
        S.dma("sp", L.xld[k][:], L.src.ap[i * 128:(i + 1) * 128, :], L.dxld[k],
              reads=[L.src.bufs[i]], writes=[L.Bxld[k]])

    def lnt_sub_a(self, L, t, s):
        S = self.S
        i = t * 4 + s
        k = L.slot[(t, s)]
        S.op("dve", I("scalar_tensor_tensor", out=L.xn[k][:], in0=L.xld[k][:], scalar=self.rstd[:, i:i + 1],
                      in1=L.wb[:], op0=ALU.mult, op1=ALU.mult),
             reads=[L.Bxld[k], self.B_rstd, L.Bwb], writes=[L.Bxn[k]])

    def lnt_sub_b(self, L, t, s):
        k = L.slot.pop((t, s))
        self.lnt_transpose_one(L, t, s, k)

    def lnt_sub(self, L, t, s):
        self.lnt_sub_a(L, t, s)
        self.lnt_sub_b(L, t, s)

    def lnt_micro(self, L, t):
        d, a, b = self.lnt_dma, self.lnt_sub_a, self.lnt_sub_b
        return [lambda: (d(L, t, 0), d(L, t, 1)),
                lambda: (a(L, t, 0), d(L, t, 2)),
                lambda: (a(L, t, 1), d(L, t, 3)),
                lambda: b(L, t, 0),
                lambda: a(L, t, 2),
                lambda: b(L, t, 1),
                lambda: a(L, t, 3),
                lambda: b(L, t, 2),
                lambda: b(L, t, 3)]

    def lnt_steps(self, L, t):
        return [lambda: (self.lnt_dma(L, t, 0), self.lnt_dma(L, t, 1)),
                lambda: (self.lnt_sub(L, t, 0), self.lnt_dma(L, t, 2)),
                lambda: (self.lnt_sub(L, t, 1), self.lnt_dma(L, t, 3)),
                lambda: self.lnt_sub(L, t, 2),
                lambda: self.lnt_sub(L, t, 3)]

    def lnt_load(self, L, t):
        for f in self.lnt_steps(L, t):
            f()

    def lnt_transpose_one(self, L, t, s, k):
        S = self.S
        tb = t % 2
        bk = 7
        pt = self.bank[bk][:].bitcast(BF16)
        fns = [(I("transpose", out=pt[:, c * 128:(c + 1) * 128], in_=L.xn[k][:, c * 128:(c + 1) * 128],
                                           identity=self.ident[:])) for c in range(8)]
        S.group("pe", fns, reads=[L.Bxn[k], self.B_ident], writes=[self.B_bank[bk]])
        if L.evac_eng == "act":
            S.op("act", I("activation", out=L.xnT[tb][:, :, s * 128:(s + 1) * 128],
                          in_=pt.rearrange("p (c t) -> p c t", c=8), func=AF.Copy),
                 reads=[self.B_bank[bk]], writes=[L.BxnT[tb][s]])
        else:
            S.op("dve", I("tensor_copy", out=L.xnT[tb][:, :, s * 128:(s + 1) * 128],
                          in_=pt.rearrange("p (c t) -> p c t", c=8)),
                 reads=[self.B_bank[bk]], writes=[L.BxnT[tb][s]])

    def phase_ffn(self, src, dst, layer, normi, ffni, ss_in, B_ss_in, ss_out, B_ss_out, W=None, final=False):
        nc, S = self.nc, self.S
        with ExitStack() as ps:
            WG, WU, WD = W if W is not None else self.ffn_weights(ps, layer, ffni)
            Wg, Wu, Wd = WG.tile, WU.tile, WD.tile
            self.compute_rstd(ss_in, B_ss_in, ps)
            L = self.lnt_alloc(ps, src, self.norm_w[layer, normi])
            hT = self.sb(ps, "hT", [128, NF, 512], BF16)
            BhT = [Buf() for _ in range(NF)]
            sg = [self.sb(ps, f"sg{i}", [128, 512], F32) for i in range(2)]
            Bsg = [Buf() for _ in range(2)]
            xr = [self.sb(ps, f"xr{i}", [128, D], F32) for i in range(2)]
            Bxr = [[Buf(), Buf()] for _ in range(2)]
            dxr = [S.dsem("xr") for _ in range(2)]
            junk = self.sb(ps, "fjunk", [128, D], BF16)
            Bj = Buf()
            S.op("dve", I("memset", ss_out[:], 0.0), writes=[B_ss_out])
            Bfs, Bwfb = Buf(), Buf()
            if final:
                fs = self.sb(ps, "ffs", [128, 4], F32)
                wfb = self.sb(ps, "wfb", [128, D], F32)
                S.dma("sp", wfb[:], self.final_norm.partition_broadcast(128), S.dsem("wfb"), writes=[Bwfb])

            self.lnt_load(L, 0)
            self.dbg("xnT0", L.xnT[0][:], L.BxnT[0], BF16)
            self.dbg("rstd", self.rstd[:], [self.B_rstd], F32)
            gu = 0
            dn = 0
            xrc = 0
            for t in range(self.NTL):
                tb = t % 2
                for f in range(NF):
                    bg, bu = (gu % 2) * 2, (gu % 2) * 2 + 1
                    gu += 1
                    fg = [(I("matmul", self.bank[bg][:], lhsT=Wg[:, k, f * 128:(f + 1) * 128],
                                                              rhs=L.xnT[tb][:, k, :], start=(k == 0), stop=(k == 7)))
                          for k in range(8)]
                    S.group("pe", fg, reads=WG.rb(f * 128, f * 128 + 128) + L.BxnT[tb], writes=[self.B_bank[bg]])
                    fu = [(I("matmul", self.bank[bu][:], lhsT=Wu[:, k, f * 128:(f + 1) * 128],
                                                              rhs=L.xnT[tb][:, k, :], start=(k == 0), stop=(k == 7)))
                          for k in range(8)]
                    S.group("pe", fu, reads=WU.rb(f * 128, f * 128 + 128) + L.BxnT[tb], writes=[self.B_bank[bu]])
                    sk = f % 2
                    if self.debug and t == 0 and f == 0:
                        self.dbgt = self.sb(ps, "dbgt", [128, 512], F32)
                        Bd = Buf()
                        S.op("dve", I("tensor_copy", out=self.dbgt[:], in_=self.bank[bg][:]),
                             reads=[self.B_bank[bg]], writes=[Bd])
                        self.dbg("g0", self.dbgt[:], [Bd], F32)
                    S.op("act", I("activation", out=sg[sk][:], in_=self.bank[bg][:], func=AF.Silu),
                         reads=[self.B_bank[bg]], writes=[Bsg[sk]])
                    S.op("dve", I("tensor_tensor", out=hT[:, f, :], in0=sg[sk][:],
                                                                            in1=self.bank[bu][:], op=ALU.mult),
                         reads=[Bsg[sk], self.B_bank[bu]], writes=[BhT[f]])
                    if t == 0 and f in (0, 21):
                        self.dbg(f"sg{f}", sg[sk][:], [Bsg[sk]], F32)
                    if 2 <= f <= 10 and t + 1 < self.NTL:
                        if f == 2:
                            micro = self.lnt_micro(L, t + 1)
                        micro[f - 2]()
                if t == 0:
                    self.dbg("hT0", hT[:], BhT, BF16)
                    self.dbg("xnT0b", L.xnT[0][:], L.BxnT[0], BF16)
                for s in range(4):
                    i = t * 4 + s
                    xk = xrc % 2
                    xrc += 1
                    S.dma("sp", xr[xk][:], src.ap[i * 128:(i + 1) * 128, :], dxr[xk], reads=[src.bufs[i]],
                          writes=Bxr[xk])
                    for nh in range(2):
                        bo = 4 + dn % 3
                        dn += 1
                        fd = [(I("matmul",
                            self.bank[bo][:], lhsT=hT[:, f, s * 128:(s + 1) * 128],
                            rhs=Wd[:, f, nh * 512:(nh + 1) * 512], start=(f == 0), stop=(f == NF - 1)))
                            for f in range(NF)]
                        S.group("pe", fd, reads=WD.B + BhT, writes=[self.B_bank[bo]])
                        S.op("dve", I("scalar_tensor_tensor",
                            out=xr[xk][:, nh * 512:(nh + 1) * 512], in0=self.bank[bo][:], scalar=0.5,
                            in1=xr[xk][:, nh * 512:(nh + 1) * 512], op0=ALU.mult, op1=ALU.add),
                            reads=[self.B_bank[bo]], writes=[Bxr[xk][nh]])
                    S.op("act", I("activation", out=junk[:], in_=xr[xk][:], func=AF.Square,
                                                                  accum_out=ss_out[:, i:i + 1]),
                         reads=Bxr[xk] + [B_ss_out], writes=[Bj, Bfs])
                    if final:
                        S.op("act", I("activation", out=fs[:, xk:xk + 1], in_=ss_out[:, i:i + 1], func=AF.Sqrt,
                                      bias=EPS, scale=1.0 / D), reads=[Bfs], writes=[Bfs])
                        S.op("dve", I("reciprocal", out=fs[:, 2 + xk:3 + xk], in_=fs[:, xk:xk + 1]),
                             reads=[Bfs], writes=[Bfs])
                        S.op("dve", I("scalar_tensor_tensor", out=xr[xk][:], in0=xr[xk][:],
                                      scalar=fs[:, 2 + xk:3 + xk], in1=wfb[:], op0=ALU.mult, op1=ALU.mult),
                             reads=[Bfs, Bwfb], writes=Bxr[xk])
                    S.dma("sp", dst.ap[i * 128:(i + 1) * 128, :], xr[xk][:], dxr[xk], reads=Bxr[xk],
                          writes=[dst.bufs[i]])
            B_ss_out.w = Tok(S.eng["act"].sem, S.eng["act"].cnt)
            S.flush()


    def resid_alloc(self, ps, src, dst, ss_out, B_ss_out):
        S = self.S
        R = Prog.LNT()
        R.src, R.dst, R.ss_out, R.B_ss_out = src, dst, ss_out, B_ss_out
        R.xr = [self.sb(ps, f"xr{i}", [128, D], F32) for i in range(2)]
        R.Bxr = [[Buf(), Buf()] for _ in range(2)]
        R.dxr = [S.dsem("xr") for _ in range(2)]
        R.junk = self.sb(ps, "rjunk", [128, D], BF16)
        R.Bjunk = Buf()
        R.cnt = 0
        S.op("dve", I("memset", ss_out[:], 0.0), writes=[B_ss_out])
        return R

    def resid_begin(self, R, i):
        S = self.S
        xk = R.cnt % 2
        R.cnt += 1
        S.dma("sp", R.xr[xk][:], R.src.ap[i * 128:(i + 1) * 128, :], R.dxr[xk], reads=[R.src.bufs[i]],
              writes=R.Bxr[xk])
        return xk

    def resid_add(self, R, xk, nh, bo, scale):
        S = self.S
        S.op("dve", I("scalar_tensor_tensor", out=R.xr[xk][:, nh * 512:(nh + 1) * 512], in0=self.bank[bo][:],
                      scalar=scale, in1=R.xr[xk][:, nh * 512:(nh + 1) * 512], op0=ALU.mult, op1=ALU.add),
             reads=[self.B_bank[bo]], writes=[R.Bxr[xk][nh]])

    def resid_end(self, R, xk, i):
        S = self.S
        S.op("act", I("activation", out=R.junk[:], in_=R.xr[xk][:], func=AF.Square,
                      accum_out=R.ss_out[:, i:i + 1]), reads=R.Bxr[xk] + [R.B_ss_out], writes=[R.Bjunk])
        S.dma("sp", R.dst.ap[i * 128:(i + 1) * 128, :], R.xr[xk][:], R.dxr[xk], reads=R.Bxr[xk],
              writes=[R.dst.bufs[i]])

    def resid_finish(self, R):
        R.B_ss_out.w = Tok(self.S.eng["act"].sem, self.S.eng["act"].cnt)

    def load_cols(self, ps, name, vec_ap, ncol):
        S = self.S
        t = self.sb(ps, name, [128, ncol], F32)
        B = Buf()
        S.dma("sp", t[:], vec_ap.rearrange("(c p) -> p c", p=128), S.dsem(name), writes=[B],
              allow_slow_non_contiguous=True)
        return t, B

    def phase_sgu_a(self, src, ss_in, B_ss_in, sT):
        S = self.S
        with ExitStack() as ps:
            WV = self.load_w(ps, "Wv", 8, SG, "col", [(n * 512, (n + 1) * 512) for n in range(6)], key="sgu_in",
                             col0=SG)
            Wv = WV.tile
            self.compute_rstd(ss_in, B_ss_in, ps)
            L = self.lnt_alloc(ps, src, self.norm_w[0, 1])
            lng, Blng = self.load_cols(ps, "lng", self.sgu_ln_g, NE)
            lnb, Blnb = self.load_cols(ps, "lnb", self.sgu_ln_b, NE)
            wsf = self.sb(ps, "wsf", [128, 8, 128], F32)
            wsb = self.sb(ps, "wsb", [128, 8, 128], BF16)
            wsT = self.sb(ps, "wsT", [128, 8, 128], BF16)
            ones = self.sb(ps, "ones", [128, 128], BF16)
            bsb = self.sb(ps, "bsb", [128, 8, 128], F32)
            C = self.sb(ps, "Cmat", [128, NE, 128], F32)
            Bwsf, Bwsb, BwsT, Bones, Bbsb, BC = Buf(), Buf(), Buf(), Buf(), Buf(), Buf()
            S.dma("sp", wsf[:], self.sgu_w_s.rearrange("g t s -> t g s"), S.dsem("wsf"), writes=[Bwsf])
            S.dma("sp", bsb[:].rearrange("p g t -> p (g t)"),
                  self.sgu_b_s.rearrange("g t -> (g t)").partition_broadcast(128), S.dsem("bsb"), writes=[Bbsb])
            S.op("dve", I("tensor_copy", out=wsb[:], in_=wsf[:]), reads=[Bwsf], writes=[Bwsb])
            S.op("dve", I("memset", ones[:], 1.0), writes=[Bones])
            pt = self.bank[7][:].bitcast(BF16)
            S.group("pe", [I("transpose", out=pt[:, g * 128:(g + 1) * 128], in_=wsb[:, g, :], identity=self.ident[:])
                           for g in range(8)], reads=[Bwsb, self.B_ident], writes=[self.B_bank[7]])
            S.op("act", I("activation", out=wsT[:].rearrange("p g t -> p (g t)"), in_=pt, func=AF.Copy),
                 reads=[self.B_bank[7]], writes=[BwsT])
            for half in range(2):
                S.group("pe", [I("matmul", self.bank[half][:, j * 128:(j + 1) * 128], lhsT=ones[:],
                                 rhs=wsT[:, half * 4 + j, :], start=True, stop=True) for j in range(4)],
                        reads=[Bones, BwsT], writes=[self.B_bank[half]])
            for ec in range(NE):
                g = ec // 3
                S.op("dve", I("scalar_tensor_tensor", out=C[:, ec, :],
                              in0=self.bank[g // 4][:, (g % 4) * 128:(g % 4 + 1) * 128], scalar=lnb[:, ec:ec + 1],
                              in1=bsb[:, g, :], op0=ALU.mult, op1=ALU.add),
                     reads=[self.B_bank[g // 4], Blnb, Bbsb], writes=[BC])
            vf = [[self.sb(ps, f"vf{q}_{i}", [128, SG], BF16) for i in range(4)] for q in range(2)]
            Bvf = [[Buf() for _ in range(4)] for _ in range(2)]
            vh = [self.sb(ps, f"vh{i}", [128, SG], BF16) for i in range(4)]
            Bvh = [Buf() for _ in range(4)]
            sTt = self.sb(ps, "sTt", [128, NE, 512], BF16)
            BsTt = [Buf() for _ in range(NE)]
            dsT = S.dsem("sTst")
            sjunk = self.sb(ps, "sjunk", [128, 512], BF16)
            Bsjunk = Buf()
            s1 = self.sb(ps, "s1", [128, 24], F32)
            s2 = self.sb(ps, "s2", [128, 24], F32)
            st = [self.sb(ps, f"stt{i}", [128, 4], F32) for i in range(7)]
            Bs1, Bs2, Bst = Buf(), Buf(), Buf()
            sT_v = sT.ap.rearrange("(c p) t -> p c t", p=128)
            def spatial(ec):
                g = ec // 3
                b = ec % 2
                S.group("pe", [I("matmul", self.bank[b][:, c * 128:(c + 1) * 128],
                                 lhsT=vh[c][:, ec * 128:(ec + 1) * 128], rhs=wsT[:, g, :], start=True, stop=True)
                               for c in range(4)], reads=Bvh + [BwsT], writes=[self.B_bank[b]])
                S.op("dve", I("scalar_tensor_tensor", out=sTt[:, ec, :].rearrange("p (c t) -> p c t", c=4),
                              in0=self.bank[b][:].rearrange("p (c t) -> p c t", c=4), scalar=lng[:, ec:ec + 1],
                              in1=C[:, ec, :].unsqueeze(1).broadcast_to([128, 4, 128]), op0=ALU.mult, op1=ALU.add),
                     reads=[self.B_bank[b], Blng, BC], writes=[BsTt[ec]])

            self.lnt_load(L, 0)
            bk = 0
            prev_t = None
            for t in range(self.NTL):
                tb = t % 2
                S.op("dve", I("memset", s1[:], 0.0), writes=[Bs1])
                S.op("dve", I("memset", s2[:], 0.0), writes=[Bs2])
                for c in range(4):
                    for n in range(6):
                        gi = c * 6 + n
                        if prev_t is not None and 6 <= gi < 18:
                            for ec in (2 * (gi - 6), 2 * (gi - 6) + 1):
                                spatial(ec)
                            if gi == 17:
                                S.dma("sp", sT_v[:, :, prev_t * 512:(prev_t + 1) * 512], sTt[:], dsT, reads=BsTt,
                                      writes=[sT.bufs[prev_t]])
                        b = 2 + bk % 4
                        bk += 1
                        S.group("pe", [I("matmul", self.bank[b][:], lhsT=L.xnT[tb][:, k, c * 128:(c + 1) * 128],
                                         rhs=Wv[:, k, n * 512:(n + 1) * 512], start=(k == 0), stop=(k == 7))
                                       for k in range(8)], reads=WV.rb(n * 512, (n + 1) * 512) + L.BxnT[tb],
                                writes=[self.B_bank[b]])
                        S.op("act", I("activation", out=vf[tb][c][:, n * 512:(n + 1) * 512], in_=self.bank[b][:],
                                      func=AF.Gelu, accum_out=s1[:, gi:gi + 1]),
                             reads=[self.B_bank[b], Bs1], writes=[Bvf[tb][c]])
                        S.op("act", I("activation", out=sjunk[:], in_=vf[tb][c][:, n * 512:(n + 1) * 512],
                                      func=AF.Square, accum_out=s2[:, gi:gi + 1]),
                             reads=[Bvf[tb][c], Bs2], writes=[Bsjunk])
                        if 2 <= gi <= 10 and t + 1 < self.NTL:
                            if gi == 2:
                                micro = self.lnt_micro(L, t + 1)
                            micro[gi - 2]()
                Bs1.w = Bs2.w = Tok(S.eng["act"].sem, S.eng["act"].cnt)
                msum, mean, msq, var, rs, nmr, s2s = st
                S.op("dve", I("tensor_reduce", out=s2s[:], in_=s2[:].rearrange("p (c n) -> p c n", n=6),
                              axis=mybir.AxisListType.X, op=ALU.add), reads=[Bs2], writes=[Bst])
                S.op("dve", I("tensor_reduce", out=msum[:], in_=s1[:].rearrange("p (c n) -> p c n", n=6),
                              axis=mybir.AxisListType.X, op=ALU.add), reads=[Bs1], writes=[Bst])
                S.op("dve", I("tensor_scalar", out=mean[:], in0=msum[:], scalar1=1.0 / SG, scalar2=None, op0=ALU.mult),
                     reads=[Bst], writes=[Bst])
                S.op("dve", I("tensor_tensor", out=msq[:], in0=mean[:], in1=mean[:], op=ALU.mult),
                     reads=[Bst], writes=[Bst])
                S.op("dve", I("scalar_tensor_tensor", out=var[:], in0=s2s[:], scalar=1.0 / SG, in1=msq[:],
                              op0=ALU.mult, op1=ALU.subtract), reads=[Bst, Bs2], writes=[Bst])
                S.op("act", I("activation", out=var[:], in_=var[:], func=AF.Sqrt, bias=EPS, scale=1.0),
                     reads=[Bst], writes=[Bst])
                S.op("dve", I("reciprocal", out=rs[:], in_=var[:]), reads=[Bst], writes=[Bst])
                S.op("dve", I("scalar_tensor_tensor", out=nmr[:], in0=mean[:], scalar=-1.0, in1=rs[:],
                              op0=ALU.mult, op1=ALU.mult), reads=[Bst], writes=[Bst])
                for c in range(4):
                    S.op("dve", I("tensor_scalar", out=vh[c][:], in0=vf[tb][c][:], scalar1=rs[:, c:c + 1],
                                  scalar2=nmr[:, c:c + 1], op0=ALU.mult, op1=ALU.add),
                         reads=[Bvf[tb][c], Bst], writes=[Bvh[c]])
                prev_t = t
            for ec in range(NE):
                spatial(ec)
            S.dma("sp", sT_v[:, :, prev_t * 512:(prev_t + 1) * 512], sTt[:], dsT, reads=BsTt,
                  writes=[sT.bufs[prev_t]])
            S.flush()

    def phase_sgu_b(self, src, dst, ss_in, B_ss_in, ss_out, B_ss_out, sT):
        S = self.S
        with ExitStack() as ps:
            WU = self.load_w(ps, "Wuin", 8, SG, "col", [(n * 768, (n + 1) * 768) for n in range(4)], key="sgu_in")
            WO = self.load_w(ps, "Wo", NE, D, "row", [(n * 6, (n + 1) * 6) for n in range(4)], key="sgu_out")
            Wu, Wo = WU.tile, WO.tile
            self.compute_rstd(ss_in, B_ss_in, ps)
            L = self.lnt_alloc(ps, src, self.norm_w[0, 1])
            R = self.resid_alloc(ps, src, dst, ss_out, B_ss_out)
            sTt = [self.sb(ps, f"sTb{i}", [128, NE, 512], BF16) for i in range(2)]
            BsTt = [[Buf() for _ in range(NE)] for _ in range(2)]
            dsT = [S.dsem("sTld") for _ in range(2)]
            ug = [self.sb(ps, f"ug{i}", [128, 512], BF16) for i in range(2)]
            Bug = [Buf(), Buf()]
            sT_v = sT.ap.rearrange("(c p) t -> p c t", p=128)
            self.lnt_load(L, 0)
            S.dma("sp", sTt[0][:], sT_v[:, :, 0:512], dsT[0], reads=[sT.bufs[0]], writes=BsTt[0])
            gu = 0
            dn = 0
            for t in range(self.NTL):
                tb = t % 2
                for ec in range(NE):
                    b = gu % 4
                    gu += 1
                    S.group("pe", [I("matmul", self.bank[b][:], lhsT=Wu[:, k, ec * 128:(ec + 1) * 128],
                                     rhs=L.xnT[tb][:, k, :], start=(k == 0), stop=(k == 7)) for k in range(8)],
                            reads=WU.rb(ec * 128, ec * 128 + 128) + L.BxnT[tb], writes=[self.B_bank[b]])
                    uk = ec % 2
                    S.op("act", I("activation", out=ug[uk][:], in_=self.bank[b][:], func=AF.Gelu),
                         reads=[self.B_bank[b]], writes=[Bug[uk]])
                    S.op("pool", I("tensor_tensor", out=sTt[tb][:, ec, :], in0=ug[uk][:], in1=sTt[tb][:, ec, :],
                                   op=ALU.mult), reads=[Bug[uk]], writes=[BsTt[tb][ec]])
                    if 2 <= ec <= 10 and t + 1 < self.NTL:
                        if ec == 2:
                            micro = self.lnt_micro(L, t + 1)
                        micro[ec - 2]()
                    if ec == 10 and t + 1 < self.NTL:
                        S.dma("sp", sTt[1 - tb][:], sT_v[:, :, (t + 1) * 512:(t + 2) * 512], dsT[1 - tb],
                              reads=[sT.bufs[t + 1]], writes=BsTt[1 - tb])
                for s in range(4):
                    i = t * 4 + s
                    xk = self.resid_begin(R, i)
                    for nh in range(2):
                        bo = 4 + dn % 3
                        dn += 1
                        S.group("pe", [I("matmul", self.bank[bo][:], lhsT=sTt[tb][:, ec, s * 128:(s + 1) * 128],
                                         rhs=Wo[:, ec, nh * 512:(nh + 1) * 512], start=(ec == 0), stop=(ec == NE - 1))
                                       for ec in range(NE)], reads=WO.B + BsTt[tb], writes=[self.B_bank[bo]])
                        self.resid_add(R, xk, nh, bo, 1.0)
                    self.resid_end(R, xk, i)
            self.resid_finish(R)
            S.flush()


    def hgrn_consts(self, es):
        S, NB = self.S, self.NB
        self.Dd = [self.sb(es, f"Dd{d}", [128, NH, NB], F32) for d in range(2)]
        self.B_Dd = [Buf(), Buf()]
        self.carry_t = self.sb(es, "carry_t", [128, 2 * NB], F32)
        self.B_carry = Buf()
        S.dma("sp", self.carry_t[:], self.carry, S.dsem("carry"), writes=[self.B_carry])
        self.mask = [self.sb(es, f"mask{d}", [128, 128], F32) for d in range(2)]
        self.B_mask = Buf()
        for d in range(2):
            S.op("pool", I("memset", self.mask[d][:], 1.0), writes=[self.B_mask])
        S.op("pool", I("affine_select", out=self.mask[0][:], in_=self.mask[0][:], pattern=[[1, 128]],
                       compare_op=ALU.is_ge, fill=0.0, base=0, channel_multiplier=-1), writes=[self.B_mask])
        S.op("pool", I("affine_select", out=self.mask[1][:], in_=self.mask[1][:], pattern=[[-1, 128]],
                       compare_op=ALU.is_ge, fill=0.0, base=0, channel_multiplier=1), writes=[self.B_mask])
        self.mreset = self.sb(es, "mreset", [128, 512], F32)
        self.B_mreset = Buf()
        S.op("pool", I("memset", self.mreset[:], 1.0), writes=[self.B_mreset])
        for c in range(4):
            S.op("pool", I("memset", self.mreset[:, c * 128:c * 128 + 1], 0.0), writes=[self.B_mreset])
        self.onesb = self.sb(es, "onesb", [128, 128], BF16)
        self.B_onesb = Buf()
        S.op("pool", I("memset", self.onesb[:], 1.0), writes=[self.B_onesb])

    def phase_hgrn_a(self, src, ss_in, B_ss_in, H):
        S = self.S
        with ExitStack() as ps:
            WIN = self.load_w(ps, "Whin", 8, 5 * D, "col", [(c * D, (c + 1) * D) for c in (0, 4, 3, 1, 2)],
                              key="hg_in")
            Win = WIN.tile
            self.compute_rstd(ss_in, B_ss_in, ps)
            L = self.lnt_alloc(ps, src, self.norm_w[1, 1], evac_eng="dve")
            noml, lnoml, lbc, Blb = [], [], [], Buf()
            for d in range(2):
                r0, B0 = self.load_cols(ps, f"r0{d}", self.hgrn_lb_raw[d, 0], 8)
                r1, B1 = self.load_cols(ps, f"r1{d}", self.hgrn_lb_raw[d, 1], 8)
                tmp = self.sb(ps, f"lbt{d}", [128, 8], F32)
                nm = self.sb(ps, f"noml{d}", [128, 8], F32)
                lo = self.sb(ps, f"lnoml{d}", [128, 8], F32)
                S.op("dve", I("tensor_tensor", out=tmp[:], in0=r0[:], in1=r1[:], op=ALU.subtract),
                     reads=[B0, B1], writes=[Blb])
                S.op("act", I("activation", out=tmp[:], in_=tmp[:], func=AF.Exp), reads=[Blb], writes=[Blb])
                S.op("dve", I("tensor_scalar", out=tmp[:], in0=tmp[:], scalar1=1.0, scalar2=None, op0=ALU.add),
                     reads=[Blb], writes=[Blb])
                S.op("dve", I("reciprocal", out=tmp[:], in_=tmp[:]), reads=[Blb], writes=[Blb])
                lbc_ = self.sb(ps, f"lbc{d}", [128, 8], F32)
                S.op("dve", I("tensor_copy", out=lbc_[:], in_=tmp[:]), reads=[Blb], writes=[Blb])
                lbc.append(lbc_)
                S.op("dve", I("tensor_scalar", out=nm[:], in0=tmp[:], scalar1=1.0, scalar2=None, op0=ALU.subtract),
                     reads=[Blb], writes=[Blb])
                S.op("dve", I("tensor_scalar", out=tmp[:], in0=nm[:], scalar1=-1.0, scalar2=None, op0=ALU.mult),
                     reads=[Blb], writes=[Blb])
                S.op("act", I("activation", out=lo[:], in_=tmp[:], func=AF.Ln), reads=[Blb], writes=[Blb])
                noml.append(nm)
                lnoml.append(lo)
            qs = self.sb(ps, "qs", [128, NH, 512], F32)
            Bqs = [Buf() for _ in range(NH)]
            gso = [self.sb(ps, f"gso{i}", [128, 512], BF16) for i in range(2)]
            Bgso = [Buf(), Buf()]
            dgso = [S.dsem("gso") for _ in range(2)]
            vtok = self.sb(ps, "vtok", [128, 4, D], BF16)
            Bvtok = [Buf() for _ in range(4)]
            dvtok = S.dsem("vtok")
            kstok = [self.sb(ps, f"kstok{d}", [128, 4, D], BF16) for d in range(2)]
            Bkstok = [[Buf() for _ in range(NH)] for _ in range(2)]
            dkstok = [S.dsem("kstok") for _ in range(2)]
            NR = 3
            qeo = [self.sb(ps, f"qeo{i}", [128, 512], BF16) for i in range(NR)]
            kdo = [self.sb(ps, f"kdo{i}", [128, 512], BF16) for i in range(NR)]
            Bqeo = [Buf() for _ in range(NR)]
            Bkdo = [Buf() for _ in range(NR)]
            dqeo = [S.dsem("qeo") for _ in range(NR)]
            dkdo = [S.dsem("kdo") for _ in range(NR)]
            sl = [self.sb(ps, f"sl{i}", [128, 512], F32) for i in range(2)]
            Bsl = [Buf(), Buf()]
            slc = 0
            NTMP = 3
            TN = ("te", "tA", "tB", "tP", "tX")
            T_ = [{n: self.sb(ps, f"{n}{i}", [128, 512], F32) for n in TN} for i in range(NTMP)]
            BT = [{n: Buf() for n in TN + ("tks",)} for i in range(NTMP)]
            tks = [self.sb(ps, f"tks{i}", [128, 512], BF16) for i in range(NTMP)]
            v_v = H["v"].ap.rearrange("(n p) f -> p n f", p=128)
            ks_v = [H["ks"][d].ap.rearrange("(n p) f -> p n f", p=128) for d in range(2)]
            self.lnt_load(L, 0)
            bk = 0
            ro = 0
            hd = 0
            for t in range(self.NTL):
                tb = t % 2
                tok = slice(t * 512, (t + 1) * 512)
                def sgroup_pe(which, h):
                    nonlocal bk
                    col = (0 if which == 0 else 4 * D) + h * 128
                    b = bk % 6
                    bk += 1
                    S.group("pe", [I("matmul", self.bank[b][:], lhsT=Win[:, k, col:col + 128],
                                     rhs=L.xnT[tb][:, k, :], start=(k == 0), stop=(k == 7)) for k in range(8)],
                            reads=WIN.rb(col, col + 128) + L.BxnT[tb], writes=[self.B_bank[b]])
                    return b

                def sgroup_act(which, h, b):
                    nonlocal slc
                    k_ = slc % 2
                    slc += 1
                    S.op("act", I("activation", out=sl[k_][:], in_=self.bank[b][:], func=AF.Exp, scale=-1.0),
                         reads=[self.B_bank[b]], writes=[Bsl[k_]])
                    S.op("act", I("activation", out=sl[k_][:], in_=sl[k_][:], func=AF.Ln, bias=1.0, scale=1.0),
                         reads=[Bsl[k_]], writes=[Bsl[k_]])
                    S.op("act", I("activation", out=sl[k_][:], in_=sl[k_][:], func=AF.Exp, scale=-1.0),
                         reads=[Bsl[k_]], writes=[Bsl[k_]])
                    if which == 0:
                        S.op("dve", I("tensor_tensor", out=qs[:, h, :], in0=self.bank[b][:], in1=sl[k_][:], op=ALU.mult),
                             reads=[self.B_bank[b], Bsl[k_]], writes=[Bqs[h]])
                    else:
                        gk = h % 2
                        S.op("dve", I("tensor_tensor", out=gso[gk][:], in0=self.bank[b][:], in1=sl[k_][:], op=ALU.mult),
                             reads=[self.B_bank[b], Bsl[k_]], writes=[Bgso[gk]])
                        S.dma("sp", H["gs"].ap[h * 128:(h + 1) * 128, tok], gso[gk][:], dgso[gk],
                              reads=[Bgso[gk]], writes=[H["gs"].bufs[t][h]])

                def vgroup(c, n):
                    nonlocal bk
                    b = bk % 6
                    bk += 1
                    S.group("pe", [I("matmul", self.bank[b][:], lhsT=L.xnT[tb][:, k, c * 128:(c + 1) * 128],
                                     rhs=Win[:, k, 3 * D + n * 512:3 * D + (n + 1) * 512], start=(k == 0),
                                     stop=(k == 7)) for k in range(8)],
                            reads=WIN.rb(3 * D, 4 * D) + L.BxnT[tb], writes=[self.B_bank[b]])
                    S.op("dve", I("tensor_copy", out=vtok[:, c, n * 512:(n + 1) * 512], in_=self.bank[b][:]),
                         reads=[self.B_bank[b]], writes=[Bvtok[c]])
                    if c == 3 and n == 1:
                        S.dma("sp", v_v[:, t * 4:(t + 1) * 4, :], vtok[:], dvtok, reads=Bvtok, writes=H["v"].bufs[t])

                vlist = [(c, n) for c in range(4) for n in range(2)]
                items = [(d, h) for d in range(2) for h in range(NH)]

                def stage0(d, h):
                    nonlocal bk
                    col = (1 + d) * D + h * 128
                    b = bk % 6
                    bk += 1
                    S.group("pe", [I("matmul", self.bank[b][:], lhsT=Win[:, k, col:col + 128],
                                     rhs=L.xnT[tb][:, k, :], start=(k == 0), stop=(k == 7)) for k in range(8)],
                            reads=WIN.rb(col, col + 128) + L.BxnT[tb], writes=[self.B_bank[b]])
                    return b

                def stage1(d, h, i_, b):
                    T, B_ = T_[i_], BT[i_]
                    S.op("act", I("activation", out=T["te"][:], in_=self.bank[b][:], func=AF.Exp),
                         reads=[self.B_bank[b]], writes=[B_["te"]])
                    S.op("act", I("activation", out=T["tA"][:], in_=T["te"][:], func=AF.Ln, bias=lbc[d][:, h:h + 1],
                                  scale=1.0), reads=[B_["te"], Blb], writes=[B_["tA"]])
                    S.op("act", I("activation", out=T["tB"][:], in_=T["te"][:], func=AF.Ln, bias=1.0, scale=1.0),
                         reads=[B_["te"]], writes=[B_["tB"]])
                    S.op("dve", I("tensor_tensor", out=T["tA"][:], in0=T["tA"][:], in1=T["tB"][:], op=ALU.subtract),
                         reads=[B_["tB"]], writes=[B_["tA"]])
                    S.op("dve", I("tensor_tensor_scan", out=T["tP"][:], data0=self.mreset[:], data1=T["tA"][:],
                                  initial=0.0, op0=ALU.mult, op1=ALU.add),
                         reads=[B_["tA"], self.B_mreset], writes=[B_["tP"]])
                    v4 = lambda ap: ap.rearrange("p (c t) -> p c t", c=4)
                    Tb = v4(T["tP"][:])[:, :, 127:128].broadcast_to([128, 4, 128])
                    if d == 0:
                        S.op("pool", I("tensor_tensor", out=T["te"][:], in0=T["tB"][:], in1=T["tP"][:], op=ALU.add),
                             reads=[B_["tB"], B_["tP"]], writes=[B_["te"]])
                        S.op("pool", I("tensor_tensor", out=v4(T["tX"][:]), in0=Tb, in1=v4(T["te"][:]),
                                       op=ALU.subtract), reads=[B_["tP"], B_["te"]], writes=[B_["tX"]])
                    else:
                        S.op("dve", I("tensor_tensor", out=T["tA"][:], in0=T["tP"][:], in1=T["tA"][:],
                                      op=ALU.subtract), reads=[B_["tP"]], writes=[B_["tA"]])
                        S.op("pool", I("tensor_tensor", out=T["tB"][:], in0=T["tA"][:], in1=T["tB"][:],
                                       op=ALU.subtract), reads=[B_["tA"]], writes=[B_["tB"]])
                        S.op("pool", I("tensor_tensor", out=v4(T["tA"][:]), in0=Tb, in1=v4(T["tA"][:]),
                                       op=ALU.subtract), reads=[B_["tP"]], writes=[B_["tA"]])
                        S.op("pool", I("tensor_tensor", out=v4(T["te"][:]), in0=Tb, in1=v4(T["tB"][:]),
                                       op=ALU.subtract), reads=[B_["tP"], B_["tB"]], writes=[B_["te"]])

                def stage2(d, h, i_):
                    nonlocal ro
                    T, B_ = T_[i_], BT[i_]
                    oml_b = lnoml[d][:, h:h + 1]
                    r_ = ro % NR
                    ro += 1
                    S.op("act", I("activation", out=self.Dd[d][:, h, t * 4:(t + 1) * 4],
                                  in_=T["tP"][:, 127:512:128], func=AF.Exp), reads=[B_["tP"]])
                    qarg, qB = (T["tP"], B_["tP"]) if d == 0 else (T["tA"], B_["tA"])
                    ksarg, ksB = (T["tX"], B_["tX"]) if d == 0 else (T["tB"], B_["tB"])
                    S.op("act", I("activation", out=qarg[:], in_=qarg[:], func=AF.Exp), reads=[qB], writes=[qB])
                    S.op("act", I("activation", out=kdo[r_][:], in_=T["te"][:], func=AF.Exp, scale=-1.0, bias=oml_b),
                         reads=[B_["te"], Blb], writes=[Bkdo[r_]])
                    S.op("act", I("activation", out=tks[i_][:], in_=ksarg[:], func=AF.Exp, bias=oml_b),
                         reads=[ksB, Blb], writes=[B_["tks"]])
                    S.op("dve", I("tensor_tensor", out=qeo[r_][:], in0=qs[:, h, :], in1=qarg[:], op=ALU.mult),
                         reads=[Bqs[h], qB], writes=[Bqeo[r_]])
                    S.dma("sp", H["qe"][d].ap[h * 128:(h + 1) * 128, tok], qeo[r_][:], dqeo[r_],
                          reads=[Bqeo[r_]], writes=[H["qe"][d].bufs[t][h]])
                    S.dma("sp", H["kd"][d].ap[h * 128:(h + 1) * 128, tok], kdo[r_][:], dkdo[r_],
                          reads=[Bkdo[r_]], writes=[H["kd"][d].bufs[t][h]])
                    pt = self.bank[6 + (h % 2)][:].bitcast(BF16)
                    S.group("pe", [I("transpose", out=pt[:, c * 128:(c + 1) * 128],
                                     in_=tks[i_][:, c * 128:(c + 1) * 128], identity=self.ident[:])
                                   for c in range(4)], reads=[B_["tks"], self.B_ident],
                            writes=[self.B_bank[6 + (h % 2)]])
                    S.op("dve", I("tensor_copy", out=kstok[d][:, :, h * 128:(h + 1) * 128],
                                  in_=pt[:, 0:512].rearrange("p (c k) -> p c k", c=4)),
                         reads=[self.B_bank[6 + (h % 2)]], writes=[Bkstok[d][h]])
                    if h == NH - 1:
                        S.dma("sp", ks_v[d][:, t * 4:(t + 1) * 4, :], kstok[d][:], dkstok[d], reads=Bkstok[d],
                              writes=H["ks"][d].bufs[t])

                AH = 3
                extras = [(0, h_) for h_ in range(3, 8)] + [(1, h_) for h_ in range(8)]
                for h0 in range(3):
                    sgroup_act(0, h0, sgroup_pe(0, h0))
                banks_ = [stage0(*items[q]) for q in range(AH)]
                xb_ = sgroup_pe(*extras[0])
                stage1(*items[0], hd % NTMP, banks_[0])
                stage1(*items[1], (hd + 1) % NTMP, banks_[1])
                for j, (d, h) in enumerate(items):
                    if j + AH < len(items):
                        banks_.append(stage0(*items[j + AH]))
                    if j + 2 < len(items):
                        stage1(*items[j + 2], (hd + 2) % NTMP, banks_[j + 2])
                    nxb_ = sgroup_pe(*extras[j + 1]) if j + 1 < len(extras) else None
                    stage2(d, h, hd % NTMP)
                    hd += 1
                    if j < len(extras):
                        sgroup_act(*extras[j], xb_)
                    xb_ = nxb_
                    if j % 2 == 1:
                        vgroup(*vlist[j // 2])
                    if 1 <= j <= 9 and t + 1 < self.NTL:
                        if j == 1:
                            micro = self.lnt_micro(L, t + 1)
                        micro[j - 1]()
            self.B_Dd[0].w = self.B_Dd[1].w = Tok(S.eng["act"].sem, S.eng["act"].cnt)
            S.flush()

    def phase_hgrn_scan(self, d, H, src=None, dst=None, ss_out=None, B_ss_out=None):
        S, NB, NTL = self.S, self.NB, self.NTL
        with ExitStack() as ps:
            NO = 3 if d == 1 else 2
            qeT = [self.sb(ps, f"qeT{i}", [128, NH, 512], BF16) for i in range(2)]
            kdT = [self.sb(ps, f"kdT{i}", [128, NH, 512], BF16) for i in range(2)]
            kst = [self.sb(ps, f"kst{i}", [128, 4, D], BF16) for i in range(2)]
            vt = [self.sb(ps, f"vt{i}", [128, 4, D], BF16) for i in range(2)]
            oT = [self.sb(ps, f"oT{i}", [128, NH, 512], F32) for i in range(NO)]
            Bld = [[Buf() for _ in range(4)] for _ in range(2)]
            BoT = [[Buf() for _ in range(NH)] for _ in range(NO)]
            dld = [[S.dsem("scld") for _ in range(4)] for _ in range(2)]
            doT = [S.dsem("oT") for _ in range(NO)]
            Am = [self.sb(ps, f"Am{i}", [128, 4, 128], BF16) for i in range(2)]
            BAm = [Buf(), Buf()]
            St = self.sb(ps, "St", [128, NH, 128], F32)
            BSt = Buf()
            Sb = [self.sb(ps, f"Sb{i}", [128, NH, 128], BF16) for i in range(2)]
            BSb = [Buf(), Buf()]
            S.op("pool", I("memset", St[:], 0.0), writes=[BSt])
            S.op("pool", I("memset", Sb[0][:], 0.0), writes=[BSb[0]])
            S.op("pool", I("memset", Sb[1][:], 0.0), writes=[BSb[1]])
            qe_v = H["qe"][d].ap.rearrange("(h p) t -> p h t", p=128)
            kd_v = H["kd"][d].ap.rearrange("(h p) t -> p h t", p=128)
            ks_v = H["ks"][d].ap.rearrange("(n p) f -> p n f", p=128)
            v_v = H["v"].ap.rearrange("(n p) f -> p n f", p=128)
            of_v = H["of"].ap.rearrange("(h p) t -> p h t", p=128)
            if d == 1:
                WO = self.load_w(ps, "Who", NH, D, "row", [(0, NH)], key="hg_out")
                Wo = WO.tile
                nw, Bnw = self.load_cols(ps, "hnw", self.hgrn_norm_w, NH)
                for h in range(NH):
                    S.op("dve", I("tensor_scalar", out=Wo[:, h, :], in0=Wo[:, h, :], scalar1=nw[:, h:h + 1],
                                  scalar2=None, op0=ALU.mult), reads=WO.B + [Bnw], writes=WO.B)
                gsT = [self.sb(ps, f"gsT{i}", [128, NH, 512], BF16) for i in range(NO)]
                Bgs = [Buf() for _ in range(NO)]
                dgs = [S.dsem("gsld") for _ in range(NO)]
                gs_v = H["gs"].ap.rearrange("(h p) t -> p h t", p=128)
                sq = [self.sb(ps, f"sq{i}", [128, 512], BF16) for i in range(2)]
                Bsq = [Buf(), Buf()]
                rt = [self.sb(ps, f"rt{i}", [128, 512], F32) for i in range(2)]
                Brt = [Buf(), Buf()]
                onT = [self.sb(ps, f"onT{i}", [128, NH, 512], BF16) for i in range(2)]
                BonT = [[Buf() for _ in range(NH)] for _ in range(2)]
                R = self.resid_alloc(ps, src, dst, ss_out, B_ss_out)

            order = list(range(NTL)) if d == 0 else list(range(NTL - 1, -1, -1))

            def issue_loads(j, parts=(0, 1, 2, 3, 4, 5)):
                t = order[j]
                p = j % 2
                po = j % NO
                tok = slice(t * 512, (t + 1) * 512)
                if 0 in parts:
                    S.dma("sp", qeT[p][:], qe_v[:, :, tok], dld[p][0], reads=H["qe"][d].bufs[t], writes=[Bld[p][0]])
                if 1 in parts:
                    S.dma("sp", kdT[p][:], kd_v[:, :, tok], dld[p][1], reads=H["kd"][d].bufs[t], writes=[Bld[p][1]])
                if 2 in parts:
                    S.dma("sp", kst[p][:], ks_v[:, t * 4:(t + 1) * 4, :], dld[p][2], reads=H["ks"][d].bufs[t],
                          writes=[Bld[p][2]])
                if 3 in parts:
                    S.dma("sp", vt[p][:], v_v[:, t * 4:(t + 1) * 4, :], dld[p][3], reads=H["v"].bufs[t],
                          writes=[Bld[p][3]])
                if d == 1 and 4 in parts:
                    S.dma("sp", oT[po][:], of_v[:, :, tok], doT[po], reads=H["of"].bufs[t], writes=BoT[po])
                if d == 1 and 5 in parts:
                    S.dma("sp", gsT[po][:], gs_v[:, :, tok], dgs[po], reads=H["gs"].bufs[t], writes=[Bgs[po]])

            class TW:
                pass

            def tile_work(j):
                w = TW()
                w.t = order[j]
                w.po = j % NO
                w.pn = j % 2
                w.xk = {}
                return w

            def sqr(w, h):
                k2 = h % 2
                S.op("act", I("activation", out=sq[k2][:], in_=oT[w.po][:, h, :], func=AF.Square),
                     reads=[BoT[w.po][h]], writes=[Bsq[k2]])

            def mmn(w, h):
                k2 = h % 2
                S.group("pe", [I("matmul", self.bank[6][:], lhsT=self.onesb[:], rhs=sq[k2][:], start=True, stop=True)],
                        reads=[Bsq[k2], self.B_onesb], writes=[self.B_bank[6]])
                S.op("act", I("activation", out=rt[k2][:], in_=self.bank[6][:], func=AF.Ln, bias=EPS,
                              scale=1.0 / 128), reads=[self.B_bank[6]], writes=[Brt[k2]])
                S.op("act", I("activation", out=rt[k2][:], in_=rt[k2][:], func=AF.Exp, scale=-0.5),
                     reads=[Brt[k2]], writes=[Brt[k2]])

            def fin(w, h):
                k2 = h % 2
                S.op("pool", I("tensor_tensor", out=rt[k2][:], in0=oT[w.po][:, h, :], in1=rt[k2][:], op=ALU.mult),
                     reads=[BoT[w.po][h], Brt[k2]], writes=[Brt[k2]])
                S.op("pool", I("tensor_tensor", out=onT[w.pn][:, h, :], in0=rt[k2][:], in1=gsT[w.po][:, h, :],
                               op=ALU.mult), reads=[Brt[k2], Bgs[w.po]], writes=[BonT[w.pn][h]])

            def outp(w, s_, nh, pe_only=False, dve_only=False, load_only=False):
                i = w.t * 4 + s_
                if load_only:
                    w.xk[s_] = self.resid_begin(R, i)
                    return
                if not dve_only:
                    if nh == 0 and s_ not in w.xk:
                        w.xk[s_] = self.resid_begin(R, i)
                    S.group("pe", [I("matmul", self.bank[7][:], lhsT=onT[w.pn][:, h, s_ * 128:(s_ + 1) * 128],
                                     rhs=Wo[:, h, nh * 512:(nh + 1) * 512], start=(h == 0), stop=(h == NH - 1))
                                   for h in range(NH)], reads=WO.B + BonT[w.pn], writes=[self.B_bank[7]])
                if pe_only:
                    return
                xk = w.xk[s_]
                self.resid_add(R, xk, nh, 7, 1.0)
                if nh == 1:
                    self.resid_end(R, xk, i)

            issue_loads(0)
            sbi = 0
            normW = None
            outW = None
            for j in range(NTL):
                t = order[j]
                p = j % 2
                po = j % NO
                if d == 0 and j + 1 < NTL:
                    issue_loads(j + 1)
                corder = range(4) if d == 0 else range(3, -1, -1)
                for ci, c in enumerate(corder):
                    n = t * 4 + c
                    cs = slice(c * 128, (c + 1) * 128)
                    if normW is not None:
                        sqr(normW, 2 * ci)
                        sqr(normW, 2 * ci + 1)
                    if d == 1 and outW is not None:
                        outp(outW, ci, 0, load_only=True)
                    if d == 1 and j + 1 < NTL and ci < 3:
                        issue_loads(j + 1, parts=((0, 1), (2, 3), (4, 5))[ci])
                    for hb in range(2):
                        S.group("pe", [I("matmul", self.bank[hb][:, q * 128:(q + 1) * 128],
                                         lhsT=kdT[p][:, hb * 4 + q, cs], rhs=qeT[p][:, hb * 4 + q, cs],
                                         start=True, stop=True) for q in range(4)],
                                reads=[Bld[p][0], Bld[p][1]], writes=[self.B_bank[hb]])
                    for hb in range(2):
                        S.group("pe", [I("matmul", self.bank[2 + hb][:, q * 128:(q + 1) * 128],
                                         lhsT=kst[p][:, c, (hb * 4 + q) * 128:(hb * 4 + q + 1) * 128],
                                         rhs=vt[p][:, c, (hb * 4 + q) * 128:(hb * 4 + q + 1) * 128],
                                         start=True, stop=True) for q in range(4)],
                                reads=[Bld[p][2], Bld[p][3]], writes=[self.B_bank[2 + hb]])
                    if normW is not None:
                        mmn(normW, 2 * ci)
                    if outW is not None:
                        outp(outW, ci, 0, pe_only=True)
                    for hb in range(2):
                        S.op("dve", I("tensor_tensor", out=Am[hb][:],
                                      in0=self.bank[hb][:].rearrange("p (q t) -> p q t", q=4),
                                      in1=self.mask[d][:].unsqueeze(1).broadcast_to([128, 4, 128]), op=ALU.mult),
                             reads=[self.B_bank[hb], self.B_mask], writes=[BAm[hb]])
                    sp_ = sbi % 2
                    for hb in range(2):
                        fns = []
                        for q in range(4):
                            h = hb * 4 + q
                            fns.append(I("matmul", self.bank[4 + hb][:, q * 128:(q + 1) * 128],
                                         lhsT=vt[p][:, c, h * 128:(h + 1) * 128], rhs=Am[hb][:, q, :],
                                         start=True, stop=False))
                            fns.append(I("matmul", self.bank[4 + hb][:, q * 128:(q + 1) * 128],
                                         lhsT=Sb[sp_][:, h, :], rhs=qeT[p][:, h, cs], start=False, stop=True))
                        S.group("pe", fns, reads=[Bld[p][3], Bld[p][0], BAm[hb], BSb[sp_]],
                                writes=[self.B_bank[4 + hb]])
                    S.op("dve", I("tensor_tensor", out=St[:], in0=St[:],
                                  in1=self.Dd[d][:, :, n:n + 1].broadcast_to([128, NH, 128]), op=ALU.mult),
                         reads=[self.B_Dd[d]], writes=[BSt])
                    for hb in range(2):
                        S.op("dve", I("tensor_tensor", out=St[:, hb * 4:(hb + 1) * 4, :],
                                      in0=St[:, hb * 4:(hb + 1) * 4, :],
                                      in1=self.bank[2 + hb][:].rearrange("p (q t) -> p q t", q=4), op=ALU.add),
                             reads=[self.B_bank[2 + hb]], writes=[BSt])
                    nxt = n + 1 if d == 0 else n - 1
                    if 0 <= nxt < NB:
                        bnd = (nxt % 16 == 0) if d == 0 else (nxt % 16 == 15)
                        if bnd:
                            ccol = nxt if d == 0 else NB + nxt
                            S.op("dve", I("tensor_scalar", out=St[:], in0=St[:],
                                          scalar1=self.carry_t[:, ccol:ccol + 1], scalar2=None, op0=ALU.mult),
                                 reads=[self.B_carry], writes=[BSt])
                    sbi += 1
                    S.op("act", I("activation", out=Sb[sbi % 2][:], in_=St[:], func=AF.Copy),
                         reads=[BSt], writes=[BSb[sbi % 2]])
                    if outW is not None:
                        outp(outW, ci, 0, dve_only=True)
                    if normW is not None:
                        mmn(normW, 2 * ci + 1)
                    if outW is not None:
                        outp(outW, ci, 1)
                    for hb in range(2):
                        ov = oT[po][:, hb * 4:(hb + 1) * 4, cs]
                        bv = self.bank[4 + hb][:].rearrange("p (q t) -> p q t", q=4)
                        if d == 0:
                            S.op("act", I("activation", out=ov, in_=bv, func=AF.Copy),
                                 reads=[self.B_bank[4 + hb]], writes=BoT[po][hb * 4:(hb + 1) * 4])
                        else:
                            S.op("dve", I("tensor_tensor", out=ov, in0=bv, in1=ov, op=ALU.add),
                                 reads=[self.B_bank[4 + hb]], writes=BoT[po][hb * 4:(hb + 1) * 4])
                    if normW is not None:
                        fin(normW, 2 * ci)
                        fin(normW, 2 * ci + 1)
                if d == 0:
                    S.dma("sp", of_v[:, :, t * 512:(t + 1) * 512], oT[po][:], doT[po], reads=BoT[po],
                          writes=H["of"].bufs[t])
                else:
                    outW = normW
                    normW = tile_work(j)
            if d == 1:
                if outW is not None:
                    for s_ in range(4):
                        outp(outW, s_, 0)
                        outp(outW, s_, 1)
                for h in range(NH):
                    sqr(normW, h)
                    mmn(normW, h)
                    fin(normW, h)
                for s_ in range(4):
                    outp(normW, s_, 0)
                    outp(normW, s_, 1)
                self.resid_finish(R)
            S.flush()

    def phase_final(self, src, ss, B_ss, copy_only=False):
        S = self.S
        with ExitStack() as ps:
            self.compute_rstd(ss, B_ss, ps)
            wb = self.sb(ps, "fwb", [128, D], F32)
            Bwb = Buf()
            dwb = S.dsem("fwb")
            S.dma("sp", wb[:], self.final_norm.partition_broadcast(128), dwb, writes=[Bwb])
            xl = [self.sb(ps, f"fx{i}", [128, D], F32) for i in range(3)]
            Bx = [Buf() for _ in range(3)]
            dx = [S.dsem("fx") for _ in range(3)]
            for i in range(self.NB):
                k = i % 3
                S.dma("sp", xl[k][:], src.ap[i * 128:(i + 1) * 128, :], dx[k], reads=[src.bufs[i]], writes=[Bx[k]])
                if not copy_only:
                    S.op("dve", I("scalar_tensor_tensor", out=xl[k][:], in0=xl[k][:],
                                                                          scalar=self.rstd[:, i:i + 1], in1=wb[:],
                                                                          op0=ALU.mult, op1=ALU.mult),
                         reads=[self.B_rstd, Bwb], writes=[Bx[k]])
                S.dma("sp", self.y_out.ap[i * 128:(i + 1) * 128, :], xl[k][:], dx[k], reads=[Bx[k]],
                      writes=[self.y_out.bufs[i]])
            S.flush()


W_NAMES = ["norm_w", "ffn_gate", "ffn_up", "ffn_down", "sgu_w_in", "sgu_ln_g", "sgu_ln_b", "sgu_w_s", "sgu_b_s",
           "sgu_w_out", "hgrn_w_in", "hgrn_lb_raw", "hgrn_norm_w", "hgrn_w_out", "final_norm"]
W_SQUEEZE = {"sgu_w_in", "sgu_ln_g", "sgu_ln_b", "sgu_w_s", "sgu_b_s", "sgu_w_out", "hgrn_w_in", "hgrn_norm_w",
             "hgrn_w_out"}


def make_carry(NB, seq_blocks):
    c = np.ones((128, 2 * NB), np.float32)
    for n in range(NB):
        if n % seq_blocks == 0:
            c[:, n] = 0.0
        if n % seq_blocks == seq_blocks - 1:
            c[:, NB + n] = 0.0
    return c


def kernel(**inputs):
    xp = np.ascontiguousarray(inputs["x_prompt"], dtype=np.float32)
    xs = np.ascontiguousarray(inputs["x_sample"], dtype=np.float32)
    NT = NT_FULL
    prog = Prog(NT)
    wmap = {}
    for n in W_NAMES:
        a = np.ascontiguousarray(inputs[n], dtype=np.float32)
        if n in W_SQUEEZE:
            a = a[0]
        wmap[n] = a
    in_maps = []
    for c in range(NCORES):
        m = dict(wmap)
        if c < 4:
            m["x"] = xp[4 * c:4 * c + 4].reshape(NT, D)
            m["carry"] = make_carry(NT // 128, SEQ_PROMPT // 128)
        else:
            m["x"] = xs[c - 4].reshape(NT, D)
            m["carry"] = make_carry(NT // 128, NT // 128)
        in_maps.append(m)
    res = run_bass_kernel_spmd(prog.nc, in_maps, core_ids=list(range(NCORES)))
    ys = [np.asarray(r["y"], dtype=np.float32) for r in res.results]
    y_prompt = np.stack(ys[:4]).reshape(16, 2048, D)
    y_sample = np.stack(ys[4:]).reshape(4, 8192, D)
    return (y_prompt, y_sample)
```

```python
import numpy as np
from contextlib import ExitStack
import concourse.bass as bass
import concourse.mybir as mybir
from concourse.bass_utils import run_bass_kernel_spmd

F32, BF16 = mybir.dt.float32, mybir.dt.bfloat16
AF = mybir.ActivationFunctionType
ALU = mybir.AluOpType

D = 1024
FF = 2816
NF = FF // 128
SG = 3072
NE = SG // 128
NH = 8
EPS = 1e-6
NCORES = 8
NT_FULL = 8192
SEQ_PROMPT = 2048


class Tok:
    __slots__ = ("sem", "val")

    def __init__(self, sem, val):
        self.sem = sem
        self.val = val


class Buf:
    __slots__ = ("w", "r", "name")

    def __init__(self, name=""):
        self.w = None
        self.r = []
        self.name = name


class DSem:
    def __init__(self, h):
        self.h = h
        self.cnt = 0


class Eng:
    def __init__(self, name):
        self.name = name
        self.ops = []
        self.sem = None
        self.cnt = 0
        self.known = {}
        self.rec = []


class Sched:
    def __init__(self, nc, es):
        self.nc = nc
        self.es = es
        self.eng = {n: Eng(n) for n in ("pe", "act", "dve", "pool", "sp")}
        self.nsem = 0
        self.dsems = []
        self.free_dsems = []
        self.phase_dsems = []
        self.bg_dsems = []
        self._new_engine_sems()

    def _newsem(self, name):
        self.nsem += 1
        return self.es.enter_context(self.nc.semaphore(f"{name}{self.nsem}"))

    def _new_engine_sems(self):
        for n in ("pe", "act", "dve", "pool"):
            E = self.eng[n]
            E.sem = self._newsem("e" + n)
            E.cnt = 0

    def dsem(self, name="d", bg=False):
        if bg:
            d = DSem(self._newsem(name))
            self.bg_dsems.append(d)
            return d
        if self.free_dsems:
            d = self.free_dsems.pop()
        else:
            d = DSem(self._newsem(name))
            self.dsems.append(d)
        self.phase_dsems.append(d)
        return d

    def _deps(self, E, reads, writes):
        need = {}

        def add(t):
            if t is None:
                return
            if E.name == "pe" and t.sem is E.sem:
                return
            k = id(t.sem)
            if E.known.get(k, 0) >= t.val:
                return
            if k not in need or need[k].val < t.val:
                need[k] = t

        for b in reads:
            add(b.w)
        for b in writes:
            add(b.w)
            for t in b.r:
                add(t)
        for k, t in need.items():
            E.known[k] = t.val
        return list(need.values())

    def _reg(self, tok, reads, writes):
        for b in reads:
            b.r.append(tok)
        for b in writes:
            b.w = tok
            b.r = []

    def op(self, en, fn, reads=(), writes=()):
        return self.group(en, [fn], reads, writes)

    def group(self, en, fns, reads=(), writes=()):
        E = self.eng[en]
        waits = self._deps(E, reads, writes)
        E.cnt += 1
        sem = E.sem
        tok = Tok(sem, E.cnt)

        def run(e):
            for t in waits:
                e.wait_ge(t.sem, t.val)
            ins = None
            for fn in fns:
                ins = getattr(e, fn[0])(*fn[1], **fn[2])
            ins.then_inc(sem, 1)

        E.ops.append(run)
        E.rec.append(([(id(t.sem), t.val) for t in waits], (id(sem), 1)))
        self._reg(tok, reads, writes)
        return tok

    def dma(self, en, out, in_, ds, reads=(), writes=(), **kw):
        E = self.eng[en]
        waits = self._deps(E, reads, writes)
        ds.cnt += 16
        tok = Tok(ds.h, ds.cnt)
        h = ds.h

        def run(e):
            for t in waits:
                e.wait_ge(t.sem, t.val)
            e.dma_start(out=out, in_=in_, **kw).then_inc(h, 16)

        E.ops.append(run)
        E.rec.append(([(id(t.sem), t.val) for t in waits], (id(h), 16)))
        self._reg(tok, reads, writes)
        return tok

    def check_deadlock(self):
        vals = getattr(self, "_simvals", {})
        ptr = {n: 0 for n in self.eng}
        prog = True
        while prog:
            prog = False
            for n, E in self.eng.items():
                while ptr[n] < len(E.rec):
                    waits, inc = E.rec[ptr[n]]
                    if all(vals.get(k, 0) >= v for k, v in waits):
                        if inc is not None:
                            vals[inc[0]] = vals.get(inc[0], 0) + inc[1]
                        ptr[n] += 1
                        prog = True
                    else:
                        break
        stuck = {n: (ptr[n], len(E.rec)) for n, E in self.eng.items() if ptr[n] < len(E.rec)}
        self._simvals = vals
        if stuck:
            msg = []
            for n, (p, tot) in stuck.items():
                waits, inc = self.eng[n].rec[p]
                msg.append(f"{n}: op {p}/{tot} waits " + str([(k % 10000, v, vals.get(k, 0)) for k, v in waits if vals.get(k, 0) < v]))
            raise RuntimeError("schedule deadlock: " + "; ".join(msg))
        for E in self.eng.values():
            E.rec = []

    def barrier(self, final=False):
        toks = [Tok(self.eng[n].sem, self.eng[n].cnt) for n in ("pe", "act", "dve", "pool") if self.eng[n].cnt > 0]
        toks += [Tok(d.h, d.cnt) for d in self.dsems if d.cnt > 0]
        for n, E in self.eng.items():
            ws = []
            for t in toks:
                if E.name == "pe" and t.sem is E.sem:
                    continue
                k = id(t.sem)
                if E.known.get(k, 0) >= t.val:
                    continue
                E.known[k] = t.val
                ws.append(t)

            def run(e, ws=ws):
                for t in ws:
                    e.wait_ge(t.sem, t.val)

            E.ops.append(run)
            E.rec.append(([(id(t.sem), t.val) for t in ws], None))

    def flush(self):
        self.barrier()
        self.check_deadlock()
        lists = {n: E.ops for n, E in self.eng.items()}
        for E in self.eng.values():
            E.ops = []
        with self.nc.Block() as block:
            @block.tensor
            def _(e):
                for f in lists["pe"]:
                    f(e)

            @block.scalar
            def _(e):
                for f in lists["act"]:
                    f(e)

            @block.vector
            def _(e):
                for f in lists["dve"]:
                    f(e)

            @block.gpsimd
            def _(e):
                for f in lists["pool"]:
                    f(e)

            @block.sync
            def _(e):
                for f in lists["sp"]:
                    f(e)
        self.free_dsems.extend(self.phase_dsems)
        self.phase_dsems = []


def I(name, *a, **kw):
    return (name, a, kw)


class DramAct:
    def __init__(self, ap, nblk, name, sub=None):
        self.ap = ap
        if sub is None:
            self.bufs = [Buf(f"{name}{i}") for i in range(nblk)]
        else:
            self.bufs = [[Buf(f"{name}{i}_{j}") for j in range(sub)] for i in range(nblk)]


class Prog:
    def __init__(self, NT, stop_after=None, debug=False):
        assert NT % 512 == 0
        self.debug = debug
        self.NT = NT
        self.NB = NT // 128
        self.NTL = NT // 512
        self.stop_after = stop_after
        self.nc = bass.Bass("TRN2", target_bir_lowering=False)
        self.es = ExitStack()
        self.S = Sched(self.nc, self.es)
        self.build()
        self.es.close()

    def dram(self, name, shape, dt, kind="Internal"):
        return self.nc.dram_tensor(name, list(shape), dt, kind=kind).ap()

    def sb(self, stack, name, shape, dt):
        self._uid = getattr(self, "_uid", 0) + 1
        return stack.enter_context(self.nc.sbuf_tensor(f"{name}_{self._uid}", list(shape), dt))

    def dbg(self, name, ap, bufs, dt, eng="sp"):
        if not getattr(self, "debug", False):
            return
        out = self.dram("dbg_" + name, list(ap.shape), dt, "ExternalOutput")
        self.S.dma(eng, out, ap, self.S.dsem("dbg"), reads=bufs)

    def build(self):
        nc, S, NT, NB = self.nc, self.S, self.NT, self.NB
        es = self.es
        self.x_in = DramAct(self.dram("x", [NT, D], F32, "ExternalInput"), NB, "xin")
        self.y_out = DramAct(self.dram("y", [NT, D], F32, "ExternalOutput"), NB, "yout")
        self.norm_w = self.dram("norm_w", [2, 3, D], F32, "ExternalInput")
        self.ffn_gate = self.dram("ffn_gate", [2, 2, D, FF], F32, "ExternalInput")
        self.ffn_up = self.dram("ffn_up", [2, 2, D, FF], F32, "ExternalInput")
        self.ffn_down = self.dram("ffn_down", [2, 2, FF, D], F32, "ExternalInput")
        self.sgu_w_in = self.dram("sgu_w_in", [D, 2 * SG], F32, "ExternalInput")
        self.sgu_ln_g = self.dram("sgu_ln_g", [SG], F32, "ExternalInput")
        self.sgu_ln_b = self.dram("sgu_ln_b", [SG], F32, "ExternalInput")
        self.sgu_w_s = self.dram("sgu_w_s", [8, 128, 128], F32, "ExternalInput")
        self.sgu_b_s = self.dram("sgu_b_s", [8, 128], F32, "ExternalInput")
        self.sgu_w_out = self.dram("sgu_w_out", [SG, D], F32, "ExternalInput")
        self.hgrn_w_in = self.dram("hgrn_w_in", [D, 5 * D], F32, "ExternalInput")
        self.hgrn_lb_raw = self.dram("hgrn_lb_raw", [2, 2, D], F32, "ExternalInput")
        self.hgrn_norm_w = self.dram("hgrn_norm_w", [D], F32, "ExternalInput")
        self.hgrn_w_out = self.dram("hgrn_w_out", [D, D], F32, "ExternalInput")
        self.final_norm = self.dram("final_norm", [D], F32, "ExternalInput")
        self.carry = self.dram("carry", [128, 2 * NB], F32, "ExternalInput")
        self.xa = DramAct(self.dram("xa", [NT, D], F32), NB, "xa")
        self.xb = DramAct(self.dram("xb", [NT, D], F32), NB, "xb")
        self.sT = DramAct(self.dram("sT", [SG, NT], BF16), self.NTL, "sT")
        NTL = self.NTL
        self.H = dict(
            qe=[DramAct(self.dram(f"qe{d}", [D, NT], BF16), NTL, f"qe{d}", 8) for d in range(2)],
            kd=[DramAct(self.dram(f"kd{d}", [D, NT], BF16), NTL, f"kd{d}", 8) for d in range(2)],
            ks=[DramAct(self.dram(f"ks{d}", [NT, D], BF16), NTL, f"ks{d}", 1) for d in range(2)],
            v=DramAct(self.dram("vtok", [NT, D], BF16), NTL, "vtok", 1),
            gs=DramAct(self.dram("gsT", [D, NT], BF16), NTL, "gsT", 8),
            of=DramAct(self.dram("ofT", [D, NT], BF16), NTL, "ofT", 1),
        )

        self.ident = self.sb(es, "ident", [128, 128], BF16)
        self.identf = self.sb(es, "identf", [128, 128], F32)
        self.ssA = self.sb(es, "ssA", [128, NB], F32)
        self.ssB = self.sb(es, "ssB", [128, NB], F32)
        self.rstd = self.sb(es, "rstd", [128, NB], F32)
        self.B_ident = Buf("ident")
        self.B_ssA = Buf("ssA")
        self.B_ssB = Buf("ssB")
        self.B_rstd = Buf("rstd")
        self.bank = [es.enter_context(nc.psum_tensor(f"bank{i}", [128, 512], F32)) for i in range(8)]
        self.B_bank = [Buf(f"bank{i}") for i in range(8)]

        S.op("pool", I("memset", self.identf[:], 0.0), writes=[self.B_ident])
        S.op("pool", I("affine_select", out=self.identf[:], in_=self.identf[:], pattern=[[-1, 128]],
                                               compare_op=ALU.not_equal, fill=1.0, base=0, channel_multiplier=1),
             writes=[self.B_ident])
        S.op("dve", I("tensor_copy", out=self.ident[:], in_=self.identf[:]), writes=[self.B_ident])
        S.op("pool", I("memset", self.ssA[:], 0.0), writes=[self.B_ssA])
        S.op("pool", I("memset", self.ssB[:], 0.0), writes=[self.B_ssB])

        w1s = ExitStack()
        W1 = self.ffn_weights(w1s, 0, 0, direct=True)
        self.phase_stats(self.x_in, self.ssA, self.B_ssA)
        self.convert_all()
        self.phase_ffn(self.x_in, self.xa, 0, 0, 0, self.ssA, self.B_ssA, self.ssB, self.B_ssB, W=W1)
        w1s.close()
        if self.stop_after == "ffn1":
            self.phase_final(self.xa, self.ssB, self.B_ssB, copy_only=True)
            return
        self.phase_sgu_a(self.xa, self.ssB, self.B_ssB, self.sT)
        self.phase_sgu_b(self.xa, self.xb, self.ssB, self.B_ssB, self.ssA, self.B_ssA, self.sT)
        if self.stop_after == "sgu":
            self.phase_final(self.xb, self.ssA, self.B_ssA, copy_only=True)
            return
        self.phase_ffn(self.xb, self.xa, 0, 2, 1, self.ssA, self.B_ssA, self.ssB, self.B_ssB)
        self.phase_ffn(self.xa, self.xb, 1, 0, 0, self.ssB, self.B_ssB, self.ssA, self.B_ssA)
        if self.stop_after == "ffn3":
            self.phase_final(self.xb, self.ssA, self.B_ssA, copy_only=True)
            return
        with ExitStack() as hes:
            self.hgrn_consts(hes)
            self.phase_hgrn_a(self.xb, self.ssA, self.B_ssA, self.H)
            self.phase_hgrn_scan(0, self.H)
            self.phase_hgrn_scan(1, self.H, self.xb, self.xa, self.ssB, self.B_ssB)
        if self.stop_after == "hgrn":
            self.phase_final(self.xa, self.ssB, self.B_ssB, copy_only=True)
            return
        self.phase_ffn(self.xa, self.y_out, 1, 2, 1, self.ssB, self.B_ssB, self.ssA, self.B_ssA, final=True)

    def phase_stats(self, src, ss, B_ss):
        nc, S = self.nc, self.S
        with ExitStack() as ps:
            xl = [self.sb(ps, f"st_x{i}", [128, D], F32) for i in range(3)]
            junk = self.sb(ps, "st_junk", [128, D], BF16)
            Bx = [Buf() for _ in range(3)]
            Bj = Buf()
            ds = [S.dsem("stx") for _ in range(3)]
            for i in range(self.NB):
                k = i % 3
                S.dma("sp", xl[k][:], src.ap[i * 128:(i + 1) * 128, :], ds[k], reads=[src.bufs[i]], writes=[Bx[k]])
                S.op("act", I("activation", out=junk[:], in_=xl[k][:], func=AF.Square,
                                                            accum_out=ss[:, i:i + 1]),
                     reads=[Bx[k], B_ss], writes=[Bj])
            B_ss.w = Tok(S.eng["act"].sem, S.eng["act"].cnt)
            S.flush()

    def compute_rstd(self, ss, B_ss, ps):
        S = self.S
        tmp = self.sb(ps, "rs_tmp", [128, self.NB], F32)
        Bt = Buf()
        S.op("act", I("activation", out=tmp[:], in_=ss[:], func=AF.Sqrt, bias=EPS, scale=1.0 / D),
             reads=[B_ss], writes=[Bt])
        S.op("dve", I("reciprocal", out=self.rstd[:], in_=tmp[:]), reads=[Bt], writes=[self.B_rstd])

    def convert_all(self):
        S = self.S
        self.wbf, self.Bconv = {}, {}
        jobs = [("sgu_in", self.sgu_w_in, [D, 2 * SG]), ("sgu_out", self.sgu_w_out, [SG, D])]
        for (l, j) in ((0, 1), (1, 0)):
            jobs += [(f"gate{l}{j}", self.ffn_gate[l, j], [D, FF]), (f"up{l}{j}", self.ffn_up[l, j], [D, FF]),
                     (f"down{l}{j}", self.ffn_down[l, j], [FF, D])]
        jobs += [("hg_in", self.hgrn_w_in, [D, 5 * D]), ("hg_out", self.hgrn_w_out, [D, D])]
        jobs += [("gate11", self.ffn_gate[1, 1], [D, FF]), ("up11", self.ffn_up[1, 1], [D, FF]),
                 ("down11", self.ffn_down[1, 1], [FF, D])]
        prev = []
        cvs = [S.dsem("cv", bg=True) for _ in range(3)]
        RC = 256
        n = 0
        for key, src, shp in jobs:
            dst = self.dram("wbf_" + key, shp, BF16)
            self.wbf[key] = dst
            self.Bconv[key] = []
            for r0 in range(0, shp[0], RC):
                B = Buf(f"conv_{key}_{r0}")
                S.dma("pool", dst[r0:r0 + RC, :], src[r0:r0 + RC, :], cvs[n % 3], reads=prev[-3:-1], writes=[B],
                      max_dma_last_dim=4096)
                n += 1
                prev.append(B)
                self.Bconv[key].append(B)

    class WT:
        def __init__(self, tile, bounds):
            self.tile = tile
            self.bounds = bounds
            self.B = [Buf() for _ in bounds]

        def rb(self, lo, hi):
            return [b for (a, c), b in zip(self.bounds, self.B) if a < hi and c > lo]

    def load_w(self, ps, name, nk, ncols, axis, bounds, key=None, col0=0, direct=None):
        S = self.S
        t = self.sb(ps, name, [128, nk, ncols], BF16)
        W = Prog.WT(t, bounds)
        if direct is not None:
            v = direct.rearrange("(k p) f -> p k f", p=128)
        else:
            v = self.wbf[key].rearrange("(k p) f -> p k f", p=128)
        for (lo, hi), B in zip(bounds, W.B):
            if axis == "col":
                o, i = t[:, :, lo:hi], v[:, :, col0 + lo:col0 + hi]
            else:
                o, i = t[:, lo:hi, :], v[:, lo:hi, col0:col0 + ncols]
            if direct is not None:
                self._w1prev = getattr(self, "_w1prev", [])
                S.dma("pool", o, i, S.dsem("w1", bg=True), reads=self._w1prev[-3:-2], writes=[B],
                      max_dma_last_dim=4096)
                self._w1prev.append(B)
            else:
                S.dma("sp", o, i, S.dsem("wl"), reads=self.Bconv[key], writes=[B])
        return W

    def ffn_weights(self, ps, l, j, direct=False):
        cb = [(0, 768), (768, 1536), (1536, 2176), (2176, 2816)]
        rb = [(0, 6), (6, 12), (12, 17), (17, 22)]
        if direct:
            Wg = self.load_w(ps, "Wg", 8, FF, "col", cb, direct=self.ffn_gate[l, j])
            Wu = self.load_w(ps, "Wu", 8, FF, "col", cb, direct=self.ffn_up[l, j])
            Wd = self.load_w(ps, "Wd", NF, D, "row", rb, direct=self.ffn_down[l, j])
        else:
            Wg = self.load_w(ps, "Wg", 8, FF, "col", cb, key=f"gate{l}{j}")
            Wu = self.load_w(ps, "Wu", 8, FF, "col", cb, key=f"up{l}{j}")
            Wd = self.load_w(ps, "Wd", NF, D, "row", rb, key=f"down{l}{j}")
        return Wg, Wu, Wd

    def load_weight_bf16(self, dst_tile, src_ap, nk, ncols, ds, B):
        S = self.S
        for k in range(nk):
            S.dma("pool", dst_tile[:, k, :], src_ap[k * 128:(k + 1) * 128, :], ds, max_dma_last_dim=4096)

    class LNT:
        pass

    def lnt_alloc(self, ps, src, normw_row, evac_eng="act"):
        S = self.S
        L = Prog.LNT()
        L.src = src
        L.xld = [self.sb(ps, f"xld{i}", [128, D], F32) for i in range(2)]
        L.Bxld = [Buf() for _ in range(2)]
        L.dxld = [S.dsem("xld") for _ in range(2)]
        L.xn = [self.sb(ps, f"xn{i}", [128, D], BF16) for i in range(2)]
        L.Bxn = [Buf() for _ in range(2)]
        L.xnT = [self.sb(ps, f"xnT{i}", [128, 8, 512], BF16) for i in range(2)]
        L.BxnT = [[Buf() for _ in range(4)] for _ in range(2)]
        L.wb = self.sb(ps, "wb", [128, D], F32)
        L.Bwb = Buf()
        L.dwb = S.dsem("wb")
        S.dma("sp", L.wb[:], normw_row.partition_broadcast(128), L.dwb, writes=[L.Bwb])
        L.cnt = 0
        L.slot = {}
        L.evac_eng = evac_eng
        return L

    def lnt_dma(self, L, t, s):
        S = self.S
        i = t * 4 + s
        k = L.cnt % 2
        L.cnt += 1
        L.slot[(t, s)] = k
        S.dma("sp", L.xld[k][:], L.src.ap[i * 128:(i + 1) * 128, :], L.dxld[k],
              reads=[L.src.bufs[i]], writes=[L.Bxld[k]])

    def lnt_sub_a(self, L, t, s):
        S = self.S
        i = t * 4 + s
        k = L.slot[(t, s)]
        S.op("dve", I("scalar_tensor_tensor", out=L.xn[k][:], in0=L.xld[k][:], scalar=self.rstd[:, i:i + 1],
                      in1=L.wb[:], op0=ALU.mult, op1=ALU.mult),
             reads=[L.Bxld[k], self.B_rstd, L.Bwb], writes=[L.Bxn[k]])

    def lnt_sub_b(self, L, t, s):
        k = L.slot.pop((t, s))
        self.lnt_transpose_one(L, t, s, k)

    def lnt_sub(self, L, t, s):
        self.lnt_sub_a(L, t, s)
        self.lnt_sub_b(L, t, s)

    def lnt_micro(self, L, t):
        d, a, b = self.lnt_dma, self.lnt_sub_a, self.lnt_sub_b
        return [lambda: (d(L, t, 0), d(L, t, 1)),
                lambda: (a(L, t, 0), d(L, t, 2)),
                lambda: (a(L, t, 1), d(L, t, 3)),
                lambda: b(L, t, 0),
                lambda: a(L, t, 2),
                lambda: b(L, t, 1),
                lambda: a(L, t, 3),
                lambda: b(L, t, 2),
                lambda: b(L, t, 3)]

    def lnt_steps(self, L, t):
        return [lambda: (self.lnt_dma(L, t, 0), self.lnt_dma(L, t, 1)),
                lambda: (self.lnt_sub(L, t, 0), self.lnt_dma(L, t, 2)),
                lambda: (self.lnt_sub(L, t, 1), self.lnt_dma(L, t, 3)),
                lambda: self.lnt_sub(L, t, 2),
                lambda: self.lnt_sub(L, t, 3)]

    def lnt_load(self, L, t):
        for f in self.lnt_steps(L, t):
            f()

    def lnt_transpose_one(self, L, t, s, k):
        S = self.S
        tb = t % 2
        bk = 7
        pt = self.bank[bk][:].bitcast(BF16)
        fns = [(I("transpose", out=pt[:, c * 128:(c + 1) * 128], in_=L.xn[k][:, c * 128:(c + 1) * 128],
                                           identity=self.ident[:])) for c in range(8)]
        S.group("pe", fns, reads=[L.Bxn[k], self.B_ident], writes=[self.B_bank[bk]])
        if L.evac_eng == "act":
            S.op("act", I("activation", out=L.xnT[tb][:, :, s * 128:(s + 1) * 128],
                          in_=pt.rearrange("p (c t) -> p c t", c=8), func=AF.Copy),
                 reads=[self.B_bank[bk]], writes=[L.BxnT[tb][s]])
        else:
            S.op("dve", I("tensor_copy", out=L.xnT[tb][:, :, s * 128:(s + 1) * 128],
                          in_=pt.rearrange("p (c t) -> p c t", c=8)),
                 reads=[self.B_bank[bk]], writes=[L.BxnT[tb][s]])

    def phase_ffn(self, src, dst, layer, normi, ffni, ss_in, B_ss_in, ss_out, B_ss_out, W=None, final=False):
        nc, S = self.nc, self.S
        with ExitStack() as ps:
            WG, WU, WD = W if W is not None else self.ffn_weights(ps, layer, ffni)
            Wg, Wu, Wd = WG.tile, WU.tile, WD.tile
            self.compute_rstd(ss_in, B_ss_in, ps)
            L = self.lnt_alloc(ps, src, self.norm_w[layer, normi])
            hT = self.sb(ps, "hT", [128, NF, 512], BF16)
            BhT = [Buf() for _ in range(NF)]
            sg = [self.sb(ps, f"sg{i}", [128, 512], F32) for i in range(2)]
            Bsg = [Buf() for _ in range(2)]
            xr = [self.sb(ps, f"xr{i}", [128, D], F32) for i in range(2)]
            Bxr = [[Buf(), Buf()] for _ in range(2)]
            dxr = [S.dsem("xr") for _ in range(2)]
            junk = self.sb(ps, "fjunk", [128, D], BF16)
            Bj = Buf()
            S.op("dve", I("memset", ss_out[:], 0.0), writes=[B_ss_out])
            Bfs, Bwfb = Buf(), Buf()
            if final:
                fs = self.sb(ps, "ffs", [128, 4], F32)
                wfb = self.sb(ps, "wfb", [128, D], F32)
                S.dma("sp", wfb[:], self.final_norm.partition_broadcast(128), S.dsem("wfb"), writes=[Bwfb])

            self.lnt_load(L, 0)
            self.dbg("xnT0", L.xnT[0][:], L.BxnT[0], BF16)
            self.dbg("rstd", self.rstd[:], [self.B_rstd], F32)
            gu = 0
            dn = 0
            xrc = 0
            for t in range(self.NTL):
                tb = t % 2
                for f in range(NF):
                    bg, bu = (gu % 2) * 2, (gu % 2) * 2 + 1
                    gu += 1
                    fg = [(I("matmul", self.bank[bg][:], lhsT=Wg[:, k, f * 128:(f + 1) * 128],
                                                              rhs=L.xnT[tb][:, k, :], start=(k == 0), stop=(k == 7)))
                          for k in range(8)]
                    S.group("pe", fg, reads=WG.rb(f * 128, f * 128 + 128) + L.BxnT[tb], writes=[self.B_bank[bg]])
                    fu = [(I("matmul", self.bank[bu][:], lhsT=Wu[:, k, f * 128:(f + 1) * 128],
                                                              rhs=L.xnT[tb][:, k, :], start=(k == 0), stop=(k == 7)))
                          for k in range(8)]
                    S.group("pe", fu, reads=WU.rb(f * 128, f * 128 + 128) + L.BxnT[tb], writes=[self.B_bank[bu]])
                    sk = f % 2
                    if self.debug and t == 0 and f == 0:
                        self.dbgt = self.sb(ps, "dbgt", [128, 512], F32)
                        Bd = Buf()
                        S.op("dve", I("tensor_copy", out=self.dbgt[:], in_=self.bank[bg][:]),
                             reads=[self.B_bank[bg]], writes=[Bd])
                        self.dbg("g0", self.dbgt[:], [Bd], F32)
                    S.op("act", I("activation", out=sg[sk][:], in_=self.bank[bg][:], func=AF.Silu),
                         reads=[self.B_bank[bg]], writes=[Bsg[sk]])
                    S.op("dve", I("tensor_tensor", out=hT[:, f, :], in0=sg[sk][:],
                                                                            in1=self.bank[bu][:], op=ALU.mult),
                         reads=[Bsg[sk], self.B_bank[bu]], writes=[BhT[f]])
                    if t == 0 and f in (0, 21):
                        self.dbg(f"sg{f}", sg[sk][:], [Bsg[sk]], F32)
                    if 2 <= f <= 10 and t + 1 < self.NTL:
                        if f == 2:
                            micro = self.lnt_micro(L, t + 1)
                        micro[f - 2]()
                if t == 0:
                    self.dbg("hT0", hT[:], BhT, BF16)
                    self.dbg("xnT0b", L.xnT[0][:], L.BxnT[0], BF16)
                for s in range(4):
                    i = t * 4 + s
                    xk = xrc % 2
                    xrc += 1
                    S.dma("sp", xr[xk][:], src.ap[i * 128:(i + 1) * 128, :], dxr[xk], reads=[src.bufs[i]],
                          writes=Bxr[xk])
                    for nh in range(2):
                        bo = 4 + dn % 3
                        dn += 1
                        fd = [(I("matmul",
                            self.bank[bo][:], lhsT=hT[:, f, s * 128:(s + 1) * 128],
                            rhs=Wd[:, f, nh * 512:(nh + 1) * 512], start=(f == 0), stop=(f == NF - 1)))
                            for f in range(NF)]
                        S.group("pe", fd, reads=WD.B + BhT, writes=[self.B_bank[bo]])
                        S.op("dve", I("scalar_tensor_tensor",
                            out=xr[xk][:, nh * 512:(nh + 1) * 512], in0=self.bank[bo][:], scalar=0.5,
                            in1=xr[xk][:, nh * 512:(nh + 1) * 512], op0=ALU.mult, op1=ALU.add),
                            reads=[self.B_bank[bo]], writes=[Bxr[xk][nh]])
                    S.op("act", I("activation", out=junk[:], in_=xr[xk][:], func=AF.Square,
                                                                  accum_out=ss_out[:, i:i + 1]),
                         reads=Bxr[xk] + [B_ss_out], writes=[Bj, Bfs])
                    if final:
                        S.op("act", I("activation", out=fs[:, xk:xk + 1], in_=ss_out[:, i:i + 1], func=AF.Sqrt,
                                      bias=EPS, scale=1.0 / D), reads=[Bfs], writes=[Bfs])
                        S.op("dve", I("reciprocal", out=fs[:, 2 + xk:3 + xk], in_=fs[:, xk:xk + 1]),
                             reads=[Bfs], writes=[Bfs])
                        S.op("dve", I("scalar_tensor_tensor", out=xr[xk][:], in0=xr[xk][:],
                                      scalar=fs[:, 2 + xk:3 + xk], in1=wfb[:], op0=ALU.mult, op1=ALU.mult),
                             reads=[Bfs, Bwfb], writes=Bxr[xk])
                    S.dma("sp", dst.ap[i * 128:(i + 1) * 128, :], xr[xk][:], dxr[xk], reads=Bxr[xk],
                          writes=[dst.bufs[i]])
            B_ss_out.w = Tok(S.eng["act"].sem, S.eng["act"].cnt)
            S.flush()


    def resid_alloc(self, ps, src, dst, ss_out, B_ss_out):
        S = self.S
        R = Prog.LNT()
        R.src, R.dst, R.ss_out, R.B_ss_out = src, dst, ss_out, B_ss_out
        R.xr = [self.sb(ps, f"xr{i}", [128, D], F32) for i in range(2)]
        R.Bxr = [[Buf(), Buf()] for _ in range(2)]
        R.dxr = [S.dsem("xr") for _ in range(2)]
        R.junk = self.sb(ps, "rjunk", [128, D], BF16)
        R.Bjunk = Buf()
        R.cnt = 0
        S.op("dve", I("memset", ss_out[:], 0.0), writes=[B_ss_out])
        return R

    def resid_begin(self, R, i):
        S = self.S
        xk = R.cnt % 2
        R.cnt += 1
        S.dma("sp", R.xr[xk][:], R.src.ap[i * 128:(i + 1) * 128, :], R.dxr[xk], reads=[R.src.bufs[i]],
              writes=R.Bxr[xk])
        return xk

    def resid_add(self, R, xk, nh, bo, scale):
        S = self.S
        S.op("dve", I("scalar_tensor_tensor", out=R.xr[xk][:, nh * 512:(nh + 1) * 512], in0=self.bank[bo][:],
                      scalar=scale, in1=R.xr[xk][:, nh * 512:(nh + 1) * 512], op0=ALU.mult, op1=ALU.add),
             reads=[self.B_bank[bo]], writes=[R.Bxr[xk][nh]])

    def resid_end(self, R, xk, i):
        S = self.S
        S.op("act", I("activation", out=R.junk[:], in_=R.xr[xk][:], func=AF.Square,
                      accum_out=R.ss_out[:, i:i + 1]), reads=R.Bxr[xk] + [R.B_ss_out], writes=[R.Bjunk])
        S.dma("sp", R.dst.ap[i * 128:(i + 1) * 128, :], R.xr[xk][:], R.dxr[xk], reads=R.Bxr[xk],
              writes=[R.dst.bufs[i]])

    def resid_finish(self, R):
        R.B_ss_out.w = Tok(self.S.eng["act"].sem, self.S.eng["act"].cnt)

    def load_cols(self, ps, name, vec_ap, ncol):
        S = self.S
        t = self.sb(ps, name, [128, ncol], F32)
        B = Buf()
        S.dma("sp", t[:], vec_ap.rearrange("(c p) -> p c", p=128), S.dsem(name), writes=[B],
              allow_slow_non_contiguous=True)
        return t, B

    def phase_sgu_a(self, src, ss_in, B_ss_in, sT):
        S = self.S
        with ExitStack() as ps:
            WV = self.load_w(ps, "Wv", 8, SG, "col", [(n * 512, (n + 1) * 512) for n in range(6)], key="sgu_in",
                             col0=SG)
            Wv = WV.tile
            self.compute_rstd(ss_in, B_ss_in, ps)
            L = self.lnt_alloc(ps, src, self.norm_w[0, 1])
            lng, Blng = self.load_cols(ps, "lng", self.sgu_ln_g, NE)
            lnb, Blnb = self.load_cols(ps, "lnb", self.sgu_ln_b, NE)
            wsf = self.sb(ps, "wsf", [128, 8, 128], F32)
            wsb = self.sb(ps, "wsb", [128, 8, 128], BF16)
            wsT = self.sb(ps, "wsT", [128, 8, 128], BF16)
            ones = self.sb(ps, "ones", [128, 128], BF16)
            bsb = self.sb(ps, "bsb", [128, 8, 128], F32)
            C = self.sb(ps, "Cmat", [128, NE, 128], F32)
            Bwsf, Bwsb, BwsT, Bones, Bbsb, BC = Buf(), Buf(), Buf(), Buf(), Buf(), Buf()
            S.dma("sp", wsf[:], self.sgu_w_s.rearrange("g t s -> t g s"), S.dsem("wsf"), writes=[Bwsf])
            S.dma("sp", bsb[:].rearrange("p g t -> p (g t)"),
                  self.sgu_b_s.rearrange("g t -> (g t)").partition_broadcast(128), S.dsem("bsb"), writes=[Bbsb])
            S.op("dve", I("tensor_copy", out=wsb[:], in_=wsf[:]), reads=[Bwsf], writes=[Bwsb])
            S.op("dve", I("memset", ones[:], 1.0), writes=[Bones])
            pt = self.bank[7][:].bitcast(BF16)
            S.group("pe", [I("transpose", out=pt[:, g * 128:(g + 1) * 128], in_=wsb[:, g, :], identity=self.ident[:])
                           for g in range(8)], reads=[Bwsb, self.B_ident], writes=[self.B_bank[7]])
            S.op("act", I("activation", out=wsT[:].rearrange("p g t -> p (g t)"), in_=pt, func=AF.Copy),
                 reads=[self.B_bank[7]], writes=[BwsT])
            for half in range(2):
                S.group("pe", [I("matmul", self.bank[half][:, j * 128:(j + 1) * 128], lhsT=ones[:],
                                 rhs=wsT[:, half * 4 + j, :], start=True, stop=True) for j in range(4)],
                        reads=[Bones, BwsT], writes=[self.B_bank[half]])
            for ec in range(NE):
                g = ec // 3
                S.op("dve", I("scalar_tensor_tensor", out=C[:, ec, :],
                              in0=self.bank[g // 4][:, (g % 4) * 128:(g % 4 + 1) * 128], scalar=lnb[:, ec:ec + 1],
                              in1=bsb[:, g, :], op0=ALU.mult, op1=ALU.add),
                     reads=[self.B_bank[g // 4], Blnb, Bbsb], writes=[BC])
            vf = [[self.sb(ps, f"vf{q}_{i}", [128, SG], BF16) for i in range(4)] for q in range(2)]
            Bvf = [[Buf() for _ in range(4)] for _ in range(2)]
            vh = [self.sb(ps, f"vh{i}", [128, SG], BF16) for i in range(4)]
            Bvh = [Buf() for _ in range(4)]
            sTt = self.sb(ps, "sTt", [128, NE, 512], BF16)
            BsTt = [Buf() for _ in range(NE)]
            dsT = S.dsem("sTst")
            sjunk = self.sb(ps, "sjunk", [128, 512], BF16)
            Bsjunk = Buf()
            s1 = self.sb(ps, "s1", [128, 24], F32)
            s2 = self.sb(ps, "s2", [128, 24], F32)
            st = [self.sb(ps, f"stt{i}", [128, 4], F32) for i in range(7)]
            Bs1, Bs2, Bst = Buf(), Buf(), Buf()
            sT_v = sT.ap.rearrange("(c p) t -> p c t", p=128)
            def spatial(ec):
                g = ec // 3
                b = ec % 2
                S.group("pe", [I("matmul", self.bank[b][:, c * 128:(c + 1) * 128],
                                 lhsT=vh[c][:, ec * 128:(ec + 1) * 128], rhs=wsT[:, g, :], start=True, stop=True)
                               for c in range(4)], reads=Bvh + [BwsT], writes=[self.B_bank[b]])
                S.op("dve", I("scalar_tensor_tensor", out=sTt[:, ec, :].rearrange("p (c t) -> p c t", c=4),
                              in0=self.bank[b][:].rearrange("p (c t) -> p c t", c=4), scalar=lng[:, ec:ec + 1],
                              in1=C[:, ec, :].unsqueeze(1).broadcast_to([128, 4, 128]), op0=ALU.mult, op1=ALU.add),
                     reads=[self.B_bank[b], Blng, BC], writes=[BsTt[ec]])

            self.lnt_load(L, 0)
            bk = 0
            prev_t = None
            for t in range(self.NTL):
                tb = t % 2
                S.op("dve", I("memset", s1[:], 0.0), writes=[Bs1])
                S.op("dve", I("memset", s2[:], 0.0), writes=[Bs2])
                for c in range(4):
                    for n in range(6):
                        gi = c * 6 + n
                        if prev_t is not None and 6 <= gi < 18:
                            for ec in (2 * (gi - 6), 2 * (gi - 6) + 1):
                                spatial(ec)
                            if gi == 17:
                                S.dma("sp", sT_v[:, :, prev_t * 512:(prev_t + 1) * 512], sTt[:], dsT, reads=BsTt,
                                      writes=[sT.bufs[prev_t]])
                        b = 2 + bk % 4
                        bk += 1
                        S.group("pe", [I("matmul", self.bank[b][:], lhsT=L.xnT[tb][:, k, c * 128:(c + 1) * 128],
                                         rhs=Wv[:, k, n * 512:(n + 1) * 512], start=(k == 0), stop=(k == 7))
                                       for k in range(8)], reads=WV.rb(n * 512, (n + 1) * 512) + L.BxnT[tb],
                                writes=[self.B_bank[b]])
                        S.op("act", I("activation", out=vf[tb][c][:, n * 512:(n + 1) * 512], in_=self.bank[b][:],
                                      func=AF.Gelu, accum_out=s1[:, gi:gi + 1]),
                             reads=[self.B_bank[b], Bs1], writes=[Bvf[tb][c]])
                        S.op("act", I("activation", out=sjunk[:], in_=vf[tb][c][:, n * 512:(n + 1) * 512],
                                      func=AF.Square, accum_out=s2[:, gi:gi + 1]),
                             reads=[Bvf[tb][c], Bs2], writes=[Bsjunk])
                        if 2 <= gi <= 10 and t + 1 < self.NTL:
                            if gi == 2:
                                micro = self.lnt_micro(L, t + 1)
                            micro[gi - 2]()
                Bs1.w = Bs2.w = Tok(S.eng["act"].sem, S.eng["act"].cnt)
                msum, mean, msq, var, rs, nmr, s2s = st
                S.op("dve", I("tensor_reduce", out=s2s[:], in_=s2[:].rearrange("p (c n) -> p c n", n=6),
                              axis=mybir.AxisListType.X, op=ALU.add), reads=[Bs2], writes=[Bst])
                S.op("dve", I("tensor_reduce", out=msum[:], in_=s1[:].rearrange("p (c n) -> p c n", n=6),
                              axis=mybir.AxisListType.X, op=ALU.add), reads=[Bs1], writes=[Bst])
                S.op("dve", I("tensor_scalar", out=mean[:], in0=msum[:], scalar1=1.0 / SG, scalar2=None, op0=ALU.mult),
                     reads=[Bst], writes=[Bst])
                S.op("dve", I("tensor_tensor", out=msq[:], in0=mean[:], in1=mean[:], op=ALU.mult),
                     reads=[Bst], writes=[Bst])
                S.op("dve", I("scalar_tensor_tensor", out=var[:], in0=s2s[:], scalar=1.0 / SG, in1=msq[:],
                              op0=ALU.mult, op1=ALU.subtract), reads=[Bst, Bs2], writes=[Bst])
                S.op("act", I("activation", out=var[:], in_=var[:], func=AF.Sqrt, bias=EPS, scale=1.0),
                     reads=[Bst], writes=[Bst])
                S.op("dve", I("reciprocal", out=rs[:], in_=var[:]), reads=[Bst], writes=[Bst])
                S.op("dve", I("scalar_tensor_tensor", out=nmr[:], in0=mean[:], scalar=-1.0, in1=rs[:],
                              op0=ALU.mult, op1=ALU.mult), reads=[Bst], writes=[Bst])
                for c in range(4):
                    S.op("dve", I("tensor_scalar", out=vh[c][:], in0=vf[tb][c][:], scalar1=rs[:, c:c + 1],
                                  scalar2=nmr[:, c:c + 1], op0=ALU.mult, op1=ALU.add),
                         reads=[Bvf[tb][c], Bst], writes=[Bvh[c]])
                prev_t = t
            for ec in range(NE):
                spatial(ec)
            S.dma("sp", sT_v[:, :, prev_t * 512:(prev_t + 1) * 512], sTt[:], dsT, reads=BsTt,
                  writes=[sT.bufs[prev_t]])
            S.flush()

    def phase_sgu_b(self, src, dst, ss_in, B_ss_in, ss_out, B_ss_out, sT):
        S = self.S
        with ExitStack() as ps:
            WU = self.load_w(ps, "Wuin", 8, SG, "col", [(n * 768, (n + 1) * 768) for n in range(4)], key="sgu_in")
            WO = self.load_w(ps, "Wo", NE, D, "row", [(n * 6, (n + 1) * 6) for n in range(4)], key="sgu_out")
            Wu, Wo = WU.tile, WO.tile
            self.compute_rstd(ss_in, B_ss_in, ps)
            L = self.lnt_alloc(ps, src, self.norm_w[0, 1])
            R = self.resid_alloc(ps, src, dst, ss_out, B_ss_out)
            sTt = [self.sb(ps, f"sTb{i}", [128, NE, 512], BF16) for i in range(2)]
            BsTt = [[Buf() for _ in range(NE)] for _ in range(2)]
            dsT = [S.dsem("sTld") for _ in range(2)]
            ug = [self.sb(ps, f"ug{i}", [128, 512], BF16) for i in range(2)]
            Bug = [Buf(), Buf()]
            sT_v = sT.ap.rearrange("(c p) t -> p c t", p=128)
            self.lnt_load(L, 0)
            S.dma("sp", sTt[0][:], sT_v[:, :, 0:512], dsT[0], reads=[sT.bufs[0]], writes=BsTt[0])
            gu = 0
            dn = 0
            for t in range(self.NTL):
                tb = t % 2
                for ec in range(NE):
                    b = gu % 4
                    gu += 1
                    S.group("pe", [I("matmul", self.bank[b][:], lhsT=Wu[:, k, ec * 128:(ec + 1) * 128],
                                     rhs=L.xnT[tb][:, k, :], start=(k == 0), stop=(k == 7)) for k in range(8)],
                            reads=WU.rb(ec * 128, ec * 128 + 128) + L.BxnT[tb], writes=[self.B_bank[b]])
                    uk = ec % 2
                    S.op("act", I("activation", out=ug[uk][:], in_=self.bank[b][:], func=AF.Gelu),
                         reads=[self.B_bank[b]], writes=[Bug[uk]])
                    S.op("pool", I("tensor_tensor", out=sTt[tb][:, ec, :], in0=ug[uk][:], in1=sTt[tb][:, ec, :],
                                   op=ALU.mult), reads=[Bug[uk]], writes=[BsTt[tb][ec]])
                    if 2 <= ec <= 10 and t + 1 < self.NTL:
                        if ec == 2:
                            micro = self.lnt_micro(L, t + 1)
                        micro[ec - 2]()
                    if ec == 10 and t + 1 < self.NTL:
                        S.dma("sp", sTt[1 - tb][:], sT_v[:, :, (t + 1) * 512:(t + 2) * 512], dsT[1 - tb],
                              reads=[sT.bufs[t + 1]], writes=BsTt[1 - tb])
                for s in range(4):
                    i = t * 4 + s
                    xk = self.resid_begin(R, i)
                    for nh in range(2):
                        bo = 4 + dn % 3
                        dn += 1
                        S.group("pe", [I("matmul", self.bank[bo][:], lhsT=sTt[tb][:, ec, s * 128:(s + 1) * 128],
                                         rhs=Wo[:, ec, nh * 512:(nh + 1) * 512], start=(ec == 0), stop=(ec == NE - 1))
                                       for ec in range(NE)], reads=WO.B + BsTt[tb], writes=[self.B_bank[bo]])
                        self.resid_add(R, xk, nh, bo, 1.0)
                    self.resid_end(R, xk, i)
            self.resid_finish(R)
            S.flush()


    def hgrn_consts(self, es):
        S, NB = self.S, self.NB
        self.Dd = [self.sb(es, f"Dd{d}", [128, NH, NB], F32) for d in range(2)]
        self.B_Dd = [Buf(), Buf()]
        self.carry_t = self.sb(es, "carry_t", [128, 2 * NB], F32)
        self.B_carry = Buf()
        S.dma("sp", self.carry_t[:], self.carry, S.dsem("carry"), writes=[self.B_carry])
        self.mask = [self.sb(es, f"mask{d}", [128, 128], F32) for d in range(2)]
        self.B_mask = Buf()
        for d in range(2):
            S.op("pool", I("memset", self.mask[d][:], 1.0), writes=[self.B_mask])
        S.op("pool", I("affine_select", out=self.mask[0][:], in_=self.mask[0][:], pattern=[[1, 128]],
                       compare_op=ALU.is_ge, fill=0.0, base=0, channel_multiplier=-1), writes=[self.B_mask])
        S.op("pool", I("affine_select", out=self.mask[1][:], in_=self.mask[1][:], pattern=[[-1, 128]],
                       compare_op=ALU.is_ge, fill=0.0, base=0, channel_multiplier=1), writes=[self.B_mask])
        self.mreset = self.sb(es, "mreset", [128, 512], F32)
        self.B_mreset = Buf()
        S.op("pool", I("memset", self.mreset[:], 1.0), writes=[self.B_mreset])
        for c in range(4):
            S.op("pool", I("memset", self.mreset[:, c * 128:c * 128 + 1], 0.0), writes=[self.B_mreset])
        self.onesb = self.sb(es, "onesb", [128, 128], BF16)
        self.B_onesb = Buf()
        S.op("pool", I("memset", self.onesb[:], 1.0), writes=[self.B_onesb])

    def phase_hgrn_a(self, src, ss_in, B_ss_in, H):
        S = self.S
        with ExitStack() as ps:
            WIN = self.load_w(ps, "Whin", 8, 5 * D, "col", [(c * D, (c + 1) * D) for c in (0, 4, 3, 1, 2)],
                              key="hg_in")
            Win = WIN.tile
            self.compute_rstd(ss_in, B_ss_in, ps)
            L = self.lnt_alloc(ps, src, self.norm_w[1, 1], evac_eng="dve")
            noml, lnoml, lbc, Blb = [], [], [], Buf()
            for d in range(2):
                r0, B0 = self.load_cols(ps, f"r0{d}", self.hgrn_lb_raw[d, 0], 8)
                r1, B1 = self.load_cols(ps, f"r1{d}", self.hgrn_lb_raw[d, 1], 8)
                tmp = self.sb(ps, f"lbt{d}", [128, 8], F32)
                nm = self.sb(ps, f"noml{d}", [128, 8], F32)
                lo = self.sb(ps, f"lnoml{d}", [128, 8], F32)
                S.op("dve", I("tensor_tensor", out=tmp[:], in0=r0[:], in1=r1[:], op=ALU.subtract),
                     reads=[B0, B1], writes=[Blb])
                S.op("act", I("activation", out=tmp[:], in_=tmp[:], func=AF.Exp), reads=[Blb], writes=[Blb])
                S.op("dve", I("tensor_scalar", out=tmp[:], in0=tmp[:], scalar1=1.0, scalar2=None, op0=ALU.add),
                     reads=[Blb], writes=[Blb])
                S.op("dve", I("reciprocal", out=tmp[:], in_=tmp[:]), reads=[Blb], writes=[Blb])
                lbc_ = self.sb(ps, f"lbc{d}", [128, 8], F32)
                S.op("dve", I("tensor_copy", out=lbc_[:], in_=tmp[:]), reads=[Blb], writes=[Blb])
                lbc.append(lbc_)
                S.op("dve", I("tensor_scalar", out=nm[:], in0=tmp[:], scalar1=1.0, scalar2=None, op0=ALU.subtract),
                     reads=[Blb], writes=[Blb])
                S.op("dve", I("tensor_scalar", out=tmp[:], in0=nm[:], scalar1=-1.0, scalar2=None, op0=ALU.mult),
                     reads=[Blb], writes=[Blb])
                S.op("act", I("activation", out=lo[:], in_=tmp[:], func=AF.Ln), reads=[Blb], writes=[Blb])
                noml.append(nm)
                lnoml.append(lo)
            qs = self.sb(ps, "qs", [128, NH, 512], F32)
            Bqs = [Buf() for _ in range(NH)]
            gso = [self.sb(ps, f"gso{i}", [128, 512], BF16) for i in range(2)]
            Bgso = [Buf(), Buf()]
            dgso = [S.dsem("gso") for _ in range(2)]
            vtok = self.sb(ps, "vtok", [128, 4, D], BF16)
            Bvtok = [Buf() for _ in range(4)]
            dvtok = S.dsem("vtok")
            kstok = [self.sb(ps, f"kstok{d}", [128, 4, D], BF16) for d in range(2)]
            Bkstok = [[Buf() for _ in range(NH)] for _ in range(2)]
            dkstok = [S.dsem("kstok") for _ in range(2)]
            NR = 3
            qeo = [self.sb(ps, f"qeo{i}", [128, 512], BF16) for i in range(NR)]
            kdo = [self.sb(ps, f"kdo{i}", [128, 512], BF16) for i in range(NR)]
            Bqeo = [Buf() for _ in range(NR)]
            Bkdo = [Buf() for _ in range(NR)]
            dqeo = [S.dsem("qeo") for _ in range(NR)]
            dkdo = [S.dsem("kdo") for _ in range(NR)]
            sl = [self.sb(ps, f"sl{i}", [128, 512], F32) for i in range(2)]
            Bsl = [Buf(), Buf()]
            slc = 0
            NTMP = 3
            TN = ("te", "tA", "tB", "tP", "tX")
            T_ = [{n: self.sb(ps, f"{n}{i}", [128, 512], F32) for n in TN} for i in range(NTMP)]
            BT = [{n: Buf() for n in TN + ("tks",)} for i in range(NTMP)]
            tks = [self.sb(ps, f"tks{i}", [128, 512], BF16) for i in range(NTMP)]
            v_v = H["v"].ap.rearrange("(n p) f -> p n f", p=128)
            ks_v = [H["ks"][d].ap.rearrange("(n p) f -> p n f", p=128) for d in range(2)]
            self.lnt_load(L, 0)
            bk = 0
            ro = 0
            hd = 0
            for t in range(self.NTL):
                tb = t % 2
                tok = slice(t * 512, (t + 1) * 512)
                def sgroup_pe(which, h):
                    nonlocal bk
                    col = (0 if which == 0 else 4 * D) + h * 128
                    b = bk % 6
                    bk += 1
                    S.group("pe", [I("matmul", self.bank[b][:], lhsT=Win[:, k, col:col + 128],
                                     rhs=L.xnT[tb][:, k, :], start=(k == 0), stop=(k == 7)) for k in range(8)],
                            reads=WIN.rb(col, col + 128) + L.BxnT[tb], writes=[self.B_bank[b]])
                    return b

                def sgroup_act(which, h, b):
                    nonlocal slc
                    k_ = slc % 2
                    slc += 1
                    S.op("act", I("activation", out=sl[k_][:], in_=self.bank[b][:], func=AF.Exp, scale=-1.0),
                         reads=[self.B_bank[b]], writes=[Bsl[k_]])
                    S.op("act", I("activation", out=sl[k_][:], in_=sl[k_][:], func=AF.Ln, bias=1.0, scale=1.0),
                         reads=[Bsl[k_]], writes=[Bsl[k_]])
                    S.op("act", I("activation", out=sl[k_][:], in_=sl[k_][:], func=AF.Exp, scale=-1.0),
                         reads=[Bsl[k_]], writes=[Bsl[k_]])
                    if which == 0:
                        S.op("dve", I("tensor_tensor", out=qs[:, h, :], in0=self.bank[b][:], in1=sl[k_][:], op=ALU.mult),
                             reads=[self.B_bank[b], Bsl[k_]], writes=[Bqs[h]])
                    else:
                        gk = h % 2
                        S.op("dve", I("tensor_tensor", out=gso[gk][:], in0=self.bank[b][:], in1=sl[k_][:], op=ALU.mult),
                             reads=[self.B_bank[b], Bsl[k_]], writes=[Bgso[gk]])
                        S.dma("sp", H["gs"].ap[h * 128:(h + 1) * 128, tok], gso[gk][:], dgso[gk],
                              reads=[Bgso[gk]], writes=[H["gs"].bufs[t][h]])

                def vgroup(c, n):
                    nonlocal bk
                    b = bk % 6
                    bk += 1
                    S.group("pe", [I("matmul", self.bank[b][:], lhsT=L.xnT[tb][:, k, c * 128:(c + 1) * 128],
                                     rhs=Win[:, k, 3 * D + n * 512:3 * D + (n + 1) * 512], start=(k == 0),
                                     stop=(k == 7)) for k in range(8)],
                            reads=WIN.rb(3 * D, 4 * D) + L.BxnT[tb], writes=[self.B_bank[b]])
                    S.op("dve", I("tensor_copy", out=vtok[:, c, n * 512:(n + 1) * 512], in_=self.bank[b][:]),
                         reads=[self.B_bank[b]], writes=[Bvtok[c]])
                    if c == 3 and n == 1:
                        S.dma("sp", v_v[:, t * 4:(t + 1) * 4, :], vtok[:], dvtok, reads=Bvtok, writes=H["v"].bufs[t])

                vlist = [(c, n) for c in range(4) for n in range(2)]
                items = [(d, h) for d in range(2) for h in range(NH)]

                def stage0(d, h):
                    nonlocal bk
                    col = (1 + d) * D + h * 128
                    b = bk % 6
                    bk += 1
                    S.group("pe", [I("matmul", self.bank[b][:], lhsT=Win[:, k, col:col + 128],
                                     rhs=L.xnT[tb][:, k, :], start=(k == 0), stop=(k == 7)) for k in range(8)],
                            reads=WIN.rb(col, col + 128) + L.BxnT[tb], writes=[self.B_bank[b]])
                    return b

                def stage1(d, h, i_, b):
                    T, B_ = T_[i_], BT[i_]
                    S.op("act", I("activation", out=T["te"][:], in_=self.bank[b][:], func=AF.Exp),
                         reads=[self.B_bank[b]], writes=[B_["te"]])
                    S.op("act", I("activation", out=T["tA"][:], in_=T["te"][:], func=AF.Ln, bias=lbc[d][:, h:h + 1],
                                  scale=1.0), reads=[B_["te"], Blb], writes=[B_["tA"]])
                    S.op("act", I("activation", out=T["tB"][:], in_=T["te"][:], func=AF.Ln, bias=1.0, scale=1.0),
                         reads=[B_["te"]], writes=[B_["tB"]])
                    S.op("pool", I("tensor_tensor", out=T["tA"][:], in0=T["tA"][:], in1=T["tB"][:], op=ALU.subtract),
                         reads=[B_["tB"]], writes=[B_["tA"]])
                    S.op("dve", I("tensor_tensor_scan", out=T["tP"][:], data0=self.mreset[:], data1=T["tA"][:],
                                  initial=0.0, op0=ALU.mult, op1=ALU.add),
                         reads=[B_["tA"], self.B_mreset], writes=[B_["tP"]])
                    v4 = lambda ap: ap.rearrange("p (c t) -> p c t", c=4)
                    Tb = v4(T["tP"][:])[:, :, 127:128].broadcast_to([128, 4, 128])
                    if d == 0:
                        S.op("pool", I("tensor_tensor", out=T["te"][:], in0=T["tB"][:], in1=T["tP"][:], op=ALU.add),
                             reads=[B_["tB"], B_["tP"]], writes=[B_["te"]])
                        S.op("pool", I("tensor_tensor", out=v4(T["tX"][:]), in0=Tb, in1=v4(T["te"][:]),
                                       op=ALU.subtract), reads=[B_["tP"], B_["te"]], writes=[B_["tX"]])
                    else:
                        S.op("dve", I("tensor_tensor", out=T["tA"][:], in0=T["tP"][:], in1=T["tA"][:],
                                      op=ALU.subtract), reads=[B_["tP"]], writes=[B_["tA"]])
                        S.op("pool", I("tensor_tensor", out=T["tB"][:], in0=T["tA"][:], in1=T["tB"][:],
                                       op=ALU.subtract), reads=[B_["tA"]], writes=[B_["tB"]])
                        S.op("pool", I("tensor_tensor", out=v4(T["tA"][:]), in0=Tb, in1=v4(T["tA"][:]),
                                       op=ALU.subtract), reads=[B_["tP"]], writes=[B_["tA"]])
                        S.op("pool", I("tensor_tensor", out=v4(T["te"][:]), in0=Tb, in1=v4(T["tB"][:]),
                                       op=ALU.subtract), reads=[B_["tP"], B_["tB"]], writes=[B_["te"]])

                def stage2(d, h, i_):
                    nonlocal ro
                    T, B_ = T_[i_], BT[i_]
                    oml_b = lnoml[d][:, h:h + 1]
                    r_ = ro % NR
                    ro += 1
                    S.op("act", I("activation", out=self.Dd[d][:, h, t * 4:(t + 1) * 4],
                                  in_=T["tP"][:, 127:512:128], func=AF.Exp), reads=[B_["tP"]])
                    qarg, qB = (T["tP"], B_["tP"]) if d == 0 else (T["tA"], B_["tA"])
                    ksarg, ksB = (T["tX"], B_["tX"]) if d == 0 else (T["tB"], B_["tB"])
                    S.op("act", I("activation", out=qarg[:], in_=qarg[:], func=AF.Exp), reads=[qB], writes=[qB])
                    S.op("act", I("activation", out=kdo[r_][:], in_=T["te"][:], func=AF.Exp, scale=-1.0, bias=oml_b),
                         reads=[B_["te"], Blb], writes=[Bkdo[r_]])
                    S.op("act", I("activation", out=tks[i_][:], in_=ksarg[:], func=AF.Exp, bias=oml_b),
                         reads=[ksB, Blb], writes=[B_["tks"]])
                    S.op("dve", I("tensor_tensor", out=qeo[r_][:], in0=qs[:, h, :], in1=qarg[:], op=ALU.mult),
                         reads=[Bqs[h], qB], writes=[Bqeo[r_]])
                    S.dma("sp", H["qe"][d].ap[h * 128:(h + 1) * 128, tok], qeo[r_][:], dqeo[r_],
                          reads=[Bqeo[r_]], writes=[H["qe"][d].bufs[t][h]])
                    S.dma("sp", H["kd"][d].ap[h * 128:(h + 1) * 128, tok], kdo[r_][:], dkdo[r_],
                          reads=[Bkdo[r_]], writes=[H["kd"][d].bufs[t][h]])
                    pt = self.bank[6 + (h % 2)][:].bitcast(BF16)
                    S.group("pe", [I("transpose", out=pt[:, c * 128:(c + 1) * 128],
                                     in_=tks[i_][:, c * 128:(c + 1) * 128], identity=self.ident[:])
                                   for c in range(4)], reads=[B_["tks"], self.B_ident],
                            writes=[self.B_bank[6 + (h % 2)]])
                    S.op("dve", I("tensor_copy", out=kstok[d][:, :, h * 128:(h + 1) * 128],
                                  in_=pt[:, 0:512].rearrange("p (c k) -> p c k", c=4)),
                         reads=[self.B_bank[6 + (h % 2)]], writes=[Bkstok[d][h]])
                    if h == NH - 1:
                        S.dma("sp", ks_v[d][:, t * 4:(t + 1) * 4, :], kstok[d][:], dkstok[d], reads=Bkstok[d],
                              writes=H["ks"][d].bufs[t])

                AH = 3
                extras = [(0, h_) for h_ in range(3, 8)] + [(1, h_) for h_ in range(8)]
                for h0 in range(3):
                    sgroup_act(0, h0, sgroup_pe(0, h0))
                banks_ = [stage0(*items[q]) for q in range(AH)]
                xb_ = sgroup_pe(*extras[0])
                stage1(*items[0], hd % NTMP, banks_[0])
                stage1(*items[1], (hd + 1) % NTMP, banks_[1])
                for j, (d, h) in enumerate(items):
                    if j + AH < len(items):
                        banks_.append(stage0(*items[j + AH]))
                    if j + 2 < len(items):
                        stage1(*items[j + 2], (hd + 2) % NTMP, banks_[j + 2])
                    nxb_ = sgroup_pe(*extras[j + 1]) if j + 1 < len(extras) else None
                    stage2(d, h, hd % NTMP)
                    hd += 1
                    if j < len(extras):
                        sgroup_act(*extras[j], xb_)
                    xb_ = nxb_
                    if j % 2 == 1:
                        vgroup(*vlist[j // 2])
                    if 1 <= j <= 9 and t + 1 < self.NTL:
                        if j == 1:
                            micro = self.lnt_micro(L, t + 1)
                        micro[j - 1]()
            self.B_Dd[0].w = self.B_Dd[1].w = Tok(S.eng["act"].sem, S.eng["act"].cnt)
            S.flush()

    def phase_hgrn_scan(self, d, H, src=None, dst=None, ss_out=None, B_ss_out=None):
        S, NB, NTL = self.S, self.NB, self.NTL
        with ExitStack() as ps:
            NO = 2
            NG = 3
            qeT = [self.sb(ps, f"qeT{i}", [128, NH, 512], BF16) for i in range(2)]
            kdT = [self.sb(ps, f"kdT{i}", [128, NH, 512], BF16) for i in range(2)]
            kst = [self.sb(ps, f"kst{i}", [128, 4, D], BF16) for i in range(2)]
            vt = [self.sb(ps, f"vt{i}", [128, 4, D], BF16) for i in range(2)]
            oT = [self.sb(ps, f"oT{i}", [128, NH, 512], F32 if d == 1 else BF16) for i in range(NO)]
            Bld = [[Buf() for _ in range(4)] for _ in range(2)]
            BoT = [[Buf() for _ in range(NH)] for _ in range(NO)]
            dld = [[S.dsem("scld") for _ in range(4)] for _ in range(2)]
            doT = [S.dsem("oT") for _ in range(NO)]
            Am = [self.sb(ps, f"Am{i}", [128, 4, 128], BF16) for i in range(2)]
            BAm = [Buf(), Buf()]
            St = self.sb(ps, "St", [128, NH, 128], F32)
            BSt = Buf()
            Sb = [self.sb(ps, f"Sb{i}", [128, NH, 128], BF16) for i in range(2)]
            BSb = [Buf(), Buf()]
            S.op("pool", I("memset", St[:], 0.0), writes=[BSt])
            S.op("pool", I("memset", Sb[0][:], 0.0), writes=[BSb[0]])
            S.op("pool", I("memset", Sb[1][:], 0.0), writes=[BSb[1]])
            qe_v = H["qe"][d].ap.rearrange("(h p) t -> p h t", p=128)
            kd_v = H["kd"][d].ap.rearrange("(h p) t -> p h t", p=128)
            ks_v = H["ks"][d].ap.rearrange("(n p) f -> p n f", p=128)
            v_v = H["v"].ap.rearrange("(n p) f -> p n f", p=128)
            of_v = H["of"].ap.rearrange("(h p) t -> p h t", p=128)
            if d == 1:
                WO = self.load_w(ps, "Who", NH, D, "row", [(0, NH)], key="hg_out")
                Wo = WO.tile
                nw, Bnw = self.load_cols(ps, "hnw", self.hgrn_norm_w, NH)
                for h in range(NH):
                    S.op("dve", I("tensor_scalar", out=Wo[:, h, :], in0=Wo[:, h, :], scalar1=nw[:, h:h + 1],
                                  scalar2=None, op0=ALU.mult), reads=WO.B + [Bnw], writes=WO.B)
                gsT = [self.sb(ps, f"gsT{i}", [128, NH, 512], BF16) for i in range(NG)]
                Bgs = [Buf() for _ in range(NG)]
                dgs = [S.dsem("gsld") for _ in range(NG)]
                ofb = [self.sb(ps, f"ofb{i}", [128, NH, 512], BF16) for i in range(2)]
                Bofb = [Buf(), Buf()]
                dofb = [S.dsem("ofb") for _ in range(2)]
                gs_v = H["gs"].ap.rearrange("(h p) t -> p h t", p=128)
                sq = [self.sb(ps, f"sq{i}", [128, 512], BF16) for i in range(2)]
                Bsq = [Buf(), Buf()]
                rt = [self.sb(ps, f"rt{i}", [128, 512], F32) for i in range(2)]
                Brt = [Buf(), Buf()]
                onT = [self.sb(ps, f"onT{i}", [128, NH, 512], BF16) for i in range(2)]
                BonT = [[Buf() for _ in range(NH)] for _ in range(2)]
                R = self.resid_alloc(ps, src, dst, ss_out, B_ss_out)

            order = list(range(NTL)) if d == 0 else list(range(NTL - 1, -1, -1))

            def issue_loads(j, parts=(0, 1, 2, 3, 4, 5)):
                t = order[j]
                p = j % 2
                po = j % NO
                tok = slice(t * 512, (t + 1) * 512)
                if 0 in parts:
                    S.dma("sp", qeT[p][:], qe_v[:, :, tok], dld[p][0], reads=H["qe"][d].bufs[t], writes=[Bld[p][0]])
                if 1 in parts:
                    S.dma("sp", kdT[p][:], kd_v[:, :, tok], dld[p][1], reads=H["kd"][d].bufs[t], writes=[Bld[p][1]])
                if 2 in parts:
                    S.dma("sp", kst[p][:], ks_v[:, t * 4:(t + 1) * 4, :], dld[p][2], reads=H["ks"][d].bufs[t],
                          writes=[Bld[p][2]])
                if 3 in parts:
                    S.dma("sp", vt[p][:], v_v[:, t * 4:(t + 1) * 4, :], dld[p][3], reads=H["v"].bufs[t],
                          writes=[Bld[p][3]])
                if d == 1 and 4 in parts:
                    S.dma("sp", ofb[p][:], of_v[:, :, tok], dofb[p], reads=H["of"].bufs[t], writes=[Bofb[p]])
                if d == 1 and 5 in parts:
                    S.dma("sp", gsT[j % NG][:], gs_v[:, :, tok], dgs[j % NG], reads=H["gs"].bufs[t],
                          writes=[Bgs[j % NG]])

            class TW:
                pass

            def tile_work(j):
                w = TW()
                w.t = order[j]
                w.po = j % NO
                w.pg = j % NG
                w.pn = j % 2
                w.xk = {}
                return w

            def sqr(w, h):
                k2 = h % 2
                S.op("act", I("activation", out=sq[k2][:], in_=oT[w.po][:, h, :], func=AF.Square),
                     reads=[BoT[w.po][h]], writes=[Bsq[k2]])

            def mmn(w, h):
                k2 = h % 2
                S.group("pe", [I("matmul", self.bank[6][:], lhsT=self.onesb[:], rhs=sq[k2][:], start=True, stop=True)],
                        reads=[Bsq[k2], self.B_onesb], writes=[self.B_bank[6]])
                S.op("act", I("activation", out=rt[k2][:], in_=self.bank[6][:], func=AF.Ln, bias=EPS,
                              scale=1.0 / 128), reads=[self.B_bank[6]], writes=[Brt[k2]])
                S.op("act", I("activation", out=rt[k2][:], in_=rt[k2][:], func=AF.Exp, scale=-0.5),
                     reads=[Brt[k2]], writes=[Brt[k2]])

            def fin(w, h):
                k2 = h % 2
                S.op("pool", I("tensor_tensor", out=rt[k2][:], in0=oT[w.po][:, h, :], in1=rt[k2][:], op=ALU.mult),
                     reads=[BoT[w.po][h], Brt[k2]], writes=[Brt[k2]])
                S.op("pool", I("tensor_tensor", out=onT[w.pn][:, h, :], in0=rt[k2][:], in1=gsT[w.pg][:, h, :],
                               op=ALU.mult), reads=[Brt[k2], Bgs[w.pg]], writes=[BonT[w.pn][h]])

            def outp(w, s_, nh, pe_only=False, dve_only=False, load_only=False):
                i = w.t * 4 + s_
                if load_only:
                    w.xk[s_] = self.resid_begin(R, i)
                    return
                if not dve_only:
                    if nh == 0 and s_ not in w.xk:
                        w.xk[s_] = self.resid_begin(R, i)
                    S.group("pe", [I("matmul", self.bank[7][:], lhsT=onT[w.pn][:, h, s_ * 128:(s_ + 1) * 128],
                                     rhs=Wo[:, h, nh * 512:(nh + 1) * 512], start=(h == 0), stop=(h == NH - 1))
                                   for h in range(NH)], reads=WO.B + BonT[w.pn], writes=[self.B_bank[7]])
                if pe_only:
                    return
                xk = w.xk[s_]
                self.resid_add(R, xk, nh, 7, 1.0)
                if nh == 1:
                    self.resid_end(R, xk, i)

            issue_loads(0)
            sbi = 0
            normW = None
            outW = None
            for j in range(NTL):
                t = order[j]
                p = j % 2
                po = j % NO
                if d == 0 and j + 1 < NTL:
                    issue_loads(j + 1)
                corder = range(4) if d == 0 else range(3, -1, -1)
                for ci, c in enumerate(corder):
                    n = t * 4 + c
                    cs = slice(c * 128, (c + 1) * 128)
                    if normW is not None:
                        sqr(normW, 2 * ci)
                        sqr(normW, 2 * ci + 1)
                    if d == 1 and outW is not None:
                        outp(outW, ci, 0, load_only=True)
                    if d == 1 and j + 1 < NTL and ci < 3:
                        issue_loads(j + 1, parts=((0, 1), (2, 3), (4, 5))[ci])
                    for hb in range(2):
                        S.group("pe", [I("matmul", self.bank[hb][:, q * 128:(q + 1) * 128],
                                         lhsT=kdT[p][:, hb * 4 + q, cs], rhs=qeT[p][:, hb * 4 + q, cs],
                                         start=True, stop=True) for q in range(4)],
                                reads=[Bld[p][0], Bld[p][1]], writes=[self.B_bank[hb]])
                    for hb in range(2):
                        S.group("pe", [I("matmul", self.bank[2 + hb][:, q * 128:(q + 1) * 128],
                                         lhsT=kst[p][:, c, (hb * 4 + q) * 128:(hb * 4 + q + 1) * 128],
                                         rhs=vt[p][:, c, (hb * 4 + q) * 128:(hb * 4 + q + 1) * 128],
                                         start=True, stop=True) for q in range(4)],
                                reads=[Bld[p][2], Bld[p][3]], writes=[self.B_bank[2 + hb]])
                    if normW is not None:
                        mmn(normW, 2 * ci)
                    if outW is not None:
                        outp(outW, ci, 0, pe_only=True)
                    for hb in range(2):
                        S.op("dve", I("tensor_tensor", out=Am[hb][:],
                                      in0=self.bank[hb][:].rearrange("p (q t) -> p q t", q=4),
                                      in1=self.mask[d][:].unsqueeze(1).broadcast_to([128, 4, 128]), op=ALU.mult),
                             reads=[self.B_bank[hb], self.B_mask], writes=[BAm[hb]])
                    sp_ = sbi % 2
                    for hb in range(2):
                        fns = []
                        for q in range(4):
                            h = hb * 4 + q
                            fns.append(I("matmul", self.bank[4 + hb][:, q * 128:(q + 1) * 128],
                                         lhsT=vt[p][:, c, h * 128:(h + 1) * 128], rhs=Am[hb][:, q, :],
                                         start=True, stop=False))
                            fns.append(I("matmul", self.bank[4 + hb][:, q * 128:(q + 1) * 128],
                                         lhsT=Sb[sp_][:, h, :], rhs=qeT[p][:, h, cs], start=False, stop=True))
                        S.group("pe", fns, reads=[Bld[p][3], Bld[p][0], BAm[hb], BSb[sp_]],
                                writes=[self.B_bank[4 + hb]])
                    S.op("dve", I("tensor_tensor", out=St[:], in0=St[:],
                                  in1=self.Dd[d][:, :, n:n + 1].broadcast_to([128, NH, 128]), op=ALU.mult),
                         reads=[self.B_Dd[d]], writes=[BSt])
                    for hb in range(2):
                        S.op("dve", I("tensor_tensor", out=St[:, hb * 4:(hb + 1) * 4, :],
                                      in0=St[:, hb * 4:(hb + 1) * 4, :],
                                      in1=self.bank[2 + hb][:].rearrange("p (q t) -> p q t", q=4), op=ALU.add),
                             reads=[self.B_bank[2 + hb]], writes=[BSt])
                    nxt = n + 1 if d == 0 else n - 1
                    if 0 <= nxt < NB:
                        bnd = (nxt % 16 == 0) if d == 0 else (nxt % 16 == 15)
                        if bnd:
                            ccol = nxt if d == 0 else NB + nxt
                            S.op("dve", I("tensor_scalar", out=St[:], in0=St[:],
                                          scalar1=self.carry_t[:, ccol:ccol + 1], scalar2=None, op0=ALU.mult),
                                 reads=[self.B_carry], writes=[BSt])
                    sbi += 1
                    S.op("act", I("activation", out=Sb[sbi % 2][:], in_=St[:], func=AF.Copy),
                         reads=[BSt], writes=[BSb[sbi % 2]])
                    if outW is not None:
                        outp(outW, ci, 0, dve_only=True)
                    if normW is not None:
                        mmn(normW, 2 * ci + 1)
                    if outW is not None:
                        outp(outW, ci, 1)
                    for hb in range(2):
                        ov = oT[po][:, hb * 4:(hb + 1) * 4, cs]
                        bv = self.bank[4 + hb][:].rearrange("p (q t) -> p q t", q=4)
                        if d == 0:
                            S.op("act", I("activation", out=ov, in_=bv, func=AF.Copy),
                                 reads=[self.B_bank[4 + hb]], writes=BoT[po][hb * 4:(hb + 1) * 4])
                        else:
                            S.op("dve", I("tensor_tensor", out=ov, in0=bv, in1=ofb[p][:, hb * 4:(hb + 1) * 4, cs],
                                          op=ALU.add),
                                 reads=[self.B_bank[4 + hb], Bofb[p]], writes=BoT[po][hb * 4:(hb + 1) * 4])
                    if normW is not None:
                        fin(normW, 2 * ci)
                        fin(normW, 2 * ci + 1)
                if d == 0:
                    S.dma("sp", of_v[:, :, t * 512:(t + 1) * 512], oT[po][:], doT[po], reads=BoT[po],
                          writes=H["of"].bufs[t])
                else:
                    outW = normW
                    normW = tile_work(j)
            if d == 1:
                if outW is not None:
                    for s_ in range(4):
                        outp(outW, s_, 0)
                        outp(outW, s_, 1)
                for h in range(NH):
                    sqr(normW, h)
                    mmn(normW, h)
                    fin(normW, h)
                for s_ in range(4):
                    outp(normW, s_, 0)
                    outp(normW, s_, 1)
                self.resid_finish(R)
            S.flush()

    def phase_final(self, src, ss, B_ss, copy_only=False):
        S = self.S
        with ExitStack() as ps:
            self.compute_rstd(ss, B_ss, ps)
            wb = self.sb(ps, "fwb", [128, D], F32)
            Bwb = Buf()
            dwb = S.dsem("fwb")
            S.dma("sp", wb[:], self.final_norm.partition_broadcast(128), dwb, writes=[Bwb])
            xl = [self.sb(ps, f"fx{i}", [128, D], F32) for i in range(3)]
            Bx = [Buf() for _ in range(3)]
            dx = [S.dsem("fx") for _ in range(3)]
            for i in range(self.NB):
                k = i % 3
                S.dma("sp", xl[k][:], src.ap[i * 128:(i + 1) * 128, :], dx[k], reads=[src.bufs[i]], writes=[Bx[k]])
                if not copy_only:
                    S.op("dve", I("scalar_tensor_tensor", out=xl[k][:], in0=xl[k][:],
                                                                          scalar=self.rstd[:, i:i + 1], in1=wb[:],
                                                                          op0=ALU.mult, op1=ALU.mult),
                         reads=[self.B_rstd, Bwb], writes=[Bx[k]])
                S.dma("sp", self.y_out.ap[i * 128:(i + 1) * 128, :], xl[k][:], dx[k], reads=[Bx[k]],
                      writes=[self.y_out.bufs[i]])
            S.flush()


W_NAMES = ["norm_w", "ffn_gate", "ffn_up", "ffn_down", "sgu_w_in", "sgu_ln_g", "sgu_ln_b", "sgu_w_s", "sgu_b_s",
           "sgu_w_out", "hgrn_w_in", "hgrn_lb_raw", "hgrn_norm_w", "hgrn_w_out", "final_norm"]
W_SQUEEZE = {"sgu_w_in", "sgu_ln_g", "sgu_ln_b", "sgu_w_s", "sgu_b_s", "sgu_w_out", "hgrn_w_in", "hgrn_norm_w",
             "hgrn_w_out"}


def make_carry(NB, seq_blocks):
    c = np.ones((128, 2 * NB), np.float32)
    for n in range(NB):
        if n % seq_blocks == 0:
            c[:, n] = 0.0
        if n % seq_blocks == seq_blocks - 1:
            c[:, NB + n] = 0.0
    return c


def kernel(**inputs):
    xp = np.ascontiguousarray(inputs["x_prompt"], dtype=np.float32)
    xs = np.ascontiguousarray(inputs["x_sample"], dtype=np.float32)
    NT = NT_FULL
    prog = Prog(NT)
    wmap = {}
    for n in W_NAMES:
        a = np.ascontiguousarray(inputs[n], dtype=np.float32)
        if n in W_SQUEEZE:
            a = a[0]
        wmap[n] = a
    in_maps = []
    for c in range(NCORES):
        m = dict(wmap)
        if c < 4:
            m["x"] = xp[4 * c:4 * c + 4].reshape(NT, D)
            m["carry"] = make_carry(NT // 128, SEQ_PROMPT // 128)
        else:
            m["x"] = xs[c - 4].reshape(NT, D)
            m["carry"] = make_carry(NT // 128, NT // 128)
        in_maps.append(m)
    res = run_bass_kernel_spmd(prog.nc, in_maps, core_ids=list(range(NCORES)))
    ys = [np.asarray(r["y"], dtype=np.float32) for r in res.results]
    y_prompt = np.stack(ys[:4]).reshape(16, 2048, D)
    y_sample = np.stack(ys[4:]).reshape(4, 8192, D)
    return (y_prompt, y_sample)
```

```python
import numpy as np
from contextlib import ExitStack
import concourse.bass as bass
import concourse.mybir as mybir
from concourse.bass_utils import run_bass_kernel_spmd

F32, BF16 = mybir.dt.float32, mybir.dt.bfloat16
AF = mybir.ActivationFunctionType
ALU = mybir.AluOpType

D = 1024
FF = 2816
NF = FF // 128
SG = 3072
NE = SG // 128
NH = 8
EPS = 1e-6
NCORES = 8
NT_FULL = 8192
SEQ_PROMPT = 2048


class Tok:
    __slots__ = ("sem", "val")

    def __init__(self, sem, val):
        self.sem = sem
        self.val = val


class Buf:
    __slots__ = ("w", "r", "name")

    def __init__(self, name=""):
        self.w = None
        self.r = []
        self.name = name


class DSem:
    def __init__(self, h):
        self.h = h
        self.cnt = 0


class Eng:
    def __init__(self, name):
        self.name = name
        self.ops = []
        self.sem = None
        self.cnt = 0
        self.known = {}
        self.rec = []


class Sched:
    def __init__(self, nc, es):
        self.nc = nc
        self.es = es
        self.eng = {n: Eng(n) for n in ("pe", "act", "dve", "pool", "sp")}
        self.nsem = 0
        self.dsems = []
        self.free_dsems = []
        self.phase_dsems = []
        self.bg_dsems = []
        self._new_engine_sems()

    def _newsem(self, name):
        self.nsem += 1
        return self.es.enter_context(self.nc.semaphore(f"{name}{self.nsem}"))

    def _new_engine_sems(self):
        for n in ("pe", "act", "dve", "pool"):
            E = self.eng[n]
            E.sem = self._newsem("e" + n)
            E.cnt = 0

    def dsem(self, name="d", bg=False):
        if bg:
            d = DSem(self._newsem(name))
            self.bg_dsems.append(d)
            return d
        if self.free_dsems:
            d = self.free_dsems.pop()
        else:
            d = DSem(self._newsem(name))
            self.dsems.append(d)
        self.phase_dsems.append(d)
        return d

    def _deps(self, E, reads, writes):
        need = {}

        def add(t):
            if t is None:
                return
            if E.name == "pe" and t.sem is E.sem:
                return
            k = id(t.sem)
            if E.known.get(k, 0) >= t.val:
                return
            if k not in need or need[k].val < t.val:
                need[k] = t

        for b in reads:
            add(b.w)
        for b in writes:
            add(b.w)
            for t in b.r:
                add(t)
        for k, t in need.items():
            E.known[k] = t.val
        return list(need.values())

    def _reg(self, tok, reads, writes):
        for b in reads:
            b.r.append(tok)
        for b in writes:
            b.w = tok
            b.r = []

    def op(self, en, fn, reads=(), writes=()):
        return self.group(en, [fn], reads, writes)

    def group(self, en, fns, reads=(), writes=()):
        E = self.eng[en]
        waits = self._deps(E, reads, writes)
        E.cnt += 1
        sem = E.sem
        tok = Tok(sem, E.cnt)

        def run(e):
            for t in waits:
                e.wait_ge(t.sem, t.val)
            ins = None
            for fn in fns:
                ins = getattr(e, fn[0])(*fn[1], **fn[2])
            ins.then_inc(sem, 1)

        E.ops.append(run)
        E.rec.append(([(id(t.sem), t.val) for t in waits], (id(sem), 1)))
        self._reg(tok, reads, writes)
        return tok

    def dma(self, en, out, in_, ds, reads=(), writes=(), **kw):
        E = self.eng[en]
        waits = self._deps(E, reads, writes)
        ds.cnt += 16
        tok = Tok(ds.h, ds.cnt)
        h = ds.h

        def run(e):
            for t in waits:
                e.wait_ge(t.sem, t.val)
            e.dma_start(out=out, in_=in_, **kw).then_inc(h, 16)

        E.ops.append(run)
        E.rec.append(([(id(t.sem), t.val) for t in waits], (id(h), 16)))
        self._reg(tok, reads, writes)
        return tok

    def check_deadlock(self):
        vals = getattr(self, "_simvals", {})
        ptr = {n: 0 for n in self.eng}
        prog = True
        while prog:
            prog = False
            for n, E in self.eng.items():
                while ptr[n] < len(E.rec):
                    waits, inc = E.rec[ptr[n]]
                    if all(vals.get(k, 0) >= v for k, v in waits):
                        if inc is not None:
                            vals[inc[0]] = vals.get(inc[0], 0) + inc[1]
                        ptr[n] += 1
                        prog = True
                    else:
                        break
        stuck = {n: (ptr[n], len(E.rec)) for n, E in self.eng.items() if ptr[n] < len(E.rec)}
        self._simvals = vals
        if stuck:
            msg = []
            for n, (p, tot) in stuck.items():
                waits, inc = self.eng[n].rec[p]
                msg.append(f"{n}: op {p}/{tot} waits " + str([(k % 10000, v, vals.get(k, 0)) for k, v in waits if vals.get(k, 0) < v]))
            raise RuntimeError("schedule deadlock: " + "; ".join(msg))
        for E in self.eng.values():
            E.rec = []

    def barrier(self, final=False):
        toks = [Tok(self.eng[n].sem, self.eng[n].cnt) for n in ("pe", "act", "dve", "pool") if self.eng[n].cnt > 0]
        toks += [Tok(d.h, d.cnt) for d in self.dsems if d.cnt > 0]
        for n, E in self.eng.items():
            ws = []
            for t in toks:
                if E.name == "pe" and t.sem is E.sem:
                    continue
                k = id(t.sem)
                if E.known.get(k, 0) >= t.val:
                    continue
                E.known[k] = t.val
                ws.append(t)

            def run(e, ws=ws):
                for t in ws:
                    e.wait_ge(t.sem, t.val)

            E.ops.append(run)
            E.rec.append(([(id(t.sem), t.val) for t in ws], None))

    def flush(self):
        self.barrier()
        self.check_deadlock()
        lists = {n: E.ops for n, E in self.eng.items()}
        for E in self.eng.values():
            E.ops = []
        with self.nc.Block() as block:
            @block.tensor
            def _(e):
                for f in lists["pe"]:
                    f(e)

            @block.scalar
            def _(e):
                for f in lists["act"]:
                    f(e)

            @block.vector
            def _(e):
                for f in lists["dve"]:
                    f(e)

            @block.gpsimd
            def _(e):
                for f in lists["pool"]:
                    f(e)

            @block.sync
            def _(e):
                for f in lists["sp"]:
                    f(e)
        self.free_dsems.extend(self.phase_dsems)
        self.phase_dsems = []


def I(name, *a, **kw):
    return (name, a, kw)


class DramAct:
    def __init__(self, ap, nblk, name, sub=None):
        self.ap = ap
        if sub is None:
            self.bufs = [Buf(f"{name}{i}") for i in range(nblk)]
        else:
            self.bufs = [[Buf(f"{name}{i}_{j}") for j in range(sub)] for i in range(nblk)]


class Prog:
    def __init__(self, NT, stop_after=None, debug=False):
        assert NT % 512 == 0
        self.debug = debug
        self.NT = NT
        self.NB = NT // 128
        self.NTL = NT // 512
        self.stop_after = stop_after
        self.nc = bass.Bass("TRN2", target_bir_lowering=False)
        self.es = ExitStack()
        self.S = Sched(self.nc, self.es)
        self.build()
        self.es.close()

    def dram(self, name, shape, dt, kind="Internal"):
        return self.nc.dram_tensor(name, list(shape), dt, kind=kind).ap()

    def sb(self, stack, name, shape, dt):
        self._uid = getattr(self, "_uid", 0) + 1
        return stack.enter_context(self.nc.sbuf_tensor(f"{name}_{self._uid}", list(shape), dt))

    def dbg(self, name, ap, bufs, dt, eng="sp"):
        if not getattr(self, "debug", False):
            return
        out = self.dram("dbg_" + name, list(ap.shape), dt, "ExternalOutput")
        self.S.dma(eng, out, ap, self.S.dsem("dbg"), reads=bufs)

    def build(self):
        nc, S, NT, NB = self.nc, self.S, self.NT, self.NB
        es = self.es
        self.x_in = DramAct(self.dram("x", [NT, D], F32, "ExternalInput"), NB, "xin")
        self.y_out = DramAct(self.dram("y", [NT, D], F32, "ExternalOutput"), NB, "yout")
        self.norm_w = self.dram("norm_w", [2, 3, D], F32, "ExternalInput")
        self.ffn_gate = self.dram("ffn_gate", [2, 2, D, FF], F32, "ExternalInput")
        self.ffn_up = self.dram("ffn_up", [2, 2, D, FF], F32, "ExternalInput")
        self.ffn_down = self.dram("ffn_down", [2, 2, FF, D], F32, "ExternalInput")
        self.sgu_w_in = self.dram("sgu_w_in", [D, 2 * SG], F32, "ExternalInput")
        self.sgu_ln_g = self.dram("sgu_ln_g", [SG], F32, "ExternalInput")
        self.sgu_ln_b = self.dram("sgu_ln_b", [SG], F32, "ExternalInput")
        self.sgu_w_s = self.dram("sgu_w_s", [8, 128, 128], F32, "ExternalInput")
        self.sgu_b_s = self.dram("sgu_b_s", [8, 128], F32, "ExternalInput")
        self.sgu_w_out = self.dram("sgu_w_out", [SG, D], F32, "ExternalInput")
        self.hgrn_w_in = self.dram("hgrn_w_in", [D, 5 * D], F32, "ExternalInput")
        self.hgrn_lb_raw = self.dram("hgrn_lb_raw", [2, 2, D], F32, "ExternalInput")
        self.hgrn_norm_w = self.dram("hgrn_norm_w", [D], F32, "ExternalInput")
        self.hgrn_w_out = self.dram("hgrn_w_out", [D, D], F32, "ExternalInput")
        self.final_norm = self.dram("final_norm", [D], F32, "ExternalInput")
        self.carry = self.dram("carry", [128, 2 * NB], F32, "ExternalInput")
        self.xa = DramAct(self.dram("xa", [NT, D], F32), NB, "xa")
        self.xb = DramAct(self.dram("xb", [NT, D], F32), NB, "xb")
        self.sT = DramAct(self.dram("sT", [SG, NT], BF16), self.NTL, "sT")
        NTL = self.NTL
        self.H = dict(
            qe=[DramAct(self.dram(f"qe{d}", [NTL, 128, NH * 512], BF16), NTL, f"qe{d}", 8) for d in range(2)],
            kd=[DramAct(self.dram(f"kd{d}", [NTL, 128, NH * 512], BF16), NTL, f"kd{d}", 8) for d in range(2)],
            ks=[DramAct(self.dram(f"ks{d}", [NTL, 128, 4 * D], BF16), NTL, f"ks{d}", 1) for d in range(2)],
            v=DramAct(self.dram("vtok", [NTL, 128, 4 * D], BF16), NTL, "vtok", 1),
            gs=DramAct(self.dram("gsT", [NTL, 128, NH * 512], BF16), NTL, "gsT", 8),
            of=DramAct(self.dram("ofT", [NTL, 128, NH * 512], BF16), NTL, "ofT", 1),
        )

        self.ident = self.sb(es, "ident", [128, 128], BF16)
        self.identf = self.sb(es, "identf", [128, 128], F32)
        self.ssA = self.sb(es, "ssA", [128, NB], F32)
        self.ssB = self.sb(es, "ssB", [128, NB], F32)
        self.rstd = self.sb(es, "rstd", [128, NB], F32)
        self.B_ident = Buf("ident")
        self.B_ssA = Buf("ssA")
        self.B_ssB = Buf("ssB")
        self.B_rstd = Buf("rstd")
        self.bank = [es.enter_context(nc.psum_tensor(f"bank{i}", [128, 512], F32)) for i in range(8)]
        self.B_bank = [Buf(f"bank{i}") for i in range(8)]

        S.op("pool", I("memset", self.identf[:], 0.0), writes=[self.B_ident])
        S.op("pool", I("affine_select", out=self.identf[:], in_=self.identf[:], pattern=[[-1, 128]],
                                               compare_op=ALU.not_equal, fill=1.0, base=0, channel_multiplier=1),
             writes=[self.B_ident])
        S.op("dve", I("tensor_copy", out=self.ident[:], in_=self.identf[:]), writes=[self.B_ident])
        S.op("pool", I("memset", self.ssA[:], 0.0), writes=[self.B_ssA])
        S.op("pool", I("memset", self.ssB[:], 0.0), writes=[self.B_ssB])

        w1s = ExitStack()
        W1 = self.ffn_weights(w1s, 0, 0, direct=True)
        self.phase_stats(self.x_in, self.ssA, self.B_ssA)
        self.convert_all()
        self.phase_ffn(self.x_in, self.xa, 0, 0, 0, self.ssA, self.B_ssA, self.ssB, self.B_ssB, W=W1)
        w1s.close()
        if self.stop_after == "ffn1":
            self.phase_final(self.xa, self.ssB, self.B_ssB, copy_only=True)
            return
        self.phase_sgu_a(self.xa, self.ssB, self.B_ssB, self.sT)
        self.phase_sgu_b(self.xa, self.xb, self.ssB, self.B_ssB, self.ssA, self.B_ssA, self.sT)
        if self.stop_after == "sgu":
            self.phase_final(self.xb, self.ssA, self.B_ssA, copy_only=True)
            return
        self.phase_ffn(self.xb, self.xa, 0, 2, 1, self.ssA, self.B_ssA, self.ssB, self.B_ssB)
        self.phase_ffn(self.xa, self.xb, 1, 0, 0, self.ssB, self.B_ssB, self.ssA, self.B_ssA)
        if self.stop_after == "ffn3":
            self.phase_final(self.xb, self.ssA, self.B_ssA, copy_only=True)
            return
        with ExitStack() as hes:
            self.hgrn_consts(hes)
            self.phase_hgrn_a(self.xb, self.ssA, self.B_ssA, self.H)
            self.phase_hgrn_scan(0, self.H)
            self.phase_hgrn_scan(1, self.H, self.xb, self.xa, self.ssB, self.B_ssB)
        if self.stop_after == "hgrn":
            self.phase_final(self.xa, self.ssB, self.B_ssB, copy_only=True)
            return
        self.phase_ffn(self.xa, self.y_out, 1, 2, 1, self.ssB, self.B_ssB, self.ssA, self.B_ssA, final=True)

    def phase_stats(self, src, ss, B_ss):
        nc, S = self.nc, self.S
        with ExitStack() as ps:
            xl = [self.sb(ps, f"st_x{i}", [128, D], F32) for i in range(3)]
            junk = self.sb(ps, "st_junk", [128, D], BF16)
            Bx = [Buf() for _ in range(3)]
            Bj = Buf()
            ds = [S.dsem("stx") for _ in range(3)]
            for i in range(self.NB):
                k = i % 3
                S.dma("sp", xl[k][:], src.ap[i * 128:(i + 1) * 128, :], ds[k], reads=[src.bufs[i]], writes=[Bx[k]])
                S.op("act", I("activation", out=junk[:], in_=xl[k][:], func=AF.Square,
                                                            accum_out=ss[:, i:i + 1]),
                     reads=[Bx[k], B_ss], writes=[Bj])
            B_ss.w = Tok(S.eng["act"].sem, S.eng["act"].cnt)
            S.flush()

    def compute_rstd(self, ss, B_ss, ps):
        S = self.S
        tmp = self.sb(ps, "rs_tmp", [128, self.NB], F32)
        Bt = Buf()
        S.op("act", I("activation", out=tmp[:], in_=ss[:], func=AF.Sqrt, bias=EPS, scale=1.0 / D),
             reads=[B_ss], writes=[Bt])
        S.op("dve", I("reciprocal", out=self.rstd[:], in_=tmp[:]), reads=[Bt], writes=[self.B_rstd])

    def convert_all(self):
        S = self.S
        self.wbf, self.Bconv = {}, {}
        jobs = [("sgu_in", self.sgu_w_in, [D, 2 * SG]), ("sgu_out", self.sgu_w_out, [SG, D])]
        for (l, j) in ((0, 1), (1, 0)):
            jobs += [(f"gate{l}{j}", self.ffn_gate[l, j], [D, FF]), (f"up{l}{j}", self.ffn_up[l, j], [D, FF]),
                     (f"down{l}{j}", self.ffn_down[l, j], [FF, D])]
        jobs += [("hg_in", self.hgrn_w_in, [D, 5 * D]), ("hg_out", self.hgrn_w_out, [D, D])]
        jobs += [("gate11", self.ffn_gate[1, 1], [D, FF]), ("up11", self.ffn_up[1, 1], [D, FF]),
                 ("down11", self.ffn_down[1, 1], [FF, D])]
        prev = []
        cvs = [S.dsem("cv", bg=True) for _ in range(3)]
        RC = 256
        n = 0
        for key, src, shp in jobs:
            dst = self.dram("wbf_" + key, shp, BF16)
            self.wbf[key] = dst
            self.Bconv[key] = []
            for r0 in range(0, shp[0], RC):
                B = Buf(f"conv_{key}_{r0}")
                S.dma("pool", dst[r0:r0 + RC, :], src[r0:r0 + RC, :], cvs[n % 3], reads=prev[-3:-1], writes=[B],
                      max_dma_last_dim=4096)
                n += 1
                prev.append(B)
                self.Bconv[key].append(B)

    class WT:
        def __init__(self, tile, bounds):
            self.tile = tile
            self.bounds = bounds
            self.B = [Buf() for _ in bounds]

        def rb(self, lo, hi):
            return [b for (a, c), b in zip(self.bounds, self.B) if a < hi and c > lo]

    def load_w(self, ps, name, nk, ncols, axis, bounds, key=None, col0=0, direct=None):
        S = self.S
        t = self.sb(ps, name, [128, nk, ncols], BF16)
        W = Prog.WT(t, bounds)
        if direct is not None:
            v = direct.rearrange("(k p) f -> p k f", p=128)
        else:
            v = self.wbf[key].rearrange("(k p) f -> p k f", p=128)
        for (lo, hi), B in zip(bounds, W.B):
            if axis == "col":
                o, i = t[:, :, lo:hi], v[:, :, col0 + lo:col0 + hi]
            else:
                o, i = t[:, lo:hi, :], v[:, lo:hi, col0:col0 + ncols]
            if direct is not None:
                self._w1prev = getattr(self, "_w1prev", [])
                S.dma("pool", o, i, S.dsem("w1", bg=True), reads=self._w1prev[-3:-2], writes=[B],
                      max_dma_last_dim=4096)
                self._w1prev.append(B)
            else:
                S.dma("sp", o, i, S.dsem("wl"), reads=self.Bconv[key], writes=[B])
        return W

    def ffn_weights(self, ps, l, j, direct=False):
        cb = [(0, 768), (768, 1536), (1536, 2176), (2176, 2816)]
        rb = [(0, 6), (6, 12), (12, 17), (17, 22)]
        if direct:
            Wg = self.load_w(ps, "Wg", 8, FF, "col", cb, direct=self.ffn_gate[l, j])
            Wu = self.load_w(ps, "Wu", 8, FF, "col", cb, direct=self.ffn_up[l, j])
            Wd = self.load_w(ps, "Wd", NF, D, "row", rb, direct=self.ffn_down[l, j])
        else:
            Wg = self.load_w(ps, "Wg", 8, FF, "col", cb, key=f"gate{l}{j}")
            Wu = self.load_w(ps, "Wu", 8, FF, "col", cb, key=f"up{l}{j}")
            Wd = self.load_w(ps, "Wd", NF, D, "row", rb, key=f"down{l}{j}")
        return Wg, Wu, Wd

    def load_weight_bf16(self, dst_tile, src_ap, nk, ncols, ds, B):
        S = self.S
        for k in range(nk):
            S.dma("pool", dst_tile[:, k, :], src_ap[k * 128:(k + 1) * 128, :], ds, max_dma_last_dim=4096)

    class LNT:
        pass

    def lnt_alloc(self, ps, src, normw_row, evac_eng="act"):
        S = self.S
        L = Prog.LNT()
        L.src = src
        L.xld = [self.sb(ps, f"xld{i}", [128, D], F32) for i in range(2)]
        L.Bxld = [Buf() for _ in range(2)]
        L.dxld = [S.dsem("xld") for _ in range(2)]
        L.xn = [self.sb(ps, f"xn{i}", [128, D], BF16) for i in range(2)]
        L.Bxn = [Buf() for _ in range(2)]
        L.xnT = [self.sb(ps, f"xnT{i}", [128, 8, 512], BF16) for i in range(2)]
        L.BxnT = [[Buf() for _ in range(4)] for _ in range(2)]
        L.wb = self.sb(ps, "wb", [128, D], F32)
        L.Bwb = Buf()
        L.dwb = S.dsem("wb")
        S.dma("sp", L.wb[:], normw_row.partition_broadcast(128), L.dwb, writes=[L.Bwb])
        L.cnt = 0
        L.slot = {}
        L.evac_eng = evac_eng
        return L

    def lnt_dma(self, L, t, s):
        S = self.S
        i = t * 4 + s
        k = L.cnt % 2
        L.cnt += 1
        L.slot[(t, s)] = k
        S.dma("sp", L.xld[k][:], L.src.ap[i * 128:(i + 1) * 128, :], L.dxld[k],
              reads=[L.src.bufs[i]], writes=[L.Bxld[k]])

    def lnt_sub_a(self, L, t, s):
        S = self.S
        i = t * 4 + s
        k = L.slot[(t, s)]
        S.op("dve", I("scalar_tensor_tensor", out=L.xn[k][:], in0=L.xld[k][:], scalar=self.rstd[:, i:i + 1],
                      in1=L.wb[:], op0=ALU.mult, op1=ALU.mult),
             reads=[L.Bxld[k], self.B_rstd, L.Bwb], writes=[L.Bxn[k]])

    def lnt_sub_b(self, L, t, s):
        k = L.slot.pop((t, s))
        self.lnt_transpose_one(L, t, s, k)

    def lnt_sub(self, L, t, s):
        self.lnt_sub_a(L, t, s)
        self.lnt_sub_b(L, t, s)

    def lnt_micro(self, L, t):
        d, a, b = self.lnt_dma, self.lnt_sub_a, self.lnt_sub_b
        return [lambda: (d(L, t, 0), d(L, t, 1)),
                lambda: (a(L, t, 0), d(L, t, 2)),
                lambda: (a(L, t, 1), d(L, t, 3)),
                lambda: b(L, t, 0),
                lambda: a(L, t, 2),
                lambda: b(L, t, 1),
                lambda: a(L, t, 3),
                lambda: b(L, t, 2),
                lambda: b(L, t, 3)]

    def lnt_steps(self, L, t):
        return [lambda: (self.lnt_dma(L, t, 0), self.lnt_dma(L, t, 1)),
                lambda: (self.lnt_sub(L, t, 0), self.lnt_dma(L, t, 2)),
                lambda: (self.lnt_sub(L, t, 1), self.lnt_dma(L, t, 3)),
                lambda: self.lnt_sub(L, t, 2),
                lambda: self.lnt_sub(L, t, 3)]

    def lnt_load(self, L, t):
        for f in self.lnt_steps(L, t):
            f()

    def lnt_transpose_one(self, L, t, s, k):
        S = self.S
        tb = t % 2
        bk = 7
        pt = self.bank[bk][:].bitcast(BF16)
        fns = [(I("transpose", out=pt[:, c * 128:(c + 1) * 128], in_=L.xn[k][:, c * 128:(c + 1) * 128],
                                           identity=self.ident[:])) for c in range(8)]
        S.group("pe", fns, reads=[L.Bxn[k], self.B_ident], writes=[self.B_bank[bk]])
        if L.evac_eng == "act":
            S.op("act", I("activation", out=L.xnT[tb][:, :, s * 128:(s + 1) * 128],
                          in_=pt.rearrange("p (c t) -> p c t", c=8), func=AF.Copy),
                 reads=[self.B_bank[bk]], writes=[L.BxnT[tb][s]])
        else:
            S.op("dve", I("tensor_copy", out=L.xnT[tb][:, :, s * 128:(s + 1) * 128],
                          in_=pt.rearrange("p (c t) -> p c t", c=8)),
                 reads=[self.B_bank[bk]], writes=[L.BxnT[tb][s]])

    def phase_ffn(self, src, dst, layer, normi, ffni, ss_in, B_ss_in, ss_out, B_ss_out, W=None, final=False):
        nc, S = self.nc, self.S
        with ExitStack() as ps:
            WG, WU, WD = W if W is not None else self.ffn_weights(ps, layer, ffni)
            Wg, Wu, Wd = WG.tile, WU.tile, WD.tile
            self.compute_rstd(ss_in, B_ss_in, ps)
            L = self.lnt_alloc(ps, src, self.norm_w[layer, normi])
            hT = self.sb(ps, "hT", [128, NF, 512], BF16)
            BhT = [Buf() for _ in range(NF)]
            sg = [self.sb(ps, f"sg{i}", [128, 512], F32) for i in range(2)]
            Bsg = [Buf() for _ in range(2)]
            xr = [self.sb(ps, f"xr{i}", [128, D], F32) for i in range(2)]
            Bxr = [[Buf(), Buf()] for _ in range(2)]
            dxr = [S.dsem("xr") for _ in range(2)]
            junk = self.sb(ps, "fjunk", [128, D], BF16)
            Bj = Buf()
            S.op("dve", I("memset", ss_out[:], 0.0), writes=[B_ss_out])
            Bfs, Bwfb = Buf(), Buf()
            if final:
                fs = self.sb(ps, "ffs", [128, 4], F32)
                wfb = self.sb(ps, "wfb", [128, D], F32)
                S.dma("sp", wfb[:], self.final_norm.partition_broadcast(128), S.dsem("wfb"), writes=[Bwfb])

            self.lnt_load(L, 0)
            self.dbg("xnT0", L.xnT[0][:], L.BxnT[0], BF16)
            self.dbg("rstd", self.rstd[:], [self.B_rstd], F32)
            gu = 0
            dn = 0
            xrc = 0
            for t in range(self.NTL):
                tb = t % 2
                for f in range(NF):
                    bg, bu = (gu % 2) * 2, (gu % 2) * 2 + 1
                    gu += 1
                    fg = [(I("matmul", self.bank[bg][:], lhsT=Wg[:, k, f * 128:(f + 1) * 128],
                                                              rhs=L.xnT[tb][:, k, :], start=(k == 0), stop=(k == 7)))
                          for k in range(8)]
                    S.group("pe", fg, reads=WG.rb(f * 128, f * 128 + 128) + L.BxnT[tb], writes=[self.B_bank[bg]])
                    fu = [(I("matmul", self.bank[bu][:], lhsT=Wu[:, k, f * 128:(f + 1) * 128],
                                                              rhs=L.xnT[tb][:, k, :], start=(k == 0), stop=(k == 7)))
                          for k in range(8)]
                    S.group("pe", fu, reads=WU.rb(f * 128, f * 128 + 128) + L.BxnT[tb], writes=[self.B_bank[bu]])
                    sk = f % 2
                    if self.debug and t == 0 and f == 0:
                        self.dbgt = self.sb(ps, "dbgt", [128, 512], F32)
                        Bd = Buf()
                        S.op("dve", I("tensor_copy", out=self.dbgt[:], in_=self.bank[bg][:]),
                             reads=[self.B_bank[bg]], writes=[Bd])
                        self.dbg("g0", self.dbgt[:], [Bd], F32)
                    S.op("act", I("activation", out=sg[sk][:], in_=self.bank[bg][:], func=AF.Silu),
                         reads=[self.B_bank[bg]], writes=[Bsg[sk]])
                    S.op("dve", I("tensor_tensor", out=hT[:, f, :], in0=sg[sk][:],
                                                                            in1=self.bank[bu][:], op=ALU.mult),
                         reads=[Bsg[sk], self.B_bank[bu]], writes=[BhT[f]])
                    if t == 0 and f in (0, 21):
                        self.dbg(f"sg{f}", sg[sk][:], [Bsg[sk]], F32)
                    if 2 <= f <= 10 and t + 1 < self.NTL:
                        if f == 2:
                            micro = self.lnt_micro(L, t + 1)
                        micro[f - 2]()
                if t == 0:
                    self.dbg("hT0", hT[:], BhT, BF16)
                    self.dbg("xnT0b", L.xnT[0][:], L.BxnT[0], BF16)
                for s in range(4):
                    i = t * 4 + s
                    xk = xrc % 2
                    xrc += 1
                    S.dma("sp", xr[xk][:], src.ap[i * 128:(i + 1) * 128, :], dxr[xk], reads=[src.bufs[i]],
                          writes=Bxr[xk])
                    for nh in range(2):
                        bo = 4 + dn % 3
                        dn += 1
                        fd = [(I("matmul",
                            self.bank[bo][:], lhsT=hT[:, f, s * 128:(s + 1) * 128],
                            rhs=Wd[:, f, nh * 512:(nh + 1) * 512], start=(f == 0), stop=(f == NF - 1)))
                            for f in range(NF)]
                        S.group("pe", fd, reads=WD.B + BhT, writes=[self.B_bank[bo]])
                        S.op("dve", I("scalar_tensor_tensor",
                            out=xr[xk][:, nh * 512:(nh + 1) * 512], in0=self.bank[bo][:], scalar=0.5,
                            in1=xr[xk][:, nh * 512:(nh + 1) * 512], op0=ALU.mult, op1=ALU.add),
                            reads=[self.B_bank[bo]], writes=[Bxr[xk][nh]])
                    S.op("act", I("activation", out=junk[:], in_=xr[xk][:], func=AF.Square,
                                                                  accum_out=ss_out[:, i:i + 1]),
                         reads=Bxr[xk] + [B_ss_out], writes=[Bj, Bfs])
                    if final:
                        S.op("act", I("activation", out=fs[:, xk:xk + 1], in_=ss_out[:, i:i + 1], func=AF.Sqrt,
                                      bias=EPS, scale=1.0 / D), reads=[Bfs], writes=[Bfs])
                        S.op("dve", I("reciprocal", out=fs[:, 2 + xk:3 + xk], in_=fs[:, xk:xk + 1]),
                             reads=[Bfs], writes=[Bfs])
                        S.op("dve", I("scalar_tensor_tensor", out=xr[xk][:], in0=xr[xk][:],
                                      scalar=fs[:, 2 + xk:3 + xk], in1=wfb[:], op0=ALU.mult, op1=ALU.mult),
                             reads=[Bfs, Bwfb], writes=Bxr[xk])
                    S.dma("sp", dst.ap[i * 128:(i + 1) * 128, :], xr[xk][:], dxr[xk], reads=Bxr[xk],
                          writes=[dst.bufs[i]])
            B_ss_out.w = Tok(S.eng["act"].sem, S.eng["act"].cnt)
            S.flush()


    def resid_alloc(self, ps, src, dst, ss_out, B_ss_out):
        S = self.S
        R = Prog.LNT()
        R.src, R.dst, R.ss_out, R.B_ss_out = src, dst, ss_out, B_ss_out
        R.xr = [self.sb(ps, f"xr{i}", [128, D], F32) for i in range(2)]
        R.Bxr = [[Buf(), Buf()] for _ in range(2)]
        R.dxr = [S.dsem("xr") for _ in range(2)]
        R.junk = self.sb(ps, "rjunk", [128, D], BF16)
        R.Bjunk = Buf()
        R.cnt = 0
        S.op("dve", I("memset", ss_out[:], 0.0), writes=[B_ss_out])
        return R

    def resid_begin(self, R, i):
        S = self.S
        xk = R.cnt % 2
        R.cnt += 1
        S.dma("sp", R.xr[xk][:], R.src.ap[i * 128:(i + 1) * 128, :], R.dxr[xk], reads=[R.src.bufs[i]],
              writes=R.Bxr[xk])
        return xk

    def resid_add(self, R, xk, nh, bo, scale):
        S = self.S
        S.op("dve", I("scalar_tensor_tensor", out=R.xr[xk][:, nh * 512:(nh + 1) * 512], in0=self.bank[bo][:],
                      scalar=scale, in1=R.xr[xk][:, nh * 512:(nh + 1) * 512], op0=ALU.mult, op1=ALU.add),
             reads=[self.B_bank[bo]], writes=[R.Bxr[xk][nh]])

    def resid_end(self, R, xk, i):
        S = self.S
        S.op("act", I("activation", out=R.junk[:], in_=R.xr[xk][:], func=AF.Square,
                      accum_out=R.ss_out[:, i:i + 1]), reads=R.Bxr[xk] + [R.B_ss_out], writes=[R.Bjunk])
        S.dma("sp", R.dst.ap[i * 128:(i + 1) * 128, :], R.xr[xk][:], R.dxr[xk], reads=R.Bxr[xk],
              writes=[R.dst.bufs[i]])

    def resid_finish(self, R):
        R.B_ss_out.w = Tok(self.S.eng["act"].sem, self.S.eng["act"].cnt)

    def load_cols(self, ps, name, vec_ap, ncol):
        S = self.S
        t = self.sb(ps, name, [128, ncol], F32)
        B = Buf()
        S.dma("sp", t[:], vec_ap.rearrange("(c p) -> p c", p=128), S.dsem(name), writes=[B],
              allow_slow_non_contiguous=True)
        return t, B

    def phase_sgu_a(self, src, ss_in, B_ss_in, sT):
        S = self.S
        with ExitStack() as ps:
            WV = self.load_w(ps, "Wv", 8, SG, "col", [(n * 512, (n + 1) * 512) for n in range(6)], key="sgu_in",
                             col0=SG)
            Wv = WV.tile
            self.compute_rstd(ss_in, B_ss_in, ps)
            L = self.lnt_alloc(ps, src, self.norm_w[0, 1])
            lng, Blng = self.load_cols(ps, "lng", self.sgu_ln_g, NE)
            lnb, Blnb = self.load_cols(ps, "lnb", self.sgu_ln_b, NE)
            wsf = self.sb(ps, "wsf", [128, 8, 128], F32)
            wsb = self.sb(ps, "wsb", [128, 8, 128], BF16)
            wsT = self.sb(ps, "wsT", [128, 8, 128], BF16)
            ones = self.sb(ps, "ones", [128, 128], BF16)
            bsb = self.sb(ps, "bsb", [128, 8, 128], F32)
            C = self.sb(ps, "Cmat", [128, NE, 128], F32)
            Bwsf, Bwsb, BwsT, Bones, Bbsb, BC = Buf(), Buf(), Buf(), Buf(), Buf(), Buf()
            S.dma("sp", wsf[:], self.sgu_w_s.rearrange("g t s -> t g s"), S.dsem("wsf"), writes=[Bwsf])
            S.dma("sp", bsb[:].rearrange("p g t -> p (g t)"),
                  self.sgu_b_s.rearrange("g t -> (g t)").partition_broadcast(128), S.dsem("bsb"), writes=[Bbsb])
            S.op("dve", I("tensor_copy", out=wsb[:], in_=wsf[:]), reads=[Bwsf], writes=[Bwsb])
            S.op("dve", I("memset", ones[:], 1.0), writes=[Bones])
            pt = self.bank[7][:].bitcast(BF16)
            S.group("pe", [I("transpose", out=pt[:, g * 128:(g + 1) * 128], in_=wsb[:, g, :], identity=self.ident[:])
                           for g in range(8)], reads=[Bwsb, self.B_ident], writes=[self.B_bank[7]])
            S.op("act", I("activation", out=wsT[:].rearrange("p g t -> p (g t)"), in_=pt, func=AF.Copy),
                 reads=[self.B_bank[7]], writes=[BwsT])
            for half in range(2):
                S.group("pe", [I("matmul", self.bank[half][:, j * 128:(j + 1) * 128], lhsT=ones[:],
                                 rhs=wsT[:, half * 4 + j, :], start=True, stop=True) for j in range(4)],
                        reads=[Bones, BwsT], writes=[self.B_bank[half]])
            for ec in range(NE):
                g = ec // 3
                S.op("dve", I("scalar_tensor_tensor", out=C[:, ec, :],
                              in0=self.bank[g // 4][:, (g % 4) * 128:(g % 4 + 1) * 128], scalar=lnb[:, ec:ec + 1],
                              in1=bsb[:, g, :], op0=ALU.mult, op1=ALU.add),
                     reads=[self.B_bank[g // 4], Blnb, Bbsb], writes=[BC])
            vf = [[self.sb(ps, f"vf{q}_{i}", [128, SG], BF16) for i in range(4)] for q in range(2)]
            Bvf = [[Buf() for _ in range(4)] for _ in range(2)]
            vh = [self.sb(ps, f"vh{i}", [128, SG], BF16) for i in range(4)]
            Bvh = [Buf() for _ in range(4)]
            sTt = self.sb(ps, "sTt", [128, NE, 512], BF16)
            BsTt = [Buf() for _ in range(NE)]
            dsT = S.dsem("sTst")
            sjunk = self.sb(ps, "sjunk", [128, 512], BF16)
            Bsjunk = Buf()
            s1 = self.sb(ps, "s1", [128, 24], F32)
            s2 = self.sb(ps, "s2", [128, 24], F32)
            st = [self.sb(ps, f"stt{i}", [128, 4], F32) for i in range(7)]
            Bs1, Bs2, Bst = Buf(), Buf(), Buf()
            sT_v = sT.ap.rearrange("(c p) t -> p c t", p=128)
            def spatial(ec):
                g = ec // 3
                b = ec % 2
                S.group("pe", [I("matmul", self.bank[b][:, c * 128:(c + 1) * 128],
                                 lhsT=vh[c][:, ec * 128:(ec + 1) * 128], rhs=wsT[:, g, :], start=True, stop=True)
                               for c in range(4)], reads=Bvh + [BwsT], writes=[self.B_bank[b]])
                S.op("dve", I("scalar_tensor_tensor", out=sTt[:, ec, :].rearrange("p (c t) -> p c t", c=4),
                              in0=self.bank[b][:].rearrange("p (c t) -> p c t", c=4), scalar=lng[:, ec:ec + 1],
                              in1=C[:, ec, :].unsqueeze(1).broadcast_to([128, 4, 128]), op0=ALU.mult, op1=ALU.add),
                     reads=[self.B_bank[b], Blng, BC], writes=[BsTt[ec]])

            self.lnt_load(L, 0)
            bk = 0
            prev_t = None
            for t in range(self.NTL):
                tb = t % 2
                S.op("dve", I("memset", s1[:], 0.0), writes=[Bs1])
                S.op("dve", I("memset", s2[:], 0.0), writes=[Bs2])
                for c in range(4):
                    for n in range(6):
                        gi = c * 6 + n
                        if prev_t is not None and 6 <= gi < 18:
                            for ec in (2 * (gi - 6), 2 * (gi - 6) + 1):
                                spatial(ec)
                            if gi == 17:
                                S.dma("sp", sT_v[:, :, prev_t * 512:(prev_t + 1) * 512], sTt[:], dsT, reads=BsTt,
                                      writes=[sT.bufs[prev_t]])
                        b = 2 + bk % 4
                        bk += 1
                        S.group("pe", [I("matmul", self.bank[b][:], lhsT=L.xnT[tb][:, k, c * 128:(c + 1) * 128],
                                         rhs=Wv[:, k, n * 512:(n + 1) * 512], start=(k == 0), stop=(k == 7))
                                       for k in range(8)], reads=WV.rb(n * 512, (n + 1) * 512) + L.BxnT[tb],
                                writes=[self.B_bank[b]])
                        S.op("act", I("activation", out=vf[tb][c][:, n * 512:(n + 1) * 512], in_=self.bank[b][:],
                                      func=AF.Gelu, accum_out=s1[:, gi:gi + 1]),
                             reads=[self.B_bank[b], Bs1], writes=[Bvf[tb][c]])
                        S.op("act", I("activation", out=sjunk[:], in_=vf[tb][c][:, n * 512:(n + 1) * 512],
                                      func=AF.Square, accum_out=s2[:, gi:gi + 1]),
                             reads=[Bvf[tb][c], Bs2], writes=[Bsjunk])
                        if 2 <= gi <= 10 and t + 1 < self.NTL:
                            if gi == 2:
                                micro = self.lnt_micro(L, t + 1)
                            micro[gi - 2]()
                Bs1.w = Bs2.w = Tok(S.eng["act"].sem, S.eng["act"].cnt)
                msum, mean, msq, var, rs, nmr, s2s = st
                S.op("dve", I("tensor_reduce", out=s2s[:], in_=s2[:].rearrange("p (c n) -> p c n", n=6),
                              axis=mybir.AxisListType.X, op=ALU.add), reads=[Bs2], writes=[Bst])
                S.op("dve", I("tensor_reduce", out=msum[:], in_=s1[:].rearrange("p (c n) -> p c n", n=6),
                              axis=mybir.AxisListType.X, op=ALU.add), reads=[Bs1], writes=[Bst])
                S.op("dve", I("tensor_scalar", out=mean[:], in0=msum[:], scalar1=1.0 / SG, scalar2=None, op0=ALU.mult),
                     reads=[Bst], writes=[Bst])
                S.op("dve", I("tensor_tensor", out=msq[:], in0=mean[:], in1=mean[:], op=ALU.mult),
                     reads=[Bst], writes=[Bst])
                S.op("dve", I("scalar_tensor_tensor", out=var[:], in0=s2s[:], scalar=1.0 / SG, in1=msq[:],
                              op0=ALU.mult, op1=ALU.subtract), reads=[Bst, Bs2], writes=[Bst])
                S.op("act", I("activation", out=var[:], in_=var[:], func=AF.Sqrt, bias=EPS, scale=1.0),
                     reads=[Bst], writes=[Bst])
                S.op("dve", I("reciprocal", out=rs[:], in_=var[:]), reads=[Bst], writes=[Bst])
                S.op("dve", I("scalar_tensor_tensor", out=nmr[:], in0=mean[:], scalar=-1.0, in1=rs[:],
                              op0=ALU.mult, op1=ALU.mult), reads=[Bst], writes=[Bst])
                for c in range(4):
                    S.op("dve", I("tensor_scalar", out=vh[c][:], in0=vf[tb][c][:], scalar1=rs[:, c:c + 1],
                                  scalar2=nmr[:, c:c + 1], op0=ALU.mult, op1=ALU.add),
                         reads=[Bvf[tb][c], Bst], writes=[Bvh[c]])
                prev_t = t
            for ec in range(NE):
                spatial(ec)
            S.dma("sp", sT_v[:, :, prev_t * 512:(prev_t + 1) * 512], sTt[:], dsT, reads=BsTt,
                  writes=[sT.bufs[prev_t]])
            S.flush()

    def phase_sgu_b(self, src, dst, ss_in, B_ss_in, ss_out, B_ss_out, sT):
        S = self.S
        with ExitStack() as ps:
            WU = self.load_w(ps, "Wuin", 8, SG, "col", [(n * 768, (n + 1) * 768) for n in range(4)], key="sgu_in")
            WO = self.load_w(ps, "Wo", NE, D, "row", [(n * 6, (n + 1) * 6) for n in range(4)], key="sgu_out")
            Wu, Wo = WU.tile, WO.tile
            self.compute_rstd(ss_in, B_ss_in, ps)
            L = self.lnt_alloc(ps, src, self.norm_w[0, 1])
            R = self.resid_alloc(ps, src, dst, ss_out, B_ss_out)
            sTt = [self.sb(ps, f"sTb{i}", [128, NE, 512], BF16) for i in range(2)]
            BsTt = [[Buf() for _ in range(NE)] for _ in range(2)]
            dsT = [S.dsem("sTld") for _ in range(2)]
            ug = [self.sb(ps, f"ug{i}", [128, 512], BF16) for i in range(2)]
            Bug = [Buf(), Buf()]
            sT_v = sT.ap.rearrange("(c p) t -> p c t", p=128)
            self.lnt_load(L, 0)
            S.dma("sp", sTt[0][:], sT_v[:, :, 0:512], dsT[0], reads=[sT.bufs[0]], writes=BsTt[0])
            gu = 0
            dn = 0
            for t in range(self.NTL):
                tb = t % 2
                for ec in range(NE):
                    b = gu % 4
                    gu += 1
                    S.group("pe", [I("matmul", self.bank[b][:], lhsT=Wu[:, k, ec * 128:(ec + 1) * 128],
                                     rhs=L.xnT[tb][:, k, :], start=(k == 0), stop=(k == 7)) for k in range(8)],
                            reads=WU.rb(ec * 128, ec * 128 + 128) + L.BxnT[tb], writes=[self.B_bank[b]])
                    uk = ec % 2
                    S.op("act", I("activation", out=ug[uk][:], in_=self.bank[b][:], func=AF.Gelu),
                         reads=[self.B_bank[b]], writes=[Bug[uk]])
                    S.op("pool", I("tensor_tensor", out=sTt[tb][:, ec, :], in0=ug[uk][:], in1=sTt[tb][:, ec, :],
                                   op=ALU.mult), reads=[Bug[uk]], writes=[BsTt[tb][ec]])
                    if 2 <= ec <= 10 and t + 1 < self.NTL:
                        if ec == 2:
                            micro = self.lnt_micro(L, t + 1)
                        micro[ec - 2]()
                    if ec == 10 and t + 1 < self.NTL:
                        S.dma("sp", sTt[1 - tb][:], sT_v[:, :, (t + 1) * 512:(t + 2) * 512], dsT[1 - tb],
                              reads=[sT.bufs[t + 1]], writes=BsTt[1 - tb])
                for s in range(4):
                    i = t * 4 + s
                    xk = self.resid_begin(R, i)
                    for nh in range(2):
                        bo = 4 + dn % 3
                        dn += 1
                        S.group("pe", [I("matmul", self.bank[bo][:], lhsT=sTt[tb][:, ec, s * 128:(s + 1) * 128],
                                         rhs=Wo[:, ec, nh * 512:(nh + 1) * 512], start=(ec == 0), stop=(ec == NE - 1))
                                       for ec in range(NE)], reads=WO.B + BsTt[tb], writes=[self.B_bank[bo]])
                        self.resid_add(R, xk, nh, bo, 1.0)
                    self.resid_end(R, xk, i)
            self.resid_finish(R)
            S.flush()


    def hgrn_consts(self, es):
        S, NB = self.S, self.NB
        self.Dd = [self.sb(es, f"Dd{d}", [128, NH, NB], F32) for d in range(2)]
        self.B_Dd = [Buf(), Buf()]
        self.carry_t = self.sb(es, "carry_t", [128, 2 * NB], F32)
        self.B_carry = Buf()
        S.dma("sp", self.carry_t[:], self.carry, S.dsem("carry"), writes=[self.B_carry])
        self.mask = [self.sb(es, f"mask{d}", [128, 128], F32) for d in range(2)]
        self.B_mask = Buf()
        for d in range(2):
            S.op("pool", I("memset", self.mask[d][:], 1.0), writes=[self.B_mask])
        S.op("pool", I("affine_select", out=self.mask[0][:], in_=self.mask[0][:], pattern=[[1, 128]],
                       compare_op=ALU.is_ge, fill=0.0, base=0, channel_multiplier=-1), writes=[self.B_mask])
        S.op("pool", I("affine_select", out=self.mask[1][:], in_=self.mask[1][:], pattern=[[-1, 128]],
                       compare_op=ALU.is_ge, fill=0.0, base=0, channel_multiplier=1), writes=[self.B_mask])
        self.mreset = self.sb(es, "mreset", [128, 512], F32)
        self.B_mreset = Buf()
        S.op("pool", I("memset", self.mreset[:], 1.0), writes=[self.B_mreset])
        for c in range(4):
            S.op("pool", I("memset", self.mreset[:, c * 128:c * 128 + 1], 0.0), writes=[self.B_mreset])
        self.onesb = self.sb(es, "onesb", [128, 128], BF16)
        self.B_onesb = Buf()
        S.op("pool", I("memset", self.onesb[:], 1.0), writes=[self.B_onesb])

    def phase_hgrn_a(self, src, ss_in, B_ss_in, H):
        S = self.S
        with ExitStack() as ps:
            WIN = self.load_w(ps, "Whin", 8, 5 * D, "col", [(c * D, (c + 1) * D) for c in (0, 4, 3, 1, 2)],
                              key="hg_in")
            Win = WIN.tile
            self.compute_rstd(ss_in, B_ss_in, ps)
            L = self.lnt_alloc(ps, src, self.norm_w[1, 1], evac_eng="dve")
            noml, lnoml, lbc, Blb = [], [], [], Buf()
            for d in range(2):
                r0, B0 = self.load_cols(ps, f"r0{d}", self.hgrn_lb_raw[d, 0], 8)
                r1, B1 = self.load_cols(ps, f"r1{d}", self.hgrn_lb_raw[d, 1], 8)
                tmp = self.sb(ps, f"lbt{d}", [128, 8], F32)
                nm = self.sb(ps, f"noml{d}", [128, 8], F32)
                lo = self.sb(ps, f"lnoml{d}", [128, 8], F32)
                S.op("dve", I("tensor_tensor", out=tmp[:], in0=r0[:], in1=r1[:], op=ALU.subtract),
                     reads=[B0, B1], writes=[Blb])
                S.op("act", I("activation", out=tmp[:], in_=tmp[:], func=AF.Exp), reads=[Blb], writes=[Blb])
                S.op("dve", I("tensor_scalar", out=tmp[:], in0=tmp[:], scalar1=1.0, scalar2=None, op0=ALU.add),
                     reads=[Blb], writes=[Blb])
                S.op("dve", I("reciprocal", out=tmp[:], in_=tmp[:]), reads=[Blb], writes=[Blb])
                lbc_ = self.sb(ps, f"lbc{d}", [128, 8], F32)
                S.op("dve", I("tensor_copy", out=lbc_[:], in_=tmp[:]), reads=[Blb], writes=[Blb])
                lbc.append(lbc_)
                S.op("dve", I("tensor_scalar", out=nm[:], in0=tmp[:], scalar1=1.0, scalar2=None, op0=ALU.subtract),
                     reads=[Blb], writes=[Blb])
                S.op("dve", I("tensor_scalar", out=tmp[:], in0=nm[:], scalar1=-1.0, scalar2=None, op0=ALU.mult),
                     reads=[Blb], writes=[Blb])
                S.op("act", I("activation", out=lo[:], in_=tmp[:], func=AF.Ln), reads=[Blb], writes=[Blb])
                noml.append(nm)
                lnoml.append(lo)
            qs = self.sb(ps, "qs", [128, NH, 512], F32)
            Bqs = [Buf() for _ in range(NH)]
            gso = [self.sb(ps, f"gso{i}", [128, 512], BF16) for i in range(2)]
            Bgso = [Buf(), Buf()]
            dgso = [S.dsem("gso") for _ in range(2)]
            vtok = self.sb(ps, "vtok", [128, 4, D], BF16)
            Bvtok = [Buf() for _ in range(4)]
            dvtok = S.dsem("vtok")
            kstok = [self.sb(ps, f"kstok{d}", [128, 4, D], BF16) for d in range(2)]
            Bkstok = [[Buf() for _ in range(NH)] for _ in range(2)]
            dkstok = [S.dsem("kstok") for _ in range(2)]
            NR = 3
            qeo = [self.sb(ps, f"qeo{i}", [128, 512], BF16) for i in range(NR)]
            kdo = [self.sb(ps, f"kdo{i}", [128, 512], BF16) for i in range(NR)]
            Bqeo = [Buf() for _ in range(NR)]
            Bkdo = [Buf() for _ in range(NR)]
            dqeo = [S.dsem("qeo") for _ in range(NR)]
            dkdo = [S.dsem("kdo") for _ in range(NR)]
            sl = [self.sb(ps, f"sl{i}", [128, 512], F32) for i in range(2)]
            Bsl = [Buf(), Buf()]
            slc = 0
            NTMP = 3
            TN = ("te", "tA", "tB", "tP", "tX")
            T_ = [{n: self.sb(ps, f"{n}{i}", [128, 512], F32) for n in TN} for i in range(NTMP)]
            BT = [{n: Buf() for n in TN + ("tks",)} for i in range(NTMP)]
            tks = [self.sb(ps, f"tks{i}", [128, 512], BF16) for i in range(NTMP)]
            self.lnt_load(L, 0)
            bk = 0
            ro = 0
            hd = 0
            for t in range(self.NTL):
                tb = t % 2
                tok = slice(t * 512, (t + 1) * 512)
                def sgroup_pe(which, h):
                    nonlocal bk
                    col = (0 if which == 0 else 4 * D) + h * 128
                    b = bk % 6
                    bk += 1
                    S.group("pe", [I("matmul", self.bank[b][:], lhsT=Win[:, k, col:col + 128],
                                     rhs=L.xnT[tb][:, k, :], start=(k == 0), stop=(k == 7)) for k in range(8)],
                            reads=WIN.rb(col, col + 128) + L.BxnT[tb], writes=[self.B_bank[b]])
                    return b

                def sgroup_act(which, h, b):
                    nonlocal slc
                    k_ = slc % 2
                    slc += 1
                    S.op("act", I("activation", out=sl[k_][:], in_=self.bank[b][:], func=AF.Exp, scale=-1.0),
                         reads=[self.B_bank[b]], writes=[Bsl[k_]])
                    S.op("act", I("activation", out=sl[k_][:], in_=sl[k_][:], func=AF.Ln, bias=1.0, scale=1.0),
                         reads=[Bsl[k_]], writes=[Bsl[k_]])
                    S.op("act", I("activation", out=sl[k_][:], in_=sl[k_][:], func=AF.Exp, scale=-1.0),
                         reads=[Bsl[k_]], writes=[Bsl[k_]])
                    if which == 0:
                        S.op("dve", I("tensor_tensor", out=qs[:, h, :], in0=self.bank[b][:], in1=sl[k_][:], op=ALU.mult),
                             reads=[self.B_bank[b], Bsl[k_]], writes=[Bqs[h]])
                    else:
                        gk = h % 2
                        S.op("dve", I("tensor_tensor", out=gso[gk][:], in0=self.bank[b][:], in1=sl[k_][:], op=ALU.mult),
                             reads=[self.B_bank[b], Bsl[k_]], writes=[Bgso[gk]])
                        S.dma("sp", H["gs"].ap[t, :, h * 512:(h + 1) * 512], gso[gk][:], dgso[gk],
                              reads=[Bgso[gk]], writes=[H["gs"].bufs[t][h]])

                def vgroup(c, n):
                    nonlocal bk
                    b = bk % 6
                    bk += 1
                    S.group("pe", [I("matmul", self.bank[b][:], lhsT=L.xnT[tb][:, k, c * 128:(c + 1) * 128],
                                     rhs=Win[:, k, 3 * D + n * 512:3 * D + (n + 1) * 512], start=(k == 0),
                                     stop=(k == 7)) for k in range(8)],
                            reads=WIN.rb(3 * D, 4 * D) + L.BxnT[tb], writes=[self.B_bank[b]])
                    S.op("dve", I("tensor_copy", out=vtok[:, c, n * 512:(n + 1) * 512], in_=self.bank[b][:]),
                         reads=[self.B_bank[b]], writes=[Bvtok[c]])
                    if c == 3 and n == 1:
                        S.dma("sp", H["v"].ap[t], vtok[:].rearrange("p c f -> p (c f)"), dvtok, reads=Bvtok,
                              writes=H["v"].bufs[t])

                vlist = [(c, n) for c in range(4) for n in range(2)]
                items = [(d, h) for d in range(2) for h in range(NH)]

                def stage0(d, h):
                    nonlocal bk
                    col = (1 + d) * D + h * 128
                    b = bk % 6
                    bk += 1
                    S.group("pe", [I("matmul", self.bank[b][:], lhsT=Win[:, k, col:col + 128],
                                     rhs=L.xnT[tb][:, k, :], start=(k == 0), stop=(k == 7)) for k in range(8)],
                            reads=WIN.rb(col, col + 128) + L.BxnT[tb], writes=[self.B_bank[b]])
                    return b

                def stage1(d, h, i_, b):
                    T, B_ = T_[i_], BT[i_]
                    S.op("act", I("activation", out=T["te"][:], in_=self.bank[b][:], func=AF.Exp),
                         reads=[self.B_bank[b]], writes=[B_["te"]])
                    S.op("act", I("activation", out=T["tA"][:], in_=T["te"][:], func=AF.Ln, bias=lbc[d][:, h:h + 1],
                                  scale=1.0), reads=[B_["te"], Blb], writes=[B_["tA"]])
                    S.op("act", I("activation", out=T["tB"][:], in_=T["te"][:], func=AF.Ln, bias=1.0, scale=1.0),
                         reads=[B_["te"]], writes=[B_["tB"]])
                    S.op("dve", I("tensor_tensor", out=T["tA"][:], in0=T["tA"][:], in1=T["tB"][:], op=ALU.subtract),
                         reads=[B_["tB"]], writes=[B_["tA"]])
                    S.op("dve", I("tensor_tensor_scan", out=T["tP"][:], data0=self.mreset[:], data1=T["tA"][:],
                                  initial=0.0, op0=ALU.mult, op1=ALU.add),
                         reads=[B_["tA"], self.B_mreset], writes=[B_["tP"]])
                    v4 = lambda ap: ap.rearrange("p (c t) -> p c t", c=4)
                    Tb = v4(T["tP"][:])[:, :, 127:128].broadcast_to([128, 4, 128])
                    if d == 0:
                        S.op("pool", I("tensor_tensor", out=T["te"][:], in0=T["tB"][:], in1=T["tP"][:], op=ALU.add),
                             reads=[B_["tB"], B_["tP"]], writes=[B_["te"]])
                        S.op("pool", I("tensor_tensor", out=v4(T["tX"][:]), in0=Tb, in1=v4(T["te"][:]),
                                       op=ALU.subtract), reads=[B_["tP"], B_["te"]], writes=[B_["tX"]])
                    else:
                        S.op("dve", I("tensor_tensor", out=T["tA"][:], in0=T["tP"][:], in1=T["tA"][:],
                                      op=ALU.subtract), reads=[B_["tP"]], writes=[B_["tA"]])
                        S.op("pool", I("tensor_tensor", out=T["tB"][:], in0=T["tA"][:], in1=T["tB"][:],
                                       op=ALU.subtract), reads=[B_["tA"]], writes=[B_["tB"]])
                        S.op("pool", I("tensor_tensor", out=v4(T["tA"][:]), in0=Tb, in1=v4(T["tA"][:]),
                                       op=ALU.subtract), reads=[B_["tP"]], writes=[B_["tA"]])
                        S.op("pool", I("tensor_tensor", out=v4(T["te"][:]), in0=Tb, in1=v4(T["tB"][:]),
                                       op=ALU.subtract), reads=[B_["tP"], B_["tB"]], writes=[B_["te"]])

                def stage2(d, h, i_):
                    nonlocal ro
                    T, B_ = T_[i_], BT[i_]
                    oml_b = lnoml[d][:, h:h + 1]
                    r_ = ro % NR
                    ro += 1
                    S.op("act", I("activation", out=self.Dd[d][:, h, t * 4:(t + 1) * 4],
                                  in_=T["tP"][:, 127:512:128], func=AF.Exp), reads=[B_["tP"]])
                    qarg, qB = (T["tP"], B_["tP"]) if d == 0 else (T["tA"], B_["tA"])
                    ksarg, ksB = (T["tX"], B_["tX"]) if d == 0 else (T["tB"], B_["tB"])
                    S.op("act", I("activation", out=qarg[:], in_=qarg[:], func=AF.Exp), reads=[qB], writes=[qB])
                    S.op("act", I("activation", out=kdo[r_][:], in_=T["te"][:], func=AF.Exp, scale=-1.0, bias=oml_b),
                         reads=[B_["te"], Blb], writes=[Bkdo[r_]])
                    S.op("act", I("activation", out=tks[i_][:], in_=ksarg[:], func=AF.Exp, bias=oml_b),
                         reads=[ksB, Blb], writes=[B_["tks"]])
                    S.op("dve", I("tensor_tensor", out=qeo[r_][:], in0=qs[:, h, :], in1=qarg[:], op=ALU.mult),
                         reads=[Bqs[h], qB], writes=[Bqeo[r_]])
                    S.dma("sp", H["qe"][d].ap[t, :, h * 512:(h + 1) * 512], qeo[r_][:], dqeo[r_],
                          reads=[Bqeo[r_]], writes=[H["qe"][d].bufs[t][h]])
                    S.dma("sp", H["kd"][d].ap[t, :, h * 512:(h + 1) * 512], kdo[r_][:], dkdo[r_],
                          reads=[Bkdo[r_]], writes=[H["kd"][d].bufs[t][h]])
                    pt = self.bank[6 + (h % 2)][:].bitcast(BF16)
                    S.group("pe", [I("transpose", out=pt[:, c * 128:(c + 1) * 128],
                                     in_=tks[i_][:, c * 128:(c + 1) * 128], identity=self.ident[:])
                                   for c in range(4)], reads=[B_["tks"], self.B_ident],
                            writes=[self.B_bank[6 + (h % 2)]])
                    S.op("dve", I("tensor_copy", out=kstok[d][:, :, h * 128:(h + 1) * 128],
                                  in_=pt[:, 0:512].rearrange("p (c k) -> p c k", c=4)),
                         reads=[self.B_bank[6 + (h % 2)]], writes=[Bkstok[d][h]])
                    if h == NH - 1:
                        S.dma("sp", H["ks"][d].ap[t], kstok[d][:].rearrange("p c f -> p (c f)"), dkstok[d], reads=Bkstok[d],
                              writes=H["ks"][d].bufs[t])

                AH = 3
                extras = [(0, h_) for h_ in range(3, 8)] + [(1, h_) for h_ in range(8)]
                for h0 in range(3):
                    sgroup_act(0, h0, sgroup_pe(0, h0))
                banks_ = [stage0(*items[q]) for q in range(AH)]
                xb_ = sgroup_pe(*extras[0])
                stage1(*items[0], hd % NTMP, banks_[0])
                stage1(*items[1], (hd + 1) % NTMP, banks_[1])
                for j, (d, h) in enumerate(items):
                    if j + AH < len(items):
                        banks_.append(stage0(*items[j + AH]))
                    if j + 2 < len(items):
                        stage1(*items[j + 2], (hd + 2) % NTMP, banks_[j + 2])
                    nxb_ = sgroup_pe(*extras[j + 1]) if j + 1 < len(extras) else None
                    stage2(d, h, hd % NTMP)
                    hd += 1
                    if j < len(extras):
                        sgroup_act(*extras[j], xb_)
                    xb_ = nxb_
                    if j % 2 == 1:
                        vgroup(*vlist[j // 2])
                    if 1 <= j <= 9 and t + 1 < self.NTL:
                        if j == 1:
                            micro = self.lnt_micro(L, t + 1)
                        micro[j - 1]()
            self.B_Dd[0].w = self.B_Dd[1].w = Tok(S.eng["act"].sem, S.eng["act"].cnt)
            S.flush()

    def phase_hgrn_scan(self, d, H, src=None, dst=None, ss_out=None, B_ss_out=None):
        S, NB, NTL = self.S, self.NB, self.NTL
        with ExitStack() as ps:
            NO = 2
            NG = 3
            qeT = [self.sb(ps, f"qeT{i}", [128, NH, 512], BF16) for i in range(2)]
            kdT = [self.sb(ps, f"kdT{i}", [128, NH, 512], BF16) for i in range(2)]
            kst = [self.sb(ps, f"kst{i}", [128, 4, D], BF16) for i in range(2)]
            vt = [self.sb(ps, f"vt{i}", [128, 4, D], BF16) for i in range(2)]
            oT = [self.sb(ps, f"oT{i}", [128, NH, 512], F32 if d == 1 else BF16) for i in range(NO)]
            Bld = [[Buf() for _ in range(4)] for _ in range(2)]
            BoT = [[Buf() for _ in range(NH)] for _ in range(NO)]
            dld = [[S.dsem("scld") for _ in range(4)] for _ in range(2)]
            doT = [S.dsem("oT") for _ in range(NO)]
            Am = [self.sb(ps, f"Am{i}", [128, 4, 128], BF16) for i in range(2)]
            BAm = [Buf(), Buf()]
            St = self.sb(ps, "St", [128, NH, 128], F32)
            BSt = Buf()
            Sb = [self.sb(ps, f"Sb{i}", [128, NH, 128], BF16) for i in range(2)]
            BSb = [Buf(), Buf()]
            S.op("pool", I("memset", St[:], 0.0), writes=[BSt])
            S.op("pool", I("memset", Sb[0][:], 0.0), writes=[BSb[0]])
            S.op("pool", I("memset", Sb[1][:], 0.0), writes=[BSb[1]])
            flat = lambda tile_: tile_[:].rearrange("p a b -> p (a b)")
            if d == 1:
                WO = self.load_w(ps, "Who", NH, D, "row", [(0, NH)], key="hg_out")
                Wo = WO.tile
                nw, Bnw = self.load_cols(ps, "hnw", self.hgrn_norm_w, NH)
                for h in range(NH):
                    S.op("dve", I("tensor_scalar", out=Wo[:, h, :], in0=Wo[:, h, :], scalar1=nw[:, h:h + 1],
                                  scalar2=None, op0=ALU.mult), reads=WO.B + [Bnw], writes=WO.B)
                gsT = [self.sb(ps, f"gsT{i}", [128, NH, 512], BF16) for i in range(NG)]
                Bgs = [Buf() for _ in range(NG)]
                dgs = [S.dsem("gsld") for _ in range(NG)]
                ofb = [self.sb(ps, f"ofb{i}", [128, NH, 512], BF16) for i in range(2)]
                Bofb = [Buf(), Buf()]
                dofb = [S.dsem("ofb") for _ in range(2)]
                sq = [self.sb(ps, f"sq{i}", [128, 512], BF16) for i in range(2)]
                Bsq = [Buf(), Buf()]
                rt = [self.sb(ps, f"rt{i}", [128, 512], F32) for i in range(2)]
                Brt = [Buf(), Buf()]
                onT = [self.sb(ps, f"onT{i}", [128, NH, 512], BF16) for i in range(2)]
                BonT = [[Buf() for _ in range(NH)] for _ in range(2)]
                R = self.resid_alloc(ps, src, dst, ss_out, B_ss_out)

            order = list(range(NTL)) if d == 0 else list(range(NTL - 1, -1, -1))

            def issue_loads(j, parts=(0, 1, 2, 3, 4, 5)):
                t = order[j]
                p = j % 2
                po = j % NO
                tok = slice(t * 512, (t + 1) * 512)
                if 0 in parts:
                    S.dma("sp", flat(qeT[p]), H["qe"][d].ap[t], dld[p][0], reads=H["qe"][d].bufs[t], writes=[Bld[p][0]])
                if 1 in parts:
                    S.dma("sp", flat(kdT[p]), H["kd"][d].ap[t], dld[p][1], reads=H["kd"][d].bufs[t], writes=[Bld[p][1]])
                if 2 in parts:
                    S.dma("sp", flat(kst[p]), H["ks"][d].ap[t], dld[p][2], reads=H["ks"][d].bufs[t],
                          writes=[Bld[p][2]])
                if 3 in parts:
                    S.dma("sp", flat(vt[p]), H["v"].ap[t], dld[p][3], reads=H["v"].bufs[t],
                          writes=[Bld[p][3]])
                if d == 1 and 4 in parts:
                    S.dma("sp", flat(ofb[p]), H["of"].ap[t], dofb[p], reads=H["of"].bufs[t], writes=[Bofb[p]])
                if d == 1 and 5 in parts:
                    S.dma("sp", flat(gsT[j % NG]), H["gs"].ap[t], dgs[j % NG], reads=H["gs"].bufs[t],
                          writes=[Bgs[j % NG]])

            class TW:
                pass

            def tile_work(j):
                w = TW()
                w.t = order[j]
                w.po = j % NO
                w.pg = j % NG
                w.pn = j % 2
                w.xk = {}
                return w

            def sqr(w, h):
                k2 = h % 2
                S.op("act", I("activation", out=sq[k2][:], in_=oT[w.po][:, h, :], func=AF.Square),
                     reads=[BoT[w.po][h]], writes=[Bsq[k2]])

            def mmn(w, h):
                k2 = h % 2
                S.group("pe", [I("matmul", self.bank[6][:], lhsT=self.onesb[:], rhs=sq[k2][:], start=True, stop=True)],
                        reads=[Bsq[k2], self.B_onesb], writes=[self.B_bank[6]])
                S.op("act", I("activation", out=rt[k2][:], in_=self.bank[6][:], func=AF.Ln, bias=EPS,
                              scale=1.0 / 128), reads=[self.B_bank[6]], writes=[Brt[k2]])
                S.op("act", I("activation", out=rt[k2][:], in_=rt[k2][:], func=AF.Exp, scale=-0.5),
                     reads=[Brt[k2]], writes=[Brt[k2]])

            def fin(w, h):
                k2 = h % 2
                S.op("pool", I("tensor_tensor", out=rt[k2][:], in0=oT[w.po][:, h, :], in1=rt[k2][:], op=ALU.mult),
                     reads=[BoT[w.po][h], Brt[k2]], writes=[Brt[k2]])
                S.op("pool", I("tensor_tensor", out=onT[w.pn][:, h, :], in0=rt[k2][:], in1=gsT[w.pg][:, h, :],
                               op=ALU.mult), reads=[Brt[k2], Bgs[w.pg]], writes=[BonT[w.pn][h]])

            def outp(w, s_, nh, pe_only=False, dve_only=False, load_only=False):
                i = w.t * 4 + s_
                if load_only:
                    w.xk[s_] = self.resid_begin(R, i)
                    return
                if not dve_only:
                    if nh == 0 and s_ not in w.xk:
                        w.xk[s_] = self.resid_begin(R, i)
                    S.group("pe", [I("matmul", self.bank[7][:], lhsT=onT[w.pn][:, h, s_ * 128:(s_ + 1) * 128],
                                     rhs=Wo[:, h, nh * 512:(nh + 1) * 512], start=(h == 0), stop=(h == NH - 1))
                                   for h in range(NH)], reads=WO.B + BonT[w.pn], writes=[self.B_bank[7]])
                if pe_only:
                    return
                xk = w.xk[s_]
                self.resid_add(R, xk, nh, 7, 1.0)
                if nh == 1:
                    self.resid_end(R, xk, i)

            issue_loads(0)
            sbi = 0
            normW = None
            outW = None
            for j in range(NTL):
                t = order[j]
                p = j % 2
                po = j % NO
                if d == 0 and j + 1 < NTL:
                    issue_loads(j + 1)
                corder = range(4) if d == 0 else range(3, -1, -1)
                for ci, c in enumerate(corder):
                    n = t * 4 + c
                    cs = slice(c * 128, (c + 1) * 128)
                    if normW is not None:
                        sqr(normW, 2 * ci)
                        sqr(normW, 2 * ci + 1)
                    if d == 1 and outW is not None:
                        outp(outW, ci, 0, load_only=True)
                    if d == 1 and j + 1 < NTL and ci < 3:
                        issue_loads(j + 1, parts=((0, 1), (2, 3), (4, 5))[ci])
                    for hb in range(2):
                        S.group("pe", [I("matmul", self.bank[hb][:, q * 128:(q + 1) * 128],
                                         lhsT=kdT[p][:, hb * 4 + q, cs], rhs=qeT[p][:, hb * 4 + q, cs],
                                         start=True, stop=True) for q in range(4)],
                                reads=[Bld[p][0], Bld[p][1]], writes=[self.B_bank[hb]])
                    for hb in range(2):
                        S.group("pe", [I("matmul", self.bank[2 + hb][:, q * 128:(q + 1) * 128],
                                         lhsT=kst[p][:, c, (hb * 4 + q) * 128:(hb * 4 + q + 1) * 128],
                                         rhs=vt[p][:, c, (hb * 4 + q) * 128:(hb * 4 + q + 1) * 128],
                                         start=True, stop=True) for q in range(4)],
                                reads=[Bld[p][2], Bld[p][3]], writes=[self.B_bank[2 + hb]])
                    if normW is not None:
                        mmn(normW, 2 * ci)
                    if outW is not None:
                        outp(outW, ci, 0, pe_only=True)
                    for hb in range(2):
                        S.op("dve", I("tensor_tensor", out=Am[hb][:],
                                      in0=self.bank[hb][:].rearrange("p (q t) -> p q t", q=4),
                                      in1=self.mask[d][:].unsqueeze(1).broadcast_to([128, 4, 128]), op=ALU.mult),
                             reads=[self.B_bank[hb], self.B_mask], writes=[BAm[hb]])
                    sp_ = sbi % 2
                    for hb in range(2):
                        fns = []
                        for q in range(4):
                            h = hb * 4 + q
                            fns.append(I("matmul", self.bank[4 + hb][:, q * 128:(q + 1) * 128],
                                         lhsT=vt[p][:, c, h * 128:(h + 1) * 128], rhs=Am[hb][:, q, :],
                                         start=True, stop=False))
                            fns.append(I("matmul", self.bank[4 + hb][:, q * 128:(q + 1) * 128],
                                         lhsT=Sb[sp_][:, h, :], rhs=qeT[p][:, h, cs], start=False, stop=True))
                        S.group("pe", fns, reads=[Bld[p][3], Bld[p][0], BAm[hb], BSb[sp_]],
                                writes=[self.B_bank[4 + hb]])
                    S.op("dve", I("tensor_tensor", out=St[:], in0=St[:],
                                  in1=self.Dd[d][:, :, n:n + 1].broadcast_to([128, NH, 128]), op=ALU.mult),
                         reads=[self.B_Dd[d]], writes=[BSt])
                    for hb in range(2):
                        S.op("dve", I("tensor_tensor", out=St[:, hb * 4:(hb + 1) * 4, :],
                                      in0=St[:, hb * 4:(hb + 1) * 4, :],
                                      in1=self.bank[2 + hb][:].rearrange("p (q t) -> p q t", q=4), op=ALU.add),
                             reads=[self.B_bank[2 + hb]], writes=[BSt])
                    nxt = n + 1 if d == 0 else n - 1
                    if 0 <= nxt < NB:
                        bnd = (nxt % 16 == 0) if d == 0 else (nxt % 16 == 15)
                        if bnd:
                            ccol = nxt if d == 0 else NB + nxt
                            S.op("dve", I("tensor_scalar", out=St[:], in0=St[:],
                                          scalar1=self.carry_t[:, ccol:ccol + 1], scalar2=None, op0=ALU.mult),
                                 reads=[self.B_carry], writes=[BSt])
                    sbi += 1
                    S.op("act", I("activation", out=Sb[sbi % 2][:], in_=St[:], func=AF.Copy),
                         reads=[BSt], writes=[BSb[sbi % 2]])
                    if outW is not None:
                        outp(outW, ci, 0, dve_only=True)
                    if normW is not None:
                        mmn(normW, 2 * ci + 1)
                    if outW is not None:
                        outp(outW, ci, 1)
                    for hb in range(2):
                        ov = oT[po][:, hb * 4:(hb + 1) * 4, cs]
                        bv = self.bank[4 + hb][:].rearrange("p (q t) -> p q t", q=4)
                        if d == 0:
                            S.op("act", I("activation", out=ov, in_=bv, func=AF.Copy),
                                 reads=[self.B_bank[4 + hb]], writes=BoT[po][hb * 4:(hb + 1) * 4])
                        else:
                            S.op("dve", I("tensor_tensor", out=ov, in0=bv, in1=ofb[p][:, hb * 4:(hb + 1) * 4, cs],
                                          op=ALU.add),
                                 reads=[self.B_bank[4 + hb], Bofb[p]], writes=BoT[po][hb * 4:(hb + 1) * 4])
                    if normW is not None:
                        fin(normW, 2 * ci)
                        fin(normW, 2 * ci + 1)
                if d == 0:
                    S.dma("sp", H["of"].ap[t], flat(oT[po]), doT[po], reads=BoT[po],
                          writes=H["of"].bufs[t])
                else:
                    outW = normW
                    normW = tile_work(j)
            if d == 1:
                if outW is not None:
                    for s_ in range(4):
                        outp(outW, s_, 0)
                        outp(outW, s_, 1)
                for h in range(NH):
                    sqr(normW, h)
                    mmn(normW, h)
                    fin(normW, h)
                for s_ in range(4):
                    outp(normW, s_, 0)
                    outp(normW, s_, 1)
                self.resid_finish(R)
            S.flush()

    def phase_final(self, src, ss, B_ss, copy_only=False):
        S = self.S
        with ExitStack() as ps:
            self.compute_rstd(ss, B_ss, ps)
            wb = self.sb(ps, "fwb", [128, D], F32)
            Bwb = Buf()
            dwb = S.dsem("fwb")
            S.dma("sp", wb[:], self.final_norm.partition_broadcast(128), dwb, writes=[Bwb])
            xl = [self.sb(ps, f"fx{i}", [128, D], F32) for i in range(3)]
            Bx = [Buf() for _ in range(3)]
            dx = [S.dsem("fx") for _ in range(3)]
            for i in range(self.NB):
                k = i % 3
                S.dma("sp", xl[k][:], src.ap[i * 128:(i + 1) * 128, :], dx[k], reads=[src.bufs[i]], writes=[Bx[k]])
                if not copy_only:
                    S.op("dve", I("scalar_tensor_tensor", out=xl[k][:], in0=xl[k][:],
                                                                          scalar=self.rstd[:, i:i + 1], in1=wb[:],
                                                                          op0=ALU.mult, op1=ALU.mult),
                         reads=[self.B_rstd, Bwb], writes=[Bx[k]])
                S.dma("sp", self.y_out.ap[i * 128:(i + 1) * 128, :], xl[k][:], dx[k], reads=[Bx[k]],
                      writes=[self.y_out.bufs[i]])
            S.flush()


W_NAMES = ["norm_w", "ffn_gate", "ffn_up", "ffn_down", "sgu_w_in", "sgu_ln_g", "sgu_ln_b", "sgu_w_s", "sgu_b_s",
           "sgu_w_out", "hgrn_w_in", "hgrn_lb_raw", "hgrn_norm_w", "hgrn_w_out", "final_norm"]
W_SQUEEZE = {"sgu_w_in", "sgu_ln_g", "sgu_ln_b", "sgu_w_s", "sgu_b_s", "sgu_w_out", "hgrn_w_in", "hgrn_norm_w",
             "hgrn_w_out"}


def make_carry(NB, seq_blocks):
    c = np.ones((128, 2 * NB), np.float32)
    for n in range(NB):
        if n % seq_blocks == 0:
            c[:, n] = 0.0
        if n % seq_blocks == seq_blocks - 1:
            c[:, NB + n] = 0.0
    return c


def kernel(**inputs):
    xp = np.ascontiguousarray(inputs["x_prompt"], dtype=np.float32)
    xs = np.ascontiguousarray(inputs["x_sample"], dtype=np.float32)
    NT = NT_FULL
    prog = Prog(NT)
    wmap = {}
    for n in W_NAMES:
        a = np.ascontiguousarray(inputs[n], dtype=np.float32)
        if n in W_SQUEEZE:
            a = a[0]
        wmap[n] = a
    in_maps = []
    for c in range(NCORES):
        m = dict(wmap)
        if c < 4:
            m["x"] = xp[4 * c:4 * c + 4].reshape(NT, D)
            m["carry"] = make_carry(NT // 128, SEQ_PROMPT // 128)
        else:
            m["x"] = xs[c - 4].reshape(NT, D)
            m["carry"] = make_carry(NT // 128, NT // 128)
        in_maps.append(m)
    res = run_bass_kernel_spmd(prog.nc, in_maps, core_ids=list(range(NCORES)))
    ys = [np.asarray(r["y"], dtype=np.float32) for r in res.results]
    y_prompt = np.stack(ys[:4]).reshape(16, 2048, D)
    y_sample = np.stack(ys[4:]).reshape(4, 8192, D)
    return (y_prompt, y_sample)
```

```python
import numpy as np
from contextlib import ExitStack
import concourse.bass as bass
import concourse.mybir as mybir
from concourse.bass_utils import run_bass_kernel_spmd

F32, BF16 = mybir.dt.float32, mybir.dt.bfloat16
AF = mybir.ActivationFunctionType
ALU = mybir.AluOpType

D = 1024
FF = 2816
NF = FF // 128
SG = 3072
NE = SG // 128
NH = 8
EPS = 1e-6
NCORES = 8
NT_FULL = 8192
SEQ_PROMPT = 2048


class Tok:
    __slots__ = ("sem", "val")

    def __init__(self, sem, val):
        self.sem = sem
        self.val = val


class Buf:
    __slots__ = ("w", "r", "name")

    def __init__(self, name=""):
        self.w = None
        self.r = []
        self.name = name


class DSem:
    def __init__(self, h):
        self.h = h
        self.cnt = 0


class Eng:
    def __init__(self, name):
        self.name = name
        self.ops = []
        self.sem = None
        self.cnt = 0
        self.known = {}
        self.rec = []


class Sched:
    def __init__(self, nc, es):
        self.nc = nc
        self.es = es
        self.eng = {n: Eng(n) for n in ("pe", "act", "dve", "pool", "sp")}
        self.nsem = 0
        self.dsems = []
        self.free_dsems = []
        self.phase_dsems = []
        self.bg_dsems = []
        self._new_engine_sems()

    def _newsem(self, name):
        self.nsem += 1
        return self.es.enter_context(self.nc.semaphore(f"{name}{self.nsem}"))

    def _new_engine_sems(self):
        for n in ("pe", "act", "dve", "pool"):
            E = self.eng[n]
            E.sem = self._newsem("e" + n)
            E.cnt = 0

    def dsem(self, name="d", bg=False):
        if bg:
            d = DSem(self._newsem(name))
            self.bg_dsems.append(d)
            return d
        if self.free_dsems:
            d = self.free_dsems.pop()
        else:
            d = DSem(self._newsem(name))
            self.dsems.append(d)
        self.phase_dsems.append(d)
        return d

    def _deps(self, E, reads, writes):
        need = {}

        def add(t):
            if t is None:
                return
            if E.name == "pe" and t.sem is E.sem:
                return
            k = id(t.sem)
            if E.known.get(k, 0) >= t.val:
                return
            if k not in need or need[k].val < t.val:
                need[k] = t

        for b in reads:
            add(b.w)
        for b in writes:
            add(b.w)
            for t in b.r:
                add(t)
        for k, t in need.items():
            E.known[k] = t.val
        return list(need.values())

    def _reg(self, tok, reads, writes):
        for b in reads:
            b.r.append(tok)
        for b in writes:
            b.w = tok
            b.r = []

    def op(self, en, fn, reads=(), writes=()):
        return self.group(en, [fn], reads, writes)

    def group(self, en, fns, reads=(), writes=()):
        E = self.eng[en]
        waits = self._deps(E, reads, writes)
        E.cnt += 1
        sem = E.sem
        tok = Tok(sem, E.cnt)

        def run(e):
            for t in waits:
                e.wait_ge(t.sem, t.val)
            ins = None
            for fn in fns:
                ins = getattr(e, fn[0])(*fn[1], **fn[2])
            ins.then_inc(sem, 1)

        E.ops.append(run)
        E.rec.append(([(id(t.sem), t.val) for t in waits], (id(sem), 1)))
        self._reg(tok, reads, writes)
        return tok

    def dma(self, en, out, in_, ds, reads=(), writes=(), **kw):
        E = self.eng[en]
        waits = self._deps(E, reads, writes)
        ds.cnt += 16
        tok = Tok(ds.h, ds.cnt)
        h = ds.h

        def run(e):
            for t in waits:
                e.wait_ge(t.sem, t.val)
            e.dma_start(out=out, in_=in_, **kw).then_inc(h, 16)

        E.ops.append(run)
        E.rec.append(([(id(t.sem), t.val) for t in waits], (id(h), 16)))
        self._reg(tok, reads, writes)
        return tok

    def check_deadlock(self):
        vals = getattr(self, "_simvals", {})
        ptr = {n: 0 for n in self.eng}
        prog = True
        while prog:
            prog = False
            for n, E in self.eng.items():
                while ptr[n] < len(E.rec):
                    waits, inc = E.rec[ptr[n]]
                    if all(vals.get(k, 0) >= v for k, v in waits):
                        if inc is not None:
                            vals[inc[0]] = vals.get(inc[0], 0) + inc[1]
                        ptr[n] += 1
                        prog = True
                    else:
                        break
        stuck = {n: (ptr[n], len(E.rec)) for n, E in self.eng.items() if ptr[n] < len(E.rec)}
        self._simvals = vals
        if stuck:
            msg = []
            for n, (p, tot) in stuck.items():
                waits, inc = self.eng[n].rec[p]
                msg.append(f"{n}: op {p}/{tot} waits " + str([(k % 10000, v, vals.get(k, 0)) for k, v in waits if vals.get(k, 0) < v]))
            raise RuntimeError("schedule deadlock: " + "; ".join(msg))
        for E in self.eng.values():
            E.rec = []

    def barrier(self, final=False):
        toks = [Tok(self.eng[n].sem, self.eng[n].cnt) for n in ("pe", "act", "dve", "pool") if self.eng[n].cnt > 0]
        toks += [Tok(d.h, d.cnt) for d in self.dsems if d.cnt > 0]
        for n, E in self.eng.items():
            ws = []
            for t in toks:
                if E.name == "pe" and t.sem is E.sem:
                    continue
                k = id(t.sem)
                if E.known.get(k, 0) >= t.val:
                    continue
                E.known[k] = t.val
                ws.append(t)

            def run(e, ws=ws):
                for t in ws:
                    e.wait_ge(t.sem, t.val)

            E.ops.append(run)
            E.rec.append(([(id(t.sem), t.val) for t in ws], None))

    def flush(self):
        self.barrier()
        self.check_deadlock()
        lists = {n: E.ops for n, E in self.eng.items()}
        for E in self.eng.values():
            E.ops = []
        with self.nc.Block() as block:
            @block.tensor
            def _(e):
                for f in lists["pe"]:
                    f(e)

            @block.scalar
            def _(e):
                for f in lists["act"]:
                    f(e)

            @block.vector
            def _(e):
                for f in lists["dve"]:
                    f(e)

            @block.gpsimd
            def _(e):
                for f in lists["pool"]:
                    f(e)

            @block.sync
            def _(e):
                for f in lists["sp"]:
                    f(e)
        self.free_dsems.extend(self.phase_dsems)
        self.phase_dsems = []


def I(name, *a, **kw):
    return (name, a, kw)


class DramAct:
    def __init__(self, ap, nblk, name, sub=None):
        self.ap = ap
        if sub is None:
            self.bufs = [Buf(f"{name}{i}") for i in range(nblk)]
        else:
            self.bufs = [[Buf(f"{name}{i}_{j}") for j in range(sub)] for i in range(nblk)]


class Prog:
    def __init__(self, NT, stop_after=None, debug=False):
        assert NT % 512 == 0
        self.debug = debug
        self.NT = NT
        self.NB = NT // 128
        self.NTL = NT // 512
        self.stop_after = stop_after
        self.nc = bass.Bass("TRN2", target_bir_lowering=False)
        self.es = ExitStack()
        self.S = Sched(self.nc, self.es)
        self.build()
        self.es.close()

    def dram(self, name, shape, dt, kind="Internal"):
        return self.nc.dram_tensor(name, list(shape), dt, kind=kind).ap()

    def sb(self, stack, name, shape, dt):
        self._uid = getattr(self, "_uid", 0) + 1
        return stack.enter_context(self.nc.sbuf_tensor(f"{name}_{self._uid}", list(shape), dt))

    def dbg(self, name, ap, bufs, dt, eng="sp"):
        if not getattr(self, "debug", False):
            return
        out = self.dram("dbg_" + name, list(ap.shape), dt, "ExternalOutput")
        self.S.dma(eng, out, ap, self.S.dsem("dbg"), reads=bufs)

    def build(self):
        nc, S, NT, NB = self.nc, self.S, self.NT, self.NB
        es = self.es
        self.x_in = DramAct(self.dram("x", [NT, D], F32, "ExternalInput"), NB, "xin")
        self.y_out = DramAct(self.dram("y", [NT, D], F32, "ExternalOutput"), NB, "yout")
        self.norm_w = self.dram("norm_w", [2, 3, D], F32, "ExternalInput")
        self.ffn_gate = self.dram("ffn_gate", [2, 2, D, FF], F32, "ExternalInput")
        self.ffn_up = self.dram("ffn_up", [2, 2, D, FF], F32, "ExternalInput")
        self.ffn_down = self.dram("ffn_down", [2, 2, FF, D], F32, "ExternalInput")
        self.sgu_w_in = self.dram("sgu_w_in", [D, 2 * SG], F32, "ExternalInput")
        self.sgu_ln_g = self.dram("sgu_ln_g", [SG], F32, "ExternalInput")
        self.sgu_ln_b = self.dram("sgu_ln_b", [SG], F32, "ExternalInput")
        self.sgu_w_s = self.dram("sgu_w_s", [8, 128, 128], F32, "ExternalInput")
        self.sgu_b_s = self.dram("sgu_b_s", [8, 128], F32, "ExternalInput")
        self.sgu_w_out = self.dram("sgu_w_out", [SG, D], F32, "ExternalInput")
        self.hgrn_w_in = self.dram("hgrn_w_in", [D, 5 * D], F32, "ExternalInput")
        self.hgrn_lb_raw = self.dram("hgrn_lb_raw", [2, 2, D], F32, "ExternalInput")
        self.hgrn_norm_w = self.dram("hgrn_norm_w", [D], F32, "ExternalInput")
        self.hgrn_w_out = self.dram("hgrn_w_out", [D, D], F32, "ExternalInput")
        self.final_norm = self.dram("final_norm", [D], F32, "ExternalInput")
        self.carry = self.dram("carry", [128, 2 * NB], F32, "ExternalInput")
        self.xa = DramAct(self.dram("xa", [NT, D], F32), NB, "xa")
        self.xb = DramAct(self.dram("xb", [NT, D], F32), NB, "xb")
        self.sT = DramAct(self.dram("sT", [SG, NT], BF16), self.NTL, "sT")
        NTL = self.NTL
        self.H = dict(
            qe=[DramAct(self.dram(f"qe{d}", [NTL, 128, NH * 512], BF16), NTL, f"qe{d}", 8) for d in range(2)],
            kd=[DramAct(self.dram(f"kd{d}", [NTL, 128, NH * 512], BF16), NTL, f"kd{d}", 8) for d in range(2)],
            ks=[DramAct(self.dram(f"ks{d}", [NTL, 128, 4 * D], BF16), NTL, f"ks{d}", 1) for d in range(2)],
            v=DramAct(self.dram("vtok", [NTL, 128, 4 * D], BF16), NTL, "vtok", 1),
            gs=DramAct(self.dram("gsT", [NTL, 128, NH * 512], BF16), NTL, "gsT", 8),
            of=DramAct(self.dram("ofT", [NTL, 128, NH * 512], BF16), NTL, "ofT", 1),
        )

        self.ident = self.sb(es, "ident", [128, 128], BF16)
        self.identf = self.sb(es, "identf", [128, 128], F32)
        self.ssA = self.sb(es, "ssA", [128, NB], F32)
        self.ssB = self.sb(es, "ssB", [128, NB], F32)
        self.rstd = self.sb(es, "rstd", [128, NB], F32)
        self.B_ident = Buf("ident")
        self.B_ssA = Buf("ssA")
        self.B_ssB = Buf("ssB")
        self.B_rstd = Buf("rstd")
        self.bank = [es.enter_context(nc.psum_tensor(f"bank{i}", [128, 512], F32)) for i in range(8)]
        self.B_bank = [Buf(f"bank{i}") for i in range(8)]

        S.op("pool", I("memset", self.identf[:], 0.0), writes=[self.B_ident])
        S.op("pool", I("affine_select", out=self.identf[:], in_=self.identf[:], pattern=[[-1, 128]],
                                               compare_op=ALU.not_equal, fill=1.0, base=0, channel_multiplier=1),
             writes=[self.B_ident])
        S.op("dve", I("tensor_copy", out=self.ident[:], in_=self.identf[:]), writes=[self.B_ident])
        S.op("pool", I("memset", self.ssA[:], 0.0), writes=[self.B_ssA])
        S.op("pool", I("memset", self.ssB[:], 0.0), writes=[self.B_ssB])

        w1s = ExitStack()
        W1 = self.ffn_weights(w1s, 0, 0, direct=True)
        self.convert_all()
        self.phase_ffn(self.x_in, self.xa, 0, 0, 0, self.ssA, self.B_ssA, self.ssB, self.B_ssB, W=W1,
                       self_stats=True)
        w1s.close()
        if self.stop_after == "ffn1":
            self.phase_final(self.xa, self.ssB, self.B_ssB, copy_only=True)
            return
        self.phase_sgu_a(self.xa, self.ssB, self.B_ssB, self.sT)
        self.phase_sgu_b(self.xa, self.xb, self.ssB, self.B_ssB, self.ssA, self.B_ssA, self.sT)
        if self.stop_after == "sgu":
            self.phase_final(self.xb, self.ssA, self.B_ssA, copy_only=True)
            return
        self.phase_ffn(self.xb, self.xa, 0, 2, 1, self.ssA, self.B_ssA, self.ssB, self.B_ssB)
        self.phase_ffn(self.xa, self.xb, 1, 0, 0, self.ssB, self.B_ssB, self.ssA, self.B_ssA)
        if self.stop_after == "ffn3":
            self.phase_final(self.xb, self.ssA, self.B_ssA, copy_only=True)
            return
        with ExitStack() as hes:
            self.hgrn_consts(hes)
            self.phase_hgrn_a(self.xb, self.ssA, self.B_ssA, self.H)
            self.phase_hgrn_scan(0, self.H)
            self.phase_hgrn_scan(1, self.H, self.xb, self.xa, self.ssB, self.B_ssB)
        if self.stop_after == "hgrn":
            self.phase_final(self.xa, self.ssB, self.B_ssB, copy_only=True)
            return
        self.phase_ffn(self.xa, self.y_out, 1, 2, 1, self.ssB, self.B_ssB, self.ssA, self.B_ssA, final=True)

    def phase_stats(self, src, ss, B_ss):
        nc, S = self.nc, self.S
        with ExitStack() as ps:
            xl = [self.sb(ps, f"st_x{i}", [128, D], F32) for i in range(3)]
            junk = self.sb(ps, "st_junk", [128, D], BF16)
            Bx = [Buf() for _ in range(3)]
            Bj = Buf()
            ds = [S.dsem("stx") for _ in range(3)]
            for i in range(self.NB):
                k = i % 3
                S.dma("sp", xl[k][:], src.ap[i * 128:(i + 1) * 128, :], ds[k], reads=[src.bufs[i]], writes=[Bx[k]])
                S.op("act", I("activation", out=junk[:], in_=xl[k][:], func=AF.Square,
                                                            accum_out=ss[:, i:i + 1]),
                     reads=[Bx[k], B_ss], writes=[Bj])
            B_ss.w = Tok(S.eng["act"].sem, S.eng["act"].cnt)
            S.flush()

    def compute_rstd(self, ss, B_ss, ps):
        S = self.S
        tmp = self.sb(ps, "rs_tmp", [128, self.NB], F32)
        Bt = Buf()
        S.op("act", I("activation", out=tmp[:], in_=ss[:], func=AF.Sqrt, bias=EPS, scale=1.0 / D),
             reads=[B_ss], writes=[Bt])
        S.op("dve", I("reciprocal", out=self.rstd[:], in_=tmp[:]), reads=[Bt], writes=[self.B_rstd])

    def convert_all(self):
        S = self.S
        self.wbf, self.Bconv = {}, {}
        jobs = [("sgu_in", self.sgu_w_in, [D, 2 * SG]), ("sgu_out", self.sgu_w_out, [SG, D])]
        for (l, j) in ((0, 1), (1, 0)):
            jobs += [(f"gate{l}{j}", self.ffn_gate[l, j], [D, FF]), (f"up{l}{j}", self.ffn_up[l, j], [D, FF]),
                     (f"down{l}{j}", self.ffn_down[l, j], [FF, D])]
        jobs += [("hg_in", self.hgrn_w_in, [D, 5 * D]), ("hg_out", self.hgrn_w_out, [D, D])]
        jobs += [("gate11", self.ffn_gate[1, 1], [D, FF]), ("up11", self.ffn_up[1, 1], [D, FF]),
                 ("down11", self.ffn_down[1, 1], [FF, D])]
        prev = []
        cvs = [S.dsem("cv", bg=True) for _ in range(3)]
        RC = 256
        n = 0
        for key, src, shp in jobs:
            dst = self.dram("wbf_" + key, shp, BF16)
            self.wbf[key] = dst
            self.Bconv[key] = []
            for r0 in range(0, shp[0], RC):
                B = Buf(f"conv_{key}_{r0}")
                S.dma("pool", dst[r0:r0 + RC, :], src[r0:r0 + RC, :], cvs[n % 3], reads=prev[-3:-1], writes=[B],
                      max_dma_last_dim=4096)
                n += 1
                prev.append(B)
                self.Bconv[key].append(B)

    class WT:
        def __init__(self, tile, bounds):
            self.tile = tile
            self.bounds = bounds
            self.B = [Buf() for _ in bounds]
            self.issue = []

        def rb(self, lo, hi):
            return [b for (a, c), b in zip(self.bounds, self.B) if a < hi and c > lo]

    def load_w(self, ps, name, nk, ncols, axis, bounds, key=None, col0=0, direct=None, issue=True):
        S = self.S
        t = self.sb(ps, name, [128, nk, ncols], BF16)
        W = Prog.WT(t, bounds)
        if direct is not None:
            v = direct.rearrange("(k p) f -> p k f", p=128)
        else:
            v = self.wbf[key].rearrange("(k p) f -> p k f", p=128)
        for (lo, hi), B in zip(bounds, W.B):
            if axis == "col":
                o, i = t[:, :, lo:hi], v[:, :, col0 + lo:col0 + hi]
            else:
                o, i = t[:, lo:hi, :], v[:, lo:hi, col0:col0 + ncols]
            if direct is not None:
                def f(o=o, i=i, B=B):
                    self._w1prev = getattr(self, "_w1prev", [])
                    S.dma("pool", o, i, S.dsem("w1", bg=True), reads=self._w1prev[-3:-2], writes=[B],
                          max_dma_last_dim=4096)
                    self._w1prev.append(B)
            else:
                def f(o=o, i=i, B=B):
                    S.dma("sp", o, i, S.dsem("wl"), reads=self.Bconv[key], writes=[B])
            W.issue.append(f)
        if issue:
            for f in W.issue:
                f()
        return W

    def ffn_weights(self, ps, l, j, direct=False):
        cb = [(0, 768), (768, 1536), (1536, 2176), (2176, 2816)]
        rb = [(0, 6), (6, 12), (12, 17), (17, 22)]
        if direct:
            Wg = self.load_w(ps, "Wg", 8, FF, "col", cb, direct=self.ffn_gate[l, j], issue=False)
            Wu = self.load_w(ps, "Wu", 8, FF, "col", cb, direct=self.ffn_up[l, j], issue=False)
            Wd = self.load_w(ps, "Wd", NF, D, "row", rb, direct=self.ffn_down[l, j], issue=False)
            for q in range(4):
                Wg.issue[q]()
                Wu.issue[q]()
            for q in range(4):
                Wd.issue[q]()
        else:
            Wg = self.load_w(ps, "Wg", 8, FF, "col", cb, key=f"gate{l}{j}")
            Wu = self.load_w(ps, "Wu", 8, FF, "col", cb, key=f"up{l}{j}")
            Wd = self.load_w(ps, "Wd", NF, D, "row", rb, key=f"down{l}{j}")
        return Wg, Wu, Wd

    def load_weight_bf16(self, dst_tile, src_ap, nk, ncols, ds, B):
        S = self.S
        for k in range(nk):
            S.dma("pool", dst_tile[:, k, :], src_ap[k * 128:(k + 1) * 128, :], ds, max_dma_last_dim=4096)

    class LNT:
        pass

    def lnt_alloc(self, ps, src, normw_row, evac_eng="act", self_stats=None):
        S = self.S
        L = Prog.LNT()
        L.src = src
        L.xld = [self.sb(ps, f"xld{i}", [128, D], F32) for i in range(2)]
        L.Bxld = [Buf() for _ in range(2)]
        L.dxld = [S.dsem("xld") for _ in range(2)]
        L.xn = [self.sb(ps, f"xn{i}", [128, D], BF16) for i in range(2)]
        L.Bxn = [Buf() for _ in range(2)]
        L.xnT = [self.sb(ps, f"xnT{i}", [128, 8, 512], BF16) for i in range(2)]
        L.BxnT = [[Buf() for _ in range(4)] for _ in range(2)]
        L.wb = self.sb(ps, "wb", [128, D], F32)
        L.Bwb = Buf()
        L.dwb = S.dsem("wb")
        S.dma("sp", L.wb[:], normw_row.partition_broadcast(128), L.dwb, writes=[L.Bwb])
        L.cnt = 0
        L.slot = {}
        L.evac_eng = evac_eng
        L.self_stats = self_stats
        L.Bssc = Buf()
        return L

    def lnt_dma(self, L, t, s):
        S = self.S
        i = t * 4 + s
        k = L.cnt % 2
        L.cnt += 1
        L.slot[(t, s)] = k
        S.dma("sp", L.xld[k][:], L.src.ap[i * 128:(i + 1) * 128, :], L.dxld[k],
              reads=[L.src.bufs[i]], writes=[L.Bxld[k]])

    def lnt_sub_a(self, L, t, s):
        S = self.S
        i = t * 4 + s
        k = L.slot[(t, s)]
        if L.self_stats is not None:
            ss, B_ss, junk, Bjunk = L.self_stats
            S.op("act", I("activation", out=junk[:], in_=L.xld[k][:], func=AF.Square, accum_out=ss[:, i:i + 1]),
                 reads=[L.Bxld[k], B_ss], writes=[Bjunk, L.Bssc])
            S.op("act", I("activation", out=self.rstd[:, i:i + 1], in_=ss[:, i:i + 1], func=AF.Sqrt, bias=EPS,
                          scale=1.0 / D), reads=[L.Bssc], writes=[self.B_rstd])
            S.op("dve", I("reciprocal", out=self.rstd[:, i:i + 1], in_=self.rstd[:, i:i + 1]),
                 reads=[self.B_rstd], writes=[self.B_rstd])
        S.op("dve", I("scalar_tensor_tensor", out=L.xn[k][:], in0=L.xld[k][:], scalar=self.rstd[:, i:i + 1],
                      in1=L.wb[:], op0=ALU.mult, op1=ALU.mult),
             reads=[L.Bxld[k], self.B_rstd, L.Bwb], writes=[L.Bxn[k]])

    def lnt_sub_b(self, L, t, s):
        k = L.slot.pop((t, s))
        self.lnt_transpose_one(L, t, s, k)

    def lnt_sub(self, L, t, s):
        self.lnt_sub_a(L, t, s)
        self.lnt_sub_b(L, t, s)

    def lnt_micro(self, L, t):
        d, a, b = self.lnt_dma, self.lnt_sub_a, self.lnt_sub_b
        return [lambda: (d(L, t, 0), d(L, t, 1)),
                lambda: (a(L, t, 0), d(L, t, 2)),
                lambda: (a(L, t, 1), d(L, t, 3)),
                lambda: b(L, t, 0),
                lambda: a(L, t, 2),
                lambda: b(L, t, 1),
                lambda: a(L, t, 3),
                lambda: b(L, t, 2),
                lambda: b(L, t, 3)]

    def lnt_steps(self, L, t):
        return [lambda: (self.lnt_dma(L, t, 0), self.lnt_dma(L, t, 1)),
                lambda: (self.lnt_sub(L, t, 0), self.lnt_dma(L, t, 2)),
                lambda: (self.lnt_sub(L, t, 1), self.lnt_dma(L, t, 3)),
                lambda: self.lnt_sub(L, t, 2),
                lambda: self.lnt_sub(L, t, 3)]

    def lnt_load(self, L, t):
        for f in self.lnt_steps(L, t):
            f()

    def lnt_transpose_one(self, L, t, s, k):
        S = self.S
        tb = t % 2
        bk = 7
        pt = self.bank[bk][:].bitcast(BF16)
        fns = [(I("transpose", out=pt[:, c * 128:(c + 1) * 128], in_=L.xn[k][:, c * 128:(c + 1) * 128],
                                           identity=self.ident[:])) for c in range(8)]
        S.group("pe", fns, reads=[L.Bxn[k], self.B_ident], writes=[self.B_bank[bk]])
        if L.evac_eng == "act":
            S.op("act", I("activation", out=L.xnT[tb][:, :, s * 128:(s + 1) * 128],
                          in_=pt.rearrange("p (c t) -> p c t", c=8), func=AF.Copy),
                 reads=[self.B_bank[bk]], writes=[L.BxnT[tb][s]])
        else:
            S.op("dve", I("tensor_copy", out=L.xnT[tb][:, :, s * 128:(s + 1) * 128],
                          in_=pt.rearrange("p (c t) -> p c t", c=8)),
                 reads=[self.B_bank[bk]], writes=[L.BxnT[tb][s]])

    def phase_ffn(self, src, dst, layer, normi, ffni, ss_in, B_ss_in, ss_out, B_ss_out, W=None, final=False,
                  self_stats=False):
        nc, S = self.nc, self.S
        with ExitStack() as ps:
            WG, WU, WD = W if W is not None else self.ffn_weights(ps, layer, ffni)
            Wg, Wu, Wd = WG.tile, WU.tile, WD.tile
            junk = self.sb(ps, "fjunk", [128, D], BF16)
            Bj = Buf()
            if self_stats:
                L = self.lnt_alloc(ps, src, self.norm_w[layer, normi], self_stats=(ss_in, B_ss_in, junk, Bj))
            else:
                self.compute_rstd(ss_in, B_ss_in, ps)
                L = self.lnt_alloc(ps, src, self.norm_w[layer, normi])
            hT = self.sb(ps, "hT", [128, NF, 512], BF16)
            BhT = [Buf() for _ in range(NF)]
            sg = [self.sb(ps, f"sg{i}", [128, 512], F32) for i in range(2)]
            Bsg = [Buf() for _ in range(2)]
            xr = [self.sb(ps, f"xr{i}", [128, D], F32) for i in range(2)]
            Bxr = [[Buf(), Buf()] for _ in range(2)]
            dxr = [S.dsem("xr") for _ in range(2)]
            S.op("dve", I("memset", ss_out[:], 0.0), writes=[B_ss_out])
            Bfs, Bwfb = Buf(), Buf()
            if final:
                fs = self.sb(ps, "ffs", [128, 4], F32)
                wfb = self.sb(ps, "wfb", [128, D], F32)
                S.dma("sp", wfb[:], self.final_norm.partition_broadcast(128), S.dsem("wfb"), writes=[Bwfb])

            self.lnt_load(L, 0)
            self.dbg("xnT0", L.xnT[0][:], L.BxnT[0], BF16)
            self.dbg("rstd", self.rstd[:], [self.B_rstd], F32)
            gu = 0
            dn = 0
            xrc = 0
            for t in range(self.NTL):
                tb = t % 2
                for f in range(NF):
                    bg, bu = (gu % 2) * 2, (gu % 2) * 2 + 1
                    gu += 1
                    fg = [(I("matmul", self.bank[bg][:], lhsT=Wg[:, k, f * 128:(f + 1) * 128],
                                                              rhs=L.xnT[tb][:, k, :], start=(k == 0), stop=(k == 7)))
                          for k in range(8)]
                    S.group("pe", fg, reads=WG.rb(f * 128, f * 128 + 128) + L.BxnT[tb], writes=[self.B_bank[bg]])
                    fu = [(I("matmul", self.bank[bu][:], lhsT=Wu[:, k, f * 128:(f + 1) * 128],
                                                              rhs=L.xnT[tb][:, k, :], start=(k == 0), stop=(k == 7)))
                          for k in range(8)]
                    S.group("pe", fu, reads=WU.rb(f * 128, f * 128 + 128) + L.BxnT[tb], writes=[self.B_bank[bu]])
                    sk = f % 2
                    if self.debug and t == 0 and f == 0:
                        self.dbgt = self.sb(ps, "dbgt", [128, 512], F32)
                        Bd = Buf()
                        S.op("dve", I("tensor_copy", out=self.dbgt[:], in_=self.bank[bg][:]),
                             reads=[self.B_bank[bg]], writes=[Bd])
                        self.dbg("g0", self.dbgt[:], [Bd], F32)
                    S.op("act", I("activation", out=sg[sk][:], in_=self.bank[bg][:], func=AF.Silu),
                         reads=[self.B_bank[bg]], writes=[Bsg[sk]])
                    S.op("dve", I("tensor_tensor", out=hT[:, f, :], in0=sg[sk][:],
                                                                            in1=self.bank[bu][:], op=ALU.mult),
                         reads=[Bsg[sk], self.B_bank[bu]], writes=[BhT[f]])
                    if t == 0 and f in (0, 21):
                        self.dbg(f"sg{f}", sg[sk][:], [Bsg[sk]], F32)
                    if 2 <= f <= 10 and t + 1 < self.NTL:
                        if f == 2:
                            micro = self.lnt_micro(L, t + 1)
                        micro[f - 2]()
                if t == 0:
                    self.dbg("hT0", hT[:], BhT, BF16)
                    self.dbg("xnT0b", L.xnT[0][:], L.BxnT[0], BF16)
                for s in range(4):
                    i = t * 4 + s
                    xk = xrc % 2
                    xrc += 1
                    S.dma("sp", xr[xk][:], src.ap[i * 128:(i + 1) * 128, :], dxr[xk], reads=[src.bufs[i]],
                          writes=Bxr[xk])
                    for nh in range(2):
                        bo = 4 + dn % 3
                        dn += 1
                        fd = [(I("matmul",
                            self.bank[bo][:], lhsT=hT[:, f, s * 128:(s + 1) * 128],
                            rhs=Wd[:, f, nh * 512:(nh + 1) * 512], start=(f == 0), stop=(f == NF - 1)))
                            for f in range(NF)]
                        S.group("pe", fd, reads=WD.B + BhT, writes=[self.B_bank[bo]])
                        S.op("dve", I("scalar_tensor_tensor",
                            out=xr[xk][:, nh * 512:(nh + 1) * 512], in0=self.bank[bo][:], scalar=0.5,
                            in1=xr[xk][:, nh * 512:(nh + 1) * 512], op0=ALU.mult, op1=ALU.add),
                            reads=[self.B_bank[bo]], writes=[Bxr[xk][nh]])
                    S.op("act", I("activation", out=junk[:], in_=xr[xk][:], func=AF.Square,
                                                                  accum_out=ss_out[:, i:i + 1]),
                         reads=Bxr[xk] + [B_ss_out], writes=[Bj, Bfs])
                    if final:
                        S.op("act", I("activation", out=fs[:, xk:xk + 1], in_=ss_out[:, i:i + 1], func=AF.Sqrt,
                                      bias=EPS, scale=1.0 / D), reads=[Bfs], writes=[Bfs])
                        S.op("dve", I("reciprocal", out=fs[:, 2 + xk:3 + xk], in_=fs[:, xk:xk + 1]),
                             reads=[Bfs], writes=[Bfs])
                        S.op("dve", I("scalar_tensor_tensor", out=xr[xk][:], in0=xr[xk][:],
                                      scalar=fs[:, 2 + xk:3 + xk], in1=wfb[:], op0=ALU.mult, op1=ALU.mult),
                             reads=[Bfs, Bwfb], writes=Bxr[xk])
                    S.dma("sp", dst.ap[i * 128:(i + 1) * 128, :], xr[xk][:], dxr[xk], reads=Bxr[xk],
                          writes=[dst.bufs[i]])
            B_ss_out.w = Tok(S.eng["act"].sem, S.eng["act"].cnt)
            S.flush()


    def resid_alloc(self, ps, src, dst, ss_out, B_ss_out):
        S = self.S
        R = Prog.LNT()
        R.src, R.dst, R.ss_out, R.B_ss_out = src, dst, ss_out, B_ss_out
        R.xr = [self.sb(ps, f"xr{i}", [128, D], F32) for i in range(2)]
        R.Bxr = [[Buf(), Buf()] for _ in range(2)]
        R.dxr = [S.dsem("xr") for _ in range(2)]
        R.junk = self.sb(ps, "rjunk", [128, D], BF16)
        R.Bjunk = Buf()
        R.cnt = 0
        S.op("dve", I("memset", ss_out[:], 0.0), writes=[B_ss_out])
        return R

    def resid_begin(self, R, i):
        S = self.S
        xk = R.cnt % 2
        R.cnt += 1
        S.dma("sp", R.xr[xk][:], R.src.ap[i * 128:(i + 1) * 128, :], R.dxr[xk], reads=[R.src.bufs[i]],
              writes=R.Bxr[xk])
        return xk

    def resid_add(self, R, xk, nh, bo, scale):
        S = self.S
        S.op("dve", I("scalar_tensor_tensor", out=R.xr[xk][:, nh * 512:(nh + 1) * 512], in0=self.bank[bo][:],
                      scalar=scale, in1=R.xr[xk][:, nh * 512:(nh + 1) * 512], op0=ALU.mult, op1=ALU.add),
             reads=[self.B_bank[bo]], writes=[R.Bxr[xk][nh]])

    def resid_end(self, R, xk, i):
        S = self.S
        S.op("act", I("activation", out=R.junk[:], in_=R.xr[xk][:], func=AF.Square,
                      accum_out=R.ss_out[:, i:i + 1]), reads=R.Bxr[xk] + [R.B_ss_out], writes=[R.Bjunk])
        S.dma("sp", R.dst.ap[i * 128:(i + 1) * 128, :], R.xr[xk][:], R.dxr[xk], reads=R.Bxr[xk],
              writes=[R.dst.bufs[i]])

    def resid_finish(self, R):
        R.B_ss_out.w = Tok(self.S.eng["act"].sem, self.S.eng["act"].cnt)

    def load_cols(self, ps, name, vec_ap, ncol):
        S = self.S
        t = self.sb(ps, name, [128, ncol], F32)
        B = Buf()
        S.dma("sp", t[:], vec_ap.rearrange("(c p) -> p c", p=128), S.dsem(name), writes=[B],
              allow_slow_non_contiguous=True)
        return t, B

    def phase_sgu_a(self, src, ss_in, B_ss_in, sT):
        S = self.S
        with ExitStack() as ps:
            WV = self.load_w(ps, "Wv", 8, SG, "col", [(n * 512, (n + 1) * 512) for n in range(6)], key="sgu_in",
                             col0=SG)
            Wv = WV.tile
            self.compute_rstd(ss_in, B_ss_in, ps)
            L = self.lnt_alloc(ps, src, self.norm_w[0, 1])
            lng, Blng = self.load_cols(ps, "lng", self.sgu_ln_g, NE)
            lnb, Blnb = self.load_cols(ps, "lnb", self.sgu_ln_b, NE)
            wsf = self.sb(ps, "wsf", [128, 8, 128], F32)
            wsb = self.sb(ps, "wsb", [128, 8, 128], BF16)
            wsT = self.sb(ps, "wsT", [128, 8, 128], BF16)
            ones = self.sb(ps, "ones", [128, 128], BF16)
            bsb = self.sb(ps, "bsb", [128, 8, 128], F32)
            C = self.sb(ps, "Cmat", [128, NE, 128], F32)
            Bwsf, Bwsb, BwsT, Bones, Bbsb, BC = Buf(), Buf(), Buf(), Buf(), Buf(), Buf()
            S.dma("sp", wsf[:], self.sgu_w_s.rearrange("g t s -> t g s"), S.dsem("wsf"), writes=[Bwsf])
            S.dma("sp", bsb[:].rearrange("p g t -> p (g t)"),
                  self.sgu_b_s.rearrange("g t -> (g t)").partition_broadcast(128), S.dsem("bsb"), writes=[Bbsb])
            S.op("dve", I("tensor_copy", out=wsb[:], in_=wsf[:]), reads=[Bwsf], writes=[Bwsb])
            S.op("dve", I("memset", ones[:], 1.0), writes=[Bones])
            pt = self.bank[7][:].bitcast(BF16)
            S.group("pe", [I("transpose", out=pt[:, g * 128:(g + 1) * 128], in_=wsb[:, g, :], identity=self.ident[:])
                           for g in range(8)], reads=[Bwsb, self.B_ident], writes=[self.B_bank[7]])
            S.op("act", I("activation", out=wsT[:].rearrange("p g t -> p (g t)"), in_=pt, func=AF.Copy),
                 reads=[self.B_bank[7]], writes=[BwsT])
            for half in range(2):
                S.group("pe", [I("matmul", self.bank[half][:, j * 128:(j + 1) * 128], lhsT=ones[:],
                                 rhs=wsT[:, half * 4 + j, :], start=True, stop=True) for j in range(4)],
                        reads=[Bones, BwsT], writes=[self.B_bank[half]])
            for ec in range(NE):
                g = ec // 3
                S.op("dve", I("scalar_tensor_tensor", out=C[:, ec, :],
                              in0=self.bank[g // 4][:, (g % 4) * 128:(g % 4 + 1) * 128], scalar=lnb[:, ec:ec + 1],
                              in1=bsb[:, g, :], op0=ALU.mult, op1=ALU.add),
                     reads=[self.B_bank[g // 4], Blnb, Bbsb], writes=[BC])
            vf = [[self.sb(ps, f"vf{q}_{i}", [128, SG], BF16) for i in range(4)] for q in range(2)]
            Bvf = [[Buf() for _ in range(4)] for _ in range(2)]
            vh = [self.sb(ps, f"vh{i}", [128, SG], BF16) for i in range(4)]
            Bvh = [Buf() for _ in range(4)]
            sTt = self.sb(ps, "sTt", [128, NE, 512], BF16)
            BsTt = [Buf() for _ in range(NE)]
            dsT = S.dsem("sTst")
            sjunk = self.sb(ps, "sjunk", [128, 512], BF16)
            Bsjunk = Buf()
            s1 = self.sb(ps, "s1", [128, 24], F32)
            s2 = self.sb(ps, "s2", [128, 24], F32)
            st = [self.sb(ps, f"stt{i}", [128, 4], F32) for i in range(7)]
            Bs1, Bs2, Bst = Buf(), Buf(), Buf()
            sT_v = sT.ap.rearrange("(c p) t -> p c t", p=128)
            def spatial(ec):
                g = ec // 3
                b = ec % 2
                S.group("pe", [I("matmul", self.bank[b][:, c * 128:(c + 1) * 128],
                                 lhsT=vh[c][:, ec * 128:(ec + 1) * 128], rhs=wsT[:, g, :], start=True, stop=True)
                               for c in range(4)], reads=Bvh + [BwsT], writes=[self.B_bank[b]])
                S.op("dve", I("scalar_tensor_tensor", out=sTt[:, ec, :].rearrange("p (c t) -> p c t", c=4),
                              in0=self.bank[b][:].rearrange("p (c t) -> p c t", c=4), scalar=lng[:, ec:ec + 1],
                              in1=C[:, ec, :].unsqueeze(1).broadcast_to([128, 4, 128]), op0=ALU.mult, op1=ALU.add),
                     reads=[self.B_bank[b], Blng, BC], writes=[BsTt[ec]])

            self.lnt_load(L, 0)
            bk = 0
            prev_t = None
            for t in range(self.NTL):
                tb = t % 2
                S.op("dve", I("memset", s1[:], 0.0), writes=[Bs1])
                S.op("dve", I("memset", s2[:], 0.0), writes=[Bs2])
                for c in range(4):
                    for n in range(6):
                        gi = c * 6 + n
                        if prev_t is not None and 6 <= gi < 18:
                            for ec in (2 * (gi - 6), 2 * (gi - 6) + 1):
                                spatial(ec)
                            if gi == 17:
                                S.dma("sp", sT_v[:, :, prev_t * 512:(prev_t + 1) * 512], sTt[:], dsT, reads=BsTt,
                                      writes=[sT.bufs[prev_t]])
                        b = 2 + bk % 4
                        bk += 1
                        S.group("pe", [I("matmul", self.bank[b][:], lhsT=L.xnT[tb][:, k, c * 128:(c + 1) * 128],
                                         rhs=Wv[:, k, n * 512:(n + 1) * 512], start=(k == 0), stop=(k == 7))
                                       for k in range(8)], reads=WV.rb(n * 512, (n + 1) * 512) + L.BxnT[tb],
                                writes=[self.B_bank[b]])
                        S.op("act", I("activation", out=vf[tb][c][:, n * 512:(n + 1) * 512], in_=self.bank[b][:],
                                      func=AF.Gelu, accum_out=s1[:, gi:gi + 1]),
                             reads=[self.B_bank[b], Bs1], writes=[Bvf[tb][c]])
                        S.op("act", I("activation", out=sjunk[:], in_=vf[tb][c][:, n * 512:(n + 1) * 512],
                                      func=AF.Square, accum_out=s2[:, gi:gi + 1]),
                             reads=[Bvf[tb][c], Bs2], writes=[Bsjunk])
                        if 2 <= gi <= 10 and t + 1 < self.NTL:
                            if gi == 2:
                                micro = self.lnt_micro(L, t + 1)
                            micro[gi - 2]()
                Bs1.w = Bs2.w = Tok(S.eng["act"].sem, S.eng["act"].cnt)
                msum, mean, msq, var, rs, nmr, s2s = st
                S.op("dve", I("tensor_reduce", out=s2s[:], in_=s2[:].rearrange("p (c n) -> p c n", n=6),
                              axis=mybir.AxisListType.X, op=ALU.add), reads=[Bs2], writes=[Bst])
                S.op("dve", I("tensor_reduce", out=msum[:], in_=s1[:].rearrange("p (c n) -> p c n", n=6),
                              axis=mybir.AxisListType.X, op=ALU.add), reads=[Bs1], writes=[Bst])
                S.op("dve", I("tensor_scalar", out=mean[:], in0=msum[:], scalar1=1.0 / SG, scalar2=None, op0=ALU.mult),
                     reads=[Bst], writes=[Bst])
                S.op("dve", I("tensor_tensor", out=msq[:], in0=mean[:], in1=mean[:], op=ALU.mult),
                     reads=[Bst], writes=[Bst])
                S.op("dve", I("scalar_tensor_tensor", out=var[:], in0=s2s[:], scalar=1.0 / SG, in1=msq[:],
                              op0=ALU.mult, op1=ALU.subtract), reads=[Bst, Bs2], writes=[Bst])
                S.op("act", I("activation", out=var[:], in_=var[:], func=AF.Sqrt, bias=EPS, scale=1.0),
                     reads=[Bst], writes=[Bst])
                S.op("dve", I("reciprocal", out=rs[:], in_=var[:]), reads=[Bst], writes=[Bst])
                S.op("dve", I("scalar_tensor_tensor", out=nmr[:], in0=mean[:], scalar=-1.0, in1=rs[:],
                              op0=ALU.mult, op1=ALU.mult), reads=[Bst], writes=[Bst])
                for c in range(4):
                    S.op("dve", I("tensor_scalar", out=vh[c][:], in0=vf[tb][c][:], scalar1=rs[:, c:c + 1],
                                  scalar2=nmr[:, c:c + 1], op0=ALU.mult, op1=ALU.add),
                         reads=[Bvf[tb][c], Bst], writes=[Bvh[c]])
                prev_t = t
            for ec in range(NE):
                spatial(ec)
            S.dma("sp", sT_v[:, :, prev_t * 512:(prev_t + 1) * 512], sTt[:], dsT, reads=BsTt,
                  writes=[sT.bufs[prev_t]])
            S.flush()

    def phase_sgu_b(self, src, dst, ss_in, B_ss_in, ss_out, B_ss_out, sT):
        S = self.S
        with ExitStack() as ps:
            WU = self.load_w(ps, "Wuin", 8, SG, "col", [(n * 768, (n + 1) * 768) for n in range(4)], key="sgu_in")
            WO = self.load_w(ps, "Wo", NE, D, "row", [(n * 6, (n + 1) * 6) for n in range(4)], key="sgu_out")
            Wu, Wo = WU.tile, WO.tile
            self.compute_rstd(ss_in, B_ss_in, ps)
            L = self.lnt_alloc(ps, src, self.norm_w[0, 1])
            R = self.resid_alloc(ps, src, dst, ss_out, B_ss_out)
            sTt = [self.sb(ps, f"sTb{i}", [128, NE, 512], BF16) for i in range(2)]
            BsTt = [[Buf() for _ in range(NE)] for _ in range(2)]
            dsT = [S.dsem("sTld") for _ in range(2)]
            ug = [self.sb(ps, f"ug{i}", [128, 512], BF16) for i in range(2)]
            Bug = [Buf(), Buf()]
            sT_v = sT.ap.rearrange("(c p) t -> p c t", p=128)
            self.lnt_load(L, 0)
            S.dma("sp", sTt[0][:], sT_v[:, :, 0:512], dsT[0], reads=[sT.bufs[0]], writes=BsTt[0])
            gu = 0
            dn = 0
            for t in range(self.NTL):
                tb = t % 2
                for ec in range(NE):
                    b = gu % 4
                    gu += 1
                    S.group("pe", [I("matmul", self.bank[b][:], lhsT=Wu[:, k, ec * 128:(ec + 1) * 128],
                                     rhs=L.xnT[tb][:, k, :], start=(k == 0), stop=(k == 7)) for k in range(8)],
                            reads=WU.rb(ec * 128, ec * 128 + 128) + L.BxnT[tb], writes=[self.B_bank[b]])
                    uk = ec % 2
                    S.op("act", I("activation", out=ug[uk][:], in_=self.bank[b][:], func=AF.Gelu),
                         reads=[self.B_bank[b]], writes=[Bug[uk]])
                    S.op("pool", I("tensor_tensor", out=sTt[tb][:, ec, :], in0=ug[uk][:], in1=sTt[tb][:, ec, :],
                                   op=ALU.mult), reads=[Bug[uk]], writes=[BsTt[tb][ec]])
                    if 2 <= ec <= 10 and t + 1 < self.NTL:
                        if ec == 2:
                            micro = self.lnt_micro(L, t + 1)
                        micro[ec - 2]()
                    if ec == 10 and t + 1 < self.NTL:
                        S.dma("sp", sTt[1 - tb][:], sT_v[:, :, (t + 1) * 512:(t + 2) * 512], dsT[1 - tb],
                              reads=[sT.bufs[t + 1]], writes=BsTt[1 - tb])
                for s in range(4):
                    i = t * 4 + s
                    xk = self.resid_begin(R, i)
                    for nh in range(2):
                        bo = 4 + dn % 3
                        dn += 1
                        S.group("pe", [I("matmul", self.bank[bo][:], lhsT=sTt[tb][:, ec, s * 128:(s + 1) * 128],
                                         rhs=Wo[:, ec, nh * 512:(nh + 1) * 512], start=(ec == 0), stop=(ec == NE - 1))
                                       for ec in range(NE)], reads=WO.B + BsTt[tb], writes=[self.B_bank[bo]])
                        self.resid_add(R, xk, nh, bo, 1.0)
                    self.resid_end(R, xk, i)
            self.resid_finish(R)
            S.flush()


    def hgrn_consts(self, es):
        S, NB = self.S, self.NB
        self.Dd = [self.sb(es, f"Dd{d}", [128, NH, NB], F32) for d in range(2)]
        self.B_Dd = [Buf(), Buf()]
        self.carry_t = self.sb(es, "carry_t", [128, 2 * NB], F32)
        self.B_carry = Buf()
        S.dma("sp", self.carry_t[:], self.carry, S.dsem("carry"), writes=[self.B_carry])
        self.mask = [self.sb(es, f"mask{d}", [128, 128], F32) for d in range(2)]
        self.B_mask = Buf()
        for d in range(2):
            S.op("pool", I("memset", self.mask[d][:], 1.0), writes=[self.B_mask])
        S.op("pool", I("affine_select", out=self.mask[0][:], in_=self.mask[0][:], pattern=[[1, 128]],
                       compare_op=ALU.is_ge, fill=0.0, base=0, channel_multiplier=-1), writes=[self.B_mask])
        S.op("pool", I("affine_select", out=self.mask[1][:], in_=self.mask[1][:], pattern=[[-1, 128]],
                       compare_op=ALU.is_ge, fill=0.0, base=0, channel_multiplier=1), writes=[self.B_mask])
        self.mreset = self.sb(es, "mreset", [128, 512], F32)
        self.B_mreset = Buf()
        S.op("pool", I("memset", self.mreset[:], 1.0), writes=[self.B_mreset])
        for c in range(4):
            S.op("pool", I("memset", self.mreset[:, c * 128:c * 128 + 1], 0.0), writes=[self.B_mreset])
        self.onesb = self.sb(es, "onesb", [128, 128], BF16)
        self.B_onesb = Buf()
        S.op("pool", I("memset", self.onesb[:], 1.0), writes=[self.B_onesb])

    def phase_hgrn_a(self, src, ss_in, B_ss_in, H):
        S = self.S
        with ExitStack() as ps:
            WIN = self.load_w(ps, "Whin", 8, 5 * D, "col", [(c * D, (c + 1) * D) for c in (0, 4, 3, 1, 2)],
                              key="hg_in")
            Win = WIN.tile
            self.compute_rstd(ss_in, B_ss_in, ps)
            L = self.lnt_alloc(ps, src, self.norm_w[1, 1], evac_eng="dve")
            noml, lnoml, lbc, Blb = [], [], [], Buf()
            for d in range(2):
                r0, B0 = self.load_cols(ps, f"r0{d}", self.hgrn_lb_raw[d, 0], 8)
                r1, B1 = self.load_cols(ps, f"r1{d}", self.hgrn_lb_raw[d, 1], 8)
                tmp = self.sb(ps, f"lbt{d}", [128, 8], F32)
                nm = self.sb(ps, f"noml{d}", [128, 8], F32)
                lo = self.sb(ps, f"lnoml{d}", [128, 8], F32)
                S.op("dve", I("tensor_tensor", out=tmp[:], in0=r0[:], in1=r1[:], op=ALU.subtract),
                     reads=[B0, B1], writes=[Blb])
                S.op("act", I("activation", out=tmp[:], in_=tmp[:], func=AF.Exp), reads=[Blb], writes=[Blb])
                S.op("dve", I("tensor_scalar", out=tmp[:], in0=tmp[:], scalar1=1.0, scalar2=None, op0=ALU.add),
                     reads=[Blb], writes=[Blb])
                S.op("dve", I("reciprocal", out=tmp[:], in_=tmp[:]), reads=[Blb], writes=[Blb])
                lbc_ = self.sb(ps, f"lbc{d}", [128, 8], F32)
                S.op("dve", I("tensor_copy", out=lbc_[:], in_=tmp[:]), reads=[Blb], writes=[Blb])
                lbc.append(lbc_)
                S.op("dve", I("tensor_scalar", out=nm[:], in0=tmp[:], scalar1=1.0, scalar2=None, op0=ALU.subtract),
                     reads=[Blb], writes=[Blb])
                S.op("dve", I("tensor_scalar", out=tmp[:], in0=nm[:], scalar1=-1.0, scalar2=None, op0=ALU.mult),
                     reads=[Blb], writes=[Blb])
                S.op("act", I("activation", out=lo[:], in_=tmp[:], func=AF.Ln), reads=[Blb], writes=[Blb])
                noml.append(nm)
                lnoml.append(lo)
            qs = self.sb(ps, "qs", [128, NH, 512], F32)
            Bqs = [Buf() for _ in range(NH)]
            gso = [self.sb(ps, f"gso{i}", [128, 512], BF16) for i in range(2)]
            Bgso = [Buf(), Buf()]
            dgso = [S.dsem("gso") for _ in range(2)]
            vtok = self.sb(ps, "vtok", [128, 4, D], BF16)
            Bvtok = [Buf() for _ in range(4)]
            dvtok = S.dsem("vtok")
            kstok = [self.sb(ps, f"kstok{d}", [128, 4, D], BF16) for d in range(2)]
            Bkstok = [[Buf() for _ in range(NH)] for _ in range(2)]
            dkstok = [S.dsem("kstok") for _ in range(2)]
            NR = 3
            qeo = [self.sb(ps, f"qeo{i}", [128, 512], BF16) for i in range(NR)]
            kdo = [self.sb(ps, f"kdo{i}", [128, 512], BF16) for i in range(NR)]
            Bqeo = [Buf() for _ in range(NR)]
            Bkdo = [Buf() for _ in range(NR)]
            dqeo = [S.dsem("qeo") for _ in range(NR)]
            dkdo = [S.dsem("kdo") for _ in range(NR)]
            sl = [self.sb(ps, f"sl{i}", [128, 512], F32) for i in range(2)]
            Bsl = [Buf(), Buf()]
            slc = 0
            NTMP = 3
            TN = ("te", "tA", "tB", "tP", "tX")
            T_ = [{n: self.sb(ps, f"{n}{i}", [128, 512], F32) for n in TN} for i in range(NTMP)]
            BT = [{n: Buf() for n in TN + ("tks",)} for i in range(NTMP)]
            tks = [self.sb(ps, f"tks{i}", [128, 512], BF16) for i in range(NTMP)]
            self.lnt_load(L, 0)
            bk = 0
            ro = 0
            hd = 0
            for t in range(self.NTL):
                tb = t % 2
                tok = slice(t * 512, (t + 1) * 512)
                def sgroup_pe(which, h):
                    nonlocal bk
                    col = (0 if which == 0 else 4 * D) + h * 128
                    b = bk % 6
                    bk += 1
                    S.group("pe", [I("matmul", self.bank[b][:], lhsT=Win[:, k, col:col + 128],
                                     rhs=L.xnT[tb][:, k, :], start=(k == 0), stop=(k == 7)) for k in range(8)],
                            reads=WIN.rb(col, col + 128) + L.BxnT[tb], writes=[self.B_bank[b]])
                    return b

                def sgroup_act(which, h, b):
                    nonlocal slc
                    k_ = slc % 2
                    slc += 1
                    S.op("act", I("activation", out=sl[k_][:], in_=self.bank[b][:], func=AF.Exp, scale=-1.0),
                         reads=[self.B_bank[b]], writes=[Bsl[k_]])
                    S.op("act", I("activation", out=sl[k_][:], in_=sl[k_][:], func=AF.Ln, bias=1.0, scale=1.0),
                         reads=[Bsl[k_]], writes=[Bsl[k_]])
                    S.op("act", I("activation", out=sl[k_][:], in_=sl[k_][:], func=AF.Exp, scale=-1.0),
                         reads=[Bsl[k_]], writes=[Bsl[k_]])
                    if which == 0:
                        S.op("dve", I("tensor_tensor", out=qs[:, h, :], in0=self.bank[b][:], in1=sl[k_][:], op=ALU.mult),
                             reads=[self.B_bank[b], Bsl[k_]], writes=[Bqs[h]])
                    else:
                        gk = h % 2
                        S.op("dve", I("tensor_tensor", out=gso[gk][:], in0=self.bank[b][:], in1=sl[k_][:], op=ALU.mult),
                             reads=[self.B_bank[b], Bsl[k_]], writes=[Bgso[gk]])
                        S.dma("sp", H["gs"].ap[t, :, h * 512:(h + 1) * 512], gso[gk][:], dgso[gk],
                              reads=[Bgso[gk]], writes=[H["gs"].bufs[t][h]])

                def vgroup(c, n):
                    nonlocal bk
                    b = bk % 6
                    bk += 1
                    S.group("pe", [I("matmul", self.bank[b][:], lhsT=L.xnT[tb][:, k, c * 128:(c + 1) * 128],
                                     rhs=Win[:, k, 3 * D + n * 512:3 * D + (n + 1) * 512], start=(k == 0),
                                     stop=(k == 7)) for k in range(8)],
                            reads=WIN.rb(3 * D, 4 * D) + L.BxnT[tb], writes=[self.B_bank[b]])
                    S.op("dve", I("tensor_copy", out=vtok[:, c, n * 512:(n + 1) * 512], in_=self.bank[b][:]),
                         reads=[self.B_bank[b]], writes=[Bvtok[c]])
                    if c == 3 and n == 1:
                        S.dma("sp", H["v"].ap[t], vtok[:].rearrange("p c f -> p (c f)"), dvtok, reads=Bvtok,
                              writes=H["v"].bufs[t])

                vlist = [(c, n) for c in range(4) for n in range(2)]
                items = [(d, h) for d in range(2) for h in range(NH)]

                def stage0(d, h):
                    nonlocal bk
                    col = (1 + d) * D + h * 128
                    b = bk % 6
                    bk += 1
                    S.group("pe", [I("matmul", self.bank[b][:], lhsT=Win[:, k, col:col + 128],
                                     rhs=L.xnT[tb][:, k, :], start=(k == 0), stop=(k == 7)) for k in range(8)],
                            reads=WIN.rb(col, col + 128) + L.BxnT[tb], writes=[self.B_bank[b]])
                    return b

                def stage1(d, h, i_, b):
                    T, B_ = T_[i_], BT[i_]
                    S.op("act", I("activation", out=T["te"][:], in_=self.bank[b][:], func=AF.Exp),
                         reads=[self.B_bank[b]], writes=[B_["te"]])
                    S.op("act", I("activation", out=T["tA"][:], in_=T["te"][:], func=AF.Ln, bias=lbc[d][:, h:h + 1],
                                  scale=1.0), reads=[B_["te"], Blb], writes=[B_["tA"]])
                    S.op("act", I("activation", out=T["tB"][:], in_=T["te"][:], func=AF.Ln, bias=1.0, scale=1.0),
                         reads=[B_["te"]], writes=[B_["tB"]])
                    S.op("dve", I("tensor_tensor", out=T["tA"][:], in0=T["tA"][:], in1=T["tB"][:], op=ALU.subtract),
                         reads=[B_["tB"]], writes=[B_["tA"]])
                    S.op("dve", I("tensor_tensor_scan", out=T["tP"][:], data0=self.mreset[:], data1=T["tA"][:],
                                  initial=0.0, op0=ALU.mult, op1=ALU.add),
                         reads=[B_["tA"], self.B_mreset], writes=[B_["tP"]])
                    v4 = lambda ap: ap.rearrange("p (c t) -> p c t", c=4)
                    Tb = v4(T["tP"][:])[:, :, 127:128].broadcast_to([128, 4, 128])
                    if d == 0:
                        S.op("pool", I("tensor_tensor", out=T["te"][:], in0=T["tB"][:], in1=T["tP"][:], op=ALU.add),
                             reads=[B_["tB"], B_["tP"]], writes=[B_["te"]])
                        S.op("pool", I("tensor_tensor", out=v4(T["tX"][:]), in0=Tb, in1=v4(T["te"][:]),
                                       op=ALU.subtract), reads=[B_["tP"], B_["te"]], writes=[B_["tX"]])
                    else:
                        S.op("dve", I("tensor_tensor", out=T["tA"][:], in0=T["tP"][:], in1=T["tA"][:],
                                      op=ALU.subtract), reads=[B_["tP"]], writes=[B_["tA"]])
                        S.op("pool", I("tensor_tensor", out=T["tB"][:], in0=T["tA"][:], in1=T["tB"][:],
                                       op=ALU.subtract), reads=[B_["tA"]], writes=[B_["tB"]])
                        S.op("pool", I("tensor_tensor", out=v4(T["tA"][:]), in0=Tb, in1=v4(T["tA"][:]),
                                       op=ALU.subtract), reads=[B_["tP"]], writes=[B_["tA"]])
                        S.op("pool", I("tensor_tensor", out=v4(T["te"][:]), in0=Tb, in1=v4(T["tB"][:]),
                                       op=ALU.subtract), reads=[B_["tP"], B_["tB"]], writes=[B_["te"]])

                def stage2(d, h, i_):
                    nonlocal ro
                    T, B_ = T_[i_], BT[i_]
                    oml_b = lnoml[d][:, h:h + 1]
                    r_ = ro % NR
                    ro += 1
                    S.op("act", I("activation", out=self.Dd[d][:, h, t * 4:(t + 1) * 4],
                                  in_=T["tP"][:, 127:512:128], func=AF.Exp), reads=[B_["tP"]])
                    qarg, qB = (T["tP"], B_["tP"]) if d == 0 else (T["tA"], B_["tA"])
                    ksarg, ksB = (T["tX"], B_["tX"]) if d == 0 else (T["tB"], B_["tB"])
                    S.op("act", I("activation", out=qarg[:], in_=qarg[:], func=AF.Exp), reads=[qB], writes=[qB])
                    S.op("act", I("activation", out=kdo[r_][:], in_=T["te"][:], func=AF.Exp, scale=-1.0, bias=oml_b),
                         reads=[B_["te"], Blb], writes=[Bkdo[r_]])
                    S.op("act", I("activation", out=tks[i_][:], in_=ksarg[:], func=AF.Exp, bias=oml_b),
                         reads=[ksB, Blb], writes=[B_["tks"]])
                    S.op("dve", I("tensor_tensor", out=qeo[r_][:], in0=qs[:, h, :], in1=qarg[:], op=ALU.mult),
                         reads=[Bqs[h], qB], writes=[Bqeo[r_]])
                    S.dma("sp", H["qe"][d].ap[t, :, h * 512:(h + 1) * 512], qeo[r_][:], dqeo[r_],
                          reads=[Bqeo[r_]], writes=[H["qe"][d].bufs[t][h]])
                    S.dma("sp", H["kd"][d].ap[t, :, h * 512:(h + 1) * 512], kdo[r_][:], dkdo[r_],
                          reads=[Bkdo[r_]], writes=[H["kd"][d].bufs[t][h]])
                    pt = self.bank[6 + (h % 2)][:].bitcast(BF16)
                    S.group("pe", [I("transpose", out=pt[:, c * 128:(c + 1) * 128],
                                     in_=tks[i_][:, c * 128:(c + 1) * 128], identity=self.ident[:])
                                   for c in range(4)], reads=[B_["tks"], self.B_ident],
                            writes=[self.B_bank[6 + (h % 2)]])
                    S.op("dve", I("tensor_copy", out=kstok[d][:, :, h * 128:(h + 1) * 128],
                                  in_=pt[:, 0:512].rearrange("p (c k) -> p c k", c=4)),
                         reads=[self.B_bank[6 + (h % 2)]], writes=[Bkstok[d][h]])
                    if h == NH - 1:
                        S.dma("sp", H["ks"][d].ap[t], kstok[d][:].rearrange("p c f -> p (c f)"), dkstok[d], reads=Bkstok[d],
                              writes=H["ks"][d].bufs[t])

                AH = 3
                extras = [(0, h_) for h_ in range(3, 8)] + [(1, h_) for h_ in range(8)]
                for h0 in range(3):
                    sgroup_act(0, h0, sgroup_pe(0, h0))
                banks_ = [stage0(*items[q]) for q in range(AH)]
                xb_ = sgroup_pe(*extras[0])
                stage1(*items[0], hd % NTMP, banks_[0])
                stage1(*items[1], (hd + 1) % NTMP, banks_[1])
                for j, (d, h) in enumerate(items):
                    if j + AH < len(items):
                        banks_.append(stage0(*items[j + AH]))
                    if j + 2 < len(items):
                        stage1(*items[j + 2], (hd + 2) % NTMP, banks_[j + 2])
                    nxb_ = sgroup_pe(*extras[j + 1]) if j + 1 < len(extras) else None
                    stage2(d, h, hd % NTMP)
                    hd += 1
                    if j < len(extras):
                        sgroup_act(*extras[j], xb_)
                    xb_ = nxb_
                    if j % 2 == 1:
                        vgroup(*vlist[j // 2])
                    if 1 <= j <= 9 and t + 1 < self.NTL:
                        if j == 1:
                            micro = self.lnt_micro(L, t + 1)
                        micro[j - 1]()
            self.B_Dd[0].w = self.B_Dd[1].w = Tok(S.eng["act"].sem, S.eng["act"].cnt)
            S.flush()

    def phase_hgrn_scan(self, d, H, src=None, dst=None, ss_out=None, B_ss_out=None):
        S, NB, NTL = self.S, self.NB, self.NTL
        with ExitStack() as ps:
            NO = 2
            NG = 3
            qeT = [self.sb(ps, f"qeT{i}", [128, NH, 512], BF16) for i in range(2)]
            kdT = [self.sb(ps, f"kdT{i}", [128, NH, 512], BF16) for i in range(2)]
            kst = [self.sb(ps, f"kst{i}", [128, 4, D], BF16) for i in range(2)]
            vt = [self.sb(ps, f"vt{i}", [128, 4, D], BF16) for i in range(2)]
            oT = [self.sb(ps, f"oT{i}", [128, NH, 512], F32 if d == 1 else BF16) for i in range(NO)]
            Bld = [[Buf() for _ in range(4)] for _ in range(2)]
            BoT = [[Buf() for _ in range(NH)] for _ in range(NO)]
            dld = [[S.dsem("scld") for _ in range(4)] for _ in range(2)]
            doT = [S.dsem("oT") for _ in range(NO)]
            Am = [self.sb(ps, f"Am{i}", [128, 4, 128], BF16) for i in range(2)]
            BAm = [Buf(), Buf()]
            St = self.sb(ps, "St", [128, NH, 128], F32)
            BSt = Buf()
            Sb = [self.sb(ps, f"Sb{i}", [128, NH, 128], BF16) for i in range(2)]
            BSb = [Buf(), Buf()]
            S.op("pool", I("memset", St[:], 0.0), writes=[BSt])
            S.op("pool", I("memset", Sb[0][:], 0.0), writes=[BSb[0]])
            S.op("pool", I("memset", Sb[1][:], 0.0), writes=[BSb[1]])
            flat = lambda tile_: tile_[:].rearrange("p a b -> p (a b)")
            if d == 1:
                WO = self.load_w(ps, "Who", NH, D, "row", [(0, NH)], key="hg_out")
                Wo = WO.tile
                nw, Bnw = self.load_cols(ps, "hnw", self.hgrn_norm_w, NH)
                for h in range(NH):
                    S.op("dve", I("tensor_scalar", out=Wo[:, h, :], in0=Wo[:, h, :], scalar1=nw[:, h:h + 1],
                                  scalar2=None, op0=ALU.mult), reads=WO.B + [Bnw], writes=WO.B)
                gsT = [self.sb(ps, f"gsT{i}", [128, NH, 512], BF16) for i in range(NG)]
                Bgs = [Buf() for _ in range(NG)]
                dgs = [S.dsem("gsld") for _ in range(NG)]
                ofb = [self.sb(ps, f"ofb{i}", [128, NH, 512], BF16) for i in range(2)]
                Bofb = [Buf(), Buf()]
                dofb = [S.dsem("ofb") for _ in range(2)]
                sq = [self.sb(ps, f"sq{i}", [128, 512], BF16) for i in range(2)]
                Bsq = [Buf(), Buf()]
                rt = [self.sb(ps, f"rt{i}", [128, 512], F32) for i in range(2)]
                Brt = [Buf(), Buf()]
                onT = [self.sb(ps, f"onT{i}", [128, NH, 512], BF16) for i in range(2)]
                BonT = [[Buf() for _ in range(NH)] for _ in range(2)]
                R = self.resid_alloc(ps, src, dst, ss_out, B_ss_out)

            order = list(range(NTL)) if d == 0 else list(range(NTL - 1, -1, -1))

            def issue_loads(j, parts=(0, 1, 2, 3, 4, 5)):
                t = order[j]
                p = j % 2
                po = j % NO
                tok = slice(t * 512, (t + 1) * 512)
                if 0 in parts:
                    S.dma("sp", flat(qeT[p]), H["qe"][d].ap[t], dld[p][0], reads=H["qe"][d].bufs[t], writes=[Bld[p][0]])
                if 1 in parts:
                    S.dma("sp", flat(kdT[p]), H["kd"][d].ap[t], dld[p][1], reads=H["kd"][d].bufs[t], writes=[Bld[p][1]])
                if 2 in parts:
                    S.dma("sp", flat(kst[p]), H["ks"][d].ap[t], dld[p][2], reads=H["ks"][d].bufs[t],
                          writes=[Bld[p][2]])
                if 3 in parts:
                    S.dma("sp", flat(vt[p]), H["v"].ap[t], dld[p][3], reads=H["v"].bufs[t],
                          writes=[Bld[p][3]])
                if d == 1 and 4 in parts:
                    S.dma("sp", flat(ofb[p]), H["of"].ap[t], dofb[p], reads=H["of"].bufs[t], writes=[Bofb[p]])
                if d == 1 and 5 in parts:
                    S.dma("sp", flat(gsT[j % NG]), H["gs"].ap[t], dgs[j % NG], reads=H["gs"].bufs[t],
                          writes=[Bgs[j % NG]])

            class TW:
                pass

            def tile_work(j):
                w = TW()
                w.t = order[j]
                w.po = j % NO
                w.pg = j % NG
                w.pn = j % 2
                w.xk = {}
                return w

            def sqr(w, h):
                k2 = h % 2
                S.op("act", I("activation", out=sq[k2][:], in_=oT[w.po][:, h, :], func=AF.Square),
                     reads=[BoT[w.po][h]], writes=[Bsq[k2]])

            def mmn(w, h):
                k2 = h % 2
                S.group("pe", [I("matmul", self.bank[6][:], lhsT=self.onesb[:], rhs=sq[k2][:], start=True, stop=True)],
                        reads=[Bsq[k2], self.B_onesb], writes=[self.B_bank[6]])
                S.op("act", I("activation", out=rt[k2][:], in_=self.bank[6][:], func=AF.Ln, bias=EPS,
                              scale=1.0 / 128), reads=[self.B_bank[6]], writes=[Brt[k2]])
                S.op("act", I("activation", out=rt[k2][:], in_=rt[k2][:], func=AF.Exp, scale=-0.5),
                     reads=[Brt[k2]], writes=[Brt[k2]])

            def fin(w, h):
                k2 = h % 2
                S.op("pool", I("tensor_tensor", out=rt[k2][:], in0=oT[w.po][:, h, :], in1=rt[k2][:], op=ALU.mult),
                     reads=[BoT[w.po][h], Brt[k2]], writes=[Brt[k2]])
                S.op("pool", I("tensor_tensor", out=onT[w.pn][:, h, :], in0=rt[k2][:], in1=gsT[w.pg][:, h, :],
                               op=ALU.mult), reads=[Brt[k2], Bgs[w.pg]], writes=[BonT[w.pn][h]])

            def outp(w, s_, nh, pe_only=False, dve_only=False, load_only=False):
                i = w.t * 4 + s_
                if load_only:
                    w.xk[s_] = self.resid_begin(R, i)
                    return
                if not dve_only:
                    if nh == 0 and s_ not in w.xk:
                        w.xk[s_] = self.resid_begin(R, i)
                    S.group("pe", [I("matmul", self.bank[7][:], lhsT=onT[w.pn][:, h, s_ * 128:(s_ + 1) * 128],
                                     rhs=Wo[:, h, nh * 512:(nh + 1) * 512], start=(h == 0), stop=(h == NH - 1))
                                   for h in range(NH)], reads=WO.B + BonT[w.pn], writes=[self.B_bank[7]])
                if pe_only:
                    return
                xk = w.xk[s_]
                self.resid_add(R, xk, nh, 7, 1.0)
                if nh == 1:
                    self.resid_end(R, xk, i)

            issue_loads(0)
            sbi = 0
            normW = None
            outW = None
            for j in range(NTL):
                t = order[j]
                p = j % 2
                po = j % NO
                if d == 0 and j + 1 < NTL:
                    issue_loads(j + 1)
                corder = range(4) if d == 0 else range(3, -1, -1)
                for ci, c in enumerate(corder):
                    n = t * 4 + c
                    cs = slice(c * 128, (c + 1) * 128)
                    if normW is not None:
                        sqr(normW, 2 * ci)
                        sqr(normW, 2 * ci + 1)
                    if d == 1 and outW is not None:
                        outp(outW, ci, 0, load_only=True)
                    if d == 1 and j + 1 < NTL and ci < 3:
                        issue_loads(j + 1, parts=((0, 1), (2, 3), (4, 5))[ci])
                    for hb in range(2):
                        S.group("pe", [I("matmul", self.bank[hb][:, q * 128:(q + 1) * 128],
                                         lhsT=kdT[p][:, hb * 4 + q, cs], rhs=qeT[p][:, hb * 4 + q, cs],
                                         start=True, stop=True) for q in range(4)],
                                reads=[Bld[p][0], Bld[p][1]], writes=[self.B_bank[hb]])
                    for hb in range(2):
                        S.group("pe", [I("matmul", self.bank[2 + hb][:, q * 128:(q + 1) * 128],
                                         lhsT=kst[p][:, c, (hb * 4 + q) * 128:(hb * 4 + q + 1) * 128],
                                         rhs=vt[p][:, c, (hb * 4 + q) * 128:(hb * 4 + q + 1) * 128],
                                         start=True, stop=True) for q in range(4)],
                                reads=[Bld[p][2], Bld[p][3]], writes=[self.B_bank[2 + hb]])
                    if normW is not None:
                        mmn(normW, 2 * ci)
                    if outW is not None:
                        outp(outW, ci, 0, pe_only=True)
                    for hb in range(2):
                        S.op("dve", I("tensor_tensor", out=Am[hb][:],
                                      in0=self.bank[hb][:].rearrange("p (q t) -> p q t", q=4),
                                      in1=self.mask[d][:].unsqueeze(1).broadcast_to([128, 4, 128]), op=ALU.mult),
                             reads=[self.B_bank[hb], self.B_mask], writes=[BAm[hb]])
                    sp_ = sbi % 2
                    for hb in range(2):
                        fns = []
                        for q in range(4):
                            h = hb * 4 + q
                            fns.append(I("matmul", self.bank[4 + hb][:, q * 128:(q + 1) * 128],
                                         lhsT=vt[p][:, c, h * 128:(h + 1) * 128], rhs=Am[hb][:, q, :],
                                         start=True, stop=False))
                            fns.append(I("matmul", self.bank[4 + hb][:, q * 128:(q + 1) * 128],
                                         lhsT=Sb[sp_][:, h, :], rhs=qeT[p][:, h, cs], start=False, stop=True))
                        S.group("pe", fns, reads=[Bld[p][3], Bld[p][0], BAm[hb], BSb[sp_]],
                                writes=[self.B_bank[4 + hb]])
                    S.op("dve", I("tensor_tensor", out=St[:], in0=St[:],
                                  in1=self.Dd[d][:, :, n:n + 1].broadcast_to([128, NH, 128]), op=ALU.mult),
                         reads=[self.B_Dd[d]], writes=[BSt])
                    for hb in range(2):
                        S.op("dve", I("tensor_tensor", out=St[:, hb * 4:(hb + 1) * 4, :],
                                      in0=St[:, hb * 4:(hb + 1) * 4, :],
                                      in1=self.bank[2 + hb][:].rearrange("p (q t) -> p q t", q=4), op=ALU.add),
                             reads=[self.B_bank[2 + hb]], writes=[BSt])
                    nxt = n + 1 if d == 0 else n - 1
                    if 0 <= nxt < NB:
                        bnd = (nxt % 16 == 0) if d == 0 else (nxt % 16 == 15)
                        if bnd:
                            ccol = nxt if d == 0 else NB + nxt
                            S.op("dve", I("tensor_scalar", out=St[:], in0=St[:],
                                          scalar1=self.carry_t[:, ccol:ccol + 1], scalar2=None, op0=ALU.mult),
                                 reads=[self.B_carry], writes=[BSt])
                    sbi += 1
                    S.op("act", I("activation", out=Sb[sbi % 2][:], in_=St[:], func=AF.Copy),
                         reads=[BSt], writes=[BSb[sbi % 2]])
                    if outW is not None:
                        outp(outW, ci, 0, dve_only=True)
                    if normW is not None:
                        mmn(normW, 2 * ci + 1)
                    if outW is not None:
                        outp(outW, ci, 1)
                    for hb in range(2):
                        ov = oT[po][:, hb * 4:(hb + 1) * 4, cs]
                        bv = self.bank[4 + hb][:].rearrange("p (q t) -> p q t", q=4)
                        if d == 0:
                            S.op("act", I("activation", out=ov, in_=bv, func=AF.Copy),
                                 reads=[self.B_bank[4 + hb]], writes=BoT[po][hb * 4:(hb + 1) * 4])
                        else:
                            S.op("dve", I("tensor_tensor", out=ov, in0=bv, in1=ofb[p][:, hb * 4:(hb + 1) * 4, cs],
                                          op=ALU.add),
                                 reads=[self.B_bank[4 + hb], Bofb[p]], writes=BoT[po][hb * 4:(hb + 1) * 4])
                    if normW is not None:
                        fin(normW, 2 * ci)
                        fin(normW, 2 * ci + 1)
                if d == 0:
                    S.dma("sp", H["of"].ap[t], flat(oT[po]), doT[po], reads=BoT[po],
                          writes=H["of"].bufs[t])
                else:
                    outW = normW
                    normW = tile_work(j)
            if d == 1:
                if outW is not None:
                    for s_ in range(4):
                        outp(outW, s_, 0)
                        outp(outW, s_, 1)
                for h in range(NH):
                    sqr(normW, h)
                    mmn(normW, h)
                    fin(normW, h)
                for s_ in range(4):
                    outp(normW, s_, 0)
                    outp(normW, s_, 1)
                self.resid_finish(R)
            S.flush()

    def phase_final(self, src, ss, B_ss, copy_only=False):
        S = self.S
        with ExitStack() as ps:
            self.compute_rstd(ss, B_ss, ps)
            wb = self.sb(ps, "fwb", [128, D], F32)
            Bwb = Buf()
            dwb = S.dsem("fwb")
            S.dma("sp", wb[:], self.final_norm.partition_broadcast(128), dwb, writes=[Bwb])
            xl = [self.sb(ps, f"fx{i}", [128, D], F32) for i in range(3)]
            Bx = [Buf() for _ in range(3)]
            dx = [S.dsem("fx") for _ in range(3)]
            for i in range(self.NB):
                k = i % 3
                S.dma("sp", xl[k][:], src.ap[i * 128:(i + 1) * 128, :], dx[k], reads=[src.bufs[i]], writes=[Bx[k]])
                if not copy_only:
                    S.op("dve", I("scalar_tensor_tensor", out=xl[k][:], in0=xl[k][:],
                                                                          scalar=self.rstd[:, i:i + 1], in1=wb[:],
                                                                          op0=ALU.mult, op1=ALU.mult),
                         reads=[self.B_rstd, Bwb], writes=[Bx[k]])
                S.dma("sp", self.y_out.ap[i * 128:(i + 1) * 128, :], xl[k][:], dx[k], reads=[Bx[k]],
                      writes=[self.y_out.bufs[i]])
            S.flush()


W_NAMES = ["norm_w", "ffn_gate", "ffn_up", "ffn_down", "sgu_w_in", "sgu_ln_g", "sgu_ln_b", "sgu_w_s", "sgu_b_s",
           "sgu_w_out", "hgrn_w_in", "hgrn_lb_raw", "hgrn_norm_w", "hgrn_w_out", "final_norm"]
W_SQUEEZE = {"sgu_w_in", "sgu_ln_g", "sgu_ln_b", "sgu_w_s", "sgu_b_s", "sgu_w_out", "hgrn_w_in", "hgrn_norm_w",
             "hgrn_w_out"}


def make_carry(NB, seq_blocks):
    c = np.ones((128, 2 * NB), np.float32)
    for n in range(NB):
        if n % seq_blocks == 0:
            c[:, n] = 0.0
        if n % seq_blocks == seq_blocks - 1:
            c[:, NB + n] = 0.0
    return c


def kernel(**inputs):
    xp = np.ascontiguousarray(inputs["x_prompt"], dtype=np.float32)
    xs = np.ascontiguousarray(inputs["x_sample"], dtype=np.float32)
    NT = NT_FULL
    prog = Prog(NT)
    wmap = {}
    for n in W_NAMES:
        a = np.ascontiguousarray(inputs[n], dtype=np.float32)
        if n in W_SQUEEZE:
            a = a[0]
        wmap[n] = a
    in_maps = []
    for c in range(NCORES):
        m = dict(wmap)
        if c < 4:
            m["x"] = xp[4 * c:4 * c + 4].reshape(NT, D)
            m["carry"] = make_carry(NT // 128, SEQ_PROMPT // 128)
        else:
            m["x"] = xs[c - 4].reshape(NT, D)
            m["carry"] = make_carry(NT // 128, NT // 128)
        in_maps.append(m)
    res = run_bass_kernel_spmd(prog.nc, in_maps, core_ids=list(range(NCORES)))
    ys = [np.asarray(r["y"], dtype=np.float32) for r in res.results]
    y_prompt = np.stack(ys[:4]).reshape(16, 2048, D)
    y_sample = np.stack(ys[4:]).reshape(4, 8192, D)
    return (y_prompt, y_sample)
```
